# Optimizing a Trainium2 kernel written in Bass

```python
import jax, jax.numpy as jnp
from jax import lax
import numpy as np

D_MODEL = 1024
BATCH = 2
SEQ = 8192
DEPTH = 1

GRID_W = 64
N_MEM = 256
D_MIX = D_MODEL
CONV_CH = D_MIX // 2
CONV_GROUPS = 8
CONV_K = 31
HEAD_DIM = 64
ATT_HEADS = (D_MIX - CONV_CH) // HEAD_DIM
ATT_KV_HEADS = 2
ATT_WIDTH = ATT_HEADS * HEAD_DIM
KV_WIDTH = ATT_KV_HEADS * HEAD_DIM
IN_COLS = 2 * CONV_CH + ATT_WIDTH + 2 * KV_WIDTH
ROPE_THETA = 10000.0
Q_BLOCK = 128
MEM_HEADS = 4
MEM_HEAD_DIM = D_MODEL // MEM_HEADS
D_FF = 2816
FFN_CONV_K = 3
EPS = 1e-6

kernel_name = "hybrid_conv_gqa_axial_encoder_block"


def rmsnorm(x, g):
    xf = x.astype(jnp.float32)
    y = xf * lax.rsqrt(jnp.mean(xf * xf, axis=-1, keepdims=True) + EPS)
    return (y * g.astype(jnp.float32)).astype(x.dtype)


def layernorm(x, g, b):
    xf = x.astype(jnp.float32)
    mu = jnp.mean(xf, axis=-1, keepdims=True)
    var = jnp.mean(jnp.square(xf - mu), axis=-1, keepdims=True)
    y = (xf - mu) * lax.rsqrt(var + EPS)
    return (y * g.astype(jnp.float32) + b.astype(jnp.float32)).astype(x.dtype)


def dwconv_centred(x, w, b):
    C = x.shape[-1]
    K = w.shape[0]
    pad = (K - 1) // 2
    y = lax.conv_general_dilated(
        x, w[:, None, :].astype(x.dtype), window_strides=(1,), padding=[(pad, pad)],
        dimension_numbers=("NWC", "WIO", "NWC"), feature_group_count=C)
    return y + b.astype(x.dtype)


def axial_rope_tables(S):
    rows = S // GRID_W
    r = jnp.repeat(jnp.arange(rows, dtype=jnp.float32), GRID_W)
    c = jnp.tile(jnp.arange(GRID_W, dtype=jnp.float32), rows)
    axis_dim = HEAD_DIM // 2
    inv = ROPE_THETA ** (-jnp.arange(0, axis_dim, 2, dtype=jnp.float32) / axis_dim)
    ang_r = r[:, None] * inv[None, :]
    ang_c = c[:, None] * inv[None, :]
    return jnp.cos(ang_r), jnp.sin(ang_r), jnp.cos(ang_c), jnp.sin(ang_c)


def rotate_half(x, cos, sin):
    x1, x2 = jnp.split(x, 2, axis=-1)
    return jnp.concatenate([x1 * cos - x2 * sin, x2 * cos + x1 * sin], axis=-1)


def apply_axial_rope(x, tabs):
    cr, sr, cc, sc = tabs
    xf = x.astype(jnp.float32)
    xr, xc = jnp.split(xf, 2, axis=-1)
    out = jnp.concatenate([rotate_half(xr, cr, sr), rotate_half(xc, cc, sc)], axis=-1)
    return out.astype(x.dtype)


def gqa_blocked(q, k, v):
    B, H, S, d = q.shape
    G = H // ATT_KV_HEADS
    nb = S // Q_BLOCK
    qb = q.reshape(B, ATT_KV_HEADS, G, nb, Q_BLOCK, d).transpose(3, 0, 1, 2, 4, 5)
    scale = d ** -0.5

    def one_block(qblk):
        s = jnp.einsum("bkgqd,bksd->bkgqs", qblk, k).astype(jnp.float32) * scale
        p = jax.nn.softmax(s, axis=-1).astype(v.dtype)
        return jnp.einsum("bkgqs,bksd->bkgqd", p, v)

    o = lax.map(one_block, qb)
    return o.transpose(1, 0, 4, 2, 3, 5).reshape(B, S, H * d)


def parallel_mixer(h, w_in, conv_dw, conv_dw_b, conv_ln_g, conv_ln_b, q_norm_g, k_norm_g, w_out, tabs):
    B, S, _ = h.shape
    z = jnp.einsum("bsd,dc->bsc", h, w_in)
    o0 = 2 * CONV_CH
    a_val, a_gate = z[..., :CONV_CH], z[..., CONV_CH:o0]
    q = z[..., o0:o0 + ATT_WIDTH]
    k = z[..., o0 + ATT_WIDTH:o0 + ATT_WIDTH + KV_WIDTH]
    v = z[..., o0 + ATT_WIDTH + KV_WIDTH:]

    a = a_val * jax.nn.sigmoid(a_gate)
    a = dwconv_centred(a, conv_dw, conv_dw_b)
    a = layernorm(a, conv_ln_g, conv_ln_b)
    a = jax.nn.silu(a)

    q = q.reshape(B, S, ATT_HEADS, HEAD_DIM).transpose(0, 2, 1, 3)
    k = k.reshape(B, S, ATT_KV_HEADS, HEAD_DIM).transpose(0, 2, 1, 3)
    v = v.reshape(B, S, ATT_KV_HEADS, HEAD_DIM).transpose(0, 2, 1, 3)
    q = apply_axial_rope(rmsnorm(q, q_norm_g), tabs)
    k = apply_axial_rope(rmsnorm(k, k_norm_g), tabs)
    att = gqa_blocked(q, k, v)

    mixed = jnp.concatenate([a, att], axis=-1)
    return jnp.einsum("bsc,cd->bsd", mixed, w_out)


def memory_cross_attention(h, mem_n, w_mem_q, w_mem_kv, w_mem_o):
    B, S, _ = h.shape
    M = mem_n.shape[1]
    q = jnp.einsum("bsd,de->bse", h, w_mem_q).reshape(B, S, MEM_HEADS, MEM_HEAD_DIM)
    kv = jnp.einsum("bmd,de->bme", mem_n, w_mem_kv)
    k = kv[..., :D_MODEL].reshape(B, M, MEM_HEADS, MEM_HEAD_DIM)
    v = kv[..., D_MODEL:].reshape(B, M, MEM_HEADS, MEM_HEAD_DIM)
    s = jnp.einsum("bshd,bmhd->bhsm", q, k).astype(jnp.float32) * (MEM_HEAD_DIM ** -0.5)
    p = jax.nn.softmax(s, axis=-1).astype(v.dtype)
    o = jnp.einsum("bhsm,bmhd->bshd", p, v).reshape(B, S, D_MODEL)
    return jnp.einsum("bse,ed->bsd", o, w_mem_o)


def conv_gated_ffn(h, w_up, ffn_dw, ffn_dw_b, w_down):
    u = jnp.einsum("bsd,df->bsf", h, w_up)
    u = dwconv_centred(u, ffn_dw, ffn_dw_b)
    gate, val = u[..., :D_FF], u[..., D_FF:]
    return jnp.einsum("bsf,fd->bsd", jax.nn.gelu(gate, approximate=True) * val, w_down)


def setup_inputs(seed: int = 0) -> dict:
    key = jax.random.key(seed)
    ks = iter(jax.random.split(key, 32))
    f32 = jnp.float32

    def nrm(shape, scale):
        return jax.random.normal(next(ks), shape, f32) * scale

    def gain(shape):
        return 1.0 + 0.02 * jax.random.normal(next(ks), shape, f32)

    L = DEPTH
    return {
        "x": jax.random.normal(next(ks), (BATCH, SEQ, D_MODEL), f32),
        "mem": jax.random.normal(next(ks), (BATCH, N_MEM, D_MODEL), f32),
        "norm_mix_pre": gain((L, D_MODEL)),
        "w_in": nrm((L, D_MODEL, IN_COLS), D_MODEL ** -0.5),
        "conv_dw": nrm((L, CONV_K, CONV_CH), CONV_K ** -0.5),
        "conv_dw_b": nrm((L, CONV_CH), 0.02),
        "conv_ln_g": gain((L, CONV_CH)),
        "conv_ln_b": nrm((L, CONV_CH), 0.02),
        "q_norm_g": gain((L, HEAD_DIM)),
        "k_norm_g": gain((L, HEAD_DIM)),
        "w_out": nrm((L, D_MIX, D_MODEL), D_MIX ** -0.5),
        "norm_mix_post": gain((L, D_MODEL)),
        "norm_mem_pre": gain((L, D_MODEL)),
        "mem_norm_g": gain((L, D_MODEL)),
        "w_mem_q": nrm((L, D_MODEL, D_MODEL), D_MODEL ** -0.5),
        "w_mem_kv": nrm((L, D_MODEL, 2 * D_MODEL), D_MODEL ** -0.5),
        "w_mem_o": nrm((L, D_MODEL, D_MODEL), D_MODEL ** -0.5),
        "norm_mem_post": gain((L, D_MODEL)),
        "norm_ffn_pre": gain((L, D_MODEL)),
        "w_up": nrm((L, D_MODEL, 2 * D_FF), D_MODEL ** -0.5),
        "ffn_dw": nrm((L, FFN_CONV_K, 2 * D_FF), FFN_CONV_K ** -0.5),
        "ffn_dw_b": nrm((L, 2 * D_FF), 0.02),
        "w_down": nrm((L, D_FF, D_MODEL), D_FF ** -0.5),
        "norm_ffn_post": gain((L, D_MODEL)),
    }


def reference(x, mem, norm_mix_pre, w_in, conv_dw, conv_dw_b, conv_ln_g, conv_ln_b,
              q_norm_g, k_norm_g, w_out, norm_mix_post, norm_mem_pre, mem_norm_g,
              w_mem_q, w_mem_kv, w_mem_o, norm_mem_post, norm_ffn_pre, w_up, ffn_dw,
              ffn_dw_b, w_down, norm_ffn_post):
    S = x.shape[1]
    tabs = axial_rope_tables(S)
    for l in range(DEPTH):
        h = rmsnorm(x, norm_mix_pre[l])
        y = parallel_mixer(h, w_in[l], conv_dw[l], conv_dw_b[l], conv_ln_g[l], conv_ln_b[l],
                           q_norm_g[l], k_norm_g[l], w_out[l], tabs)
        x = x + rmsnorm(y, norm_mix_post[l])
        h = rmsnorm(x, norm_mem_pre[l])
        mem_n = rmsnorm(mem, mem_norm_g[l])
        y = memory_cross_attention(h, mem_n, w_mem_q[l], w_mem_kv[l], w_mem_o[l])
        x = x + rmsnorm(y, norm_mem_post[l])
        h = rmsnorm(x, norm_ffn_pre[l])
        y = conv_gated_ffn(h, w_up[l], ffn_dw[l], ffn_dw_b[l], w_down[l])
        x = x + rmsnorm(y, norm_ffn_post[l])
    return x
```

```python
import numpy as np
from contextlib import ExitStack
import concourse.bass as bass
import concourse.mybir as mybir
from concourse.bass_utils import run_bass_kernel_spmd

F32 = mybir.dt.float32
BF16 = mybir.dt.bfloat16
AF = mybir.ActivationFunctionType
ALU = mybir.AluOpType

EPS = 1e-6
NE = 2080
E_LO, E_HI = 15, 2065
DFF = 2816
NFC = 22


def _esize(dt):
    return 2 if dt == BF16 else 4


class Sched:
    ENGS = ["pe", "act", "dve", "pool", "sp"]
    NDMA = 16

    def __init__(self, nc, tracked_dram=()):
        self.nc = nc
        self.ops = []
        self.w = {}
        self.r = {}
        self.tracked_dram = set(tracked_dram)

    def region(self, ap):
        t = ap.tensor
        name = t.name
        space = str(ap.space)
        if "DRAM" in space.upper() or "HBM" in space.upper() or type(t).__name__.startswith("DRam"):
            if name not in self.tracked_dram:
                return None
        es = _esize(ap.dtype)
        dims = ap.ap
        ps, pc = dims[0]
        off = int(ap.offset)
        if ps == 0:
            rs_ = int(t.shape[-1])
            p0, f0 = off // rs_, off % rs_
            p1 = p0 + 1
        else:
            p0 = off // ps
            f0 = off % ps
            p1 = p0 + pc
        ents = [(f0, 0)]
        rest = dims[1:]
        for (s, c) in rest[:-1]:
            s = abs(s)
            if s != 0 and len(ents) * c <= 64:
                ents = [(st + i * s, ex) for (st, ex) in ents for i in range(c)]
            else:
                ents = [(st, ex + (c - 1) * s) for (st, ex) in ents]
        if rest:
            s, c = rest[-1]
            ents = [(st, ex + (c - 1) * abs(s) + 1) for (st, ex) in ents]
        else:
            ents = [(st, ex + 1) for (st, ex) in ents]
        ivs = tuple(sorted((st * es, (st + ex) * es) for (st, ex) in ents))
        return (name, p0, p1, ivs)

    @staticmethod
    def _ov(a, b):
        if a[1] >= b[2] or b[1] >= a[2]:
            return False
        for (s0, e0) in a[3]:
            for (s1, e1) in b[3]:
                if s0 < e1 and s1 < e0:
                    return True
        return False

    @staticmethod
    def _covers(a, b):
        if len(a[3]) != 1:
            return a[1] <= b[1] and a[2] >= b[2] and a[3] == b[3]
        if a[1] > b[1] or a[2] < b[2]:
            return False
        s0, e0 = a[3][0]
        return all(s0 <= s1 and e1 <= e0 for (s1, e1) in b[3])

    def add(self, eng, fn, reads=(), writes=(), dma=False):
        op = {"eng": eng, "fn": fn, "deps": set(), "idx": len(self.ops), "inc": False, "dma": dma}
        for ap in reads:
            rg = self.region(ap)
            if rg is None:
                continue
            for key, d in self.w.get(rg[0], {}).items():
                if self._ov(rg, key):
                    op["deps"].update(d.values())
            self.r.setdefault(rg[0], {}).setdefault(rg, {})[eng] = op["idx"]
        for ap in writes:
            rg = self.region(ap)
            if rg is None:
                continue
            for table in (self.w, self.r):
                tb = table.get(rg[0], {})
                dead = []
                for key, d in tb.items():
                    if self._ov(rg, key):
                        op["deps"].update(d.values())
                        if self._covers(rg, key):
                            dead.append(key)
                for k in dead:
                    del tb[k]
            self.w.setdefault(rg[0], {})[rg] = {eng: op["idx"]}
        op["deps"].discard(op["idx"])
        self.ops.append(op)
        return op

    def emit(self, es):
        nc = self.nc
        ops = self.ops
        for op in ops:
            op["deps"] = {d for d in op["deps"] if not (op["eng"] == "pe" and ops[d]["eng"] == "pe" and not ops[d]["dma"])}
            for d in op["deps"]:
                ops[d]["inc"] = True
        esem = {e: es.enter_context(nc.semaphore("sem_" + e)) for e in self.ENGS}
        dsem = [es.enter_context(nc.semaphore("dsem%d" % i)) for i in range(self.NDMA)]
        cnt = {e: 0 for e in self.ENGS}
        ndma = 0
        last_out_dma = []
        for op in ops:
            if op["dma"]:
                op["dsem"] = ndma % self.NDMA
                op["dcnt"] = 16 * (ndma // self.NDMA + 1)
                ndma += 1
            elif op["inc"]:
                cnt[op["eng"]] += 1
                op["cnt"] = cnt[op["eng"]]
        self.ndma = ndma
        block = es.enter_context(nc.Block())

        def stream(engname, e):
            seen = {}

            def wait(sem, key, val):
                if seen.get(key, 0) >= val:
                    return
                seen[key] = val
                e.wait_ge(sem, val)

            for op in ops:
                if op["eng"] != engname:
                    continue
                need = {}
                for d in op["deps"]:
                    p = ops[d]
                    if p["dma"]:
                        k = ("d", p["dsem"])
                        need[k] = max(need.get(k, 0), p["dcnt"])
                    else:
                        k = ("e", p["eng"])
                        need[k] = max(need.get(k, 0), p["cnt"])
                if op["dma"] and op["dcnt"] > 16:
                    k = ("d", op["dsem"])
                    need[k] = max(need.get(k, 0), op["dcnt"] - 16)
                for k, v in need.items():
                    wait(dsem[k[1]] if k[0] == "d" else esem[k[1]], k, v)
                ins = op["fn"](e)
                if op["dma"]:
                    ins.then_inc(dsem[op["dsem"]], 16)
                elif op["inc"]:
                    ins.then_inc(esem[op["eng"]], 1)
            if engname == "sp":
                for i in range(min(self.NDMA, ndma)):
                    n_i = (ndma - 1 - i) // self.NDMA + 1
                    wait(dsem[i], ("d", i), 16 * n_i)

        @block.tensor
        def _(e):
            stream("pe", e)

        @block.scalar
        def _(e):
            stream("act", e)

        @block.vector
        def _(e):
            stream("dve", e)

        @block.gpsimd
        def _(e):
            stream("pool", e)

        @block.sync
        def _(e):
            stream("sp", e)


def col_blocks(lo, hi, w=512):
    out = []
    while lo < hi:
        n = min(w, hi - lo)
        out.append((lo, n))
        lo += n
    return out


def build_nc(debug=False):
    nc = bass.Bass("TRN2", target_bir_lowering=False)
    es = ExitStack()

    def di(name, shape, dt=F32):
        return nc.dram_tensor(name, shape, dt, kind="ExternalInput").ap()

    x_ext = di("x_ext", [NE, 1024])
    x_rest = di("x_rest", [6144, 1024])
    memd = di("mem", [256, 1024])
    w_in = di("w_in", [1024, 1792])
    w_out = di("w_out", [1024, 1024])
    w_mem_q = di("w_mem_q", [1024, 1024])
    w_mem_kv = di("w_mem_kv", [1024, 2048])
    w_mem_o = di("w_mem_o", [1024, 1024])
    w_up = di("w_up", [1024, 2 * DFF])
    w_down = di("w_down", [DFF, 1024])
    gfm = di("gfm", [128, 4 * 8])
    gpost = di("gpost", [3, 1024])
    convp = di("convp", [128, 4 * 34])
    qkg = di("qkg", [128, 2])
    ffnp = di("ffnp", [128, 44 * 4])
    cst = di("cst", [128, 512])
    ropek = di("ropek", [2, 128, 8192])
    ropeq = di("ropeq", [2, 128, NE])
    maskd = di("mask", [128, 2])
    outd = nc.dram_tensor("out", [2048, 1024], F32, kind="ExternalOutput").ap()
    xs1 = nc.dram_tensor("xs1", [NE, 1024], F32, kind="Internal").ap()
    xs2 = nc.dram_tensor("xs2", [NE, 1024], F32, kind="Internal").ap()
    rds = nc.dram_tensor("rds", [64, 512], F32, kind="Internal").ap()
    dbg = {}

    S = Sched(nc, tracked_dram=["xs1", "xs2", "out", "rds"])

    def sb(name, shape, dt=F32):
        return es.enter_context(nc.sbuf_tensor(name, shape, dt))

    BIG8 = sb("BIG8", [128, 8 * NE], BF16)
    BIGQ = sb("BIGQ", [128, 8 * NE], BF16)
    G = sb("G", [128, NFC * 2048], BF16)
    STG = [sb("STG%d" % i, [128, 1024]) for i in range(2)]
    XT = [sb("XT%d" % i, [128, 1024]) for i in range(2)]
    YT = [sb("YT%d" % i, [128, 1024]) for i in range(2)]
    HB = [sb("HB%d" % i, [128, 1024], BF16) for i in range(2)]
    TMP = [sb("TMP%d" % i, [128, 512]) for i in range(6)]
    TMPB = [sb("TMPB%d" % i, [128, 512], BF16) for i in range(4)]
    GB = sb("GB", [128, 1024])
    CST = sb("CST", [128, 512], BF16)
    ONESF = sb("ONESF", [128, 64])
    GFM = sb("GFM", [128, 32])
    CONVP = sb("CONVP", [128, 4 * 34])
    QKG = sb("QKG", [128, 2])
    FFNP = sb("FFNP", [128, 44 * 4])
    MASK = sb("MASK", [128, 2])
    STAT = sb("STAT", [128, 64])
    PS = es.enter_context(nc.psum_tensor("PS", [128, 4096], F32))

    IDENT = CST[:, 0:128]
    PERM = CST[:, 128:256]
    BONES = CST[:, 256:384]
    ONES = CST[:, 384:512]

    def bank(b, n=512):
        return PS[:, 512 * b:512 * b + n]

    def bankbf(b):
        return PS[:, 512 * b:512 * b + 512].bitcast(BF16)

    def v3(ap2d, c):
        return ap2d.rearrange("p (c t) -> p c t", c=c)

    HT = v3(BIG8[:, :], 8)
    BQ = v3(BIGQ[:, :], 8)
    AT = BQ[:, 0:4, :]
    QT = BQ[:, 4:8, :]
    KT = G[:, 0:8192]
    VV = G[:, 8192:8192 + 64 * 130].rearrange("p (t g d) -> p t g d", t=64, g=2)
    WREG = G[:, 16512:16512 + 15872]
    WIN = v3(WREG[:, 0:8 * 1792], 8)
    DIAG = WREG[:, 0:15872].rearrange("p (c k m) -> p c k m", c=4, k=31)
    KM = v3(G[:, 32384:34432], 8)
    VM = v3(G[:, 34432:36480], 2)
    MEMT = v3(G[:, 36480:38528], 8)
    WKV = v3(BIGQ[:, 0:16384], 8)
    HTB = [v3(BIGQ[:, i * 4096:(i + 1) * 4096], 8) for i in range(2)]
    WOUT = v3(BIGQ[:, 0:8192], 8)
    WMQ = v3(G[:, 0:8192], 8)
    WMO = v3(G[:, 8192:16384], 8)
    GT = v3(G[:, :], NFC)
    WU = [v3(BIGQ[:, i * 8192:(i + 1) * 8192], 8) for i in range(2)]
    PT = [WREG[:, i * 1536:(i + 1) * 1536] for i in range(4)]
    _rf = G[:, 38528:38528 + 4096].bitcast(F32)
    ROPE = [_rf[:, i * 1024:(i + 1) * 1024] for i in range(2)]
    WDA = BIGQ[:, 0:6144]
    ATT1 = TMPB[3]

    cntr = {"stg": 0, "xt": 0, "yt": 0, "hb": 0, "rope": 0, "tp": 0, "stat": 0}

    def rot(key, n):
        v = cntr[key]
        cntr[key] = (v + 1) % n
        return v

    def stat1(m):
        i = rot("stat", 64)
        return STAT[0:m, i:i + 1]

    def dma(out, in_, q="sp"):
        S.add(q, lambda e: e.dma_start(out=out, in_=in_), reads=[in_], writes=[out], dma=True)

    def mm(out, lhsT, rhs, start, stop):
        S.add("pe", lambda e: e.matmul(out, lhsT, rhs, start=start, stop=stop), reads=[lhsT, rhs], writes=[out])

    def transp(out, in_, ident):
        S.add("pe", lambda e: e.transpose(out, in_, ident), reads=[in_, ident], writes=[out])

    def act(out, in_, func, bias=None, scale=None, accum=None):
        kw = {}
        rd = [in_]
        wr = [out]
        if bias is not None:
            kw["bias"] = bias
            if not isinstance(bias, float):
                rd.append(bias)
        if scale is not None:
            kw["scale"] = scale
            if not isinstance(scale, float):
                rd.append(scale)
        if accum is not None:
            kw["accum_out"] = accum
            wr.append(accum)
        S.add("act", lambda e: e.activation(out=out, in_=in_, func=func, **kw), reads=rd, writes=wr)

    def tt(eng, out, in0, in1, op):
        S.add(eng, lambda e: e.tensor_tensor(out=out, in0=in0, in1=in1, op=op), reads=[in0, in1], writes=[out])

    def ts(eng, out, in0, s1, op0, s2=None, op1=None):
        rd = [in0] + [s for s in (s1, s2) if s is not None and not isinstance(s, float)]
        if op1 is None:
            S.add(eng, lambda e: e.tensor_scalar(out=out, in0=in0, scalar1=s1, scalar2=None, op0=op0), reads=rd, writes=[out])
        else:
            S.add(eng, lambda e: e.tensor_scalar(out=out, in0=in0, scalar1=s1, scalar2=s2, op0=op0, op1=op1), reads=rd, writes=[out])

    def stt(eng, out, in0, scalar, in1, op0, op1):
        rd = [in0, in1] + ([] if isinstance(scalar, float) else [scalar])
        S.add(eng, lambda e: e.scalar_tensor_tensor(out=out, in0=in0, scalar=scalar, in1=in1, op0=op0, op1=op1), reads=rd, writes=[out])

    def cp(eng, out, in_):
        if eng == "act":
            act(out, in_, AF.Identity)
        else:
            S.add(eng, lambda e: e.tensor_copy(out=out, in_=in_), reads=[in_], writes=[out])

    def recip(out, in_):
        S.add("dve", lambda e: e.reciprocal(out=out, in_=in_), reads=[in_], writes=[out])

    def memset(eng, ap, val):
        S.add(eng, lambda e: e.memset(ap, val), writes=[ap])

    def rsqrt_from(out, in_, scale, m=None):
        act(out, in_, AF.Ln, bias=EPSB[0:out.shape[0], 0:1] if m is None else EPSB[0:m, 0:1], scale=scale)
        act(out, out, AF.Exp, scale=-0.5)

    def wpiece(dst, srcs, ncols, scal=None, eng="dve", in_view=None):
        slot = STG[rot("stg", 2)]
        for (src, p0, p1, c0, nc_) in srcs:
            dma(slot[p0:p1, c0:c0 + nc_], src, q="pool")
        src_ap = slot[:, 0:ncols] if in_view is None else in_view(slot)
        if scal is None:
            cp(eng, dst, src_ap)
        elif eng == "act":
            act(dst, src_ap, AF.Identity, scale=scal)
        else:
            ts(eng, dst, src_ap, scal, ALU.mult)

    EPSB = sb("EPSB", [128, 1])
    memset("pool", EPSB[:, :], EPS)
    memset("pool", ONESF[:, :], 1.0)
    wpiece(CST[:, :], [(cst[:, :], 0, 128, 0, 512)], 512, None, eng="dve")
    dma(GFM[:, :], gfm[:, :])
    dma(CONVP[:, :], convp[:, :])
    dma(QKG[:, :], qkg[:, :])
    dma(FFNP[:, :], ffnp[:, :])
    dma(MASK[:, :], maskd[:, :])
    memset("pool", VV[:, :, :, 64:65], 1.0)

    def load_weight_rows(dst3, src, ncols_total, gcol, col_pieces=None, rows_of_chunk=None, nchunks=8):
        for c in range(nchunks):
            for (c0, ncol) in (col_pieces or col_blocks(0, ncols_total, 1024)):
                if rows_of_chunk is None:
                    srcs = [(src[c * 128:(c + 1) * 128, c0:c0 + ncol], 0, 128, 0, ncol)]
                else:
                    srcs = [(src[r0:r0 + nr, c0:c0 + ncol], p0, p0 + nr, 0, ncol) for (r0, nr, p0) in rows_of_chunk(c)]
                scal = None if gcol is None else GFM[:, gcol * 8 + c:gcol * 8 + c + 1]
                wpiece(dst3[:, c, c0:c0 + ncol], srcs, ncol, scal)

    def norm_to_T(src_tile, m, dst, evac_eng):
        ss = stat1(m)
        hb = HB[rot("hb", 2)]
        act(hb[0:m, :], src_tile, AF.Square, accum=ss)
        rs = stat1(m)
        rsqrt_from(rs, ss, 1.0 / 1024.0, m)
        ts("dve", hb[0:m, :], src_tile, rs, ALU.mult)
        tb = 6 + rot("tp", 2)
        tpv = v3(bankbf(tb), 8)
        for c in range(8):
            transp(tpv[:, c, 0:m], hb[0:m, c * 128:(c + 1) * 128], IDENT[0:m, 0:m])
        cp(evac_eng, dst, tpv[:, :, 0:m])

    def nr1(src, n, gain, b_ss, b_rot):
        xg = TMPB[0][:, 0:n]
        sq = TMPB[1][:, 0:n]
        act(xg, src, AF.Identity, scale=gain)
        act(sq, src, AF.Square)
        mm(bank(b_ss, n), BONES, sq, True, True)
        mm(bank(b_rot, n), PERM, xg, True, True)

    def nr2(n, cos, sin, b_ss, b_rot):
        xg = TMPB[0][:, 0:n]
        rs = TMP[0][:, 0:n]
        act(rs, bank(b_ss, n), AF.Ln, bias=EPSB[:, 0:1], scale=1.0 / 64.0)
        act(rs, rs, AF.Exp, scale=-0.5)
        tt("dve", TMP[1][:, 0:n], xg, cos, ALU.mult)
        tt("dve", TMP[2][:, 0:n], bank(b_rot, n), sin, ALU.mult)

    def nr3(n, dst):
        t1 = TMP[1][:, 0:n]
        tt("pool", t1, t1, TMP[2][:, 0:n], ALU.add)
        tt("pool", dst, t1, TMP[0][:, 0:n], ALU.mult)

    def normrope(src, n, gain, cos, sin, dst, b_ss, b_rot):
        nr1(src, n, gain, b_ss, b_rot)
        nr2(n, cos, sin, b_ss, b_rot)
        nr3(n, dst)

    def phase_c0():
      load_weight_rows(WKV, w_mem_kv, 2048, 3)
      for mt in range(2):
          xt = XT[rot("xt", 2)]
          dma(xt[:, :], memd[mt * 128:(mt + 1) * 128, :])
          norm_to_T(xt[:, :], 128, MEMT[:, :, mt * 128:(mt + 1) * 128], "dve")
      for oc in range(8):
          pb = bank(oc % 2, 256)
          for c in range(8):
              mm(pb, WKV[:, c, oc * 128:(oc + 1) * 128], MEMT[:, c, :], c == 0, c == 7)
          cp("act", KM[:, oc, :], pb)
      for mt in range(2):
          for hf in range(2):
              pb = bank(2 + (mt * 2 + hf) % 2)
              for c in range(8):
                  mm(pb, MEMT[:, c, mt * 128:(mt + 1) * 128], WKV[:, c, 1024 + hf * 512:1024 + (hf + 1) * 512], c == 0, c == 7)
              cp("dve", VM[:, mt, hf * 512:(hf + 1) * 512], pb)

    def load_win_chunk(c):
        sc = GFM[:, c:c + 1]
        wpiece(WIN[:, c, 0:1024], [(w_in[c * 128:(c + 1) * 128, 0:1024], 0, 128, 0, 1024)], 1024, sc)
        slot_view = lambda slot: slot[:, 0:512].rearrange("p (h j d) -> p j h d", h=2, j=4)
        wpiece(WIN[:, c, 1024:1536].rearrange("p (j h d) -> p j h d", j=4, h=2),
               [(w_in[c * 128:(c + 1) * 128, 1024:1792], 0, 128, 0, 768)], 512, sc, in_view=slot_view)
        last = STG[(cntr["stg"] + 1) % 2]
        ts("dve", WIN[:, c, 1536:1792], last[:, 512:768], sc, ALU.mult)

    XQ = [XT[0], XT[1], YT[0], YT[1]]

    def tile_load(t, src, m):
        dma(XQ[t % 4][0:m, :], src)

    def tile_front(t, src, m):
        xt = XQ[t % 4]
        ss = stat1(m)
        hb = HB[t % 2]
        act(hb[0:m, :], xt[0:m, :], AF.Square, accum=ss)
        rs = stat1(m)
        rsqrt_from(rs, ss, 1.0 / 1024.0, m)
        ts("dve", hb[0:m, :], xt[0:m, :], rs, ALU.mult)

    def tile_back(t, m, dst, evac_eng):
        hb = HB[t % 2]
        tpv = v3(bankbf(6 + t % 2), 8)
        for c in range(8):
            transp(tpv[:, c, 0:m], hb[0:m, c * 128:(c + 1) * 128], IDENT[0:m, 0:m])
        cp(evac_eng, dst, tpv[:, :, 0:m])

    def kv_job(hsrc, kb):
        i = kb
        rp_ = ROPE[i % 2]
        kp = bank(i % 2)
        vp = bank(2 + i % 2)

        def P():
            dma(rp_[:, 0:512], ropek[0, :, kb * 512:(kb + 1) * 512])
            dma(rp_[:, 512:1024], ropek[1, :, kb * 512:(kb + 1) * 512])
            for c in range(8):
                mm(kp, WIN[:, c, 1536:1664], hsrc[:, c, :], c == 0, c == 7)
            for t in range(4):
                for c in range(8):
                    mm(vp[:, t * 128:(t + 1) * 128], hsrc[:, c, t * 128:(t + 1) * 128], WIN[:, c, 1664:1792], c == 0, c == 7)

        def N1():
            nr1(kp, 512, QKG[:, 1:2], 4, 5)
            cp("act", VV[:, kb * 4:kb * 4 + 4, :, 0:64], vp.rearrange("p (t g d) -> p t g d", t=4, g=2))

        def N2():
            nr2(512, rp_[:, 0:512], rp_[:, 512:1024], 4, 5)

        def N3():
            nr3(512, KT[:, kb * 512:(kb + 1) * 512])
        return [P, N1, N2, N3]

    def run_tiles(tiles, jobs_ready, extra=None):
        active = []

        def step_jobs(k):
            for _ in range(k):
                if active:
                    active[0].pop(0)()
                    if not active[0]:
                        active.pop(0)
        for t0 in range(min(3, len(tiles))):
            tile_load(t0, tiles[t0][0], tiles[t0][1])
        tile_front(0, tiles[0][0], tiles[0][1])
        for t in range(len(tiles)):
            if t + 3 < len(tiles):
                tile_load(t + 3, tiles[t + 3][0], tiles[t + 3][1])
            if t + 1 < len(tiles):
                tile_front(t + 1, tiles[t + 1][0], tiles[t + 1][1])
            tile_back(t, tiles[t][1], tiles[t][2], "act" if t % 2 else "dve")
            if extra is not None:
                extra(t)
            for job in jobs_ready.get(t, []):
                active.append(job)
            step_jobs(2 if len(active) > 1 else 1)
        while active:
            step_jobs(1)

    ext_tiles = col_blocks(0, NE, 128)
    tiles = [(x_ext[e0:e0 + m, :], m, HT[:, :, e0:e0 + m]) for (e0, m) in ext_tiles]
    jobs = {4 * kb + 4: [kv_job(HT[:, :, 16 + kb * 512:16 + (kb + 1) * 512], kb)] for kb in range(4)}
    run_tiles(tiles, jobs, extra=lambda t: [load_win_chunk(2 * t), load_win_chunk(2 * t + 1)] if t < 4 else None)
    phase_c0()
    tiles = []
    jobs = {}
    for rb in range(12):
        for t in range(4):
            r0 = rb * 512 + t * 128
            tiles.append((x_rest[r0:r0 + 128, :], 128, HTB[rb % 2][:, :, t * 128:(t + 1) * 128]))
        jobs[4 * rb + 3] = [kv_job(HTB[rb % 2], 4 + rb)]
    run_tiles(tiles, jobs)

    qitems = [(bk, e0, n, j) for bk, (e0, n) in enumerate(col_blocks(E_LO, E_HI)) for j in range(4)]

    def q_a(i):
        bk, e0, n, j = qitems[i]
        rp_ = ROPE[bk % 2]
        if j == 0:
            dma(rp_[:, 0:n], ropeq[0, :, e0:e0 + n])
            dma(rp_[:, 512:512 + n], ropeq[1, :, e0:e0 + n])
        qp = bank(6 + i % 2, n)
        for c in range(8):
            mm(qp, WIN[:, c, 1024 + j * 128:1024 + (j + 1) * 128], HT[:, c, e0:e0 + n], c == 0, c == 7)

    def q_n(i):
        bk, e0, n, j = qitems[i]
        rp_ = ROPE[bk % 2]
        normrope(bank(6 + i % 2, n), n, QKG[:, 0:1], rp_[:, 0:n], rp_[:, 512:512 + n], QT[:, j, e0:e0 + n], 4, 5)

    q_a(0)
    for i in range(len(qitems)):
        if i + 1 < len(qitems):
            q_a(i + 1)
        q_n(i)
    bi = 0
    for (e0, n) in col_blocks(0, NE):
        for ci in range(4):
            av = bank(2 * (bi % 2), n)
            ag = bank(2 * (bi % 2) + 1, n)
            bi += 1
            for c in range(8):
                mm(av, WIN[:, c, ci * 128:(ci + 1) * 128], HT[:, c, e0:e0 + n], c == 0, c == 7)
            for c in range(8):
                mm(ag, WIN[:, c, 512 + ci * 128:512 + (ci + 1) * 128], HT[:, c, e0:e0 + n], c == 0, c == 7)
            sg = TMP[3 + bi % 2][:, 0:n]
            act(sg, ag, AF.Sigmoid)
            tt("dve", AT[:, ci, e0:e0 + n], av, sg, ALU.mult)

    for ci in range(4):
        for k in range(31):
            ts("dve", DIAG[:, ci, k, :], IDENT, CONVP[:, ci * 34 + k:ci * 34 + k + 1], ALU.mult)
    for (e0, n) in col_blocks(E_LO, E_HI):
        sm = bank(4, n)
        sq_ = bank(5, n)
        for ci in range(4):
            cv = bank(ci, n)
            for k in range(31):
                mm(cv, DIAG[:, ci, k, :], AT[:, ci, e0 + k - 15:e0 + k - 15 + n], k == 0, k == 30)
            bcol = CONVP[:, ci * 34 + 31:ci * 34 + 32]
            cb = TMPB[ci % 2][:, 0:n]
            cs = TMPB[2 + ci % 2][:, 0:n]
            act(cb, cv, AF.Identity, bias=bcol)
            act(cs, cv, AF.Square, bias=bcol)
            mm(sm, ONES, cb, ci == 0, ci == 3)
            mm(sq_, ONES, cs, ci == 0, ci == 3)
        mean = TMP[0][:, 0:n]
        ts("dve", mean, sm, 1.0 / 512.0, ALU.mult)
        msq = TMP[1][:, 0:n]
        tt("dve", msq, mean, mean, ALU.mult)
        var = TMP[2][:, 0:n]
        stt("dve", var, sq_, 1.0 / 512.0, msq, ALU.mult, ALU.subtract)
        act(var, var, AF.Ln, bias=EPSB[:, 0:1], scale=1.0)
        act(var, var, AF.Exp, scale=-0.5)
        for ci in range(4):
            cv = bank(ci, n)
            bcol = CONVP[:, ci * 34 + 31:ci * 34 + 32]
            t1 = TMP[3 + ci % 2][:, 0:n]
            stt("dve", t1, cv, bcol, mean, ALU.add, ALU.subtract)
            tt("pool", t1, t1, var, ALU.mult)
            act(HT[:, ci, e0:e0 + n], t1, AF.Silu, bias=CONVP[:, ci * 34 + 33:ci * 34 + 34],
                scale=CONVP[:, ci * 34 + 32:ci * 34 + 33])

    def wout_rows(c):
        if c < 4:
            return [(c * 128, 128, 0)]
        j = c - 4
        return [(512 + 64 * j, 64, 0), (512 + 64 * (4 + j), 64, 64)]
    load_weight_rows(WOUT, w_out, 1024, None, rows_of_chunk=wout_rows)
    dma(GB[:, :], gpost[0:1, :].partition_broadcast(128))

    PTX = [WREG[:, i * 2048:(i + 1) * 2048] for i in range(2)]
    PTY = [WREG[:, 4096 + i * 1024:4096 + (i + 1) * 1024] for i in range(2)]
    groups = [(j, e0, n) for j in range(4) for (e0, n) in col_blocks(E_LO, E_HI)]
    batches = []
    nxy = {"X": 0, "Y": 0}
    for gi in range(len(groups)):
        kt = 0
        turn = "X"
        while kt < 64:
            if turn == "X" and kt + 2 <= 64:
                kts = [kt, kt + 1]
                kind = "X"
            else:
                kts = [kt]
                kind = "Y"
            kt += len(kts)
            batches.append((gi, kts, kind, nxy[kind], kt == 64))
            nxy[kind] += 1
            turn = "Y" if turn == "X" else "X"

    def sbank(kind, i, h):
        return (2 * i + h) if kind == "X" else (4 + h)

    def emit_qk(bn):
        gi, kts, kind, ser, last = batches[bn]
        j, e0, n = groups[gi]
        for i, kt in enumerate(kts):
            for h in range(2):
                mm(bank(sbank(kind, i, h), n), KT[64 * h:64 * h + 64, kt * 128:(kt + 1) * 128],
                   QT[64 * h:64 * h + 64, j, e0:e0 + n], True, True)

    def emit_exp(bn):
        gi, kts, kind, ser, last = batches[bn]
        j, e0, n = groups[gi]
        nit = 2 * len(kts)
        b0 = 0 if kind == "X" else 4
        pt = (PTX if kind == "X" else PTY)[ser % 2]
        sv = PS[:, 512 * b0:512 * (b0 + nit)].rearrange("p (b t) -> p b t", b=nit)[:, :, 0:n]
        pv = pt[:, 0:512 * nit].rearrange("p (b t) -> p b t", b=nit)[:, :, 0:n]
        act(pv, sv, AF.Exp, scale=0.125)

    def emit_pv(bn):
        gi, kts, kind, ser, last = batches[bn]
        j, e0, n = groups[gi]
        pt = (PTX if kind == "X" else PTY)[ser % 2]
        for i, kt in enumerate(kts):
            for h in range(2):
                it = 2 * i + h
                mm(bank(6 + h, n)[0:65, :], VV[:, kt, h, 0:65], pt[:, 512 * it:512 * it + n], kt == 0, kt == 63)
        if last:
            for h in range(2):
                cp("dve", TMP[h][0:65, 0:n], bank(6 + h, n)[0:65, :])
            for h in range(2):
                recip(TMP[2 + h][64:65, 0:n], TMP[h][64:65, 0:n])
            for h in range(2):
                osb = TMP[h][0:65, 0:n]
                rd = TMP[2 + h][64:65, 0:n]
                rdb = TMP[4 + h][0:64, 0:n]
                row = 2 * gi + h
                dma(rds[row:row + 1, 0:n], rd)
                dma(rdb, rds[row:row + 1, 0:n].partition_broadcast(64))
                if h == 0:
                    tt("dve", HT[0:64, 4 + j, e0:e0 + n], osb[0:64, :], rdb, ALU.mult)
                else:
                    a1_ = ATT1[0:64, 0:n]
                    tt("dve", a1_, osb[0:64, :], rdb, ALU.mult)
                    dma(HT[64:128, 4 + j, e0:e0 + n], a1_)

    pend = {"X": 0, "Y": 0}
    nq = [0]

    def try_qk():
        while nq[0] < len(batches) and pend[batches[nq[0]][2]] == 0:
            emit_qk(nq[0])
            pend[batches[nq[0]][2]] += 1
            nq[0] += 1

    try_qk()
    for bn in range(len(batches)):
        emit_exp(bn)
        pend[batches[bn][2]] -= 1
        try_qk()
        emit_pv(bn)

    def outproj_phase(tiles, nk, lhs_fn, rhs_fn, xsrc_fn, xdst_fn, hdst_fn):
        def ybuf(t):
            m = tiles[t][1]
            return PS[0:m, 1024 * (t % 2):1024 * (t % 2) + 1024]

        def st_a(t):
            y = ybuf(t)
            dma(XT[t % 2][0:tiles[t][1], :], xsrc_fn(t))
            for hf in range(2):
                for c in range(nk):
                    mm(y[:, hf * 512:(hf + 1) * 512], lhs_fn(t, c), rhs_fn(c, hf), c == 0, c == nk - 1)

        def st_b1(t):
            m = tiles[t][1]
            y = ybuf(t)
            xt = XT[t % 2]
            ss = stat1(m)
            yt = YT[t % 2]
            act(yt[0:m, :], y, AF.Square, accum=ss)
            rs = stat1(m)
            rsqrt_from(rs, ss, 1.0 / 1024.0, m)
            stt("dve", yt[0:m, :], y, rs, GB[0:m, :], ALU.mult, ALU.mult)
            tt("pool", yt[0:m, :], yt[0:m, :], xt[0:m, :], ALU.add)
            dma(xdst_fn(t), yt[0:m, :])

        def st_b2(t):
            if hdst_fn is None:
                return
            m = tiles[t][1]
            yt = YT[t % 2]
            hb = HB[t % 2]
            ss = stat1(m)
            act(hb[0:m, :], yt[0:m, :], AF.Square, accum=ss)
            rs = stat1(m)
            rsqrt_from(rs, ss, 1.0 / 1024.0, m)
            ts("dve", hb[0:m, :], yt[0:m, :], rs, ALU.mult)

        def st_c(t):
            if hdst_fn is None:
                return
            tile_back(t, tiles[t][1], hdst_fn(t), "act" if t % 2 else "dve")

        stages = [st_a, st_b1, st_b2, st_c]
        for s_ in range(len(tiles) + len(stages) - 1):
            for k, st in enumerate(stages):
                t = s_ - k
                if 0 <= t < len(tiles):
                    st(t)

    tok_tiles = col_blocks(E_LO, E_HI, 128)

    def ht_tile(t):
        e0, m = tok_tiles[t]
        return HT[:, :, e0:e0 + m]

    outproj_phase(tok_tiles, 8,
                  lambda t, c: HT[:, c, tok_tiles[t][0]:tok_tiles[t][0] + tok_tiles[t][1]],
                  lambda c, hf: WOUT[:, c, hf * 512:(hf + 1) * 512],
                  lambda t: x_ext[tok_tiles[t][0]:tok_tiles[t][0] + tok_tiles[t][1], :],
                  lambda t: xs1[tok_tiles[t][0]:tok_tiles[t][0] + tok_tiles[t][1], :],
                  ht_tile)

    load_weight_rows(WMQ, w_mem_q, 1024, 1)
    load_weight_rows(WMO, w_mem_o, 1024, None)
    dma(GB[:, :], gpost[1:2, :].partition_broadcast(128))
    qi = 0
    for (e0, n) in col_blocks(E_LO, E_HI):
        for oc in range(8):
            qb = bank(qi % 2, n)
            qi += 1
            for c in range(8):
                mm(qb, WMQ[:, c, oc * 128:(oc + 1) * 128], HT[:, c, e0:e0 + n], c == 0, c == 7)
            cp("act" if oc % 2 else "dve", BQ[:, oc, e0:e0 + n], qb)
        for hd in range(4):
            for mt in range(2):
                sbk = bank(2 + mt, n)
                for dc in range(2):
                    mm(sbk, KM[:, 2 * hd + dc, mt * 128:(mt + 1) * 128], BQ[:, 2 * hd + dc, e0:e0 + n], dc == 0, dc == 1)
            pt = PTX[hd % 2]
            sv = PS[:, 1024:2048].rearrange("p (b t) -> p b t", b=2)[:, :, 0:n]
            pv = pt[:, 0:1024].rearrange("p (b t) -> p b t", b=2)[:, :, 0:n]
            act(pv, sv, AF.Exp, scale=1.0 / 16.0)
            den = bank(4, n)
            for mt in range(2):
                mm(den, ONES, pt[:, 512 * mt:512 * mt + n], mt == 0, mt == 1)
            rd = TMP[hd % 2][:, 0:n]
            act(rd, den, AF.Ln)
            act(rd, rd, AF.Exp, scale=-1.0)
            for dc in range(2):
                ob = bank(5 + dc, n)
                for mt in range(2):
                    mm(ob, VM[:, mt, (2 * hd + dc) * 128:(2 * hd + dc + 1) * 128], pt[:, 512 * mt:512 * mt + n], mt == 0, mt == 1)
                tt("dve", HT[:, 2 * hd + dc, e0:e0 + n], ob, rd, ALU.mult)
    outproj_phase(tok_tiles, 8,
                  lambda t, c: HT[:, c, tok_tiles[t][0]:tok_tiles[t][0] + tok_tiles[t][1]],
                  lambda c, hf: WMO[:, c, hf * 512:(hf + 1) * 512],
                  lambda t: xs1[tok_tiles[t][0]:tok_tiles[t][0] + tok_tiles[t][1], :],
                  lambda t: xs2[tok_tiles[t][0]:tok_tiles[t][0] + tok_tiles[t][1], :],
                  ht_tile)
    ts("dve", HT[:, :, 15:16], HT[:, :, 15:16], MASK[:, 0:1], ALU.mult)
    ts("dve", HT[:, :, 2064:2065], HT[:, :, 2064:2065], MASK[:, 1:2], ALU.mult)

    dma(GB[:, :], gpost[2:3, :].partition_broadcast(128))
    fgroups = [(f0, min(4, NFC - f0)) for f0 in range(0, NFC, 4)]
    fblocks = []
    o0 = 16
    while o0 < 2064:
        no = min(510, 2064 - o0)
        fblocks.append((o0, no))
        o0 += no

    def load_wdown(chunks, dst_fn):
        for f in chunks:
            wpiece(dst_fn(f), [(w_down[f * 128:(f + 1) * 128, :], 0, 128, 0, 1024)], 1024, None)

    def load_wup_group(gi):
        f0, nf = fgroups[gi]
        wu = WU[gi % 2]
        for c in range(8):
            ncol = nf * 128
            srcs = [(w_up[c * 128:(c + 1) * 128, f0 * 128:f0 * 128 + ncol], 0, 128, 0, ncol),
                    (w_up[c * 128:(c + 1) * 128, DFF + f0 * 128:DFF + f0 * 128 + ncol], 0, 128, 512, ncol)]
            sc = GFM[:, 16 + c:16 + c + 1]
            if nf == 4:
                wpiece(wu[:, c, :], srcs, 1024, sc)
            else:
                slot_view = lambda slot: slot[:, :].rearrange("p (a t) -> p a t", a=2)[:, :, 0:ncol]
                wpiece(wu[:, c, :].rearrange("p (a t) -> p a t", a=2)[:, :, 0:ncol], srcs, 1024, sc, in_view=slot_view)

    load_wup_group(0)
    fitems = []
    for gi, (f0, nf) in enumerate(fgroups):
        for fl in range(nf):
            for bi_, (o0, no) in enumerate(fblocks):
                fitems.append((gi, fl, f0 + fl, o0, no, fl == 0 and bi_ == 0))

    def f_a(i):
        gi, fl, f, o0, no, first = fitems[i]
        if first and gi + 1 < len(fgroups):
            load_wup_group(gi + 1)
        wu = WU[gi % 2]
        ug = bank(2 * (i % 3), no + 2)
        uv = bank(2 * (i % 3) + 1, no + 2)
        for c in range(8):
            mm(ug, wu[:, c, fl * 128:(fl + 1) * 128], HT[:, c, o0 - 1:o0 + no + 1], c == 0, c == 7)
        for c in range(8):
            mm(uv, wu[:, c, 512 + fl * 128:512 + (fl + 1) * 128], HT[:, c, o0 - 1:o0 + no + 1], c == 0, c == 7)

    def f_b(i):
        gi, fl, f, o0, no, first = fitems[i]
        pg = FFNP[:, f * 4:f * 4 + 4]
        pv_ = FFNP[:, (NFC + f) * 4:(NFC + f) * 4 + 4]
        ug = bank(2 * (i % 3), no + 2)
        uv = bank(2 * (i % 3) + 1, no + 2)
        tg = TMP[i % 3][:, 0:no]
        tv = TMP[3 + i % 3][:, 0:no]
        act(tg, ug[:, 1:1 + no], AF.Identity, bias=pg[:, 3:4], scale=pg[:, 1:2])
        act(tv, uv[:, 1:1 + no], AF.Identity, bias=pv_[:, 3:4], scale=pv_[:, 1:2])
        stt("dve", tg, ug[:, 0:no], pg[:, 0:1], tg, ALU.mult, ALU.add)
        stt("dve", tg, ug[:, 2:2 + no], pg[:, 2:3], tg, ALU.mult, ALU.add)
        stt("dve", tv, uv[:, 0:no], pv_[:, 0:1], tv, ALU.mult, ALU.add)
        stt("dve", tv, uv[:, 2:2 + no], pv_[:, 2:3], tv, ALU.mult, ALU.add)

    def f_c(i):
        gi, fl, f, o0, no, first = fitems[i]
        tg = TMP[i % 3][:, 0:no]
        tv = TMP[3 + i % 3][:, 0:no]
        act(tg, tg, AF.Gelu_apprx_tanh)
        tt("pool", GT[:, f, o0 - 16:o0 - 16 + no], tg, tv, ALU.mult)

    fst = [f_a, f_b, f_c]
    for s_ in range(len(fitems) + 2):
        for k, st in enumerate(fst):
            t = s_ - k
            if 0 <= t < len(fitems):
                st(t)
    WD1 = v3(BIG8[:, 0:16384], 16)
    load_wdown(range(0, 6), lambda f: WDA[:, f * 1024:(f + 1) * 1024])
    load_wdown(range(6, NFC), lambda f: WD1[:, f - 6, :])

    def wd(f, hf):
        if f < 6:
            return WDA[:, f * 1024 + hf * 512:f * 1024 + (hf + 1) * 512]
        return WD1[:, f - 6, hf * 512:(hf + 1) * 512]

    ffn_tiles = [(16 + ti * 128, 128) for ti in range(16)]
    outproj_phase(ffn_tiles, NFC,
                  lambda t, c: GT[:, c, t * 128:(t + 1) * 128],
                  wd,
                  lambda t: xs2[16 + t * 128:16 + (t + 1) * 128, :],
                  lambda t: outd[t * 128:(t + 1) * 128, :],
                  None)

    S.emit(es)
    es.close()
    return nc


def _rope_tables(pos):
    pos = np.asarray(pos)
    inv = (10000.0 ** (-(np.arange(0, 32, 2, dtype=np.float32)) / np.float32(32))).astype(np.float32)
    r = (pos // 64).astype(np.float32)
    c = (pos % 64).astype(np.float32)
    ang_r = r[None, :] * inv[:, None]
    ang_c = c[None, :] * inv[:, None]
    ang = np.concatenate([ang_r, ang_r, ang_c, ang_c], axis=0).astype(np.float32)
    ang = np.concatenate([ang, ang], axis=0)
    return np.stack([np.cos(ang), np.sin(ang)]).astype(np.float32)


def _consts():
    ident = np.eye(128, dtype=np.float32)
    perm = np.zeros((128, 128), np.float32)
    for m in range(128):
        if (m % 32) < 16:
            perm[m + 16, m] = -1.0
        else:
            perm[m - 16, m] = 1.0
    bones = np.zeros((128, 128), np.float32)
    bones[:64, :64] = 1.0
    bones[64:, 64:] = 1.0
    ones = np.ones((128, 128), np.float32)
    return np.ascontiguousarray(np.concatenate([ident, perm, bones, ones], axis=1))


_NC_CACHE = {}


def kernel(x, mem, norm_mix_pre, w_in, conv_dw, conv_dw_b, conv_ln_g, conv_ln_b,
           q_norm_g, k_norm_g, w_out, norm_mix_post, norm_mem_pre, mem_norm_g,
           w_mem_q, w_mem_kv, w_mem_o, norm_mem_post, norm_ffn_pre, w_up, ffn_dw,
           ffn_dw_b, w_down, norm_ffn_post):
    f = lambda a: np.ascontiguousarray(np.asarray(a, dtype=np.float32))
    x = f(x); mem = f(mem)
    B, Sq, D = x.shape

    def fm(g):
        return f(g).reshape(8, 128).T
    gfm = np.ascontiguousarray(np.concatenate([fm(norm_mix_pre[0]), fm(norm_mem_pre[0]), fm(norm_ffn_pre[0]), fm(mem_norm_g[0])], axis=1))
    gpost = np.ascontiguousarray(np.stack([f(norm_mix_post[0]), f(norm_mem_post[0]), f(norm_ffn_post[0])]))
    cw = f(conv_dw[0])
    convp = np.zeros((128, 4, 34), np.float32)
    for ci in range(4):
        convp[:, ci, 0:31] = cw[:, ci * 128:(ci + 1) * 128].T
        convp[:, ci, 31] = f(conv_dw_b[0])[ci * 128:(ci + 1) * 128]
        convp[:, ci, 32] = f(conv_ln_g[0])[ci * 128:(ci + 1) * 128]
        convp[:, ci, 33] = f(conv_ln_b[0])[ci * 128:(ci + 1) * 128]
    convp = np.ascontiguousarray(convp.reshape(128, 136))
    qkg = np.ascontiguousarray(np.stack([np.tile(f(q_norm_g[0]), 2), np.tile(f(k_norm_g[0]), 2)], axis=1))
    fw = f(ffn_dw[0])
    fb = f(ffn_dw_b[0])
    ffnp = np.zeros((128, 44, 4), np.float32)
    for fc in range(44):
        ffnp[:, fc, 0:3] = fw[:, fc * 128:(fc + 1) * 128].T
        ffnp[:, fc, 3] = fb[fc * 128:(fc + 1) * 128]
    ffnp = np.ascontiguousarray(ffnp.reshape(128, 176))
    cst = _consts()
    shared = dict(w_in=f(w_in[0]), w_out=f(w_out[0]), w_mem_q=f(w_mem_q[0]), w_mem_kv=f(w_mem_kv[0]),
                  w_mem_o=f(w_mem_o[0]), w_up=f(w_up[0]), w_down=f(w_down[0]), gfm=gfm, gpost=gpost,
                  convp=convp, qkg=qkg, ffnp=ffnp, cst=cst)
    in_maps = []
    for core in range(8):
        b, j = core // 4, core % 4
        s = j * 2048
        xe = np.zeros((NE, 1024), np.float32)
        lo, hi = max(0, s - 16), min(Sq, s + 2064)
        xe[lo - (s - 16):hi - (s - 16)] = x[b, lo:hi]
        rest_idx = np.concatenate([np.arange(0, s), np.arange(s + 2048, Sq)])
        xr = np.ascontiguousarray(x[b, rest_idx])
        key_pos = np.concatenate([np.arange(s, s + 2048), rest_idx])
        ropek = _rope_tables(key_pos)
        ext_pos = np.clip(np.arange(s - 16, s - 16 + NE), 0, Sq - 1)
        ropeq = _rope_tables(ext_pos)
        mask = np.ones((128, 2), np.float32)
        if j == 0:
            mask[:, 0] = 0.0
        if j == 3:
            mask[:, 1] = 0.0
        m = dict(shared)
        m.update(x_ext=xe, x_rest=xr, mem=np.ascontiguousarray(mem[b]), ropek=ropek, ropeq=ropeq, mask=mask)
        in_maps.append(m)
    if "nc" not in _NC_CACHE:
        _NC_CACHE["nc"] = build_nc()
    nc = _NC_CACHE["nc"]
    res = run_bass_kernel_spmd(nc, in_maps, core_ids=list(range(8)))
    out = np.zeros((B, Sq, D), np.float32)
    for core in range(8):
        b, j = core // 4, core % 4
        out[b, j * 2048:(j + 1) * 2048] = np.asarray(res.results[core]["out"], dtype=np.float32)
    return out
```

```python
import numpy as np
from contextlib import ExitStack
import concourse.bass as bass
import concourse.mybir as mybir
from concourse.bass_utils import run_bass_kernel_spmd

F32 = mybir.dt.float32
BF16 = mybir.dt.bfloat16
AF = mybir.ActivationFunctionType
ALU = mybir.AluOpType

EPS = 1e-6
NE = 2080
E_LO, E_HI = 15, 2065
DFF = 2816
NFC = 22


def _esize(dt):
    return 2 if dt == BF16 else 4


class Sched:
    ENGS = ["pe", "act", "dve", "pool", "sp"]
    NDMA = 16

    def __init__(self, nc, tracked_dram=()):
        self.nc = nc
        self.ops = []
        self.w = {}
        self.r = {}
        self.tracked_dram = set(tracked_dram)

    def region(self, ap):
        t = ap.tensor
        name = t.name
        space = str(ap.space)
        if "DRAM" in space.upper() or "HBM" in space.upper() or type(t).__name__.startswith("DRam"):
            if name not in self.tracked_dram:
                return None
        es = _esize(ap.dtype)
        dims = ap.ap
        ps, pc = dims[0]
        off = int(ap.offset)
        if ps == 0:
            rs_ = int(t.shape[-1])
            p0, f0 = off // rs_, off % rs_
            p1 = p0 + 1
        else:
            p0 = off // ps
            f0 = off % ps
            p1 = p0 + pc
        ents = [(f0, 0)]
        rest = dims[1:]
        for (s, c) in rest[:-1]:
            s = abs(s)
            if s != 0 and len(ents) * c <= 64:
                ents = [(st + i * s, ex) for (st, ex) in ents for i in range(c)]
            else:
                ents = [(st, ex + (c - 1) * s) for (st, ex) in ents]
        if rest:
            s, c = rest[-1]
            ents = [(st, ex + (c - 1) * abs(s) + 1) for (st, ex) in ents]
        else:
            ents = [(st, ex + 1) for (st, ex) in ents]
        ivs = tuple(sorted((st * es, (st + ex) * es) for (st, ex) in ents))
        return (name, p0, p1, ivs)

    @staticmethod
    def _ov(a, b):
        if a[1] >= b[2] or b[1] >= a[2]:
            return False
        for (s0, e0) in a[3]:
            for (s1, e1) in b[3]:
                if s0 < e1 and s1 < e0:
                    return True
        return False

    @staticmethod
    def _covers(a, b):
        if len(a[3]) != 1:
            return a[1] <= b[1] and a[2] >= b[2] and a[3] == b[3]
        if a[1] > b[1] or a[2] < b[2]:
            return False
        s0, e0 = a[3][0]
        return all(s0 <= s1 and e1 <= e0 for (s1, e1) in b[3])

    def add(self, eng, fn, reads=(), writes=(), dma=False):
        op = {"eng": eng, "fn": fn, "deps": set(), "idx": len(self.ops), "inc": False, "dma": dma}
        for ap in reads:
            rg = self.region(ap)
            if rg is None:
                continue
            for key, d in self.w.get(rg[0], {}).items():
                if self._ov(rg, key):
                    op["deps"].update(d.values())
            self.r.setdefault(rg[0], {}).setdefault(rg, {})[eng] = op["idx"]
        for ap in writes:
            rg = self.region(ap)
            if rg is None:
                continue
            for table in (self.w, self.r):
                tb = table.get(rg[0], {})
                dead = []
                for key, d in tb.items():
                    if self._ov(rg, key):
                        op["deps"].update(d.values())
                        if self._covers(rg, key):
                            dead.append(key)
                for k in dead:
                    del tb[k]
            self.w.setdefault(rg[0], {})[rg] = {eng: op["idx"]}
        op["deps"].discard(op["idx"])
        self.ops.append(op)
        return op

    def emit(self, es):
        nc = self.nc
        ops = self.ops
        for op in ops:
            op["deps"] = {d for d in op["deps"] if not (op["eng"] == "pe" and ops[d]["eng"] == "pe" and not ops[d]["dma"])}
            latest = {}
            keep = set()
            for d in op["deps"]:
                p = ops[d]
                if p["dma"]:
                    keep.add(d)
                else:
                    latest[p["eng"]] = max(latest.get(p["eng"], -1), d)
            op["deps"] = keep | set(latest.values())
            for d in op["deps"]:
                ops[d]["inc"] = True
        esem = {e: es.enter_context(nc.semaphore("sem_" + e)) for e in self.ENGS}
        dsem = [es.enter_context(nc.semaphore("dsem%d" % i)) for i in range(self.NDMA)]
        cnt = {e: 0 for e in self.ENGS}
        ndma = 0
        last_out_dma = []
        for op in ops:
            if op["dma"]:
                op["dsem"] = ndma % self.NDMA
                op["dcnt"] = 16 * (ndma // self.NDMA + 1)
                ndma += 1
            elif op["inc"]:
                cnt[op["eng"]] += 1
                op["cnt"] = cnt[op["eng"]]
        self.ndma = ndma
        block = es.enter_context(nc.Block())

        def stream(engname, e):
            seen = {}

            def wait(sem, key, val):
                if seen.get(key, 0) >= val:
                    return
                seen[key] = val
                e.wait_ge(sem, val)

            for op in ops:
                if op["eng"] != engname:
                    continue
                need = {}
                for d in op["deps"]:
                    p = ops[d]
                    if p["dma"]:
                        k = ("d", p["dsem"])
                        need[k] = max(need.get(k, 0), p["dcnt"])
                    else:
                        k = ("e", p["eng"])
                        need[k] = max(need.get(k, 0), p["cnt"])
                if op["dma"] and op["dcnt"] > 16:
                    k = ("d", op["dsem"])
                    need[k] = max(need.get(k, 0), op["dcnt"] - 16)
                for k, v in need.items():
                    wait(dsem[k[1]] if k[0] == "d" else esem[k[1]], k, v)
                ins = op["fn"](e)
                if op["dma"]:
                    ins.then_inc(dsem[op["dsem"]], 16)
                elif op["inc"]:
                    ins.then_inc(esem[op["eng"]], 1)
            if engname == "sp":
                for i in range(min(self.NDMA, ndma)):
                    n_i = (ndma - 1 - i) // self.NDMA + 1
                    wait(dsem[i], ("d", i), 16 * n_i)

        @block.tensor
        def _(e):
            stream("pe", e)

        @block.scalar
        def _(e):
            stream("act", e)

        @block.vector
        def _(e):
            stream("dve", e)

        @block.gpsimd
        def _(e):
            stream("pool", e)

        @block.sync
        def _(e):
            stream("sp", e)


def col_blocks(lo, hi, w=512):
    out = []
    while lo < hi:
        n = min(w, hi - lo)
        out.append((lo, n))
        lo += n
    return out


def build_nc(debug=False):
    nc = bass.Bass("TRN2", target_bir_lowering=False)
    es = ExitStack()

    def di(name, shape, dt=F32):
        return nc.dram_tensor(name, shape, dt, kind="ExternalInput").ap()

    x_ext = di("x_ext", [NE, 1024])
    x_rest = di("x_rest", [6144, 1024])
    memd = di("mem", [256, 1024])
    w_in = di("w_in", [1024, 1792])
    w_out = di("w_out", [1024, 1024])
    w_mem_q = di("w_mem_q", [1024, 1024])
    w_mem_kv = di("w_mem_kv", [1024, 2048])
    w_mem_o = di("w_mem_o", [1024, 1024])
    w_up = di("w_up", [1024, 2 * DFF])
    w_down = di("w_down", [DFF, 1024])
    gfm = di("gfm", [128, 4 * 8])
    gpost = di("gpost", [3, 1024])
    convp = di("convp", [128, 4 * 34])
    qkg = di("qkg", [128, 2])
    ffnp = di("ffnp", [128, 44 * 4])
    cst = di("cst", [128, 512])
    ropek = di("ropek", [2, 128, 8192])
    ropeq = di("ropeq", [2, 128, NE])
    maskd = di("mask", [128, 2])
    outd = nc.dram_tensor("out", [2048, 1024], F32, kind="ExternalOutput").ap()
    xs1 = nc.dram_tensor("xs1", [NE, 1024], F32, kind="Internal").ap()
    xs2 = nc.dram_tensor("xs2", [NE, 1024], F32, kind="Internal").ap()
    rds = nc.dram_tensor("rds", [64, 512], F32, kind="Internal").ap()
    dbg = {}

    S = Sched(nc, tracked_dram=["xs1", "xs2", "out", "rds"])

    def sb(name, shape, dt=F32):
        return es.enter_context(nc.sbuf_tensor(name, shape, dt))

    BIG8 = sb("BIG8", [128, 8 * NE], BF16)
    BIGQ = sb("BIGQ", [128, 8 * NE], BF16)
    G = sb("G", [128, NFC * 2048], BF16)
    STG = [sb("STG%d" % i, [128, 1024]) for i in range(2)]
    XT = [sb("XT%d" % i, [128, 1024]) for i in range(2)]
    YT = [sb("YT%d" % i, [128, 1024]) for i in range(2)]
    HB = [sb("HB%d" % i, [128, 1024], BF16) for i in range(2)]
    TMP = [sb("TMP%d" % i, [128, 512]) for i in range(6)]
    TMPB = [sb("TMPB%d" % i, [128, 512], BF16) for i in range(4)]
    GB = sb("GB", [128, 1024])
    CST = sb("CST", [128, 512], BF16)
    ONESF = sb("ONESF", [128, 64])
    GFM = sb("GFM", [128, 32])
    CONVP = sb("CONVP", [128, 4 * 34])
    QKG = sb("QKG", [128, 2])
    FFNP = sb("FFNP", [128, 44 * 4])
    MASK = sb("MASK", [128, 2])
    STAT = sb("STAT", [128, 64])
    PS = es.enter_context(nc.psum_tensor("PS", [128, 4096], F32))

    IDENT = CST[:, 0:128]
    PERM = CST[:, 128:256]
    BONES = CST[:, 256:384]
    ONES = CST[:, 384:512]

    def bank(b, n=512):
        return PS[:, 512 * b:512 * b + n]

    def bankbf(b):
        return PS[:, 512 * b:512 * b + 512].bitcast(BF16)

    def v3(ap2d, c):
        return ap2d.rearrange("p (c t) -> p c t", c=c)

    HT = v3(BIG8[:, :], 8)
    BQ = v3(BIGQ[:, :], 8)
    AT = BQ[:, 0:4, :]
    QT = BQ[:, 4:8, :]
    KT = G[:, 0:8192]
    VV = G[:, 8192:8192 + 64 * 130].rearrange("p (t g d) -> p t g d", t=64, g=2)
    WREG = G[:, 16512:16512 + 15872]
    WIN = v3(WREG[:, 0:8 * 1792], 8)
    DIAG = WREG[:, 0:15872].rearrange("p (c k m) -> p c k m", c=4, k=31)
    KM = v3(G[:, 32384:34432], 8)
    VM = v3(G[:, 34432:36480], 2)
    MEMT = v3(G[:, 36480:38528], 8)
    WKV = v3(BIGQ[:, 0:16384], 8)
    HTB = [v3(BIGQ[:, i * 4096:(i + 1) * 4096], 8) for i in range(2)]
    WOUT = v3(BIGQ[:, 0:8192], 8)
    WMQ = v3(G[:, 0:8192], 8)
    WMO = v3(G[:, 8192:16384], 8)
    GT = v3(G[:, :], NFC)
    WU = [v3(BIGQ[:, i * 8192:(i + 1) * 8192], 8) for i in range(2)]
    PT = [WREG[:, i * 1536:(i + 1) * 1536] for i in range(4)]
    _rf = G[:, 38528:38528 + 4096].bitcast(F32)
    ROPE = [_rf[:, i * 1024:(i + 1) * 1024] for i in range(2)]
    WDA = BIGQ[:, 0:6144]
    ATT1 = TMPB[3]

    cntr = {"stg": 0, "xt": 0, "yt": 0, "hb": 0, "rope": 0, "tp": 0, "stat": 0}

    def rot(key, n):
        v = cntr[key]
        cntr[key] = (v + 1) % n
        return v

    def stat1(m):
        i = rot("stat", 64)
        return STAT[0:m, i:i + 1]

    def dma(out, in_, q="sp"):
        S.add(q, lambda e: e.dma_start(out=out, in_=in_), reads=[in_], writes=[out], dma=True)

    def mm(out, lhsT, rhs, start, stop):
        S.add("pe", lambda e: e.matmul(out, lhsT, rhs, start=start, stop=stop), reads=[lhsT, rhs], writes=[out])

    def transp(out, in_, ident):
        S.add("pe", lambda e: e.transpose(out, in_, ident), reads=[in_, ident], writes=[out])

    def act(out, in_, func, bias=None, scale=None, accum=None):
        kw = {}
        rd = [in_]
        wr = [out]
        if bias is not None:
            kw["bias"] = bias
            if not isinstance(bias, float):
                rd.append(bias)
        if scale is not None:
            kw["scale"] = scale
            if not isinstance(scale, float):
                rd.append(scale)
        if accum is not None:
            kw["accum_out"] = accum
            wr.append(accum)
        S.add("act", lambda e: e.activation(out=out, in_=in_, func=func, **kw), reads=rd, writes=wr)

    def tt(eng, out, in0, in1, op):
        S.add(eng, lambda e: e.tensor_tensor(out=out, in0=in0, in1=in1, op=op), reads=[in0, in1], writes=[out])

    def ts(eng, out, in0, s1, op0, s2=None, op1=None):
        rd = [in0] + [s for s in (s1, s2) if s is not None and not isinstance(s, float)]
        if op1 is None:
            S.add(eng, lambda e: e.tensor_scalar(out=out, in0=in0, scalar1=s1, scalar2=None, op0=op0), reads=rd, writes=[out])
        else:
            S.add(eng, lambda e: e.tensor_scalar(out=out, in0=in0, scalar1=s1, scalar2=s2, op0=op0, op1=op1), reads=rd, writes=[out])

    def stt(eng, out, in0, scalar, in1, op0, op1):
        rd = [in0, in1] + ([] if isinstance(scalar, float) else [scalar])
        S.add(eng, lambda e: e.scalar_tensor_tensor(out=out, in0=in0, scalar=scalar, in1=in1, op0=op0, op1=op1), reads=rd, writes=[out])

    def cp(eng, out, in_):
        if eng == "act":
            act(out, in_, AF.Identity)
        else:
            S.add(eng, lambda e: e.tensor_copy(out=out, in_=in_), reads=[in_], writes=[out])

    def recip(out, in_):
        S.add("dve", lambda e: e.reciprocal(out=out, in_=in_), reads=[in_], writes=[out])

    def memset(eng, ap, val):
        S.add(eng, lambda e: e.memset(ap, val), writes=[ap])

    def rsqrt_from(out, in_, scale, m=None):
        act(out, in_, AF.Ln, bias=EPSB[0:out.shape[0], 0:1] if m is None else EPSB[0:m, 0:1], scale=scale)
        act(out, out, AF.Exp, scale=-0.5)

    def wpiece(dst, srcs, ncols, scal=None, eng="dve", in_view=None):
        slot = STG[rot("stg", 2)]
        for (src, p0, p1, c0, nc_) in srcs:
            dma(slot[p0:p1, c0:c0 + nc_], src, q="pool")
        src_ap = slot[:, 0:ncols] if in_view is None else in_view(slot)
        if scal is None:
            cp(eng, dst, src_ap)
        elif eng == "act":
            act(dst, src_ap, AF.Identity, scale=scal)
        else:
            ts(eng, dst, src_ap, scal, ALU.mult)

    EPSB = sb("EPSB", [128, 1])
    memset("pool", EPSB[:, :], EPS)
    memset("pool", ONESF[:, :], 1.0)
    wpiece(CST[:, :], [(cst[:, :], 0, 128, 0, 512)], 512, None, eng="dve")
    dma(GFM[:, :], gfm[:, :])
    dma(CONVP[:, :], convp[:, :])
    dma(QKG[:, :], qkg[:, :])
    dma(FFNP[:, :], ffnp[:, :])
    dma(MASK[:, :], maskd[:, :])
    memset("pool", VV[:, :, :, 64:65], 1.0)

    def load_weight_rows(dst3, src, ncols_total, gcol, col_pieces=None, rows_of_chunk=None, nchunks=8):
        for c in range(nchunks):
            for (c0, ncol) in (col_pieces or col_blocks(0, ncols_total, 1024)):
                if rows_of_chunk is None:
                    srcs = [(src[c * 128:(c + 1) * 128, c0:c0 + ncol], 0, 128, 0, ncol)]
                else:
                    srcs = [(src[r0:r0 + nr, c0:c0 + ncol], p0, p0 + nr, 0, ncol) for (r0, nr, p0) in rows_of_chunk(c)]
                scal = None if gcol is None else GFM[:, gcol * 8 + c:gcol * 8 + c + 1]
                wpiece(dst3[:, c, c0:c0 + ncol], srcs, ncol, scal)

    def norm_to_T(src_tile, m, dst, evac_eng):
        ss = stat1(m)
        hb = HB[rot("hb", 2)]
        act(hb[0:m, :], src_tile, AF.Square, accum=ss)
        rs = stat1(m)
        rsqrt_from(rs, ss, 1.0 / 1024.0, m)
        ts("dve", hb[0:m, :], src_tile, rs, ALU.mult)
        tb = 6 + rot("tp", 2)
        tpv = v3(bankbf(tb), 8)
        for c in range(8):
            transp(tpv[:, c, 0:m], hb[0:m, c * 128:(c + 1) * 128], IDENT[0:m, 0:m])
        cp(evac_eng, dst, tpv[:, :, 0:m])

    def nr1(src, n, gain, b_ss, b_rot):
        xg = TMPB[0][:, 0:n]
        sq = TMPB[1][:, 0:n]
        act(xg, src, AF.Identity, scale=gain)
        act(sq, src, AF.Square)
        mm(bank(b_ss, n), BONES, sq, True, True)
        mm(bank(b_rot, n), PERM, xg, True, True)

    def nr2(n, cos, sin, b_ss, b_rot):
        xg = TMPB[0][:, 0:n]
        rs = TMP[0][:, 0:n]
        act(rs, bank(b_ss, n), AF.Ln, bias=EPSB[:, 0:1], scale=1.0 / 64.0)
        act(rs, rs, AF.Exp, scale=-0.5)
        tt("dve", TMP[1][:, 0:n], xg, cos, ALU.mult)
        tt("dve", TMP[2][:, 0:n], bank(b_rot, n), sin, ALU.mult)

    def nr3(n, dst):
        t1 = TMP[1][:, 0:n]
        tt("pool", t1, t1, TMP[2][:, 0:n], ALU.add)
        tt("pool", dst, t1, TMP[0][:, 0:n], ALU.mult)

    def normrope(src, n, gain, cos, sin, dst, b_ss, b_rot):
        nr1(src, n, gain, b_ss, b_rot)
        nr2(n, cos, sin, b_ss, b_rot)
        nr3(n, dst)

    def phase_c0():
      load_weight_rows(WKV, w_mem_kv, 2048, 3)
      for mt in range(2):
          xt = XT[rot("xt", 2)]
          dma(xt[:, :], memd[mt * 128:(mt + 1) * 128, :])
          norm_to_T(xt[:, :], 128, MEMT[:, :, mt * 128:(mt + 1) * 128], "dve")
      for oc in range(8):
          pb = bank(oc % 2, 256)
          for c in range(8):
              mm(pb, WKV[:, c, oc * 128:(oc + 1) * 128], MEMT[:, c, :], c == 0, c == 7)
          cp("act", KM[:, oc, :], pb)
      for mt in range(2):
          for hf in range(2):
              pb = bank(2 + (mt * 2 + hf) % 2)
              for c in range(8):
                  mm(pb, MEMT[:, c, mt * 128:(mt + 1) * 128], WKV[:, c, 1024 + hf * 512:1024 + (hf + 1) * 512], c == 0, c == 7)
              cp("dve", VM[:, mt, hf * 512:(hf + 1) * 512], pb)

    def load_win_chunk(c):
        sc = GFM[:, c:c + 1]
        wpiece(WIN[:, c, 0:1024], [(w_in[c * 128:(c + 1) * 128, 0:1024], 0, 128, 0, 1024)], 1024, sc)
        slot_view = lambda slot: slot[:, 0:512].rearrange("p (h j d) -> p j h d", h=2, j=4)
        wpiece(WIN[:, c, 1024:1536].rearrange("p (j h d) -> p j h d", j=4, h=2),
               [(w_in[c * 128:(c + 1) * 128, 1024:1792], 0, 128, 0, 768)], 512, sc, in_view=slot_view)
        last = STG[(cntr["stg"] + 1) % 2]
        ts("dve", WIN[:, c, 1536:1792], last[:, 512:768], sc, ALU.mult)

    XQ = [XT[0], XT[1], YT[0], YT[1]]

    def tile_load(t, src, m):
        dma(XQ[t % 4][0:m, :], src)

    def tile_front(t, src, m):
        xt = XQ[t % 4]
        ss = stat1(m)
        hb = HB[t % 2]
        act(hb[0:m, :], xt[0:m, :], AF.Square, accum=ss)
        rs = stat1(m)
        rsqrt_from(rs, ss, 1.0 / 1024.0, m)
        ts("dve", hb[0:m, :], xt[0:m, :], rs, ALU.mult)

    def tile_back(t, m, dst, evac_eng):
        hb = HB[t % 2]
        tpv = v3(bankbf(6 + t % 2), 8)
        for c in range(8):
            transp(tpv[:, c, 0:m], hb[0:m, c * 128:(c + 1) * 128], IDENT[0:m, 0:m])
        cp(evac_eng, dst, tpv[:, :, 0:m])

    def kv_job(hsrc, kb):
        i = kb
        rp_ = ROPE[i % 2]
        kp = bank(i % 2)
        vp = bank(2 + i % 2)

        def P():
            dma(rp_[:, 0:512], ropek[0, :, kb * 512:(kb + 1) * 512])
            dma(rp_[:, 512:1024], ropek[1, :, kb * 512:(kb + 1) * 512])
            for c in range(8):
                mm(kp, WIN[:, c, 1536:1664], hsrc[:, c, :], c == 0, c == 7)
            for t in range(4):
                for c in range(8):
                    mm(vp[:, t * 128:(t + 1) * 128], hsrc[:, c, t * 128:(t + 1) * 128], WIN[:, c, 1664:1792], c == 0, c == 7)

        def N1():
            nr1(kp, 512, QKG[:, 1:2], 4, 5)
            cp("act", VV[:, kb * 4:kb * 4 + 4, :, 0:64], vp.rearrange("p (t g d) -> p t g d", t=4, g=2))

        def N2():
            nr2(512, rp_[:, 0:512], rp_[:, 512:1024], 4, 5)

        def N3():
            nr3(512, KT[:, kb * 512:(kb + 1) * 512])
        return [P, N1, N2, N3]

    def run_tiles(tiles, jobs_ready, extra=None):
        active = []

        def step_jobs(k):
            for _ in range(k):
                if active:
                    active[0].pop(0)()
                    if not active[0]:
                        active.pop(0)
        for t0 in range(min(3, len(tiles))):
            tile_load(t0, tiles[t0][0], tiles[t0][1])
        tile_front(0, tiles[0][0], tiles[0][1])
        for t in range(len(tiles)):
            if t + 3 < len(tiles):
                tile_load(t + 3, tiles[t + 3][0], tiles[t + 3][1])
            if t + 1 < len(tiles):
                tile_front(t + 1, tiles[t + 1][0], tiles[t + 1][1])
            tile_back(t, tiles[t][1], tiles[t][2], "act" if t % 2 else "dve")
            if extra is not None:
                extra(t)
            for job in jobs_ready.get(t, []):
                active.append(job)
            step_jobs(2 if len(active) > 1 else 1)
        while active:
            step_jobs(1)

    ext_tiles = col_blocks(0, NE, 128)
    tiles = [(x_ext[e0:e0 + m, :], m, HT[:, :, e0:e0 + m]) for (e0, m) in ext_tiles]
    jobs = {4 * kb + 4: [kv_job(HT[:, :, 16 + kb * 512:16 + (kb + 1) * 512], kb)] for kb in range(4)}
    run_tiles(tiles, jobs, extra=lambda t: [load_win_chunk(2 * t), load_win_chunk(2 * t + 1)] if t < 4 else None)
    phase_c0()
    tiles = []
    jobs = {}
    for rb in range(12):
        for t in range(4):
            r0 = rb * 512 + t * 128
            tiles.append((x_rest[r0:r0 + 128, :], 128, HTB[rb % 2][:, :, t * 128:(t + 1) * 128]))
        jobs[4 * rb + 3] = [kv_job(HTB[rb % 2], 4 + rb)]
    run_tiles(tiles, jobs)

    qitems = [(bk, e0, n, j) for bk, (e0, n) in enumerate(col_blocks(E_LO, E_HI)) for j in range(4)]

    def q_a(i):
        bk, e0, n, j = qitems[i]
        rp_ = ROPE[bk % 2]
        if j == 0:
            dma(rp_[:, 0:n], ropeq[0, :, e0:e0 + n])
            dma(rp_[:, 512:512 + n], ropeq[1, :, e0:e0 + n])
        qp = bank(6 + i % 2, n)
        for c in range(8):
            mm(qp, WIN[:, c, 1024 + j * 128:1024 + (j + 1) * 128], HT[:, c, e0:e0 + n], c == 0, c == 7)

    def q_n(i):
        bk, e0, n, j = qitems[i]
        rp_ = ROPE[bk % 2]
        normrope(bank(6 + i % 2, n), n, QKG[:, 0:1], rp_[:, 0:n], rp_[:, 512:512 + n], QT[:, j, e0:e0 + n], 4, 5)

    q_a(0)
    for i in range(len(qitems)):
        if i + 1 < len(qitems):
            q_a(i + 1)
        q_n(i)
    bi = 0
    for (e0, n) in col_blocks(0, NE):
        for ci in range(4):
            av = bank(2 * (bi % 2), n)
            ag = bank(2 * (bi % 2) + 1, n)
            bi += 1
            for c in range(8):
                mm(av, WIN[:, c, ci * 128:(ci + 1) * 128], HT[:, c, e0:e0 + n], c == 0, c == 7)
            for c in range(8):
                mm(ag, WIN[:, c, 512 + ci * 128:512 + (ci + 1) * 128], HT[:, c, e0:e0 + n], c == 0, c == 7)
            sg = TMP[3 + bi % 2][:, 0:n]
            act(sg, ag, AF.Sigmoid)
            tt("dve", AT[:, ci, e0:e0 + n], av, sg, ALU.mult)

    for ci in range(4):
        for k in range(31):
            ts("dve", DIAG[:, ci, k, :], IDENT, CONVP[:, ci * 34 + k:ci * 34 + k + 1], ALU.mult)
    for (e0, n) in col_blocks(E_LO, E_HI):
        sm = bank(4, n)
        sq_ = bank(5, n)
        for ci in range(4):
            cv = bank(ci, n)
            for k in range(31):
                mm(cv, DIAG[:, ci, k, :], AT[:, ci, e0 + k - 15:e0 + k - 15 + n], k == 0, k == 30)
            bcol = CONVP[:, ci * 34 + 31:ci * 34 + 32]
            cb = TMPB[ci % 2][:, 0:n]
            cs = TMPB[2 + ci % 2][:, 0:n]
            act(cb, cv, AF.Identity, bias=bcol)
            act(cs, cv, AF.Square, bias=bcol)
            mm(sm, ONES, cb, ci == 0, ci == 3)
            mm(sq_, ONES, cs, ci == 0, ci == 3)
        mean = TMP[0][:, 0:n]
        ts("dve", mean, sm, 1.0 / 512.0, ALU.mult)
        msq = TMP[1][:, 0:n]
        tt("dve", msq, mean, mean, ALU.mult)
        var = TMP[2][:, 0:n]
        stt("dve", var, sq_, 1.0 / 512.0, msq, ALU.mult, ALU.subtract)
        act(var, var, AF.Ln, bias=EPSB[:, 0:1], scale=1.0)
        act(var, var, AF.Exp, scale=-0.5)
        for ci in range(4):
            cv = bank(ci, n)
            bcol = CONVP[:, ci * 34 + 31:ci * 34 + 32]
            t1 = TMP[3 + ci % 2][:, 0:n]
            stt("dve", t1, cv, bcol, mean, ALU.add, ALU.subtract)
            tt("pool", t1, t1, var, ALU.mult)
            act(HT[:, ci, e0:e0 + n], t1, AF.Silu, bias=CONVP[:, ci * 34 + 33:ci * 34 + 34],
                scale=CONVP[:, ci * 34 + 32:ci * 34 + 33])

    def wout_rows(c):
        if c < 4:
            return [(c * 128, 128, 0)]
        j = c - 4
        return [(512 + 64 * j, 64, 0), (512 + 64 * (4 + j), 64, 64)]
    load_weight_rows(WOUT, w_out, 1024, None, rows_of_chunk=wout_rows)
    dma(GB[:, :], gpost[0:1, :].partition_broadcast(128))

    PTX = [WREG[:, i * 2048:(i + 1) * 2048] for i in range(2)]
    PTY = [WREG[:, 4096 + i * 1024:4096 + (i + 1) * 1024] for i in range(2)]
    groups = [(j, e0, n) for j in range(4) for (e0, n) in col_blocks(E_LO, E_HI)]
    batches = []
    nxy = {"X": 0, "Y": 0}
    for gi in range(len(groups)):
        kt = 0
        turn = "X"
        while kt < 64:
            if turn == "X" and kt + 2 <= 64:
                kts = [kt, kt + 1]
                kind = "X"
            else:
                kts = [kt]
                kind = "Y"
            kt += len(kts)
            batches.append((gi, kts, kind, nxy[kind], kt == 64))
            nxy[kind] += 1
            turn = "Y" if turn == "X" else "X"

    def sbank(kind, i, h):
        return (2 * i + h) if kind == "X" else (4 + h)

    def emit_qk(bn):
        gi, kts, kind, ser, last = batches[bn]
        j, e0, n = groups[gi]
        for i, kt in enumerate(kts):
            for h in range(2):
                mm(bank(sbank(kind, i, h), n), KT[64 * h:64 * h + 64, kt * 128:(kt + 1) * 128],
                   QT[64 * h:64 * h + 64, j, e0:e0 + n], True, True)

    def emit_exp(bn):
        gi, kts, kind, ser, last = batches[bn]
        j, e0, n = groups[gi]
        nit = 2 * len(kts)
        b0 = 0 if kind == "X" else 4
        pt = (PTX if kind == "X" else PTY)[ser % 2]
        sv = PS[:, 512 * b0:512 * (b0 + nit)].rearrange("p (b t) -> p b t", b=nit)[:, :, 0:n]
        pv = pt[:, 0:512 * nit].rearrange("p (b t) -> p b t", b=nit)[:, :, 0:n]
        act(pv, sv, AF.Exp, scale=0.125)

    def emit_pv(bn):
        gi, kts, kind, ser, last = batches[bn]
        j, e0, n = groups[gi]
        pt = (PTX if kind == "X" else PTY)[ser % 2]
        for i, kt in enumerate(kts):
            for h in range(2):
                it = 2 * i + h
                mm(bank(6 + h, n)[0:65, :], VV[:, kt, h, 0:65], pt[:, 512 * it:512 * it + n], kt == 0, kt == 63)
        if last:
            for h in range(2):
                cp("dve", TMP[h][0:65, 0:n], bank(6 + h, n)[0:65, :])
            for h in range(2):
                recip(TMP[2 + h][64:65, 0:n], TMP[h][64:65, 0:n])
            for h in range(2):
                osb = TMP[h][0:65, 0:n]
                rd = TMP[2 + h][64:65, 0:n]
                rdb = TMP[4 + h][0:64, 0:n]
                row = 2 * gi + h
                dma(rds[row:row + 1, 0:n], rd)
                dma(rdb, rds[row:row + 1, 0:n].partition_broadcast(64))
                if h == 0:
                    tt("dve", HT[0:64, 4 + j, e0:e0 + n], osb[0:64, :], rdb, ALU.mult)
                else:
                    a1_ = ATT1[0:64, 0:n]
                    tt("dve", a1_, osb[0:64, :], rdb, ALU.mult)
                    dma(HT[64:128, 4 + j, e0:e0 + n], a1_)

    pend = {"X": 0, "Y": 0}
    nq = [0]

    def try_qk():
        while nq[0] < len(batches) and pend[batches[nq[0]][2]] == 0:
            emit_qk(nq[0])
            pend[batches[nq[0]][2]] += 1
            nq[0] += 1

    try_qk()
    for bn in range(len(batches)):
        emit_exp(bn)
        pend[batches[bn][2]] -= 1
        try_qk()
        emit_pv(bn)

    def outproj_phase(tiles, nk, lhs_fn, rhs_fn, xsrc_fn, xdst_fn, hdst_fn):
        def ybuf(t):
            m = tiles[t][1]
            return PS[0:m, 1024 * (t % 2):1024 * (t % 2) + 1024]

        def st_a(t):
            y = ybuf(t)
            dma(XT[t % 2][0:tiles[t][1], :], xsrc_fn(t))
            for hf in range(2):
                for c in range(nk):
                    mm(y[:, hf * 512:(hf + 1) * 512], lhs_fn(t, c), rhs_fn(c, hf), c == 0, c == nk - 1)

        def st_b1(t):
            m = tiles[t][1]
            y = ybuf(t)
            xt = XT[t % 2]
            ss = stat1(m)
            yt = YT[t % 2]
            act(yt[0:m, :], y, AF.Square, accum=ss)
            rs = stat1(m)
            rsqrt_from(rs, ss, 1.0 / 1024.0, m)
            stt("dve", yt[0:m, :], y, rs, GB[0:m, :], ALU.mult, ALU.mult)
            tt("pool", yt[0:m, :], yt[0:m, :], xt[0:m, :], ALU.add)
            dma(xdst_fn(t), yt[0:m, :])

        def st_b2(t):
            if hdst_fn is None:
                return
            m = tiles[t][1]
            yt = YT[t % 2]
            hb = HB[t % 2]
            ss = stat1(m)
            act(hb[0:m, :], yt[0:m, :], AF.Square, accum=ss)
            rs = stat1(m)
            rsqrt_from(rs, ss, 1.0 / 1024.0, m)
            ts("dve", hb[0:m, :], yt[0:m, :], rs, ALU.mult)

        def st_c(t):
            if hdst_fn is None:
                return
            tile_back(t, tiles[t][1], hdst_fn(t), "act" if t % 2 else "dve")

        stages = [st_a, st_b1, st_b2, st_c]
        for s_ in range(len(tiles) + len(stages) - 1):
            for k, st in enumerate(stages):
                t = s_ - k
                if 0 <= t < len(tiles):
                    st(t)

    tok_tiles = col_blocks(E_LO, E_HI, 128)

    def ht_tile(t):
        e0, m = tok_tiles[t]
        return HT[:, :, e0:e0 + m]

    outproj_phase(tok_tiles, 8,
                  lambda t, c: HT[:, c, tok_tiles[t][0]:tok_tiles[t][0] + tok_tiles[t][1]],
                  lambda c, hf: WOUT[:, c, hf * 512:(hf + 1) * 512],
                  lambda t: x_ext[tok_tiles[t][0]:tok_tiles[t][0] + tok_tiles[t][1], :],
                  lambda t: xs1[tok_tiles[t][0]:tok_tiles[t][0] + tok_tiles[t][1], :],
                  ht_tile)

    load_weight_rows(WMQ, w_mem_q, 1024, 1)
    load_weight_rows(WMO, w_mem_o, 1024, None)
    dma(GB[:, :], gpost[1:2, :].partition_broadcast(128))
    qi = 0
    for (e0, n) in col_blocks(E_LO, E_HI):
        for oc in range(8):
            qb = bank(qi % 2, n)
            qi += 1
            for c in range(8):
                mm(qb, WMQ[:, c, oc * 128:(oc + 1) * 128], HT[:, c, e0:e0 + n], c == 0, c == 7)
            cp("act" if oc % 2 else "dve", BQ[:, oc, e0:e0 + n], qb)
        for hd in range(4):
            for mt in range(2):
                sbk = bank(2 + mt, n)
                for dc in range(2):
                    mm(sbk, KM[:, 2 * hd + dc, mt * 128:(mt + 1) * 128], BQ[:, 2 * hd + dc, e0:e0 + n], dc == 0, dc == 1)
            pt = PTX[hd % 2]
            sv = PS[:, 1024:2048].rearrange("p (b t) -> p b t", b=2)[:, :, 0:n]
            pv = pt[:, 0:1024].rearrange("p (b t) -> p b t", b=2)[:, :, 0:n]
            act(pv, sv, AF.Exp, scale=1.0 / 16.0)
            den = bank(4, n)
            for mt in range(2):
                mm(den, ONES, pt[:, 512 * mt:512 * mt + n], mt == 0, mt == 1)
            rd = TMP[hd % 2][:, 0:n]
            act(rd, den, AF.Ln)
            act(rd, rd, AF.Exp, scale=-1.0)
            for dc in range(2):
                ob = bank(5 + dc, n)
                for mt in range(2):
                    mm(ob, VM[:, mt, (2 * hd + dc) * 128:(2 * hd + dc + 1) * 128], pt[:, 512 * mt:512 * mt + n], mt == 0, mt == 1)
                tt("dve", HT[:, 2 * hd + dc, e0:e0 + n], ob, rd, ALU.mult)
    outproj_phase(tok_tiles, 8,
                  lambda t, c: HT[:, c, tok_tiles[t][0]:tok_tiles[t][0] + tok_tiles[t][1]],
                  lambda c, hf: WMO[:, c, hf * 512:(hf + 1) * 512],
                  lambda t: xs1[tok_tiles[t][0]:tok_tiles[t][0] + tok_tiles[t][1], :],
                  lambda t: xs2[tok_tiles[t][0]:tok_tiles[t][0] + tok_tiles[t][1], :],
                  ht_tile)
    ts("dve", HT[:, :, 15:16], HT[:, :, 15:16], MASK[:, 0:1], ALU.mult)
    ts("dve", HT[:, :, 2064:2065], HT[:, :, 2064:2065], MASK[:, 1:2], ALU.mult)

    dma(GB[:, :], gpost[2:3, :].partition_broadcast(128))
    fgroups = [(f0, min(4, NFC - f0)) for f0 in range(0, NFC, 4)]
    fblocks = []
    o0 = 16
    while o0 < 2064:
        no = min(510, 2064 - o0)
        fblocks.append((o0, no))
        o0 += no

    def load_wdown(chunks, dst_fn):
        for f in chunks:
            wpiece(dst_fn(f), [(w_down[f * 128:(f + 1) * 128, :], 0, 128, 0, 1024)], 1024, None)

    def load_wup_group(gi):
        f0, nf = fgroups[gi]
        wu = WU[gi % 2]
        for c in range(8):
            ncol = nf * 128
            srcs = [(w_up[c * 128:(c + 1) * 128, f0 * 128:f0 * 128 + ncol], 0, 128, 0, ncol),
                    (w_up[c * 128:(c + 1) * 128, DFF + f0 * 128:DFF + f0 * 128 + ncol], 0, 128, 512, ncol)]
            sc = GFM[:, 16 + c:16 + c + 1]
            if nf == 4:
                wpiece(wu[:, c, :], srcs, 1024, sc)
            else:
                slot_view = lambda slot: slot[:, :].rearrange("p (a t) -> p a t", a=2)[:, :, 0:ncol]
                wpiece(wu[:, c, :].rearrange("p (a t) -> p a t", a=2)[:, :, 0:ncol], srcs, 1024, sc, in_view=slot_view)

    load_wup_group(0)
    fitems = []
    for gi, (f0, nf) in enumerate(fgroups):
        for fl in range(nf):
            for bi_, (o0, no) in enumerate(fblocks):
                fitems.append((gi, fl, f0 + fl, o0, no, fl == 0 and bi_ == 0))

    def f_a(i):
        gi, fl, f, o0, no, first = fitems[i]
        if first and gi + 1 < len(fgroups):
            load_wup_group(gi + 1)
        wu = WU[gi % 2]
        ug = bank(2 * (i % 3), no + 2)
        uv = bank(2 * (i % 3) + 1, no + 2)
        for c in range(8):
            mm(ug, wu[:, c, fl * 128:(fl + 1) * 128], HT[:, c, o0 - 1:o0 + no + 1], c == 0, c == 7)
        for c in range(8):
            mm(uv, wu[:, c, 512 + fl * 128:512 + (fl + 1) * 128], HT[:, c, o0 - 1:o0 + no + 1], c == 0, c == 7)

    def f_b(i):
        gi, fl, f, o0, no, first = fitems[i]
        pg = FFNP[:, f * 4:f * 4 + 4]
        pv_ = FFNP[:, (NFC + f) * 4:(NFC + f) * 4 + 4]
        ug = bank(2 * (i % 3), no + 2)
        uv = bank(2 * (i % 3) + 1, no + 2)
        tg = TMP[i % 3][:, 0:no]
        tv = TMP[3 + i % 3][:, 0:no]
        act(tg, ug[:, 1:1 + no], AF.Identity, bias=pg[:, 3:4], scale=pg[:, 1:2])
        act(tv, uv[:, 1:1 + no], AF.Identity, bias=pv_[:, 3:4], scale=pv_[:, 1:2])
        stt("dve", tg, ug[:, 0:no], pg[:, 0:1], tg, ALU.mult, ALU.add)
        stt("dve", tg, ug[:, 2:2 + no], pg[:, 2:3], tg, ALU.mult, ALU.add)
        stt("dve", tv, uv[:, 0:no], pv_[:, 0:1], tv, ALU.mult, ALU.add)
        stt("dve", tv, uv[:, 2:2 + no], pv_[:, 2:3], tv, ALU.mult, ALU.add)

    def f_c(i):
        gi, fl, f, o0, no, first = fitems[i]
        tg = TMP[i % 3][:, 0:no]
        tv = TMP[3 + i % 3][:, 0:no]
        act(tg, tg, AF.Gelu_apprx_tanh)
        tt("pool", GT[:, f, o0 - 16:o0 - 16 + no], tg, tv, ALU.mult)

    fst = [f_a, f_b, f_c]
    for s_ in range(len(fitems) + 2):
        for k, st in enumerate(fst):
            t = s_ - k
            if 0 <= t < len(fitems):
                st(t)
    WD1 = v3(BIG8[:, 0:16384], 16)
    load_wdown(range(0, 6), lambda f: WDA[:, f * 1024:(f + 1) * 1024])
    load_wdown(range(6, NFC), lambda f: WD1[:, f - 6, :])

    def wd(f, hf):
        if f < 6:
            return WDA[:, f * 1024 + hf * 512:f * 1024 + (hf + 1) * 512]
        return WD1[:, f - 6, hf * 512:(hf + 1) * 512]

    ffn_tiles = [(16 + ti * 128, 128) for ti in range(16)]
    outproj_phase(ffn_tiles, NFC,
                  lambda t, c: GT[:, c, t * 128:(t + 1) * 128],
                  wd,
                  lambda t: xs2[16 + t * 128:16 + (t + 1) * 128, :],
                  lambda t: outd[t * 128:(t + 1) * 128, :],
                  None)

    S.emit(es)
    es.close()
    return nc


def _rope_tables(pos):
    pos = np.asarray(pos)
    inv = (10000.0 ** (-(np.arange(0, 32, 2, dtype=np.float32)) / np.float32(32))).astype(np.float32)
    r = (pos // 64).astype(np.float32)
    c = (pos % 64).astype(np.float32)
    ang_r = r[None, :] * inv[:, None]
    ang_c = c[None, :] * inv[:, None]
    ang = np.concatenate([ang_r, ang_r, ang_c, ang_c], axis=0).astype(np.float32)
    ang = np.concatenate([ang, ang], axis=0)
    return np.stack([np.cos(ang), np.sin(ang)]).astype(np.float32)


def _consts():
    ident = np.eye(128, dtype=np.float32)
    perm = np.zeros((128, 128), np.float32)
    for m in range(128):
        if (m % 32) < 16:
            perm[m + 16, m] = -1.0
        else:
            perm[m - 16, m] = 1.0
    bones = np.zeros((128, 128), np.float32)
    bones[:64, :64] = 1.0
    bones[64:, 64:] = 1.0
    ones = np.ones((128, 128), np.float32)
    return np.ascontiguousarray(np.concatenate([ident, perm, bones, ones], axis=1))


_NC_CACHE = {}


def kernel(x, mem, norm_mix_pre, w_in, conv_dw, conv_dw_b, conv_ln_g, conv_ln_b,
           q_norm_g, k_norm_g, w_out, norm_mix_post, norm_mem_pre, mem_norm_g,
           w_mem_q, w_mem_kv, w_mem_o, norm_mem_post, norm_ffn_pre, w_up, ffn_dw,
           ffn_dw_b, w_down, norm_ffn_post):
    f = lambda a: np.ascontiguousarray(np.asarray(a, dtype=np.float32))
    x = f(x); mem = f(mem)
    B, Sq, D = x.shape

    def fm(g):
        return f(g).reshape(8, 128).T
    gfm = np.ascontiguousarray(np.concatenate([fm(norm_mix_pre[0]), fm(norm_mem_pre[0]), fm(norm_ffn_pre[0]), fm(mem_norm_g[0])], axis=1))
    gpost = np.ascontiguousarray(np.stack([f(norm_mix_post[0]), f(norm_mem_post[0]), f(norm_ffn_post[0])]))
    cw = f(conv_dw[0])
    convp = np.zeros((128, 4, 34), np.float32)
    for ci in range(4):
        convp[:, ci, 0:31] = cw[:, ci * 128:(ci + 1) * 128].T
        convp[:, ci, 31] = f(conv_dw_b[0])[ci * 128:(ci + 1) * 128]
        convp[:, ci, 32] = f(conv_ln_g[0])[ci * 128:(ci + 1) * 128]
        convp[:, ci, 33] = f(conv_ln_b[0])[ci * 128:(ci + 1) * 128]
    convp = np.ascontiguousarray(convp.reshape(128, 136))
    qkg = np.ascontiguousarray(np.stack([np.tile(f(q_norm_g[0]), 2), np.tile(f(k_norm_g[0]), 2)], axis=1))
    fw = f(ffn_dw[0])
    fb = f(ffn_dw_b[0])
    ffnp = np.zeros((128, 44, 4), np.float32)
    for fc in range(44):
        ffnp[:, fc, 0:3] = fw[:, fc * 128:(fc + 1) * 128].T
        ffnp[:, fc, 3] = fb[fc * 128:(fc + 1) * 128]
    ffnp = np.ascontiguousarray(ffnp.reshape(128, 176))
    cst = _consts()
    shared = dict(w_in=f(w_in[0]), w_out=f(w_out[0]), w_mem_q=f(w_mem_q[0]), w_mem_kv=f(w_mem_kv[0]),
                  w_mem_o=f(w_mem_o[0]), w_up=f(w_up[0]), w_down=f(w_down[0]), gfm=gfm, gpost=gpost,
                  convp=convp, qkg=qkg, ffnp=ffnp, cst=cst)
    in_maps = []
    for core in range(8):
        b, j = core // 4, core % 4
        s = j * 2048
        xe = np.zeros((NE, 1024), np.float32)
        lo, hi = max(0, s - 16), min(Sq, s + 2064)
        xe[lo - (s - 16):hi - (s - 16)] = x[b, lo:hi]
        rest_idx = np.concatenate([np.arange(0, s), np.arange(s + 2048, Sq)])
        xr = np.ascontiguousarray(x[b, rest_idx])
        key_pos = np.concatenate([np.arange(s, s + 2048), rest_idx])
        ropek = _rope_tables(key_pos)
        ext_pos = np.clip(np.arange(s - 16, s - 16 + NE), 0, Sq - 1)
        ropeq = _rope_tables(ext_pos)
        mask = np.ones((128, 2), np.float32)
        if j == 0:
            mask[:, 0] = 0.0
        if j == 3:
            mask[:, 1] = 0.0
        m = dict(shared)
        m.update(x_ext=xe, x_rest=xr, mem=np.ascontiguousarray(mem[b]), ropek=ropek, ropeq=ropeq, mask=mask)
        in_maps.append(m)
    if "nc" not in _NC_CACHE:
        _NC_CACHE["nc"] = build_nc()
    nc = _NC_CACHE["nc"]
    res = run_bass_kernel_spmd(nc, in_maps, core_ids=list(range(8)))
    out = np.zeros((B, Sq, D), np.float32)
    for core in range(8):
        b, j = core // 4, core % 4
        out[b, j * 2048:(j + 1) * 2048] = np.asarray(res.results[core]["out"], dtype=np.float32)
    return out
```

```python
import numpy as np
from contextlib import ExitStack
import concourse.bass as bass
import concourse.mybir as mybir
from concourse.bass_utils import run_bass_kernel_spmd

F32 = mybir.dt.float32
BF16 = mybir.dt.bfloat16
AF = mybir.ActivationFunctionType
ALU = mybir.AluOpType

EPS = 1e-6
NE = 2080
E_LO, E_HI = 15, 2065
DFF = 2816
NFC = 22


def _esize(dt):
    return 2 if dt == BF16 else 4


class Sched:
    ENGS = ["pe", "act", "dve", "pool", "sp"]
    NDMA = 16

    def __init__(self, nc, tracked_dram=()):
        self.nc = nc
        self.ops = []
        self.w = {}
        self.r = {}
        self.tracked_dram = set(tracked_dram)

    def region(self, ap):
        t = ap.tensor
        name = t.name
        space = str(ap.space)
        if "DRAM" in space.upper() or "HBM" in space.upper() or type(t).__name__.startswith("DRam"):
            if name not in self.tracked_dram:
                return None
        es = _esize(ap.dtype)
        dims = ap.ap
        ps, pc = dims[0]
        off = int(ap.offset)
        if ps == 0:
            rs_ = int(t.shape[-1])
            p0, f0 = off // rs_, off % rs_
            p1 = p0 + 1
        else:
            p0 = off // ps
            f0 = off % ps
            p1 = p0 + pc
        ents = [(f0, 0)]
        rest = dims[1:]
        for (s, c) in rest[:-1]:
            s = abs(s)
            if s != 0 and len(ents) * c <= 64:
                ents = [(st + i * s, ex) for (st, ex) in ents for i in range(c)]
            else:
                ents = [(st, ex + (c - 1) * s) for (st, ex) in ents]
        if rest:
            s, c = rest[-1]
            ents = [(st, ex + (c - 1) * abs(s) + 1) for (st, ex) in ents]
        else:
            ents = [(st, ex + 1) for (st, ex) in ents]
        ivs = tuple(sorted((st * es, (st + ex) * es) for (st, ex) in ents))
        return (name, p0, p1, ivs)

    @staticmethod
    def _ov(a, b):
        if a[1] >= b[2] or b[1] >= a[2]:
            return False
        for (s0, e0) in a[3]:
            for (s1, e1) in b[3]:
                if s0 < e1 and s1 < e0:
                    return True
        return False

    @staticmethod
    def _covers(a, b):
        if len(a[3]) != 1:
            return a[1] <= b[1] and a[2] >= b[2] and a[3] == b[3]
        if a[1] > b[1] or a[2] < b[2]:
            return False
        s0, e0 = a[3][0]
        return all(s0 <= s1 and e1 <= e0 for (s1, e1) in b[3])

    def add(self, eng, fn, reads=(), writes=(), dma=False):
        op = {"eng": eng, "fn": fn, "deps": set(), "idx": len(self.ops), "inc": False, "dma": dma}
        for ap in reads:
            rg = self.region(ap)
            if rg is None:
                continue
            for key, d in self.w.get(rg[0], {}).items():
                if self._ov(rg, key):
                    op["deps"].update(d.values())
            self.r.setdefault(rg[0], {}).setdefault(rg, {})[eng] = op["idx"]
        for ap in writes:
            rg = self.region(ap)
            if rg is None:
                continue
            for table in (self.w, self.r):
                tb = table.get(rg[0], {})
                dead = []
                for key, d in tb.items():
                    if self._ov(rg, key):
                        op["deps"].update(d.values())
                        if self._covers(rg, key):
                            dead.append(key)
                for k in dead:
                    del tb[k]
            self.w.setdefault(rg[0], {})[rg] = {eng: op["idx"]}
        op["deps"].discard(op["idx"])
        self.ops.append(op)
        return op

    def emit(self, es):
        nc = self.nc
        ops = self.ops
        for op in ops:
            op["deps"] = {d for d in op["deps"] if not (op["eng"] == "pe" and ops[d]["eng"] == "pe" and not ops[d]["dma"])}
            latest = {}
            keep = set()
            for d in op["deps"]:
                p = ops[d]
                if p["dma"]:
                    keep.add(d)
                else:
                    latest[p["eng"]] = max(latest.get(p["eng"], -1), d)
            op["deps"] = keep | set(latest.values())
            for d in op["deps"]:
                ops[d]["inc"] = True
        esem = {e: es.enter_context(nc.semaphore("sem_" + e)) for e in self.ENGS}
        dsem = [es.enter_context(nc.semaphore("dsem%d" % i)) for i in range(self.NDMA)]
        cnt = {e: 0 for e in self.ENGS}
        ndma = 0
        last_out_dma = []
        for op in ops:
            if op["dma"]:
                op["dsem"] = ndma % self.NDMA
                op["dcnt"] = 16 * (ndma // self.NDMA + 1)
                ndma += 1
            elif op["inc"]:
                cnt[op["eng"]] += 1
                op["cnt"] = cnt[op["eng"]]
        self.ndma = ndma
        block = es.enter_context(nc.Block())

        def stream(engname, e):
            seen = {}

            def wait(sem, key, val):
                if seen.get(key, 0) >= val:
                    return
                seen[key] = val
                e.wait_ge(sem, val)

            for op in ops:
                if op["eng"] != engname:
                    continue
                need = {}
                for d in op["deps"]:
                    p = ops[d]
                    if p["dma"]:
                        k = ("d", p["dsem"])
                        need[k] = max(need.get(k, 0), p["dcnt"])
                    else:
                        k = ("e", p["eng"])
                        need[k] = max(need.get(k, 0), p["cnt"])
                if op["dma"] and op["dcnt"] > 16:
                    k = ("d", op["dsem"])
                    need[k] = max(need.get(k, 0), op["dcnt"] - 16)
                for k, v in need.items():
                    wait(dsem[k[1]] if k[0] == "d" else esem[k[1]], k, v)
                ins = op["fn"](e)
                if op["dma"]:
                    ins.then_inc(dsem[op["dsem"]], 16)
                elif op["inc"]:
                    ins.then_inc(esem[op["eng"]], 1)
            if engname == "sp":
                for i in range(min(self.NDMA, ndma)):
                    n_i = (ndma - 1 - i) // self.NDMA + 1
                    wait(dsem[i], ("d", i), 16 * n_i)

        @block.tensor
        def _(e):
            stream("pe", e)

        @block.scalar
        def _(e):
            stream("act", e)

        @block.vector
        def _(e):
            stream("dve", e)

        @block.gpsimd
        def _(e):
            stream("pool", e)

        @block.sync
        def _(e):
            stream("sp", e)


def col_blocks(lo, hi, w=512):
    out = []
    while lo < hi:
        n = min(w, hi - lo)
        out.append((lo, n))
        lo += n
    return out


def build_nc(debug=False):
    nc = bass.Bass("TRN2", target_bir_lowering=False)
    es = ExitStack()

    def di(name, shape, dt=F32):
        return nc.dram_tensor(name, shape, dt, kind="ExternalInput").ap()

    x_ext = di("x_ext", [NE, 1024])
    x_rest = di("x_rest", [6144, 1024])
    memd = di("mem", [256, 1024])
    w_in = di("w_in", [1024, 1792])
    w_out = di("w_out", [1024, 1024])
    w_mem_q = di("w_mem_q", [1024, 1024])
    w_mem_kv = di("w_mem_kv", [1024, 2048])
    w_mem_o = di("w_mem_o", [1024, 1024])
    w_up = di("w_up", [1024, 2 * DFF])
    w_down = di("w_down", [DFF, 1024])
    gfm = di("gfm", [128, 4 * 8])
    gpost = di("gpost", [3, 1024])
    convp = di("convp", [128, 4 * 34])
    qkg = di("qkg", [128, 2])
    ffnp = di("ffnp", [128, 44 * 4])
    cst = di("cst", [128, 512])
    ropek = di("ropek", [2, 128, 8192])
    ropeq = di("ropeq", [2, 128, NE])
    maskd = di("mask", [128, 2])
    outd = nc.dram_tensor("out", [2048, 1024], F32, kind="ExternalOutput").ap()
    xs1 = nc.dram_tensor("xs1", [NE, 1024], F32, kind="Internal").ap()
    xs2 = nc.dram_tensor("xs2", [NE, 1024], F32, kind="Internal").ap()
    rds = nc.dram_tensor("rds", [64, 512], F32, kind="Internal").ap()
    dbg = {}

    S = Sched(nc, tracked_dram=["xs1", "xs2", "out", "rds"])

    def sb(name, shape, dt=F32):
        return es.enter_context(nc.sbuf_tensor(name, shape, dt))

    BIG8 = sb("BIG8", [128, 8 * NE], BF16)
    BIGQ = sb("BIGQ", [128, 8 * NE], BF16)
    G = sb("G", [128, NFC * 2048], BF16)
    STG = [sb("STG%d" % i, [128, 1024]) for i in range(2)]
    XT = [sb("XT%d" % i, [128, 1024]) for i in range(2)]
    YT = [sb("YT%d" % i, [128, 1024]) for i in range(2)]
    HB = [sb("HB%d" % i, [128, 1024], BF16) for i in range(2)]
    TMP = [sb("TMP%d" % i, [128, 512]) for i in range(6)]
    TMPB = [sb("TMPB%d" % i, [128, 512], BF16) for i in range(4)]
    GB = sb("GB", [128, 1024])
    CST = sb("CST", [128, 512], BF16)
    ONESF = sb("ONESF", [128, 64])
    GFM = sb("GFM", [128, 32])
    CONVP = sb("CONVP", [128, 4 * 34])
    QKG = sb("QKG", [128, 2])
    FFNP = sb("FFNP", [128, 44 * 4])
    MASK = sb("MASK", [128, 2])
    STAT = sb("STAT", [128, 64])
    PS = es.enter_context(nc.psum_tensor("PS", [128, 4096], F32))

    IDENT = CST[:, 0:128]
    PERM = CST[:, 128:256]
    BONES = CST[:, 256:384]
    ONES = CST[:, 384:512]

    def bank(b, n=512):
        return PS[:, 512 * b:512 * b + n]

    def bankbf(b):
        return PS[:, 512 * b:512 * b + 512].bitcast(BF16)

    def v3(ap2d, c):
        return ap2d.rearrange("p (c t) -> p c t", c=c)

    HT = v3(BIG8[:, :], 8)
    BQ = v3(BIGQ[:, :], 8)
    AT = BQ[:, 0:4, :]
    QT = BQ[:, 4:8, :]
    KT = G[:, 0:8192]
    VV = G[:, 8192:8192 + 64 * 130].rearrange("p (t g d) -> p t g d", t=64, g=2)
    WREG = G[:, 16512:16512 + 15872]
    WIN = v3(WREG[:, 0:8 * 1792], 8)
    DIAG = WREG[:, 0:15872].rearrange("p (c k m) -> p c k m", c=4, k=31)
    KM = v3(G[:, 32384:34432], 8)
    VM = v3(G[:, 34432:36480], 2)
    MEMT = v3(G[:, 36480:38528], 8)
    WKV = v3(BIGQ[:, 0:16384], 8)
    HTB = [v3(BIGQ[:, i * 4096:(i + 1) * 4096], 8) for i in range(2)]
    WOUT = v3(BIGQ[:, 0:8192], 8)
    WMQ = v3(G[:, 0:8192], 8)
    WMO = v3(G[:, 8192:16384], 8)
    GT = v3(G[:, :], NFC)
    WU = [v3(BIGQ[:, i * 8192:(i + 1) * 8192], 8) for i in range(2)]
    PT = [WREG[:, i * 1536:(i + 1) * 1536] for i in range(4)]
    _rf = G[:, 38528:38528 + 4096].bitcast(F32)
    ROPE = [_rf[:, i * 1024:(i + 1) * 1024] for i in range(2)]
    WDA = BIGQ[:, 0:6144]
    ATT1 = TMPB[3]

    cntr = {"stg": 0, "xt": 0, "yt": 0, "hb": 0, "rope": 0, "tp": 0, "stat": 0}

    def rot(key, n):
        v = cntr[key]
        cntr[key] = (v + 1) % n
        return v

    def stat1(m):
        i = rot("stat", 64)
        return STAT[0:m, i:i + 1]

    def dma(out, in_, q="sp"):
        S.add(q, lambda e: e.dma_start(out=out, in_=in_), reads=[in_], writes=[out], dma=True)

    def mm(out, lhsT, rhs, start, stop):
        S.add("pe", lambda e: e.matmul(out, lhsT, rhs, start=start, stop=stop), reads=[lhsT, rhs], writes=[out])

    def transp(out, in_, ident):
        S.add("pe", lambda e: e.transpose(out, in_, ident), reads=[in_, ident], writes=[out])

    def act(out, in_, func, bias=None, scale=None, accum=None):
        kw = {}
        rd = [in_]
        wr = [out]
        if bias is not None:
            kw["bias"] = bias
            if not isinstance(bias, float):
                rd.append(bias)
        if scale is not None:
            kw["scale"] = scale
            if not isinstance(scale, float):
                rd.append(scale)
        if accum is not None:
            kw["accum_out"] = accum
            wr.append(accum)
        S.add("act", lambda e: e.activation(out=out, in_=in_, func=func, **kw), reads=rd, writes=wr)

    def tt(eng, out, in0, in1, op):
        S.add(eng, lambda e: e.tensor_tensor(out=out, in0=in0, in1=in1, op=op), reads=[in0, in1], writes=[out])

    def ts(eng, out, in0, s1, op0, s2=None, op1=None):
        rd = [in0] + [s for s in (s1, s2) if s is not None and not isinstance(s, float)]
        if op1 is None:
            S.add(eng, lambda e: e.tensor_scalar(out=out, in0=in0, scalar1=s1, scalar2=None, op0=op0), reads=rd, writes=[out])
        else:
            S.add(eng, lambda e: e.tensor_scalar(out=out, in0=in0, scalar1=s1, scalar2=s2, op0=op0, op1=op1), reads=rd, writes=[out])

    def stt(eng, out, in0, scalar, in1, op0, op1, accum=None):
        rd = [in0, in1] + ([] if isinstance(scalar, float) else [scalar])
        if accum is None:
            S.add(eng, lambda e: e.scalar_tensor_tensor(out=out, in0=in0, scalar=scalar, in1=in1, op0=op0, op1=op1), reads=rd, writes=[out])
        else:
            S.add(eng, lambda e: e.scalar_tensor_tensor(out=out, in0=in0, scalar=scalar, in1=in1, op0=op0, op1=op1, accum_out=accum),
                  reads=rd, writes=[out, accum])

    def cp(eng, out, in_):
        if eng == "act":
            act(out, in_, AF.Identity)
        else:
            S.add(eng, lambda e: e.tensor_copy(out=out, in_=in_), reads=[in_], writes=[out])

    def recip(out, in_):
        S.add("dve", lambda e: e.reciprocal(out=out, in_=in_), reads=[in_], writes=[out])

    def memset(eng, ap, val):
        S.add(eng, lambda e: e.memset(ap, val), writes=[ap])

    def rsqrt_from(out, in_, scale, m=None):
        act(out, in_, AF.Ln, bias=EPSB[0:out.shape[0], 0:1] if m is None else EPSB[0:m, 0:1], scale=scale)
        act(out, out, AF.Exp, scale=-0.5)

    def wpiece(dst, srcs, ncols, scal=None, eng="dve", in_view=None):
        slot = STG[rot("stg", 2)]
        for (src, p0, p1, c0, nc_) in srcs:
            dma(slot[p0:p1, c0:c0 + nc_], src, q="pool")
        src_ap = slot[:, 0:ncols] if in_view is None else in_view(slot)
        if scal is None:
            cp(eng, dst, src_ap)
        elif eng == "act":
            act(dst, src_ap, AF.Identity, scale=scal)
        else:
            ts(eng, dst, src_ap, scal, ALU.mult)

    EPSB = sb("EPSB", [128, 1])
    memset("pool", EPSB[:, :], EPS)
    memset("pool", ONESF[:, :], 1.0)
    wpiece(CST[:, :], [(cst[:, :], 0, 128, 0, 512)], 512, None, eng="dve")
    dma(GFM[:, :], gfm[:, :])
    dma(CONVP[:, :], convp[:, :])
    dma(QKG[:, :], qkg[:, :])
    dma(FFNP[:, :], ffnp[:, :])
    dma(MASK[:, :], maskd[:, :])
    memset("pool", VV[:, :, :, 64:65], 1.0)

    def load_weight_rows(dst3, src, ncols_total, gcol, col_pieces=None, rows_of_chunk=None, nchunks=8):
        for c in range(nchunks):
            for (c0, ncol) in (col_pieces or col_blocks(0, ncols_total, 1024)):
                if rows_of_chunk is None:
                    srcs = [(src[c * 128:(c + 1) * 128, c0:c0 + ncol], 0, 128, 0, ncol)]
                else:
                    srcs = [(src[r0:r0 + nr, c0:c0 + ncol], p0, p0 + nr, 0, ncol) for (r0, nr, p0) in rows_of_chunk(c)]
                scal = None if gcol is None else GFM[:, gcol * 8 + c:gcol * 8 + c + 1]
                wpiece(dst3[:, c, c0:c0 + ncol], srcs, ncol, scal)

    def norm_to_T(src_tile, m, dst, evac_eng):
        ss = stat1(m)
        hb = HB[rot("hb", 2)]
        act(hb[0:m, :], src_tile, AF.Square, accum=ss)
        rs = stat1(m)
        rsqrt_from(rs, ss, 1.0 / 1024.0, m)
        ts("dve", hb[0:m, :], src_tile, rs, ALU.mult)
        tb = 6 + rot("tp", 2)
        tpv = v3(bankbf(tb), 8)
        for c in range(8):
            transp(tpv[:, c, 0:m], hb[0:m, c * 128:(c + 1) * 128], IDENT[0:m, 0:m])
        cp(evac_eng, dst, tpv[:, :, 0:m])

    def nr1(src, n, gain, b_ss, b_rot):
        xg = TMPB[0][:, 0:n]
        sq = TMPB[1][:, 0:n]
        act(xg, src, AF.Identity, scale=gain)
        act(sq, src, AF.Square)
        mm(bank(b_ss, n), BONES, sq, True, True)
        mm(bank(b_rot, n), PERM, xg, True, True)

    def nr2(n, cos, sin, b_ss, b_rot):
        xg = TMPB[0][:, 0:n]
        rs = TMP[0][:, 0:n]
        act(rs, bank(b_ss, n), AF.Ln, bias=EPSB[:, 0:1], scale=1.0 / 64.0)
        act(rs, rs, AF.Exp, scale=-0.5)
        tt("dve", TMP[1][:, 0:n], xg, cos, ALU.mult)
        tt("dve", TMP[2][:, 0:n], bank(b_rot, n), sin, ALU.mult)

    def nr3(n, dst):
        t1 = TMP[1][:, 0:n]
        tt("pool", t1, t1, TMP[2][:, 0:n], ALU.add)
        tt("pool", dst, t1, TMP[0][:, 0:n], ALU.mult)

    def normrope(src, n, gain, cos, sin, dst, b_ss, b_rot):
        nr1(src, n, gain, b_ss, b_rot)
        nr2(n, cos, sin, b_ss, b_rot)
        nr3(n, dst)

    def phase_c0():
      load_weight_rows(WKV, w_mem_kv, 2048, 3)
      for mt in range(2):
          xt = XT[rot("xt", 2)]
          dma(xt[:, :], memd[mt * 128:(mt + 1) * 128, :])
          norm_to_T(xt[:, :], 128, MEMT[:, :, mt * 128:(mt + 1) * 128], "dve")
      for oc in range(8):
          pb = bank(oc % 2, 256)
          for c in range(8):
              mm(pb, WKV[:, c, oc * 128:(oc + 1) * 128], MEMT[:, c, :], c == 0, c == 7)
          cp("act", KM[:, oc, :], pb)
      for mt in range(2):
          for hf in range(2):
              pb = bank(2 + (mt * 2 + hf) % 2)
              for c in range(8):
                  mm(pb, MEMT[:, c, mt * 128:(mt + 1) * 128], WKV[:, c, 1024 + hf * 512:1024 + (hf + 1) * 512], c == 0, c == 7)
              cp("dve", VM[:, mt, hf * 512:(hf + 1) * 512], pb)

    def load_win_chunk(c):
        sc = GFM[:, c:c + 1]
        wpiece(WIN[:, c, 0:1024], [(w_in[c * 128:(c + 1) * 128, 0:1024], 0, 128, 0, 1024)], 1024, sc)
        slot_view = lambda slot: slot[:, 0:512].rearrange("p (h j d) -> p j h d", h=2, j=4)
        wpiece(WIN[:, c, 1024:1536].rearrange("p (j h d) -> p j h d", j=4, h=2),
               [(w_in[c * 128:(c + 1) * 128, 1024:1792], 0, 128, 0, 768)], 512, sc, in_view=slot_view)
        last = STG[(cntr["stg"] + 1) % 2]
        ts("dve", WIN[:, c, 1536:1792], last[:, 512:768], sc, ALU.mult)

    XQ = [XT[0], XT[1], YT[0], YT[1]]

    def tile_load(t, src, m):
        dma(XQ[t % 4][0:m, :], src)

    def tile_front(t, src, m):
        xt = XQ[t % 4]
        ss = stat1(m)
        hb = HB[t % 2]
        if t % 2 == 0:
            act(hb[0:m, :], xt[0:m, :], AF.Square, accum=ss)
        else:
            stt("dve", hb[0:m, :], xt[0:m, :], 1.0, xt[0:m, :], ALU.mult, ALU.mult, accum=ss)
        rs = stat1(m)
        rsqrt_from(rs, ss, 1.0 / 1024.0, m)
        ts("dve", hb[0:m, :], xt[0:m, :], rs, ALU.mult)

    def tile_back(t, m, dst, evac_eng):
        hb = HB[t % 2]
        tpv = v3(bankbf(6 + t % 2), 8)
        for c in range(8):
            transp(tpv[:, c, 0:m], hb[0:m, c * 128:(c + 1) * 128], IDENT[0:m, 0:m])
        cp(evac_eng, dst, tpv[:, :, 0:m])

    def kv_job(hsrc, kb):
        i = kb
        rp_ = ROPE[i % 2]
        kp = bank(i % 2)
        vp = bank(2 + i % 2)

        def P():
            dma(rp_[:, 0:512], ropek[0, :, kb * 512:(kb + 1) * 512])
            dma(rp_[:, 512:1024], ropek[1, :, kb * 512:(kb + 1) * 512])
            for c in range(8):
                mm(kp, WIN[:, c, 1536:1664], hsrc[:, c, :], c == 0, c == 7)
            for t in range(4):
                for c in range(8):
                    mm(vp[:, t * 128:(t + 1) * 128], hsrc[:, c, t * 128:(t + 1) * 128], WIN[:, c, 1664:1792], c == 0, c == 7)

        def N1():
            nr1(kp, 512, QKG[:, 1:2], 4, 5)
            cp("act", VV[:, kb * 4:kb * 4 + 4, :, 0:64], vp.rearrange("p (t g d) -> p t g d", t=4, g=2))

        def N2():
            nr2(512, rp_[:, 0:512], rp_[:, 512:1024], 4, 5)

        def N3():
            nr3(512, KT[:, kb * 512:(kb + 1) * 512])
        return [P, N1, N2, N3]

    def run_tiles(tiles, jobs_ready, extra=None):
        active = []

        def step_jobs(k):
            for _ in range(k):
                if active:
                    active[0].pop(0)()
                    if not active[0]:
                        active.pop(0)
        for t0 in range(min(3, len(tiles))):
            tile_load(t0, tiles[t0][0], tiles[t0][1])
        tile_front(0, tiles[0][0], tiles[0][1])
        for t in range(len(tiles)):
            if t + 3 < len(tiles):
                tile_load(t + 3, tiles[t + 3][0], tiles[t + 3][1])
            if t + 1 < len(tiles):
                tile_front(t + 1, tiles[t + 1][0], tiles[t + 1][1])
            tile_back(t, tiles[t][1], tiles[t][2], "act" if t % 2 else "dve")
            if extra is not None:
                extra(t)
            for job in jobs_ready.get(t, []):
                active.append(job)
            step_jobs(2 if len(active) > 1 else 1)
        while active:
            step_jobs(1)

    ext_tiles = col_blocks(0, NE, 128)
    tiles = [(x_ext[e0:e0 + m, :], m, HT[:, :, e0:e0 + m]) for (e0, m) in ext_tiles]
    jobs = {4 * kb + 4: [kv_job(HT[:, :, 16 + kb * 512:16 + (kb + 1) * 512], kb)] for kb in range(4)}
    run_tiles(tiles, jobs, extra=lambda t: [load_win_chunk(2 * t), load_win_chunk(2 * t + 1)] if t < 4 else None)
    phase_c0()
    tiles = []
    jobs = {}
    for rb in range(12):
        for t in range(4):
            r0 = rb * 512 + t * 128
            tiles.append((x_rest[r0:r0 + 128, :], 128, HTB[rb % 2][:, :, t * 128:(t + 1) * 128]))
        jobs[4 * rb + 3] = [kv_job(HTB[rb % 2], 4 + rb)]
    run_tiles(tiles, jobs)

    qitems = [(bk, e0, n, j) for bk, (e0, n) in enumerate(col_blocks(E_LO, E_HI)) for j in range(4)]

    def q_a(i):
        bk, e0, n, j = qitems[i]
        rp_ = ROPE[bk % 2]
        if j == 0:
            dma(rp_[:, 0:n], ropeq[0, :, e0:e0 + n])
            dma(rp_[:, 512:512 + n], ropeq[1, :, e0:e0 + n])
        qp = bank(6 + i % 2, n)
        for c in range(8):
            mm(qp, WIN[:, c, 1024 + j * 128:1024 + (j + 1) * 128], HT[:, c, e0:e0 + n], c == 0, c == 7)

    def q_n(i):
        bk, e0, n, j = qitems[i]
        rp_ = ROPE[bk % 2]
        normrope(bank(6 + i % 2, n), n, QKG[:, 0:1], rp_[:, 0:n], rp_[:, 512:512 + n], QT[:, j, e0:e0 + n], 4, 5)

    q_a(0)
    for i in range(len(qitems)):
        if i + 1 < len(qitems):
            q_a(i + 1)
        q_n(i)
    bi = 0
    for (e0, n) in col_blocks(0, NE):
        for ci in range(4):
            av = bank(2 * (bi % 2), n)
            ag = bank(2 * (bi % 2) + 1, n)
            bi += 1
            for c in range(8):
                mm(av, WIN[:, c, ci * 128:(ci + 1) * 128], HT[:, c, e0:e0 + n], c == 0, c == 7)
            for c in range(8):
                mm(ag, WIN[:, c, 512 + ci * 128:512 + (ci + 1) * 128], HT[:, c, e0:e0 + n], c == 0, c == 7)
            sg = TMP[3 + bi % 2][:, 0:n]
            act(sg, ag, AF.Sigmoid)
            tt("dve", AT[:, ci, e0:e0 + n], av, sg, ALU.mult)

    for ci in range(4):
        for k in range(31):
            ts("dve", DIAG[:, ci, k, :], IDENT, CONVP[:, ci * 34 + k:ci * 34 + k + 1], ALU.mult)
    for cbi, (e0, n) in enumerate(col_blocks(E_LO, E_HI)):
        sm = bank(4, n)
        sq_ = bank(5, n)
        cvb = [0, 1, 2, 3] if cbi % 2 == 0 else [6, 7, 2, 3]
        for ci in range(4):
            cv = bank(cvb[ci], n)
            for k in range(31):
                mm(cv, DIAG[:, ci, k, :], AT[:, ci, e0 + k - 15:e0 + k - 15 + n], k == 0, k == 30)
            bcol = CONVP[:, ci * 34 + 31:ci * 34 + 32]
            cb = TMPB[ci % 2][:, 0:n]
            cs = TMPB[2 + ci % 2][:, 0:n]
            act(cb, cv, AF.Identity, bias=bcol)
            act(cs, cv, AF.Square, bias=bcol)
            mm(sm, ONES, cb, ci == 0, ci == 3)
            mm(sq_, ONES, cs, ci == 0, ci == 3)
        mean = TMP[0][:, 0:n]
        ts("dve", mean, sm, 1.0 / 512.0, ALU.mult)
        msq = TMP[1][:, 0:n]
        tt("dve", msq, mean, mean, ALU.mult)
        var = TMP[2][:, 0:n]
        stt("dve", var, sq_, 1.0 / 512.0, msq, ALU.mult, ALU.subtract)
        act(var, var, AF.Ln, bias=EPSB[:, 0:1], scale=1.0)
        act(var, var, AF.Exp, scale=-0.5)
        for ci in range(4):
            cv = bank(cvb[ci], n)
            bcol = CONVP[:, ci * 34 + 31:ci * 34 + 32]
            t1 = TMP[3 + ci % 2][:, 0:n]
            stt("dve", t1, cv, bcol, mean, ALU.add, ALU.subtract)
            tt("pool", t1, t1, var, ALU.mult)
            act(HT[:, ci, e0:e0 + n], t1, AF.Silu, bias=CONVP[:, ci * 34 + 33:ci * 34 + 34],
                scale=CONVP[:, ci * 34 + 32:ci * 34 + 33])

    def wout_rows(c):
        if c < 4:
            return [(c * 128, 128, 0)]
        j = c - 4
        return [(512 + 64 * j, 64, 0), (512 + 64 * (4 + j), 64, 64)]
    load_weight_rows(WOUT, w_out, 1024, None, rows_of_chunk=wout_rows)
    dma(GB[:, :], gpost[0:1, :].partition_broadcast(128))

    PTX = [WREG[:, i * 2048:(i + 1) * 2048] for i in range(2)]
    PTY = [WREG[:, 4096 + i * 1024:4096 + (i + 1) * 1024] for i in range(2)]
    groups = [(j, e0, n) for j in range(4) for (e0, n) in col_blocks(E_LO, E_HI)]
    batches = []
    nxy = {"X": 0, "Y": 0}
    for gi in range(len(groups)):
        kt = 0
        turn = "X"
        while kt < 64:
            if turn == "X" and kt + 2 <= 64:
                kts = [kt, kt + 1]
                kind = "X"
            else:
                kts = [kt]
                kind = "Y"
            kt += len(kts)
            batches.append((gi, kts, kind, nxy[kind], kt == 64))
            nxy[kind] += 1
            turn = "Y" if turn == "X" else "X"

    def sbank(kind, i, h):
        return (2 * i + h) if kind == "X" else (4 + h)

    def emit_qk(bn):
        gi, kts, kind, ser, last = batches[bn]
        j, e0, n = groups[gi]
        for i, kt in enumerate(kts):
            for h in range(2):
                mm(bank(sbank(kind, i, h), n), KT[64 * h:64 * h + 64, kt * 128:(kt + 1) * 128],
                   QT[64 * h:64 * h + 64, j, e0:e0 + n], True, True)

    def emit_exp(bn):
        gi, kts, kind, ser, last = batches[bn]
        j, e0, n = groups[gi]
        nit = 2 * len(kts)
        b0 = 0 if kind == "X" else 4
        pt = (PTX if kind == "X" else PTY)[ser % 2]
        sv = PS[:, 512 * b0:512 * (b0 + nit)].rearrange("p (b t) -> p b t", b=nit)[:, :, 0:n]
        pv = pt[:, 0:512 * nit].rearrange("p (b t) -> p b t", b=nit)[:, :, 0:n]
        act(pv, sv, AF.Exp, scale=0.125)

    def emit_pv(bn):
        gi, kts, kind, ser, last = batches[bn]
        j, e0, n = groups[gi]
        pt = (PTX if kind == "X" else PTY)[ser % 2]
        for i, kt in enumerate(kts):
            for h in range(2):
                it = 2 * i + h
                mm(bank(6 + h, n)[0:65, :], VV[:, kt, h, 0:65], pt[:, 512 * it:512 * it + n], kt == 0, kt == 63)
        if last:
            for h in range(2):
                cp("dve", TMP[h][0:65, 0:n], bank(6 + h, n)[0:65, :])
            for h in range(2):
                recip(TMP[2 + h][64:65, 0:n], TMP[h][64:65, 0:n])
            for h in range(2):
                osb = TMP[h][0:65, 0:n]
                rd = TMP[2 + h][64:65, 0:n]
                rdb = TMP[4 + h][0:64, 0:n]
                row = 2 * gi + h
                dma(rds[row:row + 1, 0:n], rd)
                dma(rdb, rds[row:row + 1, 0:n].partition_broadcast(64))
                if h == 0:
                    tt("dve", HT[0:64, 4 + j, e0:e0 + n], osb[0:64, :], rdb, ALU.mult)
                else:
                    a1_ = ATT1[0:64, 0:n]
                    tt("dve", a1_, osb[0:64, :], rdb, ALU.mult)
                    dma(HT[64:128, 4 + j, e0:e0 + n], a1_)

    pend = {"X": 0, "Y": 0}
    nq = [0]

    def try_qk():
        while nq[0] < len(batches) and pend[batches[nq[0]][2]] == 0:
            emit_qk(nq[0])
            pend[batches[nq[0]][2]] += 1
            nq[0] += 1

    try_qk()
    for bn in range(len(batches)):
        emit_exp(bn)
        pend[batches[bn][2]] -= 1
        try_qk()
        emit_pv(bn)

    def outproj_phase(tiles, nk, lhs_fn, rhs_fn, xsrc_fn, xdst_fn, hdst_fn):
        def ybuf(t):
            m = tiles[t][1]
            return PS[0:m, 1024 * (t % 2):1024 * (t % 2) + 1024]

        def st_a(t):
            y = ybuf(t)
            dma(XT[t % 2][0:tiles[t][1], :], xsrc_fn(t))
            for hf in range(2):
                for c in range(nk):
                    mm(y[:, hf * 512:(hf + 1) * 512], lhs_fn(t, c), rhs_fn(c, hf), c == 0, c == nk - 1)

        def st_b1(t):
            m = tiles[t][1]
            y = ybuf(t)
            xt = XT[t % 2]
            ss = stat1(m)
            yt = YT[t % 2]
            act(yt[0:m, :], y, AF.Square, accum=ss)
            rs = stat1(m)
            rsqrt_from(rs, ss, 1.0 / 1024.0, m)
            stt("dve", yt[0:m, :], y, rs, GB[0:m, :], ALU.mult, ALU.mult)
            tt("pool", yt[0:m, :], yt[0:m, :], xt[0:m, :], ALU.add)
            dma(xdst_fn(t), yt[0:m, :])

        def st_b2(t):
            if hdst_fn is None:
                return
            m = tiles[t][1]
            yt = YT[t % 2]
            hb = HB[t % 2]
            ss = stat1(m)
            act(hb[0:m, :], yt[0:m, :], AF.Square, accum=ss)
            rs = stat1(m)
            rsqrt_from(rs, ss, 1.0 / 1024.0, m)
            ts("dve", hb[0:m, :], yt[0:m, :], rs, ALU.mult)

        def st_c(t):
            if hdst_fn is None:
                return
            tile_back(t, tiles[t][1], hdst_fn(t), "act" if t % 2 else "dve")

        stages = [st_a, st_b1, st_b2, st_c]
        for s_ in range(len(tiles) + len(stages) - 1):
            for k, st in enumerate(stages):
                t = s_ - k
                if 0 <= t < len(tiles):
                    st(t)

    tok_tiles = col_blocks(E_LO, E_HI, 128)

    def ht_tile(t):
        e0, m = tok_tiles[t]
        return HT[:, :, e0:e0 + m]

    outproj_phase(tok_tiles, 8,
                  lambda t, c: HT[:, c, tok_tiles[t][0]:tok_tiles[t][0] + tok_tiles[t][1]],
                  lambda c, hf: WOUT[:, c, hf * 512:(hf + 1) * 512],
                  lambda t: x_ext[tok_tiles[t][0]:tok_tiles[t][0] + tok_tiles[t][1], :],
                  lambda t: xs1[tok_tiles[t][0]:tok_tiles[t][0] + tok_tiles[t][1], :],
                  ht_tile)

    load_weight_rows(WMQ, w_mem_q, 1024, 1)
    load_weight_rows(WMO, w_mem_o, 1024, None)
    dma(GB[:, :], gpost[1:2, :].partition_broadcast(128))
    qi = 0
    for (e0, n) in col_blocks(E_LO, E_HI):
        for oc in range(8):
            qb = bank(qi % 2, n)
            qi += 1
            for c in range(8):
                mm(qb, WMQ[:, c, oc * 128:(oc + 1) * 128], HT[:, c, e0:e0 + n], c == 0, c == 7)
            cp("act" if oc % 2 else "dve", BQ[:, oc, e0:e0 + n], qb)
        for hd in range(4):
            for mt in range(2):
                sbk = bank(2 + mt, n)
                for dc in range(2):
                    mm(sbk, KM[:, 2 * hd + dc, mt * 128:(mt + 1) * 128], BQ[:, 2 * hd + dc, e0:e0 + n], dc == 0, dc == 1)
            pt = PTX[hd % 2]
            sv = PS[:, 1024:2048].rearrange("p (b t) -> p b t", b=2)[:, :, 0:n]
            pv = pt[:, 0:1024].rearrange("p (b t) -> p b t", b=2)[:, :, 0:n]
            act(pv, sv, AF.Exp, scale=1.0 / 16.0)
            den = bank(4, n)
            for mt in range(2):
                mm(den, ONES, pt[:, 512 * mt:512 * mt + n], mt == 0, mt == 1)
            rd = TMP[hd % 2][:, 0:n]
            act(rd, den, AF.Ln)
            act(rd, rd, AF.Exp, scale=-1.0)
            for dc in range(2):
                ob = bank(5 + dc, n)
                for mt in range(2):
                    mm(ob, VM[:, mt, (2 * hd + dc) * 128:(2 * hd + dc + 1) * 128], pt[:, 512 * mt:512 * mt + n], mt == 0, mt == 1)
                tt("dve", HT[:, 2 * hd + dc, e0:e0 + n], ob, rd, ALU.mult)
    outproj_phase(tok_tiles, 8,
                  lambda t, c: HT[:, c, tok_tiles[t][0]:tok_tiles[t][0] + tok_tiles[t][1]],
                  lambda c, hf: WMO[:, c, hf * 512:(hf + 1) * 512],
                  lambda t: xs1[tok_tiles[t][0]:tok_tiles[t][0] + tok_tiles[t][1], :],
                  lambda t: xs2[tok_tiles[t][0]:tok_tiles[t][0] + tok_tiles[t][1], :],
                  ht_tile)
    ts("dve", HT[:, :, 15:16], HT[:, :, 15:16], MASK[:, 0:1], ALU.mult)
    ts("dve", HT[:, :, 2064:2065], HT[:, :, 2064:2065], MASK[:, 1:2], ALU.mult)

    dma(GB[:, :], gpost[2:3, :].partition_broadcast(128))
    fgroups = [(f0, min(4, NFC - f0)) for f0 in range(0, NFC, 4)]
    fblocks = []
    o0 = 16
    while o0 < 2064:
        no = min(510, 2064 - o0)
        fblocks.append((o0, no))
        o0 += no

    def load_wdown(chunks, dst_fn):
        for f in chunks:
            wpiece(dst_fn(f), [(w_down[f * 128:(f + 1) * 128, :], 0, 128, 0, 1024)], 1024, None)

    def load_wup_group(gi):
        f0, nf = fgroups[gi]
        wu = WU[gi % 2]
        for c in range(8):
            ncol = nf * 128
            srcs = [(w_up[c * 128:(c + 1) * 128, f0 * 128:f0 * 128 + ncol], 0, 128, 0, ncol),
                    (w_up[c * 128:(c + 1) * 128, DFF + f0 * 128:DFF + f0 * 128 + ncol], 0, 128, 512, ncol)]
            sc = GFM[:, 16 + c:16 + c + 1]
            if nf == 4:
                wpiece(wu[:, c, :], srcs, 1024, sc)
            else:
                slot_view = lambda slot: slot[:, :].rearrange("p (a t) -> p a t", a=2)[:, :, 0:ncol]
                wpiece(wu[:, c, :].rearrange("p (a t) -> p a t", a=2)[:, :, 0:ncol], srcs, 1024, sc, in_view=slot_view)

    load_wup_group(0)
    fitems = []
    for gi, (f0, nf) in enumerate(fgroups):
        for fl in range(nf):
            for bi_, (o0, no) in enumerate(fblocks):
                fitems.append((gi, fl, f0 + fl, o0, no, fl == 0 and bi_ == 0))

    def f_a(i):
        gi, fl, f, o0, no, first = fitems[i]
        if first and gi + 1 < len(fgroups):
            load_wup_group(gi + 1)
        wu = WU[gi % 2]
        ug = bank(2 * (i % 3), no + 2)
        uv = bank(2 * (i % 3) + 1, no + 2)
        for c in range(8):
            mm(ug, wu[:, c, fl * 128:(fl + 1) * 128], HT[:, c, o0 - 1:o0 + no + 1], c == 0, c == 7)
        for c in range(8):
            mm(uv, wu[:, c, 512 + fl * 128:512 + (fl + 1) * 128], HT[:, c, o0 - 1:o0 + no + 1], c == 0, c == 7)

    def f_b(i):
        gi, fl, f, o0, no, first = fitems[i]
        pg = FFNP[:, f * 4:f * 4 + 4]
        pv_ = FFNP[:, (NFC + f) * 4:(NFC + f) * 4 + 4]
        ug = bank(2 * (i % 3), no + 2)
        uv = bank(2 * (i % 3) + 1, no + 2)
        tg = TMP[i % 3][:, 0:no]
        tv = TMP[3 + i % 3][:, 0:no]
        act(tg, ug[:, 1:1 + no], AF.Identity, bias=pg[:, 3:4], scale=pg[:, 1:2])
        act(tv, uv[:, 1:1 + no], AF.Identity, bias=pv_[:, 3:4], scale=pv_[:, 1:2])
        stt("dve", tg, ug[:, 0:no], pg[:, 0:1], tg, ALU.mult, ALU.add)
        stt("dve", tg, ug[:, 2:2 + no], pg[:, 2:3], tg, ALU.mult, ALU.add)
        stt("dve", tv, uv[:, 0:no], pv_[:, 0:1], tv, ALU.mult, ALU.add)
        stt("dve", tv, uv[:, 2:2 + no], pv_[:, 2:3], tv, ALU.mult, ALU.add)

    def f_c(i):
        gi, fl, f, o0, no, first = fitems[i]
        tg = TMP[i % 3][:, 0:no]
        tv = TMP[3 + i % 3][:, 0:no]
        act(tg, tg, AF.Gelu_apprx_tanh)
        tt("pool", GT[:, f, o0 - 16:o0 - 16 + no], tg, tv, ALU.mult)

    fst = [f_a, f_b, f_c]
    for s_ in range(len(fitems) + 2):
        for k, st in enumerate(fst):
            t = s_ - k
            if 0 <= t < len(fitems):
                st(t)
    WD1 = v3(BIG8[:, 0:16384], 16)
    load_wdown(range(0, 6), lambda f: WDA[:, f * 1024:(f + 1) * 1024])
    load_wdown(range(6, NFC), lambda f: WD1[:, f - 6, :])

    def wd(f, hf):
        if f < 6:
            return WDA[:, f * 1024 + hf * 512:f * 1024 + (hf + 1) * 512]
        return WD1[:, f - 6, hf * 512:(hf + 1) * 512]

    ffn_tiles = [(16 + ti * 128, 128) for ti in range(16)]
    outproj_phase(ffn_tiles, NFC,
                  lambda t, c: GT[:, c, t * 128:(t + 1) * 128],
                  wd,
                  lambda t: xs2[16 + t * 128:16 + (t + 1) * 128, :],
                  lambda t: outd[t * 128:(t + 1) * 128, :],
                  None)

    S.emit(es)
    es.close()
    return nc


def _rope_tables(pos):
    pos = np.asarray(pos)
    inv = (10000.0 ** (-(np.arange(0, 32, 2, dtype=np.float32)) / np.float32(32))).astype(np.float32)
    r = (pos // 64).astype(np.float32)
    c = (pos % 64).astype(np.float32)
    ang_r = r[None, :] * inv[:, None]
    ang_c = c[None, :] * inv[:, None]
    ang = np.concatenate([ang_r, ang_r, ang_c, ang_c], axis=0).astype(np.float32)
    ang = np.concatenate([ang, ang], axis=0)
    return np.stack([np.cos(ang), np.sin(ang)]).astype(np.float32)


def _consts():
    ident = np.eye(128, dtype=np.float32)
    perm = np.zeros((128, 128), np.float32)
    for m in range(128):
        if (m % 32) < 16:
            perm[m + 16, m] = -1.0
        else:
            perm[m - 16, m] = 1.0
    bones = np.zeros((128, 128), np.float32)
    bones[:64, :64] = 1.0
    bones[64:, 64:] = 1.0
    ones = np.ones((128, 128), np.float32)
    return np.ascontiguousarray(np.concatenate([ident, perm, bones, ones], axis=1))


_NC_CACHE = {}


def kernel(x, mem, norm_mix_pre, w_in, conv_dw, conv_dw_b, conv_ln_g, conv_ln_b,
           q_norm_g, k_norm_g, w_out, norm_mix_post, norm_mem_pre, mem_norm_g,
           w_mem_q, w_mem_kv, w_mem_o, norm_mem_post, norm_ffn_pre, w_up, ffn_dw,
           ffn_dw_b, w_down, norm_ffn_post):
    f = lambda a: np.ascontiguousarray(np.asarray(a, dtype=np.float32))
    x = f(x); mem = f(mem)
    B, Sq, D = x.shape

    def fm(g):
        return f(g).reshape(8, 128).T
    gfm = np.ascontiguousarray(np.concatenate([fm(norm_mix_pre[0]), fm(norm_mem_pre[0]), fm(norm_ffn_pre[0]), fm(mem_norm_g[0])], axis=1))
    gpost = np.ascontiguousarray(np.stack([f(norm_mix_post[0]), f(norm_mem_post[0]), f(norm_ffn_post[0])]))
    cw = f(conv_dw[0])
    convp = np.zeros((128, 4, 34), np.float32)
    for ci in range(4):
        convp[:, ci, 0:31] = cw[:, ci * 128:(ci + 1) * 128].T
        convp[:, ci, 31] = f(conv_dw_b[0])[ci * 128:(ci + 1) * 128]
        convp[:, ci, 32] = f(conv_ln_g[0])[ci * 128:(ci + 1) * 128]
        convp[:, ci, 33] = f(conv_ln_b[0])[ci * 128:(ci + 1) * 128]
    convp = np.ascontiguousarray(convp.reshape(128, 136))
    qkg = np.ascontiguousarray(np.stack([np.tile(f(q_norm_g[0]), 2), np.tile(f(k_norm_g[0]), 2)], axis=1))
    fw = f(ffn_dw[0])
    fb = f(ffn_dw_b[0])
    ffnp = np.zeros((128, 44, 4), np.float32)
    for fc in range(44):
        ffnp[:, fc, 0:3] = fw[:, fc * 128:(fc + 1) * 128].T
        ffnp[:, fc, 3] = fb[fc * 128:(fc + 1) * 128]
    ffnp = np.ascontiguousarray(ffnp.reshape(128, 176))
    cst = _consts()
    shared = dict(w_in=f(w_in[0]), w_out=f(w_out[0]), w_mem_q=f(w_mem_q[0]), w_mem_kv=f(w_mem_kv[0]),
                  w_mem_o=f(w_mem_o[0]), w_up=f(w_up[0]), w_down=f(w_down[0]), gfm=gfm, gpost=gpost,
                  convp=convp, qkg=qkg, ffnp=ffnp, cst=cst)
    in_maps = []
    for core in range(8):
        b, j = core // 4, core % 4
        s = j * 2048
        xe = np.zeros((NE, 1024), np.float32)
        lo, hi = max(0, s - 16), min(Sq, s + 2064)
        xe[lo - (s - 16):hi - (s - 16)] = x[b, lo:hi]
        rest_idx = np.concatenate([np.arange(0, s), np.arange(s + 2048, Sq)])
        xr = np.ascontiguousarray(x[b, rest_idx])
        key_pos = np.concatenate([np.arange(s, s + 2048), rest_idx])
        ropek = _rope_tables(key_pos)
        ext_pos = np.clip(np.arange(s - 16, s - 16 + NE), 0, Sq - 1)
        ropeq = _rope_tables(ext_pos)
        mask = np.ones((128, 2), np.float32)
        if j == 0:
            mask[:, 0] = 0.0
        if j == 3:
            mask[:, 1] = 0.0
        m = dict(shared)
        m.update(x_ext=xe, x_rest=xr, mem=np.ascontiguousarray(mem[b]), ropek=ropek, ropeq=ropeq, mask=mask)
        in_maps.append(m)
    if "nc" not in _NC_CACHE:
        _NC_CACHE["nc"] = build_nc()
    nc = _NC_CACHE["nc"]
    res = run_bass_kernel_spmd(nc, in_maps, core_ids=list(range(8)))
    out = np.zeros((B, Sq, D), np.float32)
    for core in range(8):
        b, j = core // 4, core % 4
        out[b, j * 2048:(j + 1) * 2048] = np.asarray(res.results[core]["out"], dtype=np.float32)
    return out
```

```python
import numpy as np
from contextlib import ExitStack
import concourse.bass as bass
import concourse.mybir as mybir
from concourse.bass_utils import run_bass_kernel_spmd

F32 = mybir.dt.float32
BF16 = mybir.dt.bfloat16
AF = mybir.ActivationFunctionType
ALU = mybir.AluOpType

EPS = 1e-6
NE = 2080
E_LO, E_HI = 15, 2065
DFF = 2816
NFC = 22


def _esize(dt):
    return 2 if dt == BF16 else 4


class Sched:
    ENGS = ["pe", "act", "dve", "pool", "sp"]
    NDMA = 16

    def __init__(self, nc, tracked_dram=()):
        self.nc = nc
        self.ops = []
        self.w = {}
        self.r = {}
        self.tracked_dram = set(tracked_dram)

    def region(self, ap):
        t = ap.tensor
        name = t.name
        space = str(ap.space)
        if "DRAM" in space.upper() or "HBM" in space.upper() or type(t).__name__.startswith("DRam"):
            if name not in self.tracked_dram:
                return None
        es = _esize(ap.dtype)
        dims = ap.ap
        ps, pc = dims[0]
        off = int(ap.offset)
        if ps == 0:
            rs_ = int(t.shape[-1])
            p0, f0 = off // rs_, off % rs_
            p1 = p0 + 1
        else:
            p0 = off // ps
            f0 = off % ps
            p1 = p0 + pc
        ents = [(f0, 0)]
        rest = dims[1:]
        for (s, c) in rest[:-1]:
            s = abs(s)
            if s != 0 and len(ents) * c <= 64:
                ents = [(st + i * s, ex) for (st, ex) in ents for i in range(c)]
            else:
                ents = [(st, ex + (c - 1) * s) for (st, ex) in ents]
        if rest:
            s, c = rest[-1]
            ents = [(st, ex + (c - 1) * abs(s) + 1) for (st, ex) in ents]
        else:
            ents = [(st, ex + 1) for (st, ex) in ents]
        ivs = tuple(sorted((st * es, (st + ex) * es) for (st, ex) in ents))
        return (name, p0, p1, ivs)

    @staticmethod
    def _ov(a, b):
        if a[1] >= b[2] or b[1] >= a[2]:
            return False
        for (s0, e0) in a[3]:
            for (s1, e1) in b[3]:
                if s0 < e1 and s1 < e0:
                    return True
        return False

    @staticmethod
    def _covers(a, b):
        if len(a[3]) != 1:
            return a[1] <= b[1] and a[2] >= b[2] and a[3] == b[3]
        if a[1] > b[1] or a[2] < b[2]:
            return False
        s0, e0 = a[3][0]
        return all(s0 <= s1 and e1 <= e0 for (s1, e1) in b[3])

    def add(self, eng, fn, reads=(), writes=(), dma=False):
        op = {"eng": eng, "fn": fn, "deps": set(), "idx": len(self.ops), "inc": False, "dma": dma}
        for ap in reads:
            rg = self.region(ap)
            if rg is None:
                continue
            for key, d in self.w.get(rg[0], {}).items():
                if self._ov(rg, key):
                    op["deps"].update(d.values())
            self.r.setdefault(rg[0], {}).setdefault(rg, {})[eng] = op["idx"]
        for ap in writes:
            rg = self.region(ap)
            if rg is None:
                continue
            for table in (self.w, self.r):
                tb = table.get(rg[0], {})
                dead = []
                for key, d in tb.items():
                    if self._ov(rg, key):
                        op["deps"].update(d.values())
                        if self._covers(rg, key):
                            dead.append(key)
                for k in dead:
                    del tb[k]
            self.w.setdefault(rg[0], {})[rg] = {eng: op["idx"]}
        op["deps"].discard(op["idx"])
        self.ops.append(op)
        return op

    def emit(self, es):
        nc = self.nc
        ops = self.ops
        for op in ops:
            op["deps"] = {d for d in op["deps"] if not (op["eng"] == "pe" and ops[d]["eng"] == "pe" and not ops[d]["dma"])}
            latest = {}
            keep = set()
            for d in op["deps"]:
                p = ops[d]
                if p["dma"]:
                    keep.add(d)
                else:
                    latest[p["eng"]] = max(latest.get(p["eng"], -1), d)
            op["deps"] = keep | set(latest.values())
            for d in op["deps"]:
                ops[d]["inc"] = True
        esem = {e: es.enter_context(nc.semaphore("sem_" + e)) for e in self.ENGS}
        dsem = [es.enter_context(nc.semaphore("dsem%d" % i)) for i in range(self.NDMA)]
        cnt = {e: 0 for e in self.ENGS}
        ndma = 0
        last_out_dma = []
        for op in ops:
            if op["dma"]:
                op["dsem"] = ndma % self.NDMA
                op["dcnt"] = 16 * (ndma // self.NDMA + 1)
                ndma += 1
            elif op["inc"]:
                cnt[op["eng"]] += 1
                op["cnt"] = cnt[op["eng"]]
        self.ndma = ndma
        block = es.enter_context(nc.Block())

        def stream(engname, e):
            seen = {}

            def wait(sem, key, val):
                if seen.get(key, 0) >= val:
                    return
                seen[key] = val
                e.wait_ge(sem, val)

            for op in ops:
                if op["eng"] != engname:
                    continue
                need = {}
                for d in op["deps"]:
                    p = ops[d]
                    if p["dma"]:
                        k = ("d", p["dsem"])
                        need[k] = max(need.get(k, 0), p["dcnt"])
                    else:
                        k = ("e", p["eng"])
                        need[k] = max(need.get(k, 0), p["cnt"])
                if op["dma"] and op["dcnt"] > 16:
                    k = ("d", op["dsem"])
                    need[k] = max(need.get(k, 0), op["dcnt"] - 16)
                for k, v in need.items():
                    wait(dsem[k[1]] if k[0] == "d" else esem[k[1]], k, v)
                ins = op["fn"](e)
                if op["dma"]:
                    ins.then_inc(dsem[op["dsem"]], 16)
                elif op["inc"]:
                    ins.then_inc(esem[op["eng"]], 1)
            if engname == "sp":
                for i in range(min(self.NDMA, ndma)):
                    n_i = (ndma - 1 - i) // self.NDMA + 1
                    wait(dsem[i], ("d", i), 16 * n_i)

        @block.tensor
        def _(e):
            stream("pe", e)

        @block.scalar
        def _(e):
            stream("act", e)

        @block.vector
        def _(e):
            stream("dve", e)

        @block.gpsimd
        def _(e):
            stream("pool", e)

        @block.sync
        def _(e):
            stream("sp", e)


def col_blocks(lo, hi, w=512):
    out = []
    while lo < hi:
        n = min(w, hi - lo)
        out.append((lo, n))
        lo += n
    return out


def build_nc(debug=False):
    nc = bass.Bass("TRN2", target_bir_lowering=False)
    es = ExitStack()

    def di(name, shape, dt=F32):
        return nc.dram_tensor(name, shape, dt, kind="ExternalInput").ap()

    x_ext = di("x_ext", [NE, 1024])
    x_rest = di("x_rest", [6144, 1024])
    memd = di("mem", [256, 1024])
    w_in = di("w_in", [1024, 1792])
    w_out = di("w_out", [1024, 1024])
    w_mem_q = di("w_mem_q", [1024, 1024])
    w_mem_kv = di("w_mem_kv", [1024, 2048])
    w_mem_o = di("w_mem_o", [1024, 1024])
    w_up = di("w_up", [1024, 2 * DFF])
    w_down = di("w_down", [DFF, 1024])
    gfm = di("gfm", [128, 4 * 8])
    gpost = di("gpost", [3, 1024])
    convp = di("convp", [128, 4 * 34])
    qkg = di("qkg", [128, 2])
    ffnp = di("ffnp", [128, 44 * 4])
    cst = di("cst", [128, 512])
    ropek = di("ropek", [2, 128, 8192])
    ropeq = di("ropeq", [2, 128, NE])
    maskd = di("mask", [128, 2])
    outd = nc.dram_tensor("out", [2048, 1024], F32, kind="ExternalOutput").ap()
    xs1 = nc.dram_tensor("xs1", [NE, 1024], F32, kind="Internal").ap()
    xs2 = nc.dram_tensor("xs2", [NE, 1024], F32, kind="Internal").ap()
    rds = nc.dram_tensor("rds", [64, 512], F32, kind="Internal").ap()
    dbg = {}

    S = Sched(nc, tracked_dram=["xs1", "xs2", "out", "rds"])

    def sb(name, shape, dt=F32):
        return es.enter_context(nc.sbuf_tensor(name, shape, dt))

    BIG8 = sb("BIG8", [128, 8 * NE], BF16)
    BIGQ = sb("BIGQ", [128, 8 * NE], BF16)
    G = sb("G", [128, NFC * 2048], BF16)
    STG = [sb("STG%d" % i, [128, 1024]) for i in range(2)]
    XT = [sb("XT%d" % i, [128, 1024]) for i in range(2)]
    YT = [sb("YT%d" % i, [128, 1024]) for i in range(2)]
    HB = [sb("HB%d" % i, [128, 1024], BF16) for i in range(2)]
    TMP = [sb("TMP%d" % i, [128, 512]) for i in range(6)]
    TMPB = [sb("TMPB%d" % i, [128, 512], BF16) for i in range(4)]
    GB = sb("GB", [128, 1024])
    CST = sb("CST", [128, 512], BF16)
    ONESF = sb("ONESF", [128, 64])
    GFM = sb("GFM", [128, 32])
    CONVP = sb("CONVP", [128, 4 * 34])
    QKG = sb("QKG", [128, 2])
    FFNP = sb("FFNP", [128, 44 * 4])
    MASK = sb("MASK", [128, 2])
    STAT = sb("STAT", [128, 64])
    PS = es.enter_context(nc.psum_tensor("PS", [128, 4096], F32))

    IDENT = CST[:, 0:128]
    PERM = CST[:, 128:256]
    BONES = CST[:, 256:384]
    ONES = CST[:, 384:512]

    def bank(b, n=512):
        return PS[:, 512 * b:512 * b + n]

    def bankbf(b):
        return PS[:, 512 * b:512 * b + 512].bitcast(BF16)

    def v3(ap2d, c):
        return ap2d.rearrange("p (c t) -> p c t", c=c)

    HT = v3(BIG8[:, :], 8)
    BQ = v3(BIGQ[:, :], 8)
    AT = BQ[:, 0:4, :]
    QT = BQ[:, 4:8, :]
    KT = G[:, 0:8192]
    VV = G[:, 8192:8192 + 64 * 130].rearrange("p (t g d) -> p t g d", t=64, g=2)
    WREG = G[:, 16512:16512 + 15872]
    WIN = v3(WREG[:, 0:8 * 1792], 8)
    DIAG = WREG[:, 0:15872].rearrange("p (c k m) -> p c k m", c=4, k=31)
    KM = v3(G[:, 32384:34432], 8)
    VM = v3(G[:, 34432:36480], 2)
    MEMT = v3(G[:, 36480:38528], 8)
    WKV = v3(BIGQ[:, 0:16384], 8)
    HTB = [v3(BIGQ[:, i * 4096:(i + 1) * 4096], 8) for i in range(2)]
    WOUT = v3(BIGQ[:, 0:8192], 8)
    WMQ = v3(G[:, 16512 + 6144:16512 + 14336], 8)
    WMO = v3(G[:, 36480:44672], 8)
    WUG0 = v3(G[:, 8192:16384], 8)
    GT = v3(G[:, :], NFC)
    WU = [v3(BIGQ[:, i * 8192:(i + 1) * 8192], 8) for i in range(2)]
    PT = [WREG[:, i * 1536:(i + 1) * 1536] for i in range(4)]
    _rf = G[:, 38528:38528 + 4096].bitcast(F32)
    ROPE = [_rf[:, i * 1024:(i + 1) * 1024] for i in range(2)]
    WDA = BIGQ[:, 0:6144]
    ATT1 = TMPB[3]

    cntr = {"stg": 0, "xt": 0, "yt": 0, "hb": 0, "rope": 0, "tp": 0, "stat": 0}

    def rot(key, n):
        v = cntr[key]
        cntr[key] = (v + 1) % n
        return v

    def stat1(m):
        i = rot("stat", 64)
        return STAT[0:m, i:i + 1]

    def dma(out, in_, q="sp"):
        S.add(q, lambda e: e.dma_start(out=out, in_=in_), reads=[in_], writes=[out], dma=True)

    def mm(out, lhsT, rhs, start, stop):
        S.add("pe", lambda e: e.matmul(out, lhsT, rhs, start=start, stop=stop), reads=[lhsT, rhs], writes=[out])

    def transp(out, in_, ident):
        S.add("pe", lambda e: e.transpose(out, in_, ident), reads=[in_, ident], writes=[out])

    def act(out, in_, func, bias=None, scale=None, accum=None):
        kw = {}
        rd = [in_]
        wr = [out]
        if bias is not None:
            kw["bias"] = bias
            if not isinstance(bias, float):
                rd.append(bias)
        if scale is not None:
            kw["scale"] = scale
            if not isinstance(scale, float):
                rd.append(scale)
        if accum is not None:
            kw["accum_out"] = accum
            wr.append(accum)
        S.add("act", lambda e: e.activation(out=out, in_=in_, func=func, **kw), reads=rd, writes=wr)

    def tt(eng, out, in0, in1, op):
        S.add(eng, lambda e: e.tensor_tensor(out=out, in0=in0, in1=in1, op=op), reads=[in0, in1], writes=[out])

    def ts(eng, out, in0, s1, op0, s2=None, op1=None):
        rd = [in0] + [s for s in (s1, s2) if s is not None and not isinstance(s, float)]
        if op1 is None:
            S.add(eng, lambda e: e.tensor_scalar(out=out, in0=in0, scalar1=s1, scalar2=None, op0=op0), reads=rd, writes=[out])
        else:
            S.add(eng, lambda e: e.tensor_scalar(out=out, in0=in0, scalar1=s1, scalar2=s2, op0=op0, op1=op1), reads=rd, writes=[out])

    def stt(eng, out, in0, scalar, in1, op0, op1, accum=None):
        rd = [in0, in1] + ([] if isinstance(scalar, float) else [scalar])
        if accum is None:
            S.add(eng, lambda e: e.scalar_tensor_tensor(out=out, in0=in0, scalar=scalar, in1=in1, op0=op0, op1=op1), reads=rd, writes=[out])
        else:
            S.add(eng, lambda e: e.scalar_tensor_tensor(out=out, in0=in0, scalar=scalar, in1=in1, op0=op0, op1=op1, accum_out=accum),
                  reads=rd, writes=[out, accum])

    def cp(eng, out, in_):
        if eng == "act":
            act(out, in_, AF.Identity)
        else:
            S.add(eng, lambda e: e.tensor_copy(out=out, in_=in_), reads=[in_], writes=[out])

    def recip(out, in_):
        S.add("dve", lambda e: e.reciprocal(out=out, in_=in_), reads=[in_], writes=[out])

    def memset(eng, ap, val):
        S.add(eng, lambda e: e.memset(ap, val), writes=[ap])

    def rsqrt_from(out, in_, scale, m=None):
        act(out, in_, AF.Ln, bias=EPSB[0:out.shape[0], 0:1] if m is None else EPSB[0:m, 0:1], scale=scale)
        act(out, out, AF.Exp, scale=-0.5)

    def wpiece(dst, srcs, ncols, scal=None, eng="dve", in_view=None):
        slot = STG[rot("stg", 2)]
        for (src, p0, p1, c0, nc_) in srcs:
            dma(slot[p0:p1, c0:c0 + nc_], src, q="pool")
        src_ap = slot[:, 0:ncols] if in_view is None else in_view(slot)
        if scal is None:
            cp(eng, dst, src_ap)
        elif eng == "act":
            act(dst, src_ap, AF.Identity, scale=scal)
        else:
            ts(eng, dst, src_ap, scal, ALU.mult)

    EPSB = sb("EPSB", [128, 1])
    memset("pool", EPSB[:, :], EPS)
    memset("pool", ONESF[:, :], 1.0)
    wpiece(CST[:, :], [(cst[:, :], 0, 128, 0, 512)], 512, None, eng="dve")
    dma(GFM[:, :], gfm[:, :])
    dma(CONVP[:, :], convp[:, :])
    dma(QKG[:, :], qkg[:, :])
    dma(FFNP[:, :], ffnp[:, :])
    dma(MASK[:, :], maskd[:, :])
    memset("pool", VV[:, :, :, 64:65], 1.0)

    def load_weight_rows(dst3, src, ncols_total, gcol, col_pieces=None, rows_of_chunk=None, nchunks=8):
        for c in range(nchunks):
            for (c0, ncol) in (col_pieces or col_blocks(0, ncols_total, 1024)):
                if rows_of_chunk is None:
                    srcs = [(src[c * 128:(c + 1) * 128, c0:c0 + ncol], 0, 128, 0, ncol)]
                else:
                    srcs = [(src[r0:r0 + nr, c0:c0 + ncol], p0, p0 + nr, 0, ncol) for (r0, nr, p0) in rows_of_chunk(c)]
                scal = None if gcol is None else GFM[:, gcol * 8 + c:gcol * 8 + c + 1]
                wpiece(dst3[:, c, c0:c0 + ncol], srcs, ncol, scal)

    def norm_to_T(src_tile, m, dst, evac_eng):
        ss = stat1(m)
        hb = HB[rot("hb", 2)]
        act(hb[0:m, :], src_tile, AF.Square, accum=ss)
        rs = stat1(m)
        rsqrt_from(rs, ss, 1.0 / 1024.0, m)
        ts("dve", hb[0:m, :], src_tile, rs, ALU.mult)
        tb = 6 + rot("tp", 2)
        tpv = v3(bankbf(tb), 8)
        for c in range(8):
            transp(tpv[:, c, 0:m], hb[0:m, c * 128:(c + 1) * 128], IDENT[0:m, 0:m])
        cp(evac_eng, dst, tpv[:, :, 0:m])

    def nr1(src, n, gain, b_ss, b_rot):
        xg = TMPB[0][:, 0:n]
        sq = TMPB[1][:, 0:n]
        act(xg, src, AF.Identity, scale=gain)
        act(sq, src, AF.Square)
        mm(bank(b_ss, n), BONES, sq, True, True)
        mm(bank(b_rot, n), PERM, xg, True, True)

    def nr2(n, cos, sin, b_ss, b_rot):
        xg = TMPB[0][:, 0:n]
        rs = TMP[0][:, 0:n]
        act(rs, bank(b_ss, n), AF.Ln, bias=EPSB[:, 0:1], scale=1.0 / 64.0)
        act(rs, rs, AF.Exp, scale=-0.5)
        tt("dve", TMP[1][:, 0:n], xg, cos, ALU.mult)
        tt("dve", TMP[2][:, 0:n], bank(b_rot, n), sin, ALU.mult)

    def nr3(n, dst):
        t1 = TMP[1][:, 0:n]
        tt("pool", t1, t1, TMP[2][:, 0:n], ALU.add)
        tt("pool", dst, t1, TMP[0][:, 0:n], ALU.mult)

    def normrope(src, n, gain, cos, sin, dst, b_ss, b_rot):
        nr1(src, n, gain, b_ss, b_rot)
        nr2(n, cos, sin, b_ss, b_rot)
        nr3(n, dst)

    def phase_c0():
      load_weight_rows(WKV, w_mem_kv, 2048, 3)
      for mt in range(2):
          xt = XT[rot("xt", 2)]
          dma(xt[:, :], memd[mt * 128:(mt + 1) * 128, :])
          norm_to_T(xt[:, :], 128, MEMT[:, :, mt * 128:(mt + 1) * 128], "dve")
      for oc in range(8):
          pb = bank(oc % 2, 256)
          for c in range(8):
              mm(pb, WKV[:, c, oc * 128:(oc + 1) * 128], MEMT[:, c, :], c == 0, c == 7)
          cp("act", KM[:, oc, :], pb)
      for mt in range(2):
          for hf in range(2):
              pb = bank(2 + (mt * 2 + hf) % 2)
              for c in range(8):
                  mm(pb, MEMT[:, c, mt * 128:(mt + 1) * 128], WKV[:, c, 1024 + hf * 512:1024 + (hf + 1) * 512], c == 0, c == 7)
              cp("dve", VM[:, mt, hf * 512:(hf + 1) * 512], pb)

    def load_win_chunk(c):
        sc = GFM[:, c:c + 1]
        wpiece(WIN[:, c, 0:1024], [(w_in[c * 128:(c + 1) * 128, 0:1024], 0, 128, 0, 1024)], 1024, sc)
        slot_view = lambda slot: slot[:, 0:512].rearrange("p (h j d) -> p j h d", h=2, j=4)
        wpiece(WIN[:, c, 1024:1536].rearrange("p (j h d) -> p j h d", j=4, h=2),
               [(w_in[c * 128:(c + 1) * 128, 1024:1792], 0, 128, 0, 768)], 512, sc, in_view=slot_view)
        last = STG[(cntr["stg"] + 1) % 2]
        ts("dve", WIN[:, c, 1536:1792], last[:, 512:768], sc, ALU.mult)

    XQ = [XT[0], XT[1], YT[0], YT[1]]

    def tile_load(t, src, m):
        dma(XQ[t % 4][0:m, :], src)

    def tile_front(t, src, m):
        xt = XQ[t % 4]
        ss = stat1(m)
        hb = HB[t % 2]
        if t % 2 == 0:
            act(hb[0:m, :], xt[0:m, :], AF.Square, accum=ss)
        else:
            stt("dve", hb[0:m, :], xt[0:m, :], 1.0, xt[0:m, :], ALU.mult, ALU.mult, accum=ss)
        rs = stat1(m)
        rsqrt_from(rs, ss, 1.0 / 1024.0, m)
        ts("dve", hb[0:m, :], xt[0:m, :], rs, ALU.mult)

    def tile_back(t, m, dst, evac_eng):
        hb = HB[t % 2]
        tpv = v3(bankbf(6 + t % 2), 8)
        for c in range(8):
            transp(tpv[:, c, 0:m], hb[0:m, c * 128:(c + 1) * 128], IDENT[0:m, 0:m])
        cp(evac_eng, dst, tpv[:, :, 0:m])

    def kv_job(hsrc, kb):
        i = kb
        rp_ = ROPE[i % 2]
        kp = bank(i % 2)
        vp = bank(2 + i % 2)

        def P():
            dma(rp_[:, 0:512], ropek[0, :, kb * 512:(kb + 1) * 512])
            dma(rp_[:, 512:1024], ropek[1, :, kb * 512:(kb + 1) * 512])
            for c in range(8):
                mm(kp, WIN[:, c, 1536:1664], hsrc[:, c, :], c == 0, c == 7)
            for t in range(4):
                for c in range(8):
                    mm(vp[:, t * 128:(t + 1) * 128], hsrc[:, c, t * 128:(t + 1) * 128], WIN[:, c, 1664:1792], c == 0, c == 7)

        def N1():
            nr1(kp, 512, QKG[:, 1:2], 4, 5)
            cp("act", VV[:, kb * 4:kb * 4 + 4, :, 0:64], vp.rearrange("p (t g d) -> p t g d", t=4, g=2))

        def N2():
            nr2(512, rp_[:, 0:512], rp_[:, 512:1024], 4, 5)

        def N3():
            nr3(512, KT[:, kb * 512:(kb + 1) * 512])
        return [P, N1, N2, N3]

    def run_tiles(tiles, jobs_ready, extra=None):
        active = []

        def step_jobs(k):
            for _ in range(k):
                if active:
                    active[0].pop(0)()
                    if not active[0]:
                        active.pop(0)
        for t0 in range(min(3, len(tiles))):
            tile_load(t0, tiles[t0][0], tiles[t0][1])
        tile_front(0, tiles[0][0], tiles[0][1])
        for t in range(len(tiles)):
            if t + 3 < len(tiles):
                tile_load(t + 3, tiles[t + 3][0], tiles[t + 3][1])
            if t + 1 < len(tiles):
                tile_front(t + 1, tiles[t + 1][0], tiles[t + 1][1])
            tile_back(t, tiles[t][1], tiles[t][2], "act" if t % 2 else "dve")
            if extra is not None:
                extra(t)
            for job in jobs_ready.get(t, []):
                active.append(job)
            step_jobs(2 if len(active) > 1 else 1)
        while active:
            step_jobs(1)

    ext_tiles = col_blocks(0, NE, 128)
    tiles = [(x_ext[e0:e0 + m, :], m, HT[:, :, e0:e0 + m]) for (e0, m) in ext_tiles]
    jobs = {4 * kb + 4: [kv_job(HT[:, :, 16 + kb * 512:16 + (kb + 1) * 512], kb)] for kb in range(4)}
    run_tiles(tiles, jobs, extra=lambda t: [load_win_chunk(2 * t), load_win_chunk(2 * t + 1)] if t < 4 else None)
    phase_c0()
    tiles = []
    jobs = {}
    for rb in range(12):
        for t in range(4):
            r0 = rb * 512 + t * 128
            tiles.append((x_rest[r0:r0 + 128, :], 128, HTB[rb % 2][:, :, t * 128:(t + 1) * 128]))
        jobs[4 * rb + 3] = [kv_job(HTB[rb % 2], 4 + rb)]
    run_tiles(tiles, jobs)

    qitems = [(bk, e0, n, j) for bk, (e0, n) in enumerate(col_blocks(E_LO, E_HI)) for j in range(4)]

    def q_a(i):
        bk, e0, n, j = qitems[i]
        rp_ = ROPE[bk % 2]
        if j == 0:
            dma(rp_[:, 0:n], ropeq[0, :, e0:e0 + n])
            dma(rp_[:, 512:512 + n], ropeq[1, :, e0:e0 + n])
        qp = bank(6 + i % 2, n)
        for c in range(8):
            mm(qp, WIN[:, c, 1024 + j * 128:1024 + (j + 1) * 128], HT[:, c, e0:e0 + n], c == 0, c == 7)

    def q_n(i):
        bk, e0, n, j = qitems[i]
        rp_ = ROPE[bk % 2]
        normrope(bank(6 + i % 2, n), n, QKG[:, 0:1], rp_[:, 0:n], rp_[:, 512:512 + n], QT[:, j, e0:e0 + n], 4, 5)

    q_a(0)
    for i in range(len(qitems)):
        if i + 1 < len(qitems):
            q_a(i + 1)
        q_n(i)
    bi = 0
    for (e0, n) in col_blocks(0, NE):
        for ci in range(4):
            av = bank(2 * (bi % 2), n)
            ag = bank(2 * (bi % 2) + 1, n)
            bi += 1
            for c in range(8):
                mm(av, WIN[:, c, ci * 128:(ci + 1) * 128], HT[:, c, e0:e0 + n], c == 0, c == 7)
            for c in range(8):
                mm(ag, WIN[:, c, 512 + ci * 128:512 + (ci + 1) * 128], HT[:, c, e0:e0 + n], c == 0, c == 7)
            sg = TMP[3 + bi % 2][:, 0:n]
            act(sg, ag, AF.Sigmoid)
            tt("dve", AT[:, ci, e0:e0 + n], av, sg, ALU.mult)

    for ci in range(4):
        for k in range(31):
            ts("dve", DIAG[:, ci, k, :], IDENT, CONVP[:, ci * 34 + k:ci * 34 + k + 1], ALU.mult)
    for cbi, (e0, n) in enumerate(col_blocks(E_LO, E_HI)):
        sm = bank(4, n)
        sq_ = bank(5, n)
        cvb = [0, 1, 2, 3] if cbi % 2 == 0 else [6, 7, 2, 3]
        for ci in range(4):
            cv = bank(cvb[ci], n)
            for k in range(31):
                mm(cv, DIAG[:, ci, k, :], AT[:, ci, e0 + k - 15:e0 + k - 15 + n], k == 0, k == 30)
            bcol = CONVP[:, ci * 34 + 31:ci * 34 + 32]
            cb = TMPB[ci % 2][:, 0:n]
            cs = TMPB[2 + ci % 2][:, 0:n]
            act(cb, cv, AF.Identity, bias=bcol)
            act(cs, cv, AF.Square, bias=bcol)
            mm(sm, ONES, cb, ci == 0, ci == 3)
            mm(sq_, ONES, cs, ci == 0, ci == 3)
        mean = TMP[0][:, 0:n]
        ts("dve", mean, sm, 1.0 / 512.0, ALU.mult)
        msq = TMP[1][:, 0:n]
        tt("dve", msq, mean, mean, ALU.mult)
        var = TMP[2][:, 0:n]
        stt("dve", var, sq_, 1.0 / 512.0, msq, ALU.mult, ALU.subtract)
        act(var, var, AF.Ln, bias=EPSB[:, 0:1], scale=1.0)
        act(var, var, AF.Exp, scale=-0.5)
        for ci in range(4):
            cv = bank(cvb[ci], n)
            bcol = CONVP[:, ci * 34 + 31:ci * 34 + 32]
            t1 = TMP[3 + ci % 2][:, 0:n]
            stt("dve", t1, cv, bcol, mean, ALU.add, ALU.subtract)
            tt("pool", t1, t1, var, ALU.mult)
            act(HT[:, ci, e0:e0 + n], t1, AF.Silu, bias=CONVP[:, ci * 34 + 33:ci * 34 + 34],
                scale=CONVP[:, ci * 34 + 32:ci * 34 + 33])

    def wout_rows(c):
        if c < 4:
            return [(c * 128, 128, 0)]
        j = c - 4
        return [(512 + 64 * j, 64, 0), (512 + 64 * (4 + j), 64, 64)]
    load_weight_rows(WOUT, w_out, 1024, None, rows_of_chunk=wout_rows)
    dma(GB[:, :], gpost[0:1, :].partition_broadcast(128))
    load_weight_rows(WMQ, w_mem_q, 1024, 1)
    load_weight_rows(WMO, w_mem_o, 1024, None)

    PTX = [WREG[:, i * 2048:(i + 1) * 2048] for i in range(2)]
    PTY = [WREG[:, 4096 + i * 1024:4096 + (i + 1) * 1024] for i in range(2)]
    groups = [(j, e0, n) for j in range(4) for (e0, n) in col_blocks(E_LO, E_HI)]
    batches = []
    nxy = {"X": 0, "Y": 0}
    for gi in range(len(groups)):
        kt = 0
        turn = "X"
        while kt < 64:
            if turn == "X" and kt + 2 <= 64:
                kts = [kt, kt + 1]
                kind = "X"
            else:
                kts = [kt]
                kind = "Y"
            kt += len(kts)
            batches.append((gi, kts, kind, nxy[kind], kt == 64))
            nxy[kind] += 1
            turn = "Y" if turn == "X" else "X"

    def sbank(kind, i, h):
        return (2 * i + h) if kind == "X" else (4 + h)

    def emit_qk(bn):
        gi, kts, kind, ser, last = batches[bn]
        j, e0, n = groups[gi]
        for i, kt in enumerate(kts):
            for h in range(2):
                mm(bank(sbank(kind, i, h), n), KT[64 * h:64 * h + 64, kt * 128:(kt + 1) * 128],
                   QT[64 * h:64 * h + 64, j, e0:e0 + n], True, True)

    def emit_exp(bn):
        gi, kts, kind, ser, last = batches[bn]
        j, e0, n = groups[gi]
        nit = 2 * len(kts)
        b0 = 0 if kind == "X" else 4
        pt = (PTX if kind == "X" else PTY)[ser % 2]
        sv = PS[:, 512 * b0:512 * (b0 + nit)].rearrange("p (b t) -> p b t", b=nit)[:, :, 0:n]
        pv = pt[:, 0:512 * nit].rearrange("p (b t) -> p b t", b=nit)[:, :, 0:n]
        act(pv, sv, AF.Exp, scale=0.125)

    def emit_pv(bn):
        gi, kts, kind, ser, last = batches[bn]
        j, e0, n = groups[gi]
        pt = (PTX if kind == "X" else PTY)[ser % 2]
        for i, kt in enumerate(kts):
            for h in range(2):
                it = 2 * i + h
                mm(bank(6 + h, n)[0:65, :], VV[:, kt, h, 0:65], pt[:, 512 * it:512 * it + n], kt == 0, kt == 63)
        if last:
            for h in range(2):
                cp("dve", TMP[h][0:65, 0:n], bank(6 + h, n)[0:65, :])
            for h in range(2):
                recip(TMP[2 + h][64:65, 0:n], TMP[h][64:65, 0:n])
            for h in range(2):
                osb = TMP[h][0:65, 0:n]
                rd = TMP[2 + h][64:65, 0:n]
                rdb = TMP[4 + h][0:64, 0:n]
                row = 2 * gi + h
                dma(rds[row:row + 1, 0:n], rd)
                dma(rdb, rds[row:row + 1, 0:n].partition_broadcast(64))
                if h == 0:
                    tt("dve", HT[0:64, 4 + j, e0:e0 + n], osb[0:64, :], rdb, ALU.mult)
                else:
                    a1_ = ATT1[0:64, 0:n]
                    tt("dve", a1_, osb[0:64, :], rdb, ALU.mult)
                    dma(HT[64:128, 4 + j, e0:e0 + n], a1_)

    pend = {"X": 0, "Y": 0}
    nq = [0]

    def try_qk():
        while nq[0] < len(batches) and pend[batches[nq[0]][2]] == 0:
            emit_qk(nq[0])
            pend[batches[nq[0]][2]] += 1
            nq[0] += 1

    try_qk()
    for bn in range(len(batches)):
        emit_exp(bn)
        pend[batches[bn][2]] -= 1
        try_qk()
        emit_pv(bn)

    def outproj_phase(tiles, nk, lhs_fn, rhs_fn, xsrc_fn, xdst_fn, hdst_fn):
        def ybuf(t):
            m = tiles[t][1]
            return PS[0:m, 1024 * (t % 2):1024 * (t % 2) + 1024]

        def st_a(t):
            y = ybuf(t)
            dma(XT[t % 2][0:tiles[t][1], :], xsrc_fn(t))
            for hf in range(2):
                for c in range(nk):
                    mm(y[:, hf * 512:(hf + 1) * 512], lhs_fn(t, c), rhs_fn(c, hf), c == 0, c == nk - 1)

        def st_b1(t):
            m = tiles[t][1]
            y = ybuf(t)
            xt = XT[t % 2]
            ss = stat1(m)
            yt = YT[t % 2]
            act(yt[0:m, :], y, AF.Square, accum=ss)
            rs = stat1(m)
            rsqrt_from(rs, ss, 1.0 / 1024.0, m)
            stt("dve", yt[0:m, :], y, rs, GB[0:m, :], ALU.mult, ALU.mult)
            tt("pool", yt[0:m, :], yt[0:m, :], xt[0:m, :], ALU.add)
            dma(xdst_fn(t), yt[0:m, :])

        def st_b2(t):
            if hdst_fn is None:
                return
            m = tiles[t][1]
            yt = YT[t % 2]
            hb = HB[t % 2]
            ss = stat1(m)
            act(hb[0:m, :], yt[0:m, :], AF.Square, accum=ss)
            rs = stat1(m)
            rsqrt_from(rs, ss, 1.0 / 1024.0, m)
            ts("dve", hb[0:m, :], yt[0:m, :], rs, ALU.mult)

        def st_c(t):
            if hdst_fn is None:
                return
            tile_back(t, tiles[t][1], hdst_fn(t), "act" if t % 2 else "dve")

        stages = [st_a, st_b1, st_b2, st_c]
        for s_ in range(len(tiles) + len(stages) - 1):
            for k, st in enumerate(stages):
                t = s_ - k
                if 0 <= t < len(tiles):
                    st(t)

    tok_tiles = col_blocks(E_LO, E_HI, 128)

    def ht_tile(t):
        e0, m = tok_tiles[t]
        return HT[:, :, e0:e0 + m]

    outproj_phase(tok_tiles, 8,
                  lambda t, c: HT[:, c, tok_tiles[t][0]:tok_tiles[t][0] + tok_tiles[t][1]],
                  lambda c, hf: WOUT[:, c, hf * 512:(hf + 1) * 512],
                  lambda t: x_ext[tok_tiles[t][0]:tok_tiles[t][0] + tok_tiles[t][1], :],
                  lambda t: xs1[tok_tiles[t][0]:tok_tiles[t][0] + tok_tiles[t][1], :],
                  ht_tile)

    fgroups = [(f0, min(4, NFC - f0)) for f0 in range(0, NFC, 4)]
    fblocks = []
    o0 = 16
    while o0 < 2064:
        no = min(510, 2064 - o0)
        fblocks.append((o0, no))
        o0 += no

    def load_wdown(chunks, dst_fn):
        for f in chunks:
            wpiece(dst_fn(f), [(w_down[f * 128:(f + 1) * 128, :], 0, 128, 0, 1024)], 1024, None)

    def load_wup_group(gi):
        f0, nf = fgroups[gi]
        wu = WUG0 if gi == 0 else WU[gi % 2]
        for c in range(8):
            ncol = nf * 128
            srcs = [(w_up[c * 128:(c + 1) * 128, f0 * 128:f0 * 128 + ncol], 0, 128, 0, ncol),
                    (w_up[c * 128:(c + 1) * 128, DFF + f0 * 128:DFF + f0 * 128 + ncol], 0, 128, 512, ncol)]
            sc = GFM[:, 16 + c:16 + c + 1]
            if nf == 4:
                wpiece(wu[:, c, :], srcs, 1024, sc)
            else:
                slot_view = lambda slot: slot[:, :].rearrange("p (a t) -> p a t", a=2)[:, :, 0:ncol]
                wpiece(wu[:, c, :].rearrange("p (a t) -> p a t", a=2)[:, :, 0:ncol], srcs, 1024, sc, in_view=slot_view)

    dma(GB[:, :], gpost[1:2, :].partition_broadcast(128))
    load_wup_group(0)
    qi = 0
    for (e0, n) in col_blocks(E_LO, E_HI):
        for oc in range(8):
            qb = bank(qi % 2, n)
            qi += 1
            for c in range(8):
                mm(qb, WMQ[:, c, oc * 128:(oc + 1) * 128], HT[:, c, e0:e0 + n], c == 0, c == 7)
            cp("act", BQ[:, oc, e0:e0 + n], qb)
        for hd in range(4):
            for mt in range(2):
                sbk = bank(2 + mt, n)
                for dc in range(2):
                    mm(sbk, KM[:, 2 * hd + dc, mt * 128:(mt + 1) * 128], BQ[:, 2 * hd + dc, e0:e0 + n], dc == 0, dc == 1)
            pt = PTX[hd % 2]
            sv = PS[:, 1024:2048].rearrange("p (b t) -> p b t", b=2)[:, :, 0:n]
            pv = pt[:, 0:1024].rearrange("p (b t) -> p b t", b=2)[:, :, 0:n]
            act(pv, sv, AF.Exp, scale=1.0 / 16.0)
            den = bank(4, n)
            for mt in range(2):
                mm(den, ONES, pt[:, 512 * mt:512 * mt + n], mt == 0, mt == 1)
            rd = TMP[hd % 2][:, 0:n]
            act(rd, den, AF.Ln)
            act(rd, rd, AF.Exp, scale=-1.0)
            for dc in range(2):
                ob = bank(5 + dc, n)
                for mt in range(2):
                    mm(ob, VM[:, mt, (2 * hd + dc) * 128:(2 * hd + dc + 1) * 128], pt[:, 512 * mt:512 * mt + n], mt == 0, mt == 1)
                tt("dve", HT[:, 2 * hd + dc, e0:e0 + n], ob, rd, ALU.mult)
    outproj_phase(tok_tiles, 8,
                  lambda t, c: HT[:, c, tok_tiles[t][0]:tok_tiles[t][0] + tok_tiles[t][1]],
                  lambda c, hf: WMO[:, c, hf * 512:(hf + 1) * 512],
                  lambda t: xs1[tok_tiles[t][0]:tok_tiles[t][0] + tok_tiles[t][1], :],
                  lambda t: xs2[tok_tiles[t][0]:tok_tiles[t][0] + tok_tiles[t][1], :],
                  ht_tile)
    ts("dve", HT[:, :, 15:16], HT[:, :, 15:16], MASK[:, 0:1], ALU.mult)
    ts("dve", HT[:, :, 2064:2065], HT[:, :, 2064:2065], MASK[:, 1:2], ALU.mult)

    dma(GB[:, :], gpost[2:3, :].partition_broadcast(128))
    fitems = []
    for gi, (f0, nf) in enumerate(fgroups):
        for fl in range(nf):
            for bi_, (o0, no) in enumerate(fblocks):
                fitems.append((gi, fl, f0 + fl, o0, no, fl == 0 and bi_ == 0))

    def f_a(i):
        gi, fl, f, o0, no, first = fitems[i]
        if first and gi + 1 < len(fgroups):
            load_wup_group(gi + 1)
        if first and gi == len(fgroups) - 1:
            load_wdown(range(0, 6), lambda f_: WDA[:, f_ * 1024:(f_ + 1) * 1024])
        wu = WUG0 if gi == 0 else WU[gi % 2]
        ug = bank(2 * (i % 3), no + 2)
        uv = bank(2 * (i % 3) + 1, no + 2)
        for c in range(8):
            mm(ug, wu[:, c, fl * 128:(fl + 1) * 128], HT[:, c, o0 - 1:o0 + no + 1], c == 0, c == 7)
        for c in range(8):
            mm(uv, wu[:, c, 512 + fl * 128:512 + (fl + 1) * 128], HT[:, c, o0 - 1:o0 + no + 1], c == 0, c == 7)

    def f_b(i):
        gi, fl, f, o0, no, first = fitems[i]
        pg = FFNP[:, f * 4:f * 4 + 4]
        pv_ = FFNP[:, (NFC + f) * 4:(NFC + f) * 4 + 4]
        ug = bank(2 * (i % 3), no + 2)
        uv = bank(2 * (i % 3) + 1, no + 2)
        tg = TMP[i % 3][:, 0:no]
        tv = TMP[3 + i % 3][:, 0:no]
        act(tg, ug[:, 1:1 + no], AF.Identity, bias=pg[:, 3:4], scale=pg[:, 1:2])
        act(tv, uv[:, 1:1 + no], AF.Identity, bias=pv_[:, 3:4], scale=pv_[:, 1:2])
        stt("dve", tg, ug[:, 0:no], pg[:, 0:1], tg, ALU.mult, ALU.add)
        stt("dve", tg, ug[:, 2:2 + no], pg[:, 2:3], tg, ALU.mult, ALU.add)
        stt("dve", tv, uv[:, 0:no], pv_[:, 0:1], tv, ALU.mult, ALU.add)
        stt("dve", tv, uv[:, 2:2 + no], pv_[:, 2:3], tv, ALU.mult, ALU.add)

    def f_c(i):
        gi, fl, f, o0, no, first = fitems[i]
        tg = TMP[i % 3][:, 0:no]
        tv = TMP[3 + i % 3][:, 0:no]
        act(tg, tg, AF.Gelu_apprx_tanh)
        tt("pool", GT[:, f, o0 - 16:o0 - 16 + no], tg, tv, ALU.mult)

    fst = [f_a, f_b, f_c]
    for s_ in range(len(fitems) + 2):
        for k, st in enumerate(fst):
            t = s_ - k
            if 0 <= t < len(fitems):
                st(t)
    WD1 = v3(BIG8[:, 0:16384], 16)
    load_wdown(range(6, NFC), lambda f: WD1[:, f - 6, :])

    def wd(f, hf):
        if f < 6:
            return WDA[:, f * 1024 + hf * 512:f * 1024 + (hf + 1) * 512]
        return WD1[:, f - 6, hf * 512:(hf + 1) * 512]

    ffn_tiles = [(16 + ti * 128, 128) for ti in range(16)]
    outproj_phase(ffn_tiles, NFC,
                  lambda t, c: GT[:, c, t * 128:(t + 1) * 128],
                  wd,
                  lambda t: xs2[16 + t * 128:16 + (t + 1) * 128, :],
                  lambda t: outd[t * 128:(t + 1) * 128, :],
                  None)

    S.emit(es)
    es.close()
    return nc


def _rope_tables(pos):
    pos = np.asarray(pos)
    inv = (10000.0 ** (-(np.arange(0, 32, 2, dtype=np.float32)) / np.float32(32))).astype(np.float32)
    r = (pos // 64).astype(np.float32)
    c = (pos % 64).astype(np.float32)
    ang_r = r[None, :] * inv[:, None]
    ang_c = c[None, :] * inv[:, None]
    ang = np.concatenate([ang_r, ang_r, ang_c, ang_c], axis=0).astype(np.float32)
    ang = np.concatenate([ang, ang], axis=0)
    return np.stack([np.cos(ang), np.sin(ang)]).astype(np.float32)


def _consts():
    ident = np.eye(128, dtype=np.float32)
    perm = np.zeros((128, 128), np.float32)
    for m in range(128):
        if (m % 32) < 16:
            perm[m + 16, m] = -1.0
        else:
            perm[m - 16, m] = 1.0
    bones = np.zeros((128, 128), np.float32)
    bones[:64, :64] = 1.0
    bones[64:, 64:] = 1.0
    ones = np.ones((128, 128), np.float32)
    return np.ascontiguousarray(np.concatenate([ident, perm, bones, ones], axis=1))


_NC_CACHE = {}


def kernel(x, mem, norm_mix_pre, w_in, conv_dw, conv_dw_b, conv_ln_g, conv_ln_b,
           q_norm_g, k_norm_g, w_out, norm_mix_post, norm_mem_pre, mem_norm_g,
           w_mem_q, w_mem_kv, w_mem_o, norm_mem_post, norm_ffn_pre, w_up, ffn_dw,
           ffn_dw_b, w_down, norm_ffn_post):
    f = lambda a: np.ascontiguousarray(np.asarray(a, dtype=np.float32))
    x = f(x); mem = f(mem)
    B, Sq, D = x.shape

    def fm(g):
        return f(g).reshape(8, 128).T
    gfm = np.ascontiguousarray(np.concatenate([fm(norm_mix_pre[0]), fm(norm_mem_pre[0]), fm(norm_ffn_pre[0]), fm(mem_norm_g[0])], axis=1))
    gpost = np.ascontiguousarray(np.stack([f(norm_mix_post[0]), f(norm_mem_post[0]), f(norm_ffn_post[0])]))
    cw = f(conv_dw[0])
    convp = np.zeros((128, 4, 34), np.float32)
    for ci in range(4):
        convp[:, ci, 0:31] = cw[:, ci * 128:(ci + 1) * 128].T
        convp[:, ci, 31] = f(conv_dw_b[0])[ci * 128:(ci + 1) * 128]
        convp[:, ci, 32] = f(conv_ln_g[0])[ci * 128:(ci + 1) * 128]
        convp[:, ci, 33] = f(conv_ln_b[0])[ci * 128:(ci + 1) * 128]
    convp = np.ascontiguousarray(convp.reshape(128, 136))
    qkg = np.ascontiguousarray(np.stack([np.tile(f(q_norm_g[0]), 2), np.tile(f(k_norm_g[0]), 2)], axis=1))
    fw = f(ffn_dw[0])
    fb = f(ffn_dw_b[0])
    ffnp = np.zeros((128, 44, 4), np.float32)
    for fc in range(44):
        ffnp[:, fc, 0:3] = fw[:, fc * 128:(fc + 1) * 128].T
        ffnp[:, fc, 3] = fb[fc * 128:(fc + 1) * 128]
    ffnp = np.ascontiguousarray(ffnp.reshape(128, 176))
    cst = _consts()
    shared = dict(w_in=f(w_in[0]), w_out=f(w_out[0]), w_mem_q=f(w_mem_q[0]), w_mem_kv=f(w_mem_kv[0]),
                  w_mem_o=f(w_mem_o[0]), w_up=f(w_up[0]), w_down=f(w_down[0]), gfm=gfm, gpost=gpost,
                  convp=convp, qkg=qkg, ffnp=ffnp, cst=cst)
    in_maps = []
    for core in range(8):
        b, j = core // 4, core % 4
        s = j * 2048
        xe = np.zeros((NE, 1024), np.float32)
        lo, hi = max(0, s - 16), min(Sq, s + 2064)
        xe[lo - (s - 16):hi - (s - 16)] = x[b, lo:hi]
        rest_idx = np.concatenate([np.arange(0, s), np.arange(s + 2048, Sq)])
        xr = np.ascontiguousarray(x[b, rest_idx])
        key_pos = np.concatenate([np.arange(s, s + 2048), rest_idx])
        ropek = _rope_tables(key_pos)
        ext_pos = np.clip(np.arange(s - 16, s - 16 + NE), 0, Sq - 1)
        ropeq = _rope_tables(ext_pos)
        mask = np.ones((128, 2), np.float32)
        if j == 0:
            mask[:, 0] = 0.0
        if j == 3:
            mask[:, 1] = 0.0
        m = dict(shared)
        m.update(x_ext=xe, x_rest=xr, mem=np.ascontiguousarray(mem[b]), ropek=ropek, ropeq=ropeq, mask=mask)
        in_maps.append(m)
    if "nc" not in _NC_CACHE:
        _NC_CACHE["nc"] = build_nc()
    nc = _NC_CACHE["nc"]
    res = run_bass_kernel_spmd(nc, in_maps, core_ids=list(range(8)))
    out = np.zeros((B, Sq, D), np.float32)
    for core in range(8):
        b, j = core // 4, core % 4
        out[b, j * 2048:(j + 1) * 2048] = np.asarray(res.results[core]["out"], dtype=np.float32)
    return out
```

```python
import numpy as np
from contextlib import ExitStack
import concourse.bass as bass
import concourse.mybir as mybir
from concourse.bass_utils import run_bass_kernel_spmd

F32 = mybir.dt.float32
BF16 = mybir.dt.bfloat16
AF = mybir.ActivationFunctionType
ALU = mybir.AluOpType

EPS = 1e-6
NE = 2080
E_LO, E_HI = 15, 2065
DFF = 2816
NFC = 22


def _esize(dt):
    return 2 if dt == BF16 else 4


class Sched:
    ENGS = ["pe", "act", "dve", "pool", "sp"]
    NDMA = 16

    def __init__(self, nc, tracked_dram=()):
        self.nc = nc
        self.ops = []
        self.w = {}
        self.r = {}
        self.tracked_dram = set(tracked_dram)

    def region(self, ap):
        t = ap.tensor
        name = t.name
        space = str(ap.space)
        if "DRAM" in space.upper() or "HBM" in space.upper() or type(t).__name__.startswith("DRam"):
            if name not in self.tracked_dram:
                return None
        es = _esize(ap.dtype)
        dims = ap.ap
        ps, pc = dims[0]
        off = int(ap.offset)
        if ps == 0:
            rs_ = int(t.shape[-1])
            p0, f0 = off // rs_, off % rs_
            p1 = p0 + 1
        else:
            p0 = off // ps
            f0 = off % ps
            p1 = p0 + pc
        ents = [(f0, 0)]
        rest = dims[1:]
        for (s, c) in rest[:-1]:
            s = abs(s)
            if s != 0 and len(ents) * c <= 64:
                ents = [(st + i * s, ex) for (st, ex) in ents for i in range(c)]
            else:
                ents = [(st, ex + (c - 1) * s) for (st, ex) in ents]
        if rest:
            s, c = rest[-1]
            ents = [(st, ex + (c - 1) * abs(s) + 1) for (st, ex) in ents]
        else:
            ents = [(st, ex + 1) for (st, ex) in ents]
        ivs = tuple(sorted((st * es, (st + ex) * es) for (st, ex) in ents))
        return (name, p0, p1, ivs)

    @staticmethod
    def _ov(a, b):
        if a[1] >= b[2] or b[1] >= a[2]:
            return False
        for (s0, e0) in a[3]:
            for (s1, e1) in b[3]:
                if s0 < e1 and s1 < e0:
                    return True
        return False

    @staticmethod
    def _covers(a, b):
        if len(a[3]) != 1:
            return a[1] <= b[1] and a[2] >= b[2] and a[3] == b[3]
        if a[1] > b[1] or a[2] < b[2]:
            return False
        s0, e0 = a[3][0]
        return all(s0 <= s1 and e1 <= e0 for (s1, e1) in b[3])

    def add(self, eng, fn, reads=(), writes=(), dma=False):
        op = {"eng": eng, "fn": fn, "deps": set(), "idx": len(self.ops), "inc": False, "dma": dma}
        for ap in reads:
            rg = self.region(ap)
            if rg is None:
                continue
            for key, d in self.w.get(rg[0], {}).items():
                if self._ov(rg, key):
                    op["deps"].update(d.values())
            self.r.setdefault(rg[0], {}).setdefault(rg, {})[eng] = op["idx"]
        for ap in writes:
            rg = self.region(ap)
            if rg is None:
                continue
            for table in (self.w, self.r):
                tb = table.get(rg[0], {})
                dead = []
                for key, d in tb.items():
                    if self._ov(rg, key):
                        op["deps"].update(d.values())
                        if self._covers(rg, key):
                            dead.append(key)
                for k in dead:
                    del tb[k]
            self.w.setdefault(rg[0], {})[rg] = {eng: op["idx"]}
        op["deps"].discard(op["idx"])
        self.ops.append(op)
        return op

    def emit(self, es):
        nc = self.nc
        ops = self.ops
        for op in ops:
            op["deps"] = {d for d in op["deps"] if not (op["eng"] == "pe" and ops[d]["eng"] == "pe" and not ops[d]["dma"])}
            latest = {}
            keep = set()
            for d in op["deps"]:
                p = ops[d]
                if p["dma"]:
                    keep.add(d)
                else:
                    latest[p["eng"]] = max(latest.get(p["eng"], -1), d)
            op["deps"] = keep | set(latest.values())
            for d in op["deps"]:
                ops[d]["inc"] = True
        esem = {e: es.enter_context(nc.semaphore("sem_" + e)) for e in self.ENGS}
        dsem = [es.enter_context(nc.semaphore("dsem%d" % i)) for i in range(self.NDMA)]
        cnt = {e: 0 for e in self.ENGS}
        ndma = 0
        last_out_dma = []
        for op in ops:
            if op["dma"]:
                op["dsem"] = ndma % self.NDMA
                op["dcnt"] = 16 * (ndma // self.NDMA + 1)
                ndma += 1
            elif op["inc"]:
                cnt[op["eng"]] += 1
                op["cnt"] = cnt[op["eng"]]
        self.ndma = ndma
        block = es.enter_context(nc.Block())

        def stream(engname, e):
            seen = {}

            def wait(sem, key, val):
                if seen.get(key, 0) >= val:
                    return
                seen[key] = val
                e.wait_ge(sem, val)

            for op in ops:
                if op["eng"] != engname:
                    continue
                need = {}
                for d in op["deps"]:
                    p = ops[d]
                    if p["dma"]:
                        k = ("d", p["dsem"])
                        need[k] = max(need.get(k, 0), p["dcnt"])
                    else:
                        k = ("e", p["eng"])
                        need[k] = max(need.get(k, 0), p["cnt"])
                if op["dma"] and op["dcnt"] > 16:
                    k = ("d", op["dsem"])
                    need[k] = max(need.get(k, 0), op["dcnt"] - 16)
                for k, v in need.items():
                    wait(dsem[k[1]] if k[0] == "d" else esem[k[1]], k, v)
                ins = op["fn"](e)
                if op["dma"]:
                    ins.then_inc(dsem[op["dsem"]], 16)
                elif op["inc"]:
                    ins.then_inc(esem[op["eng"]], 1)
            if engname == "sp":
                for i in range(min(self.NDMA, ndma)):
                    n_i = (ndma - 1 - i) // self.NDMA + 1
                    wait(dsem[i], ("d", i), 16 * n_i)

        @block.tensor
        def _(e):
            stream("pe", e)

        @block.scalar
        def _(e):
            stream("act", e)

        @block.vector
        def _(e):
            stream("dve", e)

        @block.gpsimd
        def _(e):
            stream("pool", e)

        @block.sync
        def _(e):
            stream("sp", e)


def col_blocks(lo, hi, w=512):
    out = []
    while lo < hi:
        n = min(w, hi - lo)
        out.append((lo, n))
        lo += n
    return out


def build_nc(debug=False):
    nc = bass.Bass("TRN2", target_bir_lowering=False)
    es = ExitStack()

    def di(name, shape, dt=F32):
        return nc.dram_tensor(name, shape, dt, kind="ExternalInput").ap()

    x_ext = di("x_ext", [NE, 1024])
    x_rest = di("x_rest", [6144, 1024])
    memd = di("mem", [256, 1024])
    w_in = di("w_in", [1024, 1792])
    w_out = di("w_out", [1024, 1024])
    w_mem_q = di("w_mem_q", [1024, 1024])
    w_mem_kv = di("w_mem_kv", [1024, 2048])
    w_mem_o = di("w_mem_o", [1024, 1024])
    w_up = di("w_up", [1024, 2 * DFF])
    w_down = di("w_down", [DFF, 1024])
    gfm = di("gfm", [128, 4 * 8])
    gpost = di("gpost", [3, 1024])
    convp = di("convp", [128, 4 * 34])
    qkg = di("qkg", [128, 2])
    ffnp = di("ffnp", [128, 44 * 4])
    cst = di("cst", [128, 512])
    ropek = di("ropek", [2, 128, 8192])
    ropeq = di("ropeq", [2, 128, NE])
    maskd = di("mask", [128, 2])
    outd = nc.dram_tensor("out", [2048, 1024], F32, kind="ExternalOutput").ap()
    xs1 = nc.dram_tensor("xs1", [NE, 1024], F32, kind="Internal").ap()
    xs2 = nc.dram_tensor("xs2", [NE, 1024], F32, kind="Internal").ap()
    rds = nc.dram_tensor("rds", [64, 512], F32, kind="Internal").ap()
    dbg = {}

    S = Sched(nc, tracked_dram=["xs1", "xs2", "out", "rds"])

    def sb(name, shape, dt=F32):
        return es.enter_context(nc.sbuf_tensor(name, shape, dt))

    BIG8 = sb("BIG8", [128, 8 * NE], BF16)
    BIGQ = sb("BIGQ", [128, 8 * NE], BF16)
    G = sb("G", [128, NFC * 2048], BF16)
    STG = [sb("STG%d" % i, [128, 1024]) for i in range(2)]
    XT = [sb("XT%d" % i, [128, 1024]) for i in range(2)]
    YT = [sb("YT%d" % i, [128, 1024]) for i in range(2)]
    HB = [sb("HB%d" % i, [128, 1024], BF16) for i in range(2)]
    TMP = [sb("TMP%d" % i, [128, 512]) for i in range(6)]
    TMPB = [sb("TMPB%d" % i, [128, 512], BF16) for i in range(4)]
    GB = sb("GB", [128, 1024])
    CST = sb("CST", [128, 512], BF16)
    ONESF = sb("ONESF", [128, 64])
    GFM = sb("GFM", [128, 32])
    CONVP = sb("CONVP", [128, 4 * 34])
    QKG = sb("QKG", [128, 2])
    FFNP = sb("FFNP", [128, 44 * 4])
    MASK = sb("MASK", [128, 2])
    STAT = sb("STAT", [128, 32])
    PS = es.enter_context(nc.psum_tensor("PS", [128, 4096], F32))

    IDENT = CST[:, 0:128]
    PERM = CST[:, 128:256]
    BONES = CST[:, 256:384]
    ONES = CST[:, 384:512]

    def bank(b, n=512):
        return PS[:, 512 * b:512 * b + n]

    def bankbf(b):
        return PS[:, 512 * b:512 * b + 512].bitcast(BF16)

    def v3(ap2d, c):
        return ap2d.rearrange("p (c t) -> p c t", c=c)

    HT = v3(BIG8[:, :], 8)
    BQ = v3(BIGQ[:, :], 8)
    AT = BQ[:, 0:4, :]
    QT = BQ[:, 4:8, :]
    KT = G[:, 0:8192]
    VV = G[:, 8192:8192 + 64 * 130].rearrange("p (t g d) -> p t g d", t=64, g=2)
    WREG = G[:, 16512:16512 + 15872]
    WIN = v3(WREG[:, 0:8 * 1792], 8)
    DIAG = WREG[:, 0:15872].rearrange("p (c k m) -> p c k m", c=4, k=31)
    KM = v3(G[:, 32384:34432], 8)
    VM = v3(G[:, 34432:36480], 2)
    MEMT = v3(G[:, 36480:38528], 8)
    WKV = v3(BIGQ[:, 0:16384], 8)
    HTB = [v3(BIGQ[:, i * 4096:(i + 1) * 4096], 8) for i in range(2)]
    WOUT = v3(BIGQ[:, 0:8192], 8)
    WMQ = v3(G[:, 16512 + 6144:16512 + 14336], 8)
    WMO = v3(G[:, 36480:44672], 8)
    WUG0 = v3(G[:, 8192:16384], 8)
    GT = v3(G[:, :], NFC)
    WU = [v3(BIGQ[:, i * 8192:(i + 1) * 8192], 8) for i in range(2)]
    PT = [WREG[:, i * 1536:(i + 1) * 1536] for i in range(4)]
    _rf = G[:, 38528:38528 + 4096].bitcast(F32)
    ROPE = [_rf[:, i * 1024:(i + 1) * 1024] for i in range(2)]
    WDA = BIGQ[:, 0:6144]
    ATT1 = TMPB[3]

    cntr = {"stg": 0, "xt": 0, "yt": 0, "hb": 0, "rope": 0, "tp": 0, "stat": 0}

    def rot(key, n):
        v = cntr[key]
        cntr[key] = (v + 1) % n
        return v

    def stat1(m):
        i = rot("stat", 32)
        return STAT[0:m, i:i + 1]

    def dma(out, in_, q="sp"):
        S.add(q, lambda e: e.dma_start(out=out, in_=in_), reads=[in_], writes=[out], dma=True)

    def mm(out, lhsT, rhs, start, stop):
        S.add("pe", lambda e: e.matmul(out, lhsT, rhs, start=start, stop=stop), reads=[lhsT, rhs], writes=[out])

    def transp(out, in_, ident):
        S.add("pe", lambda e: e.transpose(out, in_, ident), reads=[in_, ident], writes=[out])

    def act(out, in_, func, bias=None, scale=None, accum=None):
        kw = {}
        rd = [in_]
        wr = [out]
        if bias is not None:
            kw["bias"] = bias
            if not isinstance(bias, float):
                rd.append(bias)
        if scale is not None:
            kw["scale"] = scale
            if not isinstance(scale, float):
                rd.append(scale)
        if accum is not None:
            kw["accum_out"] = accum
            wr.append(accum)
        S.add("act", lambda e: e.activation(out=out, in_=in_, func=func, **kw), reads=rd, writes=wr)

    def tt(eng, out, in0, in1, op):
        S.add(eng, lambda e: e.tensor_tensor(out=out, in0=in0, in1=in1, op=op), reads=[in0, in1], writes=[out])

    def ts(eng, out, in0, s1, op0, s2=None, op1=None):
        rd = [in0] + [s for s in (s1, s2) if s is not None and not isinstance(s, float)]
        if op1 is None:
            S.add(eng, lambda e: e.tensor_scalar(out=out, in0=in0, scalar1=s1, scalar2=None, op0=op0), reads=rd, writes=[out])
        else:
            S.add(eng, lambda e: e.tensor_scalar(out=out, in0=in0, scalar1=s1, scalar2=s2, op0=op0, op1=op1), reads=rd, writes=[out])

    def stt(eng, out, in0, scalar, in1, op0, op1, accum=None):
        rd = [in0, in1] + ([] if isinstance(scalar, float) else [scalar])
        if accum is None:
            S.add(eng, lambda e: e.scalar_tensor_tensor(out=out, in0=in0, scalar=scalar, in1=in1, op0=op0, op1=op1), reads=rd, writes=[out])
        else:
            S.add(eng, lambda e: e.scalar_tensor_tensor(out=out, in0=in0, scalar=scalar, in1=in1, op0=op0, op1=op1, accum_out=accum),
                  reads=rd, writes=[out, accum])

    def cp(eng, out, in_):
        if eng == "act":
            act(out, in_, AF.Identity)
        else:
            S.add(eng, lambda e: e.tensor_copy(out=out, in_=in_), reads=[in_], writes=[out])

    def recip(out, in_):
        S.add("dve", lambda e: e.reciprocal(out=out, in_=in_), reads=[in_], writes=[out])

    def memset(eng, ap, val):
        S.add(eng, lambda e: e.memset(ap, val), writes=[ap])

    def rsqrt_from(out, in_, scale, m=None):
        act(out, in_, AF.Ln, bias=EPSB[0:out.shape[0], 0:1] if m is None else EPSB[0:m, 0:1], scale=scale)
        act(out, out, AF.Exp, scale=-0.5)

    def wpiece(dst, srcs, ncols, scal=None, eng="dve", in_view=None):
        slot = STG[rot("stg", 2)]
        for (src, p0, p1, c0, nc_) in srcs:
            dma(slot[p0:p1, c0:c0 + nc_], src, q="pool")
        src_ap = slot[:, 0:ncols] if in_view is None else in_view(slot)
        if scal is None:
            cp(eng, dst, src_ap)
        elif eng == "act":
            act(dst, src_ap, AF.Identity, scale=scal)
        else:
            ts(eng, dst, src_ap, scal, ALU.mult)

    EPSB = sb("EPSB", [128, 1])
    memset("pool", EPSB[:, :], EPS)
    memset("pool", ONESF[:, :], 1.0)
    wpiece(CST[:, :], [(cst[:, :], 0, 128, 0, 512)], 512, None, eng="dve")
    dma(GFM[:, :], gfm[:, :])
    dma(CONVP[:, :], convp[:, :])
    dma(QKG[:, :], qkg[:, :])
    dma(FFNP[:, :], ffnp[:, :])
    dma(MASK[:, :], maskd[:, :])
    memset("pool", VV[:, :, :, 64:65], 1.0)

    def load_weight_rows(dst3, src, ncols_total, gcol, col_pieces=None, rows_of_chunk=None, nchunks=8):
        for c in range(nchunks):
            for (c0, ncol) in (col_pieces or col_blocks(0, ncols_total, 1024)):
                if rows_of_chunk is None:
                    srcs = [(src[c * 128:(c + 1) * 128, c0:c0 + ncol], 0, 128, 0, ncol)]
                else:
                    srcs = [(src[r0:r0 + nr, c0:c0 + ncol], p0, p0 + nr, 0, ncol) for (r0, nr, p0) in rows_of_chunk(c)]
                scal = None if gcol is None else GFM[:, gcol * 8 + c:gcol * 8 + c + 1]
                wpiece(dst3[:, c, c0:c0 + ncol], srcs, ncol, scal)

    def norm_to_T(src_tile, m, dst, evac_eng):
        ss = stat1(m)
        hb = HB[rot("hb", 2)]
        act(hb[0:m, :], src_tile, AF.Square, accum=ss)
        rs = stat1(m)
        rsqrt_from(rs, ss, 1.0 / 1024.0, m)
        ts("dve", hb[0:m, :], src_tile, rs, ALU.mult)
        tb = 6 + rot("tp", 2)
        tpv = v3(bankbf(tb), 8)
        for c in range(8):
            transp(tpv[:, c, 0:m], hb[0:m, c * 128:(c + 1) * 128], IDENT[0:m, 0:m])
        cp(evac_eng, dst, tpv[:, :, 0:m])

    def nr1(src, n, gain, b_ss, b_rot):
        xg = TMPB[0][:, 0:n]
        sq = TMPB[1][:, 0:n]
        act(xg, src, AF.Identity, scale=gain)
        act(sq, src, AF.Square)
        mm(bank(b_ss, n), BONES, sq, True, True)
        mm(bank(b_rot, n), PERM, xg, True, True)

    def nr2(n, cos, sin, b_ss, b_rot):
        xg = TMPB[0][:, 0:n]
        rs = TMP[0][:, 0:n]
        act(rs, bank(b_ss, n), AF.Ln, bias=EPSB[:, 0:1], scale=1.0 / 64.0)
        act(rs, rs, AF.Exp, scale=-0.5)
        tt("dve", TMP[1][:, 0:n], xg, cos, ALU.mult)
        tt("dve", TMP[2][:, 0:n], bank(b_rot, n), sin, ALU.mult)

    def nr3(n, dst):
        t1 = TMP[1][:, 0:n]
        tt("pool", t1, t1, TMP[2][:, 0:n], ALU.add)
        tt("pool", dst, t1, TMP[0][:, 0:n], ALU.mult)

    def normrope(src, n, gain, cos, sin, dst, b_ss, b_rot):
        nr1(src, n, gain, b_ss, b_rot)
        nr2(n, cos, sin, b_ss, b_rot)
        nr3(n, dst)

    def phase_c0():
      load_weight_rows(WKV, w_mem_kv, 2048, 3)
      for mt in range(2):
          xt = XT[rot("xt", 2)]
          dma(xt[:, :], memd[mt * 128:(mt + 1) * 128, :])
          norm_to_T(xt[:, :], 128, MEMT[:, :, mt * 128:(mt + 1) * 128], "dve")
      for oc in range(8):
          pb = bank(oc % 2, 256)
          for c in range(8):
              mm(pb, WKV[:, c, oc * 128:(oc + 1) * 128], MEMT[:, c, :], c == 0, c == 7)
          cp("act", KM[:, oc, :], pb)
      for mt in range(2):
          for hf in range(2):
              pb = bank(2 + (mt * 2 + hf) % 2)
              for c in range(8):
                  mm(pb, MEMT[:, c, mt * 128:(mt + 1) * 128], WKV[:, c, 1024 + hf * 512:1024 + (hf + 1) * 512], c == 0, c == 7)
              cp("dve", VM[:, mt, hf * 512:(hf + 1) * 512], pb)

    def load_win_chunk(c):
        sc = GFM[:, c:c + 1]
        wpiece(WIN[:, c, 0:1024], [(w_in[c * 128:(c + 1) * 128, 0:1024], 0, 128, 0, 1024)], 1024, sc)
        slot_view = lambda slot: slot[:, 0:512].rearrange("p (h j d) -> p j h d", h=2, j=4)
        wpiece(WIN[:, c, 1024:1536].rearrange("p (j h d) -> p j h d", j=4, h=2),
               [(w_in[c * 128:(c + 1) * 128, 1024:1792], 0, 128, 0, 768)], 512, sc, in_view=slot_view)
        last = STG[(cntr["stg"] + 1) % 2]
        ts("dve", WIN[:, c, 1536:1792], last[:, 512:768], sc, ALU.mult)

    XQ = [XT[0], XT[1], YT[0], YT[1]]

    def tile_load(t, src, m):
        dma(XQ[t % 4][0:m, :], src)

    JUNKA = sb("JUNKA", [128, 1024], BF16)
    JUNKD = sb("JUNKD", [128, 1024], BF16)
    tstat = {}

    def tile_f1(t, m):
        xt = XQ[t % 4]
        ss = stat1(m)
        tstat[t] = ss
        if t % 2 == 0:
            act(JUNKA[0:m, :], xt[0:m, :], AF.Square, accum=ss)
        else:
            stt("dve", JUNKD[0:m, :], xt[0:m, :], 1.0, xt[0:m, :], ALU.mult, ALU.mult, accum=ss)

    def tile_f2(t, m):
        xt = XQ[t % 4]
        hb = HB[t % 2]
        rs = stat1(m)
        rsqrt_from(rs, tstat[t], 1.0 / 1024.0, m)
        ts("dve", hb[0:m, :], xt[0:m, :], rs, ALU.mult)

    def tile_back(t, m, dst, evac_eng):
        hb = HB[t % 2]
        tpv = v3(bankbf(6 + t % 2), 8)
        for c in range(8):
            transp(tpv[:, c, 0:m], hb[0:m, c * 128:(c + 1) * 128], IDENT[0:m, 0:m])
        cp(evac_eng, dst, tpv[:, :, 0:m])

    def kv_job(hsrc, kb):
        i = kb
        rp_ = ROPE[i % 2]
        kp = bank(i % 2)
        vp = bank(2 + i % 2)

        def P():
            dma(rp_[:, 0:512], ropek[0, :, kb * 512:(kb + 1) * 512])
            dma(rp_[:, 512:1024], ropek[1, :, kb * 512:(kb + 1) * 512])
            for c in range(8):
                mm(kp, WIN[:, c, 1536:1664], hsrc[:, c, :], c == 0, c == 7)
            for t in range(4):
                for c in range(8):
                    mm(vp[:, t * 128:(t + 1) * 128], hsrc[:, c, t * 128:(t + 1) * 128], WIN[:, c, 1664:1792], c == 0, c == 7)

        def N1():
            nr1(kp, 512, QKG[:, 1:2], 4, 5)
            cp("act", VV[:, kb * 4:kb * 4 + 4, :, 0:64], vp.rearrange("p (t g d) -> p t g d", t=4, g=2))

        def N2():
            nr2(512, rp_[:, 0:512], rp_[:, 512:1024], 4, 5)

        def N3():
            nr3(512, KT[:, kb * 512:(kb + 1) * 512])
        return [P, N1, N2, N3]

    def run_tiles(tiles, jobs_ready, extra=None):
        active = []

        def step_jobs(k):
            for _ in range(k):
                if active:
                    active[0].pop(0)()
                    if not active[0]:
                        active.pop(0)
        for t0 in range(min(3, len(tiles))):
            tile_load(t0, tiles[t0][0], tiles[t0][1])
        nt = len(tiles)
        tile_f1(0, tiles[0][1])
        if nt > 1:
            tile_f1(1, tiles[1][1])
        tile_f2(0, tiles[0][1])
        for t in range(nt):
            if t + 3 < nt:
                tile_load(t + 3, tiles[t + 3][0], tiles[t + 3][1])
            if t + 2 < nt:
                tile_f1(t + 2, tiles[t + 2][1])
            if t + 1 < nt:
                tile_f2(t + 1, tiles[t + 1][1])
            tile_back(t, tiles[t][1], tiles[t][2], "act" if t % 2 else "dve")
            if extra is not None:
                extra(t)
            for job in jobs_ready.get(t, []):
                active.append(job)
            step_jobs(2 if len(active) > 1 else 1)
        while active:
            step_jobs(1)

    ext_tiles = col_blocks(0, NE, 128)
    tiles = [(x_ext[e0:e0 + m, :], m, HT[:, :, e0:e0 + m]) for (e0, m) in ext_tiles]
    jobs = {4 * kb + 4: [kv_job(HT[:, :, 16 + kb * 512:16 + (kb + 1) * 512], kb)] for kb in range(4)}
    run_tiles(tiles, jobs, extra=lambda t: [load_win_chunk(2 * t), load_win_chunk(2 * t + 1)] if t < 4 else None)
    phase_c0()
    tiles = []
    jobs = {}
    for rb in range(12):
        for t in range(4):
            r0 = rb * 512 + t * 128
            tiles.append((x_rest[r0:r0 + 128, :], 128, HTB[rb % 2][:, :, t * 128:(t + 1) * 128]))
        jobs[4 * rb + 3] = [kv_job(HTB[rb % 2], 4 + rb)]
    run_tiles(tiles, jobs)

    qitems = [(bk, e0, n, j) for bk, (e0, n) in enumerate(col_blocks(E_LO, E_HI)) for j in range(4)]

    def q_a(i):
        bk, e0, n, j = qitems[i]
        rp_ = ROPE[bk % 2]
        if j == 0:
            dma(rp_[:, 0:n], ropeq[0, :, e0:e0 + n])
            dma(rp_[:, 512:512 + n], ropeq[1, :, e0:e0 + n])
        qp = bank(6 + i % 2, n)
        for c in range(8):
            mm(qp, WIN[:, c, 1024 + j * 128:1024 + (j + 1) * 128], HT[:, c, e0:e0 + n], c == 0, c == 7)

    def q_n(i):
        bk, e0, n, j = qitems[i]
        rp_ = ROPE[bk % 2]
        normrope(bank(6 + i % 2, n), n, QKG[:, 0:1], rp_[:, 0:n], rp_[:, 512:512 + n], QT[:, j, e0:e0 + n], 4, 5)

    q_a(0)
    for i in range(len(qitems)):
        if i + 1 < len(qitems):
            q_a(i + 1)
        q_n(i)
    bi = 0
    for (e0, n) in col_blocks(0, NE):
        for ci in range(4):
            av = bank(2 * (bi % 2), n)
            ag = bank(2 * (bi % 2) + 1, n)
            bi += 1
            for c in range(8):
                mm(av, WIN[:, c, ci * 128:(ci + 1) * 128], HT[:, c, e0:e0 + n], c == 0, c == 7)
            for c in range(8):
                mm(ag, WIN[:, c, 512 + ci * 128:512 + (ci + 1) * 128], HT[:, c, e0:e0 + n], c == 0, c == 7)
            sg = TMP[3 + bi % 2][:, 0:n]
            act(sg, ag, AF.Sigmoid)
            tt("dve", AT[:, ci, e0:e0 + n], av, sg, ALU.mult)

    for ci in range(4):
        for k in range(31):
            ts("dve", DIAG[:, ci, k, :], IDENT, CONVP[:, ci * 34 + k:ci * 34 + k + 1], ALU.mult)
    for cbi, (e0, n) in enumerate(col_blocks(E_LO, E_HI)):
        sm = bank(4, n)
        sq_ = bank(5, n)
        cvb = [0, 1, 2, 3] if cbi % 2 == 0 else [6, 7, 2, 3]
        for ci in range(4):
            cv = bank(cvb[ci], n)
            for k in range(31):
                mm(cv, DIAG[:, ci, k, :], AT[:, ci, e0 + k - 15:e0 + k - 15 + n], k == 0, k == 30)
            bcol = CONVP[:, ci * 34 + 31:ci * 34 + 32]
            cb = TMPB[ci % 2][:, 0:n]
            cs = TMPB[2 + ci % 2][:, 0:n]
            act(cb, cv, AF.Identity, bias=bcol)
            act(cs, cv, AF.Square, bias=bcol)
            mm(sm, ONES, cb, ci == 0, ci == 3)
            mm(sq_, ONES, cs, ci == 0, ci == 3)
        mean = TMP[0][:, 0:n]
        ts("dve", mean, sm, 1.0 / 512.0, ALU.mult)
        msq = TMP[1][:, 0:n]
        tt("dve", msq, mean, mean, ALU.mult)
        var = TMP[2][:, 0:n]
        stt("dve", var, sq_, 1.0 / 512.0, msq, ALU.mult, ALU.subtract)
        act(var, var, AF.Ln, bias=EPSB[:, 0:1], scale=1.0)
        act(var, var, AF.Exp, scale=-0.5)
        for ci in range(4):
            cv = bank(cvb[ci], n)
            bcol = CONVP[:, ci * 34 + 31:ci * 34 + 32]
            t1 = TMP[3 + ci % 2][:, 0:n]
            stt("dve", t1, cv, bcol, mean, ALU.add, ALU.subtract)
            tt("pool", t1, t1, var, ALU.mult)
            act(HT[:, ci, e0:e0 + n], t1, AF.Silu, bias=CONVP[:, ci * 34 + 33:ci * 34 + 34],
                scale=CONVP[:, ci * 34 + 32:ci * 34 + 33])

    def wout_rows(c):
        if c < 4:
            return [(c * 128, 128, 0)]
        j = c - 4
        return [(512 + 64 * j, 64, 0), (512 + 64 * (4 + j), 64, 64)]
    load_weight_rows(WOUT, w_out, 1024, None, rows_of_chunk=wout_rows)
    dma(GB[:, :], gpost[0:1, :].partition_broadcast(128))
    load_weight_rows(WMQ, w_mem_q, 1024, 1)
    load_weight_rows(WMO, w_mem_o, 1024, None)

    PTX = [WREG[:, i * 2048:(i + 1) * 2048] for i in range(2)]
    PTY = [WREG[:, 4096 + i * 1024:4096 + (i + 1) * 1024] for i in range(2)]
    groups = [(j, e0, n) for j in range(4) for (e0, n) in col_blocks(E_LO, E_HI)]
    batches = []
    nxy = {"X": 0, "Y": 0}
    for gi in range(len(groups)):
        kt = 0
        turn = "X"
        while kt < 64:
            if turn == "X" and kt + 2 <= 64:
                kts = [kt, kt + 1]
                kind = "X"
            else:
                kts = [kt]
                kind = "Y"
            kt += len(kts)
            batches.append((gi, kts, kind, nxy[kind], kt == 64))
            nxy[kind] += 1
            turn = "Y" if turn == "X" else "X"

    def sbank(kind, i, h):
        return (2 * i + h) if kind == "X" else (4 + h)

    def emit_qk(bn):
        gi, kts, kind, ser, last = batches[bn]
        j, e0, n = groups[gi]
        for i, kt in enumerate(kts):
            for h in range(2):
                mm(bank(sbank(kind, i, h), n), KT[64 * h:64 * h + 64, kt * 128:(kt + 1) * 128],
                   QT[64 * h:64 * h + 64, j, e0:e0 + n], True, True)

    def emit_exp(bn):
        gi, kts, kind, ser, last = batches[bn]
        j, e0, n = groups[gi]
        nit = 2 * len(kts)
        b0 = 0 if kind == "X" else 4
        pt = (PTX if kind == "X" else PTY)[ser % 2]
        sv = PS[:, 512 * b0:512 * (b0 + nit)].rearrange("p (b t) -> p b t", b=nit)[:, :, 0:n]
        pv = pt[:, 0:512 * nit].rearrange("p (b t) -> p b t", b=nit)[:, :, 0:n]
        act(pv, sv, AF.Exp, scale=0.125)

    def emit_pv(bn):
        gi, kts, kind, ser, last = batches[bn]
        j, e0, n = groups[gi]
        pt = (PTX if kind == "X" else PTY)[ser % 2]
        for i, kt in enumerate(kts):
            for h in range(2):
                it = 2 * i + h
                mm(bank(6 + h, n)[0:65, :], VV[:, kt, h, 0:65], pt[:, 512 * it:512 * it + n], kt == 0, kt == 63)
        if last:
            for h in range(2):
                cp("dve", TMP[h][0:65, 0:n], bank(6 + h, n)[0:65, :])
            for h in range(2):
                recip(TMP[2 + h][64:65, 0:n], TMP[h][64:65, 0:n])
            for h in range(2):
                osb = TMP[h][0:65, 0:n]
                rd = TMP[2 + h][64:65, 0:n]
                rdb = TMP[4 + h][0:64, 0:n]
                row = 2 * gi + h
                dma(rds[row:row + 1, 0:n], rd)
                dma(rdb, rds[row:row + 1, 0:n].partition_broadcast(64))
                if h == 0:
                    tt("dve", HT[0:64, 4 + j, e0:e0 + n], osb[0:64, :], rdb, ALU.mult)
                else:
                    a1_ = ATT1[0:64, 0:n]
                    tt("dve", a1_, osb[0:64, :], rdb, ALU.mult)
                    dma(HT[64:128, 4 + j, e0:e0 + n], a1_)

    pend = {"X": 0, "Y": 0}
    nq = [0]

    def try_qk():
        while nq[0] < len(batches) and pend[batches[nq[0]][2]] == 0:
            emit_qk(nq[0])
            pend[batches[nq[0]][2]] += 1
            nq[0] += 1

    try_qk()
    for bn in range(len(batches)):
        emit_exp(bn)
        pend[batches[bn][2]] -= 1
        try_qk()
        emit_pv(bn)

    def outproj_phase(tiles, nk, lhs_fn, rhs_fn, xsrc_fn, xdst_fn, hdst_fn):
        def ybuf(t):
            m = tiles[t][1]
            return PS[0:m, 1024 * (t % 2):1024 * (t % 2) + 1024]

        def st_a(t):
            y = ybuf(t)
            dma(XT[t % 2][0:tiles[t][1], :], xsrc_fn(t))
            for hf in range(2):
                for c in range(nk):
                    mm(y[:, hf * 512:(hf + 1) * 512], lhs_fn(t, c), rhs_fn(c, hf), c == 0, c == nk - 1)

        def st_b1(t):
            m = tiles[t][1]
            y = ybuf(t)
            xt = XT[t % 2]
            ss = stat1(m)
            yt = YT[t % 2]
            act(yt[0:m, :], y, AF.Square, accum=ss)
            rs = stat1(m)
            rsqrt_from(rs, ss, 1.0 / 1024.0, m)
            stt("dve", yt[0:m, :], y, rs, GB[0:m, :], ALU.mult, ALU.mult)
            tt("pool", yt[0:m, :], yt[0:m, :], xt[0:m, :], ALU.add)
            dma(xdst_fn(t), yt[0:m, :])

        def st_b2(t):
            if hdst_fn is None:
                return
            m = tiles[t][1]
            yt = YT[t % 2]
            hb = HB[t % 2]
            ss = stat1(m)
            act(hb[0:m, :], yt[0:m, :], AF.Square, accum=ss)
            rs = stat1(m)
            rsqrt_from(rs, ss, 1.0 / 1024.0, m)
            ts("dve", hb[0:m, :], yt[0:m, :], rs, ALU.mult)

        def st_c(t):
            if hdst_fn is None:
                return
            tile_back(t, tiles[t][1], hdst_fn(t), "act" if t % 2 else "dve")

        stages = [st_a, st_b1, st_b2, st_c]
        for s_ in range(len(tiles) + len(stages) - 1):
            for k, st in enumerate(stages):
                t = s_ - k
                if 0 <= t < len(tiles):
                    st(t)

    tok_tiles = col_blocks(E_LO, E_HI, 128)

    def ht_tile(t):
        e0, m = tok_tiles[t]
        return HT[:, :, e0:e0 + m]

    outproj_phase(tok_tiles, 8,
                  lambda t, c: HT[:, c, tok_tiles[t][0]:tok_tiles[t][0] + tok_tiles[t][1]],
                  lambda c, hf: WOUT[:, c, hf * 512:(hf + 1) * 512],
                  lambda t: x_ext[tok_tiles[t][0]:tok_tiles[t][0] + tok_tiles[t][1], :],
                  lambda t: xs1[tok_tiles[t][0]:tok_tiles[t][0] + tok_tiles[t][1], :],
                  ht_tile)

    fgroups = [(f0, min(4, NFC - f0)) for f0 in range(0, NFC, 4)]
    fblocks = []
    o0 = 16
    while o0 < 2064:
        no = min(510, 2064 - o0)
        fblocks.append((o0, no))
        o0 += no

    def load_wdown(chunks, dst_fn):
        for f in chunks:
            wpiece(dst_fn(f), [(w_down[f * 128:(f + 1) * 128, :], 0, 128, 0, 1024)], 1024, None)

    def load_wup_group(gi):
        f0, nf = fgroups[gi]
        wu = WUG0 if gi == 0 else WU[gi % 2]
        for c in range(8):
            ncol = nf * 128
            srcs = [(w_up[c * 128:(c + 1) * 128, f0 * 128:f0 * 128 + ncol], 0, 128, 0, ncol),
                    (w_up[c * 128:(c + 1) * 128, DFF + f0 * 128:DFF + f0 * 128 + ncol], 0, 128, 512, ncol)]
            sc = GFM[:, 16 + c:16 + c + 1]
            if nf == 4:
                wpiece(wu[:, c, :], srcs, 1024, sc)
            else:
                slot_view = lambda slot: slot[:, :].rearrange("p (a t) -> p a t", a=2)[:, :, 0:ncol]
                wpiece(wu[:, c, :].rearrange("p (a t) -> p a t", a=2)[:, :, 0:ncol], srcs, 1024, sc, in_view=slot_view)

    dma(GB[:, :], gpost[1:2, :].partition_broadcast(128))
    load_wup_group(0)
    qi = 0
    for (e0, n) in col_blocks(E_LO, E_HI):
        for oc in range(8):
            qb = bank(qi % 2, n)
            qi += 1
            for c in range(8):
                mm(qb, WMQ[:, c, oc * 128:(oc + 1) * 128], HT[:, c, e0:e0 + n], c == 0, c == 7)
            cp("act", BQ[:, oc, e0:e0 + n], qb)
        for hd in range(4):
            for mt in range(2):
                sbk = bank(2 + mt, n)
                for dc in range(2):
                    mm(sbk, KM[:, 2 * hd + dc, mt * 128:(mt + 1) * 128], BQ[:, 2 * hd + dc, e0:e0 + n], dc == 0, dc == 1)
            pt = PTX[hd % 2]
            sv = PS[:, 1024:2048].rearrange("p (b t) -> p b t", b=2)[:, :, 0:n]
            pv = pt[:, 0:1024].rearrange("p (b t) -> p b t", b=2)[:, :, 0:n]
            act(pv, sv, AF.Exp, scale=1.0 / 16.0)
            den = bank(4, n)
            for mt in range(2):
                mm(den, ONES, pt[:, 512 * mt:512 * mt + n], mt == 0, mt == 1)
            rd = TMP[hd % 2][:, 0:n]
            act(rd, den, AF.Ln)
            act(rd, rd, AF.Exp, scale=-1.0)
            for dc in range(2):
                ob = bank(5 + dc, n)
                for mt in range(2):
                    mm(ob, VM[:, mt, (2 * hd + dc) * 128:(2 * hd + dc + 1) * 128], pt[:, 512 * mt:512 * mt + n], mt == 0, mt == 1)
                tt("dve", HT[:, 2 * hd + dc, e0:e0 + n], ob, rd, ALU.mult)
    outproj_phase(tok_tiles, 8,
                  lambda t, c: HT[:, c, tok_tiles[t][0]:tok_tiles[t][0] + tok_tiles[t][1]],
                  lambda c, hf: WMO[:, c, hf * 512:(hf + 1) * 512],
                  lambda t: xs1[tok_tiles[t][0]:tok_tiles[t][0] + tok_tiles[t][1], :],
                  lambda t: xs2[tok_tiles[t][0]:tok_tiles[t][0] + tok_tiles[t][1], :],
                  ht_tile)
    ts("dve", HT[:, :, 15:16], HT[:, :, 15:16], MASK[:, 0:1], ALU.mult)
    ts("dve", HT[:, :, 2064:2065], HT[:, :, 2064:2065], MASK[:, 1:2], ALU.mult)

    dma(GB[:, :], gpost[2:3, :].partition_broadcast(128))
    fitems = []
    for gi, (f0, nf) in enumerate(fgroups):
        for fl in range(nf):
            for bi_, (o0, no) in enumerate(fblocks):
                fitems.append((gi, fl, f0 + fl, o0, no, fl == 0 and bi_ == 0))

    def f_a(i):
        gi, fl, f, o0, no, first = fitems[i]
        if first and gi + 1 < len(fgroups):
            load_wup_group(gi + 1)
        if first and gi == len(fgroups) - 1:
            load_wdown(range(0, 6), lambda f_: WDA[:, f_ * 1024:(f_ + 1) * 1024])
        wu = WUG0 if gi == 0 else WU[gi % 2]
        ug = bank(2 * (i % 3), no + 2)
        uv = bank(2 * (i % 3) + 1, no + 2)
        for c in range(8):
            mm(ug, wu[:, c, fl * 128:(fl + 1) * 128], HT[:, c, o0 - 1:o0 + no + 1], c == 0, c == 7)
        for c in range(8):
            mm(uv, wu[:, c, 512 + fl * 128:512 + (fl + 1) * 128], HT[:, c, o0 - 1:o0 + no + 1], c == 0, c == 7)

    def f_b(i):
        gi, fl, f, o0, no, first = fitems[i]
        pg = FFNP[:, f * 4:f * 4 + 4]
        pv_ = FFNP[:, (NFC + f) * 4:(NFC + f) * 4 + 4]
        ug = bank(2 * (i % 3), no + 2)
        uv = bank(2 * (i % 3) + 1, no + 2)
        tg = TMP[i % 3][:, 0:no]
        tv = TMP[3 + i % 3][:, 0:no]
        act(tg, ug[:, 1:1 + no], AF.Identity, bias=pg[:, 3:4], scale=pg[:, 1:2])
        act(tv, uv[:, 1:1 + no], AF.Identity, bias=pv_[:, 3:4], scale=pv_[:, 1:2])
        stt("dve", tg, ug[:, 0:no], pg[:, 0:1], tg, ALU.mult, ALU.add)
        stt("dve", tg, ug[:, 2:2 + no], pg[:, 2:3], tg, ALU.mult, ALU.add)
        stt("dve", tv, uv[:, 0:no], pv_[:, 0:1], tv, ALU.mult, ALU.add)
        stt("dve", tv, uv[:, 2:2 + no], pv_[:, 2:3], tv, ALU.mult, ALU.add)

    def f_c(i):
        gi, fl, f, o0, no, first = fitems[i]
        tg = TMP[i % 3][:, 0:no]
        tv = TMP[3 + i % 3][:, 0:no]
        act(tg, tg, AF.Gelu_apprx_tanh)
        tt("pool", GT[:, f, o0 - 16:o0 - 16 + no], tg, tv, ALU.mult)

    fst = [f_a, f_b, f_c]
    for s_ in range(len(fitems) + 2):
        for k, st in enumerate(fst):
            t = s_ - k
            if 0 <= t < len(fitems):
                st(t)
    WD1 = v3(BIG8[:, 0:16384], 16)
    load_wdown(range(6, NFC), lambda f: WD1[:, f - 6, :])

    def wd(f, hf):
        if f < 6:
            return WDA[:, f * 1024 + hf * 512:f * 1024 + (hf + 1) * 512]
        return WD1[:, f - 6, hf * 512:(hf + 1) * 512]

    ffn_tiles = [(16 + ti * 128, 128) for ti in range(16)]
    outproj_phase(ffn_tiles, NFC,
                  lambda t, c: GT[:, c, t * 128:(t + 1) * 128],
                  wd,
                  lambda t: xs2[16 + t * 128:16 + (t + 1) * 128, :],
                  lambda t: outd[t * 128:(t + 1) * 128, :],
                  None)

    S.emit(es)
    es.close()
    return nc


def _rope_tables(pos):
    pos = np.asarray(pos)
    inv = (10000.0 ** (-(np.arange(0, 32, 2, dtype=np.float32)) / np.float32(32))).astype(np.float32)
    r = (pos // 64).astype(np.float32)
    c = (pos % 64).astype(np.float32)
    ang_r = r[None, :] * inv[:, None]
    ang_c = c[None, :] * inv[:, None]
    ang = np.concatenate([ang_r, ang_r, ang_c, ang_c], axis=0).astype(np.float32)
    ang = np.concatenate([ang, ang], axis=0)
    return np.stack([np.cos(ang), np.sin(ang)]).astype(np.float32)


def _consts():
    ident = np.eye(128, dtype=np.float32)
    perm = np.zeros((128, 128), np.float32)
    for m in range(128):
        if (m % 32) < 16:
            perm[m + 16, m] = -1.0
        else:
            perm[m - 16, m] = 1.0
    bones = np.zeros((128, 128), np.float32)
    bones[:64, :64] = 1.0
    bones[64:, 64:] = 1.0
    ones = np.ones((128, 128), np.float32)
    return np.ascontiguousarray(np.concatenate([ident, perm, bones, ones], axis=1))


_NC_CACHE = {}


def kernel(x, mem, norm_mix_pre, w_in, conv_dw, conv_dw_b, conv_ln_g, conv_ln_b,
           q_norm_g, k_norm_g, w_out, norm_mix_post, norm_mem_pre, mem_norm_g,
           w_mem_q, w_mem_kv, w_mem_o, norm_mem_post, norm_ffn_pre, w_up, ffn_dw,
           ffn_dw_b, w_down, norm_ffn_post):
    f = lambda a: np.ascontiguousarray(np.asarray(a, dtype=np.float32))
    x = f(x); mem = f(mem)
    B, Sq, D = x.shape

    def fm(g):
        return f(g).reshape(8, 128).T
    gfm = np.ascontiguousarray(np.concatenate([fm(norm_mix_pre[0]), fm(norm_mem_pre[0]), fm(norm_ffn_pre[0]), fm(mem_norm_g[0])], axis=1))
    gpost = np.ascontiguousarray(np.stack([f(norm_mix_post[0]), f(norm_mem_post[0]), f(norm_ffn_post[0])]))
    cw = f(conv_dw[0])
    convp = np.zeros((128, 4, 34), np.float32)
    for ci in range(4):
        convp[:, ci, 0:31] = cw[:, ci * 128:(ci + 1) * 128].T
        convp[:, ci, 31] = f(conv_dw_b[0])[ci * 128:(ci + 1) * 128]
        convp[:, ci, 32] = f(conv_ln_g[0])[ci * 128:(ci + 1) * 128]
        convp[:, ci, 33] = f(conv_ln_b[0])[ci * 128:(ci + 1) * 128]
    convp = np.ascontiguousarray(convp.reshape(128, 136))
    qkg = np.ascontiguousarray(np.stack([np.tile(f(q_norm_g[0]), 2), np.tile(f(k_norm_g[0]), 2)], axis=1))
    fw = f(ffn_dw[0])
    fb = f(ffn_dw_b[0])
    ffnp = np.zeros((128, 44, 4), np.float32)
    for fc in range(44):
        ffnp[:, fc, 0:3] = fw[:, fc * 128:(fc + 1) * 128].T
        ffnp[:, fc, 3] = fb[fc * 128:(fc + 1) * 128]
    ffnp = np.ascontiguousarray(ffnp.reshape(128, 176))
    cst = _consts()
    shared = dict(w_in=f(w_in[0]), w_out=f(w_out[0]), w_mem_q=f(w_mem_q[0]), w_mem_kv=f(w_mem_kv[0]),
                  w_mem_o=f(w_mem_o[0]), w_up=f(w_up[0]), w_down=f(w_down[0]), gfm=gfm, gpost=gpost,
                  convp=convp, qkg=qkg, ffnp=ffnp, cst=cst)
    in_maps = []
    for core in range(8):
        b, j = core // 4, core % 4
        s = j * 2048
        xe = np.zeros((NE, 1024), np.float32)
        lo, hi = max(0, s - 16), min(Sq, s + 2064)
        xe[lo - (s - 16):hi - (s - 16)] = x[b, lo:hi]
        rest_idx = np.concatenate([np.arange(0, s), np.arange(s + 2048, Sq)])
        xr = np.ascontiguousarray(x[b, rest_idx])
        key_pos = np.concatenate([np.arange(s, s + 2048), rest_idx])
        ropek = _rope_tables(key_pos)
        ext_pos = np.clip(np.arange(s - 16, s - 16 + NE), 0, Sq - 1)
        ropeq = _rope_tables(ext_pos)
        mask = np.ones((128, 2), np.float32)
        if j == 0:
            mask[:, 0] = 0.0
        if j == 3:
            mask[:, 1] = 0.0
        m = dict(shared)
        m.update(x_ext=xe, x_rest=xr, mem=np.ascontiguousarray(mem[b]), ropek=ropek, ropeq=ropeq, mask=mask)
        in_maps.append(m)
    if "nc" not in _NC_CACHE:
        _NC_CACHE["nc"] = build_nc()
    nc = _NC_CACHE["nc"]
    res = run_bass_kernel_spmd(nc, in_maps, core_ids=list(range(8)))
    out = np.zeros((B, Sq, D), np.float32)
    for core in range(8):
        b, j = core // 4, core % 4
        out[b, j * 2048:(j + 1) * 2048] = np.asarray(res.results[core]["out"], dtype=np.float32)
    return out
```

```python
import numpy as np
from contextlib import ExitStack
import concourse.bass as bass
import concourse.mybir as mybir
from concourse.bass_utils import run_bass_kernel_spmd

F32 = mybir.dt.float32
BF16 = mybir.dt.bfloat16
AF = mybir.ActivationFunctionType
ALU = mybir.AluOpType

EPS = 1e-6
NE = 2080
E_LO, E_HI = 15, 2065
DFF = 2816
NFC = 22


def _esize(dt):
    return 2 if dt == BF16 else 4


class Sched:
    ENGS = ["pe", "act", "dve", "pool", "sp"]
    NDMA = 16

    def __init__(self, nc, tracked_dram=()):
        self.nc = nc
        self.ops = []
        self.w = {}
        self.r = {}
        self.tracked_dram = set(tracked_dram)

    def region(self, ap):
        t = ap.tensor
        name = t.name
        space = str(ap.space)
        if "DRAM" in space.upper() or "HBM" in space.upper() or type(t).__name__.startswith("DRam"):
            if name not in self.tracked_dram:
                return None
        es = _esize(ap.dtype)
        dims = ap.ap
        ps, pc = dims[0]
        off = int(ap.offset)
        if ps == 0:
            rs_ = int(t.shape[-1])
            p0, f0 = off // rs_, off % rs_
            p1 = p0 + 1
        else:
            p0 = off // ps
            f0 = off % ps
            p1 = p0 + pc
        ents = [(f0, 0)]
        rest = dims[1:]
        for (s, c) in rest[:-1]:
            s = abs(s)
            if s != 0 and len(ents) * c <= 64:
                ents = [(st + i * s, ex) for (st, ex) in ents for i in range(c)]
            else:
                ents = [(st, ex + (c - 1) * s) for (st, ex) in ents]
        if rest:
            s, c = rest[-1]
            ents = [(st, ex + (c - 1) * abs(s) + 1) for (st, ex) in ents]
        else:
            ents = [(st, ex + 1) for (st, ex) in ents]
        ivs = tuple(sorted((st * es, (st + ex) * es) for (st, ex) in ents))
        return (name, p0, p1, ivs)

    @staticmethod
    def _ov(a, b):
        if a[1] >= b[2] or b[1] >= a[2]:
            return False
        for (s0, e0) in a[3]:
            for (s1, e1) in b[3]:
                if s0 < e1 and s1 < e0:
                    return True
        return False

    @staticmethod
    def _covers(a, b):
        if len(a[3]) != 1:
            return a[1] <= b[1] and a[2] >= b[2] and a[3] == b[3]
        if a[1] > b[1] or a[2] < b[2]:
            return False
        s0, e0 = a[3][0]
        return all(s0 <= s1 and e1 <= e0 for (s1, e1) in b[3])

    def add(self, eng, fn, reads=(), writes=(), dma=False):
        op = {"eng": eng, "fn": fn, "deps": set(), "idx": len(self.ops), "inc": False, "dma": dma}
        for ap in reads:
            rg = self.region(ap)
            if rg is None:
                continue
            for key, d in self.w.get(rg[0], {}).items():
                if self._ov(rg, key):
                    op["deps"].update(d.values())
            self.r.setdefault(rg[0], {}).setdefault(rg, {})[eng] = op["idx"]
        for ap in writes:
            rg = self.region(ap)
            if rg is None:
                continue
            for table in (self.w, self.r):
                tb = table.get(rg[0], {})
                dead = []
                for key, d in tb.items():
                    if self._ov(rg, key):
                        op["deps"].update(d.values())
                        if self._covers(rg, key):
                            dead.append(key)
                for k in dead:
                    del tb[k]
            self.w.setdefault(rg[0], {})[rg] = {eng: op["idx"]}
        op["deps"].discard(op["idx"])
        self.ops.append(op)
        return op

    def emit(self, es):
        nc = self.nc
        ops = self.ops
        for op in ops:
            op["deps"] = {d for d in op["deps"] if not (op["eng"] == "pe" and ops[d]["eng"] == "pe" and not ops[d]["dma"])}
            latest = {}
            keep = set()
            for d in op["deps"]:
                p = ops[d]
                if p["dma"]:
                    keep.add(d)
                else:
                    latest[p["eng"]] = max(latest.get(p["eng"], -1), d)
            op["deps"] = keep | set(latest.values())
            for d in op["deps"]:
                ops[d]["inc"] = True
        esem = {e: es.enter_context(nc.semaphore("sem_" + e)) for e in self.ENGS}
        dsem = [es.enter_context(nc.semaphore("dsem%d" % i)) for i in range(self.NDMA)]
        cnt = {e: 0 for e in self.ENGS}
        ndma = 0
        last_out_dma = []
        for op in ops:
            if op["dma"]:
                op["dsem"] = ndma % self.NDMA
                op["dcnt"] = 16 * (ndma // self.NDMA + 1)
                ndma += 1
            elif op["inc"]:
                cnt[op["eng"]] += 1
                op["cnt"] = cnt[op["eng"]]
        self.ndma = ndma
        block = es.enter_context(nc.Block())

        def stream(engname, e):
            seen = {}

            def wait(sem, key, val):
                if seen.get(key, 0) >= val:
                    return
                seen[key] = val
                e.wait_ge(sem, val)

            for op in ops:
                if op["eng"] != engname:
                    continue
                need = {}
                for d in op["deps"]:
                    p = ops[d]
                    if p["dma"]:
                        k = ("d", p["dsem"])
                        need[k] = max(need.get(k, 0), p["dcnt"])
                    else:
                        k = ("e", p["eng"])
                        need[k] = max(need.get(k, 0), p["cnt"])
                if op["dma"] and op["dcnt"] > 16:
                    k = ("d", op["dsem"])
                    need[k] = max(need.get(k, 0), op["dcnt"] - 16)
                for k, v in need.items():
                    wait(dsem[k[1]] if k[0] == "d" else esem[k[1]], k, v)
                ins = op["fn"](e)
                if op["dma"]:
                    ins.then_inc(dsem[op["dsem"]], 16)
                elif op["inc"]:
                    ins.then_inc(esem[op["eng"]], 1)
            if engname == "sp":
                for i in range(min(self.NDMA, ndma)):
                    n_i = (ndma - 1 - i) // self.NDMA + 1
                    wait(dsem[i], ("d", i), 16 * n_i)

        @block.tensor
        def _(e):
            stream("pe", e)

        @block.scalar
        def _(e):
            stream("act", e)

        @block.vector
        def _(e):
            stream("dve", e)

        @block.gpsimd
        def _(e):
            stream("pool", e)

        @block.sync
        def _(e):
            stream("sp", e)


def col_blocks(lo, hi, w=512):
    out = []
    while lo < hi:
        n = min(w, hi - lo)
        out.append((lo, n))
        lo += n
    return out


def build_nc(debug=False):
    nc = bass.Bass("TRN2", target_bir_lowering=False)
    es = ExitStack()

    def di(name, shape, dt=F32):
        return nc.dram_tensor(name, shape, dt, kind="ExternalInput").ap()

    x_ext = di("x_ext", [NE, 1024])
    x_rest = di("x_rest", [6144, 1024])
    memd = di("mem", [256, 1024])
    w_in = di("w_in", [1024, 1792])
    w_out = di("w_out", [1024, 1024])
    w_mem_q = di("w_mem_q", [1024, 1024])
    w_mem_kv = di("w_mem_kv", [1024, 2048])
    w_mem_o = di("w_mem_o", [1024, 1024])
    w_up = di("w_up", [1024, 2 * DFF])
    w_down = di("w_down", [DFF, 1024])
    gfm = di("gfm", [128, 4 * 8])
    gpost = di("gpost", [3, 1024])
    convp = di("convp", [128, 4 * 34])
    qkg = di("qkg", [128, 2])
    ffnp = di("ffnp", [128, 44 * 4])
    cst = di("cst", [128, 512])
    ropek = di("ropek", [2, 128, 8192])
    ropeq = di("ropeq", [2, 128, NE])
    maskd = di("mask", [128, 2])
    outd = nc.dram_tensor("out", [2048, 1024], F32, kind="ExternalOutput").ap()
    xs1 = nc.dram_tensor("xs1", [NE, 1024], F32, kind="Internal").ap()
    xs2 = nc.dram_tensor("xs2", [NE, 1024], F32, kind="Internal").ap()
    rds = nc.dram_tensor("rds", [64, 512], F32, kind="Internal").ap()
    dbg = {}

    S = Sched(nc, tracked_dram=["xs1", "xs2", "out", "rds"])

    def sb(name, shape, dt=F32):
        return es.enter_context(nc.sbuf_tensor(name, shape, dt))

    BIG8 = sb("BIG8", [128, 8 * NE], BF16)
    BIGQ = sb("BIGQ", [128, 8 * NE], BF16)
    G = sb("G", [128, NFC * 2048], BF16)
    STG = [sb("STG%d" % i, [128, 1024]) for i in range(2)]
    XT = [sb("XT%d" % i, [128, 1024]) for i in range(2)]
    YT = [sb("YT%d" % i, [128, 1024]) for i in range(2)]
    HB = [sb("HB%d" % i, [128, 1024], BF16) for i in range(2)]
    TMP = [sb("TMP%d" % i, [128, 512]) for i in range(6)]
    TMPB = [sb("TMPB%d" % i, [128, 512], BF16) for i in range(4)]
    GB = sb("GB", [128, 1024])
    CST = sb("CST", [128, 512], BF16)
    ONESF = sb("ONESF", [128, 64])
    GFM = sb("GFM", [128, 32])
    CONVP = sb("CONVP", [128, 4 * 34])
    QKG = sb("QKG", [128, 2])
    FFNP = sb("FFNP", [128, 44 * 4])
    MASK = sb("MASK", [128, 2])
    STAT = sb("STAT", [128, 32])
    PS = es.enter_context(nc.psum_tensor("PS", [128, 4096], F32))

    IDENT = CST[:, 0:128]
    PERM = CST[:, 128:256]
    BONES = CST[:, 256:384]
    ONES = CST[:, 384:512]

    def bank(b, n=512):
        return PS[:, 512 * b:512 * b + n]

    def bankbf(b):
        return PS[:, 512 * b:512 * b + 512].bitcast(BF16)

    def v3(ap2d, c):
        return ap2d.rearrange("p (c t) -> p c t", c=c)

    HT = v3(BIG8[:, :], 8)
    BQ = v3(BIGQ[:, :], 8)
    AT = BQ[:, 0:4, :]
    QT = BQ[:, 4:8, :]
    KT = G[:, 0:8192]
    VV = G[:, 8192:8192 + 64 * 130].rearrange("p (t g d) -> p t g d", t=64, g=2)
    WREG = G[:, 16512:16512 + 15872]
    WIN = v3(WREG[:, 0:8 * 1792], 8)
    DIAG = WREG[:, 0:15872].rearrange("p (c k m) -> p c k m", c=4, k=31)
    KM = v3(G[:, 32384:34432], 8)
    VM = v3(G[:, 34432:36480], 2)
    MEMT = v3(G[:, 36480:38528], 8)
    WKV = v3(BIGQ[:, 0:16384], 8)
    HTB = [v3(BIGQ[:, i * 4096:(i + 1) * 4096], 8) for i in range(2)]
    WOUT = v3(BIGQ[:, 0:8192], 8)
    WMQ = v3(G[:, 16512 + 6144:16512 + 14336], 8)
    WMO = v3(G[:, 36480:44672], 8)
    WUG0 = v3(G[:, 8192:16384], 8)
    GT = v3(G[:, :], NFC)
    WU = [v3(BIGQ[:, i * 8192:(i + 1) * 8192], 8) for i in range(2)]
    PT = [WREG[:, i * 1536:(i + 1) * 1536] for i in range(4)]
    _rf = G[:, 38528:38528 + 4096].bitcast(F32)
    ROPE = [_rf[:, i * 1024:(i + 1) * 1024] for i in range(2)]
    WDA = BIGQ[:, 0:6144]
    ATT1 = TMPB[3]

    cntr = {"stg": 0, "xt": 0, "yt": 0, "hb": 0, "rope": 0, "tp": 0, "stat": 0}

    def rot(key, n):
        v = cntr[key]
        cntr[key] = (v + 1) % n
        return v

    def stat1(m):
        i = rot("stat", 32)
        return STAT[0:m, i:i + 1]

    def dma(out, in_, q="sp"):
        S.add(q, lambda e: e.dma_start(out=out, in_=in_), reads=[in_], writes=[out], dma=True)

    def mm(out, lhsT, rhs, start, stop):
        S.add("pe", lambda e: e.matmul(out, lhsT, rhs, start=start, stop=stop), reads=[lhsT, rhs], writes=[out])

    def transp(out, in_, ident):
        S.add("pe", lambda e: e.transpose(out, in_, ident), reads=[in_, ident], writes=[out])

    def act(out, in_, func, bias=None, scale=None, accum=None):
        kw = {}
        rd = [in_]
        wr = [out]
        if bias is not None:
            kw["bias"] = bias
            if not isinstance(bias, float):
                rd.append(bias)
        if scale is not None:
            kw["scale"] = scale
            if not isinstance(scale, float):
                rd.append(scale)
        if accum is not None:
            kw["accum_out"] = accum
            wr.append(accum)
        S.add("act", lambda e: e.activation(out=out, in_=in_, func=func, **kw), reads=rd, writes=wr)

    def tt(eng, out, in0, in1, op):
        S.add(eng, lambda e: e.tensor_tensor(out=out, in0=in0, in1=in1, op=op), reads=[in0, in1], writes=[out])

    def ts(eng, out, in0, s1, op0, s2=None, op1=None):
        rd = [in0] + [s for s in (s1, s2) if s is not None and not isinstance(s, float)]
        if op1 is None:
            S.add(eng, lambda e: e.tensor_scalar(out=out, in0=in0, scalar1=s1, scalar2=None, op0=op0), reads=rd, writes=[out])
        else:
            S.add(eng, lambda e: e.tensor_scalar(out=out, in0=in0, scalar1=s1, scalar2=s2, op0=op0, op1=op1), reads=rd, writes=[out])

    def stt(eng, out, in0, scalar, in1, op0, op1, accum=None):
        rd = [in0, in1] + ([] if isinstance(scalar, float) else [scalar])
        if accum is None:
            S.add(eng, lambda e: e.scalar_tensor_tensor(out=out, in0=in0, scalar=scalar, in1=in1, op0=op0, op1=op1), reads=rd, writes=[out])
        else:
            S.add(eng, lambda e: e.scalar_tensor_tensor(out=out, in0=in0, scalar=scalar, in1=in1, op0=op0, op1=op1, accum_out=accum),
                  reads=rd, writes=[out, accum])

    def cp(eng, out, in_):
        if eng == "act":
            act(out, in_, AF.Identity)
        else:
            S.add(eng, lambda e: e.tensor_copy(out=out, in_=in_), reads=[in_], writes=[out])

    def recip(out, in_):
        S.add("dve", lambda e: e.reciprocal(out=out, in_=in_), reads=[in_], writes=[out])

    def memset(eng, ap, val):
        S.add(eng, lambda e: e.memset(ap, val), writes=[ap])

    def rsqrt_from(out, in_, scale, m=None):
        act(out, in_, AF.Ln, bias=EPSB[0:out.shape[0], 0:1] if m is None else EPSB[0:m, 0:1], scale=scale)
        act(out, out, AF.Exp, scale=-0.5)

    def wpiece(dst, srcs, ncols, scal=None, eng="dve", in_view=None, defer=False):
        slot = STG[rot("stg", 2)]
        for (src, p0, p1, c0, nc_) in srcs:
            dma(slot[p0:p1, c0:c0 + nc_], src, q="pool")
        src_ap = slot[:, 0:ncols] if in_view is None else in_view(slot)

        def cast():
            if scal is None:
                cp(eng, dst, src_ap)
            elif eng == "act":
                act(dst, src_ap, AF.Identity, scale=scal)
            else:
                ts(eng, dst, src_ap, scal, ALU.mult)
        if defer:
            return cast
        cast()

    EPSB = sb("EPSB", [128, 1])
    memset("pool", EPSB[:, :], EPS)
    memset("pool", ONESF[:, :], 1.0)
    wpiece(CST[:, :], [(cst[:, :], 0, 128, 0, 512)], 512, None, eng="dve")
    dma(GFM[:, :], gfm[:, :])
    dma(CONVP[:, :], convp[:, :])
    dma(QKG[:, :], qkg[:, :])
    dma(FFNP[:, :], ffnp[:, :])
    dma(MASK[:, :], maskd[:, :])
    memset("pool", VV[:, :, :, 64:65], 1.0)

    def load_weight_rows(dst3, src, ncols_total, gcol, col_pieces=None, rows_of_chunk=None, nchunks=8):
        for c in range(nchunks):
            for (c0, ncol) in (col_pieces or col_blocks(0, ncols_total, 1024)):
                if rows_of_chunk is None:
                    srcs = [(src[c * 128:(c + 1) * 128, c0:c0 + ncol], 0, 128, 0, ncol)]
                else:
                    srcs = [(src[r0:r0 + nr, c0:c0 + ncol], p0, p0 + nr, 0, ncol) for (r0, nr, p0) in rows_of_chunk(c)]
                scal = None if gcol is None else GFM[:, gcol * 8 + c:gcol * 8 + c + 1]
                wpiece(dst3[:, c, c0:c0 + ncol], srcs, ncol, scal)

    def norm_to_T(src_tile, m, dst, evac_eng):
        ss = stat1(m)
        hb = HB[rot("hb", 2)]
        act(hb[0:m, :], src_tile, AF.Square, accum=ss)
        rs = stat1(m)
        rsqrt_from(rs, ss, 1.0 / 1024.0, m)
        ts("dve", hb[0:m, :], src_tile, rs, ALU.mult)
        tb = 6 + rot("tp", 2)
        tpv = v3(bankbf(tb), 8)
        for c in range(8):
            transp(tpv[:, c, 0:m], hb[0:m, c * 128:(c + 1) * 128], IDENT[0:m, 0:m])
        cp(evac_eng, dst, tpv[:, :, 0:m])

    def nr1(src, n, gain, b_ss, b_rot):
        xg = TMPB[0][:, 0:n]
        sq = TMPB[1][:, 0:n]
        act(xg, src, AF.Identity, scale=gain)
        act(sq, src, AF.Square)
        mm(bank(b_ss, n), BONES, sq, True, True)
        mm(bank(b_rot, n), PERM, xg, True, True)

    def nr2(n, cos, sin, b_ss, b_rot):
        xg = TMPB[0][:, 0:n]
        rs = TMP[0][:, 0:n]
        act(rs, bank(b_ss, n), AF.Ln, bias=EPSB[:, 0:1], scale=1.0 / 64.0)
        act(rs, rs, AF.Exp, scale=-0.5)
        tt("dve", TMP[1][:, 0:n], xg, cos, ALU.mult)
        tt("dve", TMP[2][:, 0:n], bank(b_rot, n), sin, ALU.mult)

    def nr3(n, dst):
        t1 = TMP[1][:, 0:n]
        tt("pool", t1, t1, TMP[2][:, 0:n], ALU.add)
        tt("pool", dst, t1, TMP[0][:, 0:n], ALU.mult)

    def normrope(src, n, gain, cos, sin, dst, b_ss, b_rot):
        nr1(src, n, gain, b_ss, b_rot)
        nr2(n, cos, sin, b_ss, b_rot)
        nr3(n, dst)

    def phase_c0():
      load_weight_rows(WKV, w_mem_kv, 2048, 3)
      for mt in range(2):
          xt = XT[rot("xt", 2)]
          dma(xt[:, :], memd[mt * 128:(mt + 1) * 128, :])
          norm_to_T(xt[:, :], 128, MEMT[:, :, mt * 128:(mt + 1) * 128], "dve")
      for oc in range(8):
          pb = bank(oc % 2, 256)
          for c in range(8):
              mm(pb, WKV[:, c, oc * 128:(oc + 1) * 128], MEMT[:, c, :], c == 0, c == 7)
          cp("act", KM[:, oc, :], pb)
      for mt in range(2):
          for hf in range(2):
              pb = bank(2 + (mt * 2 + hf) % 2)
              for c in range(8):
                  mm(pb, MEMT[:, c, mt * 128:(mt + 1) * 128], WKV[:, c, 1024 + hf * 512:1024 + (hf + 1) * 512], c == 0, c == 7)
              cp("dve", VM[:, mt, hf * 512:(hf + 1) * 512], pb)

    def load_win_chunk(c):
        sc = GFM[:, c:c + 1]
        wpiece(WIN[:, c, 0:1024], [(w_in[c * 128:(c + 1) * 128, 0:1024], 0, 128, 0, 1024)], 1024, sc)
        slot_view = lambda slot: slot[:, 0:512].rearrange("p (h j d) -> p j h d", h=2, j=4)
        wpiece(WIN[:, c, 1024:1536].rearrange("p (j h d) -> p j h d", j=4, h=2),
               [(w_in[c * 128:(c + 1) * 128, 1024:1792], 0, 128, 0, 768)], 512, sc, in_view=slot_view)
        last = STG[(cntr["stg"] + 1) % 2]
        ts("dve", WIN[:, c, 1536:1792], last[:, 512:768], sc, ALU.mult)

    XQ = [XT[0], XT[1], YT[0], YT[1]]

    def tile_load(t, src, m):
        dma(XQ[t % 4][0:m, :], src)

    JUNKA = sb("JUNKA", [128, 1024], BF16)
    JUNKD = sb("JUNKD", [128, 1024], BF16)
    tstat = {}

    def tile_f1(t, m):
        xt = XQ[t % 4]
        ss = stat1(m)
        tstat[t] = ss
        if t % 2 == 0:
            act(JUNKA[0:m, :], xt[0:m, :], AF.Square, accum=ss)
        else:
            stt("dve", JUNKD[0:m, :], xt[0:m, :], 1.0, xt[0:m, :], ALU.mult, ALU.mult, accum=ss)

    def tile_f2(t, m):
        xt = XQ[t % 4]
        hb = HB[t % 2]
        rs = stat1(m)
        rsqrt_from(rs, tstat[t], 1.0 / 1024.0, m)
        ts("dve", hb[0:m, :], xt[0:m, :], rs, ALU.mult)

    def tile_back(t, m, dst, evac_eng):
        hb = HB[t % 2]
        tpv = v3(bankbf(6 + t % 2), 8)
        for c in range(8):
            transp(tpv[:, c, 0:m], hb[0:m, c * 128:(c + 1) * 128], IDENT[0:m, 0:m])
        cp(evac_eng, dst, tpv[:, :, 0:m])

    def kv_job(hsrc, kb):
        i = kb
        rp_ = ROPE[i % 2]
        kp = bank(i % 2)
        vp = bank(2 + i % 2)

        def P():
            dma(rp_[:, 0:512], ropek[0, :, kb * 512:(kb + 1) * 512])
            dma(rp_[:, 512:1024], ropek[1, :, kb * 512:(kb + 1) * 512])
            for c in range(8):
                mm(kp, WIN[:, c, 1536:1664], hsrc[:, c, :], c == 0, c == 7)
            for t in range(4):
                for c in range(8):
                    mm(vp[:, t * 128:(t + 1) * 128], hsrc[:, c, t * 128:(t + 1) * 128], WIN[:, c, 1664:1792], c == 0, c == 7)

        def N1():
            nr1(kp, 512, QKG[:, 1:2], 4, 5)
            cp("act", VV[:, kb * 4:kb * 4 + 4, :, 0:64], vp.rearrange("p (t g d) -> p t g d", t=4, g=2))

        def N2():
            nr2(512, rp_[:, 0:512], rp_[:, 512:1024], 4, 5)

        def N3():
            nr3(512, KT[:, kb * 512:(kb + 1) * 512])
        return [P, N1, N2, N3]

    def run_tiles(tiles, jobs_ready, extra=None):
        active = []

        def step_jobs(k):
            for _ in range(k):
                if active:
                    active[0].pop(0)()
                    if not active[0]:
                        active.pop(0)
        for t0 in range(min(3, len(tiles))):
            tile_load(t0, tiles[t0][0], tiles[t0][1])
        nt = len(tiles)
        tile_f1(0, tiles[0][1])
        if nt > 1:
            tile_f1(1, tiles[1][1])
        tile_f2(0, tiles[0][1])
        for t in range(nt):
            if t + 3 < nt:
                tile_load(t + 3, tiles[t + 3][0], tiles[t + 3][1])
            if t + 2 < nt:
                tile_f1(t + 2, tiles[t + 2][1])
            if t + 1 < nt:
                tile_f2(t + 1, tiles[t + 1][1])
            tile_back(t, tiles[t][1], tiles[t][2], "act" if t % 2 else "dve")
            if extra is not None:
                extra(t)
            for job in jobs_ready.get(t, []):
                active.append(job)
            step_jobs(2 if len(active) > 1 else 1)
        while active:
            step_jobs(1)

    ext_tiles = col_blocks(0, NE, 128)
    tiles = [(x_ext[e0:e0 + m, :], m, HT[:, :, e0:e0 + m]) for (e0, m) in ext_tiles]
    jobs = {4 * kb + 4: [kv_job(HT[:, :, 16 + kb * 512:16 + (kb + 1) * 512], kb)] for kb in range(4)}
    run_tiles(tiles, jobs, extra=lambda t: [load_win_chunk(2 * t), load_win_chunk(2 * t + 1)] if t < 4 else None)
    phase_c0()
    tiles = []
    jobs = {}
    for rb in range(12):
        for t in range(4):
            r0 = rb * 512 + t * 128
            tiles.append((x_rest[r0:r0 + 128, :], 128, HTB[rb % 2][:, :, t * 128:(t + 1) * 128]))
        jobs[4 * rb + 3] = [kv_job(HTB[rb % 2], 4 + rb)]
    run_tiles(tiles, jobs)

    qitems = [(bk, e0, n, j) for bk, (e0, n) in enumerate(col_blocks(E_LO, E_HI)) for j in range(4)]

    def q_a(i):
        bk, e0, n, j = qitems[i]
        rp_ = ROPE[bk % 2]
        if j == 0:
            dma(rp_[:, 0:n], ropeq[0, :, e0:e0 + n])
            dma(rp_[:, 512:512 + n], ropeq[1, :, e0:e0 + n])
        qp = bank(6 + i % 2, n)
        for c in range(8):
            mm(qp, WIN[:, c, 1024 + j * 128:1024 + (j + 1) * 128], HT[:, c, e0:e0 + n], c == 0, c == 7)

    def q_n(i):
        bk, e0, n, j = qitems[i]
        rp_ = ROPE[bk % 2]
        normrope(bank(6 + i % 2, n), n, QKG[:, 0:1], rp_[:, 0:n], rp_[:, 512:512 + n], QT[:, j, e0:e0 + n], 4, 5)

    q_a(0)
    for i in range(len(qitems)):
        if i + 1 < len(qitems):
            q_a(i + 1)
        q_n(i)
    bi = 0
    for (e0, n) in col_blocks(0, NE):
        for ci in range(4):
            av = bank(2 * (bi % 2), n)
            ag = bank(2 * (bi % 2) + 1, n)
            bi += 1
            for c in range(8):
                mm(av, WIN[:, c, ci * 128:(ci + 1) * 128], HT[:, c, e0:e0 + n], c == 0, c == 7)
            for c in range(8):
                mm(ag, WIN[:, c, 512 + ci * 128:512 + (ci + 1) * 128], HT[:, c, e0:e0 + n], c == 0, c == 7)
            sg = TMP[3 + bi % 2][:, 0:n]
            act(sg, ag, AF.Sigmoid)
            tt("dve", AT[:, ci, e0:e0 + n], av, sg, ALU.mult)

    for ci in range(4):
        for k in range(31):
            ts("dve", DIAG[:, ci, k, :], IDENT, CONVP[:, ci * 34 + k:ci * 34 + k + 1], ALU.mult)
    for cbi, (e0, n) in enumerate(col_blocks(E_LO, E_HI)):
        sm = bank(4, n)
        sq_ = bank(5, n)
        cvb = [0, 1, 2, 3] if cbi % 2 == 0 else [6, 7, 2, 3]
        for ci in range(4):
            cv = bank(cvb[ci], n)
            for k in range(31):
                mm(cv, DIAG[:, ci, k, :], AT[:, ci, e0 + k - 15:e0 + k - 15 + n], k == 0, k == 30)
            bcol = CONVP[:, ci * 34 + 31:ci * 34 + 32]
            cb = TMPB[ci % 2][:, 0:n]
            cs = TMPB[2 + ci % 2][:, 0:n]
            act(cb, cv, AF.Identity, bias=bcol)
            act(cs, cv, AF.Square, bias=bcol)
            mm(sm, ONES, cb, ci == 0, ci == 3)
            mm(sq_, ONES, cs, ci == 0, ci == 3)
        mean = TMP[0][:, 0:n]
        ts("dve", mean, sm, 1.0 / 512.0, ALU.mult)
        msq = TMP[1][:, 0:n]
        tt("dve", msq, mean, mean, ALU.mult)
        var = TMP[2][:, 0:n]
        stt("dve", var, sq_, 1.0 / 512.0, msq, ALU.mult, ALU.subtract)
        act(var, var, AF.Ln, bias=EPSB[:, 0:1], scale=1.0)
        act(var, var, AF.Exp, scale=-0.5)
        for ci in range(4):
            cv = bank(cvb[ci], n)
            bcol = CONVP[:, ci * 34 + 31:ci * 34 + 32]
            t1 = TMP[3 + ci % 2][:, 0:n]
            stt("dve", t1, cv, bcol, mean, ALU.add, ALU.subtract)
            tt("pool", t1, t1, var, ALU.mult)
            act(HT[:, ci, e0:e0 + n], t1, AF.Silu, bias=CONVP[:, ci * 34 + 33:ci * 34 + 34],
                scale=CONVP[:, ci * 34 + 32:ci * 34 + 33])

    def wout_rows(c):
        if c < 4:
            return [(c * 128, 128, 0)]
        j = c - 4
        return [(512 + 64 * j, 64, 0), (512 + 64 * (4 + j), 64, 64)]
    load_weight_rows(WOUT, w_out, 1024, None, rows_of_chunk=wout_rows)
    dma(GB[:, :], gpost[0:1, :].partition_broadcast(128))
    load_weight_rows(WMQ, w_mem_q, 1024, 1)
    load_weight_rows(WMO, w_mem_o, 1024, None)

    PTX = [WREG[:, i * 2048:(i + 1) * 2048] for i in range(2)]
    PTY = [WREG[:, 4096 + i * 1024:4096 + (i + 1) * 1024] for i in range(2)]
    groups = [(j, e0, n) for j in range(4) for (e0, n) in col_blocks(E_LO, E_HI)]
    batches = []
    nxy = {"X": 0, "Y": 0}
    for gi in range(len(groups)):
        kt = 0
        turn = "X"
        while kt < 64:
            if turn == "X" and kt + 2 <= 64:
                kts = [kt, kt + 1]
                kind = "X"
            else:
                kts = [kt]
                kind = "Y"
            kt += len(kts)
            batches.append((gi, kts, kind, nxy[kind], kt == 64))
            nxy[kind] += 1
            turn = "Y" if turn == "X" else "X"

    def sbank(kind, i, h):
        return (2 * i + h) if kind == "X" else (4 + h)

    def emit_qk(bn):
        gi, kts, kind, ser, last = batches[bn]
        j, e0, n = groups[gi]
        for i, kt in enumerate(kts):
            for h in range(2):
                mm(bank(sbank(kind, i, h), n), KT[64 * h:64 * h + 64, kt * 128:(kt + 1) * 128],
                   QT[64 * h:64 * h + 64, j, e0:e0 + n], True, True)

    def emit_exp(bn):
        gi, kts, kind, ser, last = batches[bn]
        j, e0, n = groups[gi]
        nit = 2 * len(kts)
        b0 = 0 if kind == "X" else 4
        pt = (PTX if kind == "X" else PTY)[ser % 2]
        sv = PS[:, 512 * b0:512 * (b0 + nit)].rearrange("p (b t) -> p b t", b=nit)[:, :, 0:n]
        pv = pt[:, 0:512 * nit].rearrange("p (b t) -> p b t", b=nit)[:, :, 0:n]
        act(pv, sv, AF.Exp, scale=0.125)

    def emit_pv(bn):
        gi, kts, kind, ser, last = batches[bn]
        j, e0, n = groups[gi]
        pt = (PTX if kind == "X" else PTY)[ser % 2]
        for i, kt in enumerate(kts):
            for h in range(2):
                it = 2 * i + h
                mm(bank(6 + h, n)[0:65, :], VV[:, kt, h, 0:65], pt[:, 512 * it:512 * it + n], kt == 0, kt == 63)
        if last:
            for h in range(2):
                cp("dve", TMP[h][0:65, 0:n], bank(6 + h, n)[0:65, :])
            for h in range(2):
                recip(TMP[2 + h][64:65, 0:n], TMP[h][64:65, 0:n])
            for h in range(2):
                osb = TMP[h][0:65, 0:n]
                rd = TMP[2 + h][64:65, 0:n]
                rdb = TMP[4 + h][0:64, 0:n]
                row = 2 * gi + h
                dma(rds[row:row + 1, 0:n], rd)
                dma(rdb, rds[row:row + 1, 0:n].partition_broadcast(64))
                if h == 0:
                    tt("dve", HT[0:64, 4 + j, e0:e0 + n], osb[0:64, :], rdb, ALU.mult)
                else:
                    a1_ = ATT1[0:64, 0:n]
                    tt("dve", a1_, osb[0:64, :], rdb, ALU.mult)
                    dma(HT[64:128, 4 + j, e0:e0 + n], a1_)

    pend = {"X": 0, "Y": 0}
    nq = [0]

    def try_qk():
        while nq[0] < len(batches) and pend[batches[nq[0]][2]] == 0:
            emit_qk(nq[0])
            pend[batches[nq[0]][2]] += 1
            nq[0] += 1

    try_qk()
    for bn in range(len(batches)):
        emit_exp(bn)
        pend[batches[bn][2]] -= 1
        try_qk()
        emit_pv(bn)

    def outproj_phase(tiles, nk, lhs_fn, rhs_fn, xsrc_fn, xdst_fn, hdst_fn):
        def ybuf(t):
            m = tiles[t][1]
            return PS[0:m, 1024 * (t % 2):1024 * (t % 2) + 1024]

        def st_a(t):
            y = ybuf(t)
            dma(XT[t % 2][0:tiles[t][1], :], xsrc_fn(t))
            for hf in range(2):
                for c in range(nk):
                    mm(y[:, hf * 512:(hf + 1) * 512], lhs_fn(t, c), rhs_fn(c, hf), c == 0, c == nk - 1)

        def st_b1(t):
            m = tiles[t][1]
            y = ybuf(t)
            xt = XT[t % 2]
            ss = stat1(m)
            yt = YT[t % 2]
            act(yt[0:m, :], y, AF.Square, accum=ss)
            rs = stat1(m)
            rsqrt_from(rs, ss, 1.0 / 1024.0, m)
            stt("dve", yt[0:m, :], y, rs, GB[0:m, :], ALU.mult, ALU.mult)
            tt("pool", yt[0:m, :], yt[0:m, :], xt[0:m, :], ALU.add)
            dma(xdst_fn(t), yt[0:m, :])

        def st_b2(t):
            if hdst_fn is None:
                return
            m = tiles[t][1]
            yt = YT[t % 2]
            hb = HB[t % 2]
            ss = stat1(m)
            act(hb[0:m, :], yt[0:m, :], AF.Square, accum=ss)
            rs = stat1(m)
            rsqrt_from(rs, ss, 1.0 / 1024.0, m)
            ts("dve", hb[0:m, :], yt[0:m, :], rs, ALU.mult)

        def st_c(t):
            if hdst_fn is None:
                return
            tile_back(t, tiles[t][1], hdst_fn(t), "act" if t % 2 else "dve")

        stages = [st_a, st_b1, st_b2, st_c]
        for s_ in range(len(tiles) + len(stages) - 1):
            for k, st in enumerate(stages):
                t = s_ - k
                if 0 <= t < len(tiles):
                    st(t)

    tok_tiles = col_blocks(E_LO, E_HI, 128)

    def ht_tile(t):
        e0, m = tok_tiles[t]
        return HT[:, :, e0:e0 + m]

    outproj_phase(tok_tiles, 8,
                  lambda t, c: HT[:, c, tok_tiles[t][0]:tok_tiles[t][0] + tok_tiles[t][1]],
                  lambda c, hf: WOUT[:, c, hf * 512:(hf + 1) * 512],
                  lambda t: x_ext[tok_tiles[t][0]:tok_tiles[t][0] + tok_tiles[t][1], :],
                  lambda t: xs1[tok_tiles[t][0]:tok_tiles[t][0] + tok_tiles[t][1], :],
                  ht_tile)

    fgroups = [(f0, min(4, NFC - f0)) for f0 in range(0, NFC, 4)]
    fblocks = []
    o0 = 16
    while o0 < 2064:
        no = min(510, 2064 - o0)
        fblocks.append((o0, no))
        o0 += no

    def load_wdown(chunks, dst_fn):
        for f in chunks:
            wpiece(dst_fn(f), [(w_down[f * 128:(f + 1) * 128, :], 0, 128, 0, 1024)], 1024, None)

    def wup_piece(gi, c, defer=False):
        f0, nf = fgroups[gi]
        wu = WUG0 if gi == 0 else WU[gi % 2]
        ncol = nf * 128
        srcs = [(w_up[c * 128:(c + 1) * 128, f0 * 128:f0 * 128 + ncol], 0, 128, 0, ncol),
                (w_up[c * 128:(c + 1) * 128, DFF + f0 * 128:DFF + f0 * 128 + ncol], 0, 128, 512, ncol)]
        sc = GFM[:, 16 + c:16 + c + 1]
        if nf == 4:
            return wpiece(wu[:, c, :], srcs, 1024, sc, defer=defer)
        slot_view = lambda slot: slot[:, :].rearrange("p (a t) -> p a t", a=2)[:, :, 0:ncol]
        return wpiece(wu[:, c, :].rearrange("p (a t) -> p a t", a=2)[:, :, 0:ncol], srcs, 1024, sc, in_view=slot_view, defer=defer)

    def load_wup_group(gi):
        for c in range(8):
            wup_piece(gi, c)

    dma(GB[:, :], gpost[1:2, :].partition_broadcast(128))
    load_wup_group(0)
    qi = 0
    for (e0, n) in col_blocks(E_LO, E_HI):
        for oc in range(8):
            qb = bank(qi % 2, n)
            qi += 1
            for c in range(8):
                mm(qb, WMQ[:, c, oc * 128:(oc + 1) * 128], HT[:, c, e0:e0 + n], c == 0, c == 7)
            cp("act", BQ[:, oc, e0:e0 + n], qb)
        for hd in range(4):
            for mt in range(2):
                sbk = bank(2 + mt, n)
                for dc in range(2):
                    mm(sbk, KM[:, 2 * hd + dc, mt * 128:(mt + 1) * 128], BQ[:, 2 * hd + dc, e0:e0 + n], dc == 0, dc == 1)
            pt = PTX[hd % 2]
            sv = PS[:, 1024:2048].rearrange("p (b t) -> p b t", b=2)[:, :, 0:n]
            pv = pt[:, 0:1024].rearrange("p (b t) -> p b t", b=2)[:, :, 0:n]
            act(pv, sv, AF.Exp, scale=1.0 / 16.0)
            den = bank(4, n)
            for mt in range(2):
                mm(den, ONES, pt[:, 512 * mt:512 * mt + n], mt == 0, mt == 1)
            rd = TMP[hd % 2][:, 0:n]
            act(rd, den, AF.Ln)
            act(rd, rd, AF.Exp, scale=-1.0)
            for dc in range(2):
                ob = bank(5 + dc, n)
                for mt in range(2):
                    mm(ob, VM[:, mt, (2 * hd + dc) * 128:(2 * hd + dc + 1) * 128], pt[:, 512 * mt:512 * mt + n], mt == 0, mt == 1)
                tt("dve", HT[:, 2 * hd + dc, e0:e0 + n], ob, rd, ALU.mult)
    outproj_phase(tok_tiles, 8,
                  lambda t, c: HT[:, c, tok_tiles[t][0]:tok_tiles[t][0] + tok_tiles[t][1]],
                  lambda c, hf: WMO[:, c, hf * 512:(hf + 1) * 512],
                  lambda t: xs1[tok_tiles[t][0]:tok_tiles[t][0] + tok_tiles[t][1], :],
                  lambda t: xs2[tok_tiles[t][0]:tok_tiles[t][0] + tok_tiles[t][1], :],
                  ht_tile)
    ts("dve", HT[:, :, 15:16], HT[:, :, 15:16], MASK[:, 0:1], ALU.mult)
    ts("dve", HT[:, :, 2064:2065], HT[:, :, 2064:2065], MASK[:, 1:2], ALU.mult)

    dma(GB[:, :], gpost[2:3, :].partition_broadcast(128))
    fitems = []
    for gi, (f0, nf) in enumerate(fgroups):
        for fl in range(nf):
            for bi_, (o0, no) in enumerate(fblocks):
                fitems.append((gi, fl, f0 + fl, o0, no, fl * len(fblocks) + bi_))
    wpend = []

    def f_a(i):
        gi, fl, f, o0, no, k = fitems[i]
        while wpend:
            wpend.pop(0)()
        if gi + 1 < len(fgroups) and k < 8:
            wpend.append(wup_piece(gi + 1, k, defer=True))
        if gi == len(fgroups) - 1 and k < 6:
            wpend.append(wpiece(WDA[:, k * 1024:(k + 1) * 1024], [(w_down[k * 128:(k + 1) * 128, :], 0, 128, 0, 1024)],
                                1024, None, defer=True))
        wu = WUG0 if gi == 0 else WU[gi % 2]
        ug = bank(2 * (i % 3), no + 2)
        uv = bank(2 * (i % 3) + 1, no + 2)
        for c in range(8):
            mm(ug, wu[:, c, fl * 128:(fl + 1) * 128], HT[:, c, o0 - 1:o0 + no + 1], c == 0, c == 7)
        for c in range(8):
            mm(uv, wu[:, c, 512 + fl * 128:512 + (fl + 1) * 128], HT[:, c, o0 - 1:o0 + no + 1], c == 0, c == 7)

    def f_b(i):
        gi, fl, f, o0, no, k = fitems[i]
        pg = FFNP[:, f * 4:f * 4 + 4]
        pv_ = FFNP[:, (NFC + f) * 4:(NFC + f) * 4 + 4]
        ug = bank(2 * (i % 3), no + 2)
        uv = bank(2 * (i % 3) + 1, no + 2)
        tg = TMP[i % 3][:, 0:no]
        tv = TMP[3 + i % 3][:, 0:no]
        act(tg, ug[:, 1:1 + no], AF.Identity, bias=pg[:, 3:4], scale=pg[:, 1:2])
        act(tv, uv[:, 1:1 + no], AF.Identity, bias=pv_[:, 3:4], scale=pv_[:, 1:2])
        stt("dve", tg, ug[:, 0:no], pg[:, 0:1], tg, ALU.mult, ALU.add)
        stt("dve", tg, ug[:, 2:2 + no], pg[:, 2:3], tg, ALU.mult, ALU.add)
        stt("dve", tv, uv[:, 0:no], pv_[:, 0:1], tv, ALU.mult, ALU.add)
        stt("dve", tv, uv[:, 2:2 + no], pv_[:, 2:3], tv, ALU.mult, ALU.add)

    def f_c(i):
        gi, fl, f, o0, no, k = fitems[i]
        tg = TMP[i % 3][:, 0:no]
        tv = TMP[3 + i % 3][:, 0:no]
        act(tg, tg, AF.Gelu_apprx_tanh)
        tt("pool", GT[:, f, o0 - 16:o0 - 16 + no], tg, tv, ALU.mult)

    fst = [f_a, f_b, f_c]
    for s_ in range(len(fitems) + 2):
        for k, st in enumerate(fst):
            t = s_ - k
            if 0 <= t < len(fitems):
                st(t)
    while wpend:
        wpend.pop(0)()
    WD1 = v3(BIG8[:, 0:16384], 16)
    load_wdown(range(6, NFC), lambda f: WD1[:, f - 6, :])

    def wd(f, hf):
        if f < 6:
            return WDA[:, f * 1024 + hf * 512:f * 1024 + (hf + 1) * 512]
        return WD1[:, f - 6, hf * 512:(hf + 1) * 512]

    ffn_tiles = [(16 + ti * 128, 128) for ti in range(16)]
    outproj_phase(ffn_tiles, NFC,
                  lambda t, c: GT[:, c, t * 128:(t + 1) * 128],
                  wd,
                  lambda t: xs2[16 + t * 128:16 + (t + 1) * 128, :],
                  lambda t: outd[t * 128:(t + 1) * 128, :],
                  None)

    S.emit(es)
    es.close()
    return nc


def _rope_tables(pos):
    pos = np.asarray(pos)
    inv = (10000.0 ** (-(np.arange(0, 32, 2, dtype=np.float32)) / np.float32(32))).astype(np.float32)
    r = (pos // 64).astype(np.float32)
    c = (pos % 64).astype(np.float32)
    ang_r = r[None, :] * inv[:, None]
    ang_c = c[None, :] * inv[:, None]
    ang = np.concatenate([ang_r, ang_r, ang_c, ang_c], axis=0).astype(np.float32)
    ang = np.concatenate([ang, ang], axis=0)
    return np.stack([np.cos(ang), np.sin(ang)]).astype(np.float32)


def _consts():
    ident = np.eye(128, dtype=np.float32)
    perm = np.zeros((128, 128), np.float32)
    for m in range(128):
        if (m % 32) < 16:
            perm[m + 16, m] = -1.0
        else:
            perm[m - 16, m] = 1.0
    bones = np.zeros((128, 128), np.float32)
    bones[:64, :64] = 1.0
    bones[64:, 64:] = 1.0
    ones = np.ones((128, 128), np.float32)
    return np.ascontiguousarray(np.concatenate([ident, perm, bones, ones], axis=1))


_NC_CACHE = {}


def kernel(x, mem, norm_mix_pre, w_in, conv_dw, conv_dw_b, conv_ln_g, conv_ln_b,
           q_norm_g, k_norm_g, w_out, norm_mix_post, norm_mem_pre, mem_norm_g,
           w_mem_q, w_mem_kv, w_mem_o, norm_mem_post, norm_ffn_pre, w_up, ffn_dw,
           ffn_dw_b, w_down, norm_ffn_post):
    f = lambda a: np.ascontiguousarray(np.asarray(a, dtype=np.float32))
    x = f(x); mem = f(mem)
    B, Sq, D = x.shape

    def fm(g):
        return f(g).reshape(8, 128).T
    gfm = np.ascontiguousarray(np.concatenate([fm(norm_mix_pre[0]), fm(norm_mem_pre[0]), fm(norm_ffn_pre[0]), fm(mem_norm_g[0])], axis=1))
    gpost = np.ascontiguousarray(np.stack([f(norm_mix_post[0]), f(norm_mem_post[0]), f(norm_ffn_post[0])]))
    cw = f(conv_dw[0])
    convp = np.zeros((128, 4, 34), np.float32)
    for ci in range(4):
        convp[:, ci, 0:31] = cw[:, ci * 128:(ci + 1) * 128].T
        convp[:, ci, 31] = f(conv_dw_b[0])[ci * 128:(ci + 1) * 128]
        convp[:, ci, 32] = f(conv_ln_g[0])[ci * 128:(ci + 1) * 128]
        convp[:, ci, 33] = f(conv_ln_b[0])[ci * 128:(ci + 1) * 128]
    convp = np.ascontiguousarray(convp.reshape(128, 136))
    qkg = np.ascontiguousarray(np.stack([np.tile(f(q_norm_g[0]), 2), np.tile(f(k_norm_g[0]), 2)], axis=1))
    fw = f(ffn_dw[0])
    fb = f(ffn_dw_b[0])
    ffnp = np.zeros((128, 44, 4), np.float32)
    for fc in range(44):
        ffnp[:, fc, 0:3] = fw[:, fc * 128:(fc + 1) * 128].T
        ffnp[:, fc, 3] = fb[fc * 128:(fc + 1) * 128]
    ffnp = np.ascontiguousarray(ffnp.reshape(128, 176))
    cst = _consts()
    shared = dict(w_in=f(w_in[0]), w_out=f(w_out[0]), w_mem_q=f(w_mem_q[0]), w_mem_kv=f(w_mem_kv[0]),
                  w_mem_o=f(w_mem_o[0]), w_up=f(w_up[0]), w_down=f(w_down[0]), gfm=gfm, gpost=gpost,
                  convp=convp, qkg=qkg, ffnp=ffnp, cst=cst)
    in_maps = []
    for core in range(8):
        b, j = core // 4, core % 4
        s = j * 2048
        xe = np.zeros((NE, 1024), np.float32)
        lo, hi = max(0, s - 16), min(Sq, s + 2064)
        xe[lo - (s - 16):hi - (s - 16)] = x[b, lo:hi]
        rest_idx = np.concatenate([np.arange(0, s), np.arange(s + 2048, Sq)])
        xr = np.ascontiguousarray(x[b, rest_idx])
        key_pos = np.concatenate([np.arange(s, s + 2048), rest_idx])
        ropek = _rope_tables(key_pos)
        ext_pos = np.clip(np.arange(s - 16, s - 16 + NE), 0, Sq - 1)
        ropeq = _rope_tables(ext_pos)
        mask = np.ones((128, 2), np.float32)
        if j == 0:
            mask[:, 0] = 0.0
        if j == 3:
            mask[:, 1] = 0.0
        m = dict(shared)
        m.update(x_ext=xe, x_rest=xr, mem=np.ascontiguousarray(mem[b]), ropek=ropek, ropeq=ropeq, mask=mask)
        in_maps.append(m)
    if "nc" not in _NC_CACHE:
        _NC_CACHE["nc"] = build_nc()
    nc = _NC_CACHE["nc"]
    res = run_bass_kernel_spmd(nc, in_maps, core_ids=list(range(8)))
    out = np.zeros((B, Sq, D), np.float32)
    for core in range(8):
        b, j = core // 4, core % 4
        out[b, j * 2048:(j + 1) * 2048] = np.asarray(res.results[core]["out"], dtype=np.float32)
    return out
```

```python
import numpy as np
from contextlib import ExitStack
import concourse.bass as bass
import concourse.mybir as mybir
from concourse.bass_utils import run_bass_kernel_spmd

F32 = mybir.dt.float32
BF16 = mybir.dt.bfloat16
AF = mybir.ActivationFunctionType
ALU = mybir.AluOpType

EPS = 1e-6
NE = 2080
E_LO, E_HI = 15, 2065
DFF = 2816
NFC = 22


def _esize(dt):
    return 2 if dt == BF16 else 4


class Sched:
    ENGS = ["pe", "act", "dve", "pool", "sp"]
    NDMA = 16

    def __init__(self, nc, tracked_dram=()):
        self.nc = nc
        self.ops = []
        self.w = {}
        self.r = {}
        self.tracked_dram = set(tracked_dram)

    def region(self, ap):
        t = ap.tensor
        name = t.name
        space = str(ap.space)
        if "DRAM" in space.upper() or "HBM" in space.upper() or type(t).__name__.startswith("DRam"):
            if name not in self.tracked_dram:
                return None
        es = _esize(ap.dtype)
        dims = ap.ap
        ps, pc = dims[0]
        off = int(ap.offset)
        if ps == 0:
            rs_ = int(t.shape[-1])
            p0, f0 = off // rs_, off % rs_
            p1 = p0 + 1
        else:
            p0 = off // ps
            f0 = off % ps
            p1 = p0 + pc
        ents = [(f0, 0)]
        rest = dims[1:]
        for (s, c) in rest[:-1]:
            s = abs(s)
            if s != 0 and len(ents) * c <= 64:
                ents = [(st + i * s, ex) for (st, ex) in ents for i in range(c)]
            else:
                ents = [(st, ex + (c - 1) * s) for (st, ex) in ents]
        if rest:
            s, c = rest[-1]
            ents = [(st, ex + (c - 1) * abs(s) + 1) for (st, ex) in ents]
        else:
            ents = [(st, ex + 1) for (st, ex) in ents]
        ivs = tuple(sorted((st * es, (st + ex) * es) for (st, ex) in ents))
        return (name, p0, p1, ivs)

    @staticmethod
    def _ov(a, b):
        if a[1] >= b[2] or b[1] >= a[2]:
            return False
        for (s0, e0) in a[3]:
            for (s1, e1) in b[3]:
                if s0 < e1 and s1 < e0:
                    return True
        return False

    @staticmethod
    def _covers(a, b):
        if len(a[3]) != 1:
            return a[1] <= b[1] and a[2] >= b[2] and a[3] == b[3]
        if a[1] > b[1] or a[2] < b[2]:
            return False
        s0, e0 = a[3][0]
        return all(s0 <= s1 and e1 <= e0 for (s1, e1) in b[3])

    def add(self, eng, fn, reads=(), writes=(), dma=False):
        op = {"eng": eng, "fn": fn, "deps": set(), "idx": len(self.ops), "inc": False, "dma": dma}
        for ap in reads:
            rg = self.region(ap)
            if rg is None:
                continue
            for key, d in self.w.get(rg[0], {}).items():
                if self._ov(rg, key):
                    op["deps"].update(d.values())
            self.r.setdefault(rg[0], {}).setdefault(rg, {})[eng] = op["idx"]
        for ap in writes:
            rg = self.region(ap)
            if rg is None:
                continue
            for table in (self.w, self.r):
                tb = table.get(rg[0], {})
                dead = []
                for key, d in tb.items():
                    if self._ov(rg, key):
                        op["deps"].update(d.values())
                        if self._covers(rg, key):
                            dead.append(key)
                for k in dead:
                    del tb[k]
            self.w.setdefault(rg[0], {})[rg] = {eng: op["idx"]}
        op["deps"].discard(op["idx"])
        self.ops.append(op)
        return op

    def emit(self, es):
        nc = self.nc
        ops = self.ops
        for op in ops:
            op["deps"] = {d for d in op["deps"] if not (op["eng"] == "pe" and ops[d]["eng"] == "pe" and not ops[d]["dma"])}
            latest = {}
            keep = set()
            for d in op["deps"]:
                p = ops[d]
                if p["dma"]:
                    keep.add(d)
                else:
                    latest[p["eng"]] = max(latest.get(p["eng"], -1), d)
            op["deps"] = keep | set(latest.values())
            for d in op["deps"]:
                ops[d]["inc"] = True
        esem = {e: es.enter_context(nc.semaphore("sem_" + e)) for e in self.ENGS}
        dsem = [es.enter_context(nc.semaphore("dsem%d" % i)) for i in range(self.NDMA)]
        cnt = {e: 0 for e in self.ENGS}
        ndma = 0
        last_out_dma = []
        for op in ops:
            if op["dma"]:
                op["dsem"] = ndma % self.NDMA
                op["dcnt"] = 16 * (ndma // self.NDMA + 1)
                ndma += 1
            elif op["inc"]:
                cnt[op["eng"]] += 1
                op["cnt"] = cnt[op["eng"]]
        self.ndma = ndma
        block = es.enter_context(nc.Block())

        def stream(engname, e):
            seen = {}

            def wait(sem, key, val):
                if seen.get(key, 0) >= val:
                    return
                seen[key] = val
                e.wait_ge(sem, val)

            for op in ops:
                if op["eng"] != engname:
                    continue
                need = {}
                for d in op["deps"]:
                    p = ops[d]
                    if p["dma"]:
                        k = ("d", p["dsem"])
                        need[k] = max(need.get(k, 0), p["dcnt"])
                    else:
                        k = ("e", p["eng"])
                        need[k] = max(need.get(k, 0), p["cnt"])
                if op["dma"] and op["dcnt"] > 16:
                    k = ("d", op["dsem"])
                    need[k] = max(need.get(k, 0), op["dcnt"] - 16)
                for k, v in need.items():
                    wait(dsem[k[1]] if k[0] == "d" else esem[k[1]], k, v)
                ins = op["fn"](e)
                if op["dma"]:
                    ins.then_inc(dsem[op["dsem"]], 16)
                elif op["inc"]:
                    ins.then_inc(esem[op["eng"]], 1)
            if engname == "sp":
                for i in range(min(self.NDMA, ndma)):
                    n_i = (ndma - 1 - i) // self.NDMA + 1
                    wait(dsem[i], ("d", i), 16 * n_i)

        @block.tensor
        def _(e):
            stream("pe", e)

        @block.scalar
        def _(e):
            stream("act", e)

        @block.vector
        def _(e):
            stream("dve", e)

        @block.gpsimd
        def _(e):
            stream("pool", e)

        @block.sync
        def _(e):
            stream("sp", e)


def col_blocks(lo, hi, w=512):
    out = []
    while lo < hi:
        n = min(w, hi - lo)
        out.append((lo, n))
        lo += n
    return out


def build_nc(debug=False):
    nc = bass.Bass("TRN2", target_bir_lowering=False)
    es = ExitStack()

    def di(name, shape, dt=F32):
        return nc.dram_tensor(name, shape, dt, kind="ExternalInput").ap()

    x_ext = di("x_ext", [NE, 1024])
    x_rest = di("x_rest", [6144, 1024])
    memd = di("mem", [256, 1024])
    w_in = di("w_in", [1024, 1792])
    w_out = di("w_out", [1024, 1024])
    w_mem_q = di("w_mem_q", [1024, 1024])
    w_mem_kv = di("w_mem_kv", [1024, 2048])
    w_mem_o = di("w_mem_o", [1024, 1024])
    w_up = di("w_up", [1024, 2 * DFF])
    w_down = di("w_down", [DFF, 1024])
    gfm = di("gfm", [128, 4 * 8])
    gpost = di("gpost", [3, 1024])
    convp = di("convp", [128, 4 * 34])
    qkg = di("qkg", [128, 2])
    ffnp = di("ffnp", [128, 44 * 4])
    cst = di("cst", [128, 512])
    ropek = di("ropek", [2, 128, 8192])
    ropeq = di("ropeq", [2, 128, NE])
    maskd = di("mask", [128, 2])
    outd = nc.dram_tensor("out", [2048, 1024], F32, kind="ExternalOutput").ap()
    xs1 = nc.dram_tensor("xs1", [NE, 1024], F32, kind="Internal").ap()
    xs2 = nc.dram_tensor("xs2", [NE, 1024], F32, kind="Internal").ap()
    rds = nc.dram_tensor("rds", [64, 512], F32, kind="Internal").ap()
    dbg = {}

    S = Sched(nc, tracked_dram=["xs1", "xs2", "out", "rds"])

    def sb(name, shape, dt=F32):
        return es.enter_context(nc.sbuf_tensor(name, shape, dt))

    BIG8 = sb("BIG8", [128, 8 * NE], BF16)
    BIGQ = sb("BIGQ", [128, 8 * NE], BF16)
    G = sb("G", [128, NFC * 2048], BF16)
    STG = [sb("STG%d" % i, [128, 1024]) for i in range(2)]
    XT = [sb("XT%d" % i, [128, 1024]) for i in range(2)]
    YT = [sb("YT%d" % i, [128, 1024]) for i in range(2)]
    HB = [sb("HB%d" % i, [128, 1024], BF16) for i in range(2)]
    TMP = [sb("TMP%d" % i, [128, 512]) for i in range(6)]
    TMPB = [sb("TMPB%d" % i, [128, 512], BF16) for i in range(4)]
    GB = sb("GB", [128, 1024])
    CST = sb("CST", [128, 512], BF16)
    ONESF = sb("ONESF", [128, 64])
    GFM = sb("GFM", [128, 32])
    CONVP = sb("CONVP", [128, 4 * 34])
    QKG = sb("QKG", [128, 2])
    FFNP = sb("FFNP", [128, 44 * 4])
    MASK = sb("MASK", [128, 2])
    STAT = sb("STAT", [128, 32])
    PS = es.enter_context(nc.psum_tensor("PS", [128, 4096], F32))

    IDENT = CST[:, 0:128]
    PERM = CST[:, 128:256]
    BONES = CST[:, 256:384]
    ONES = CST[:, 384:512]

    def bank(b, n=512):
        return PS[:, 512 * b:512 * b + n]

    def bankbf(b):
        return PS[:, 512 * b:512 * b + 512].bitcast(BF16)

    def v3(ap2d, c):
        return ap2d.rearrange("p (c t) -> p c t", c=c)

    HT = v3(BIG8[:, :], 8)
    BQ = v3(BIGQ[:, :], 8)
    AT = BQ[:, 0:4, :]
    QT = BQ[:, 4:8, :]
    KT = G[:, 0:8192]
    VV = G[:, 8192:8192 + 64 * 130].rearrange("p (t g d) -> p t g d", t=64, g=2)
    WREG = G[:, 16512:16512 + 15872]
    WIN = v3(WREG[:, 0:8 * 1792], 8)
    DIAG = WREG[:, 0:15872].rearrange("p (c k m) -> p c k m", c=4, k=31)
    KM = v3(G[:, 32384:34432], 8)
    VM = v3(G[:, 34432:36480], 2)
    MEMT = v3(G[:, 36480:38528], 8)
    WKV = v3(BIGQ[:, 0:16384], 8)
    HTB = [v3(BIGQ[:, i * 4096:(i + 1) * 4096], 8) for i in range(2)]
    WOUT = v3(BIGQ[:, 0:8192], 8)
    WMQ = v3(G[:, 16512 + 6144:16512 + 14336], 8)
    WMO = v3(G[:, 36480:44672], 8)
    WUG0 = v3(G[:, 8192:16384], 8)
    GT = v3(G[:, :], NFC)
    WU = [v3(BIGQ[:, i * 8192:(i + 1) * 8192], 8) for i in range(2)]
    PT = [WREG[:, i * 1536:(i + 1) * 1536] for i in range(4)]
    _rf = G[:, 38528:38528 + 4096].bitcast(F32)
    ROPE = [_rf[:, i * 1024:(i + 1) * 1024] for i in range(2)]
    WDA = BIGQ[:, 0:6144]
    ATT1 = TMPB[3]

    cntr = {"stg": 0, "xt": 0, "yt": 0, "hb": 0, "rope": 0, "tp": 0, "stat": 0}

    def rot(key, n):
        v = cntr[key]
        cntr[key] = (v + 1) % n
        return v

    def stat1(m):
        i = rot("stat", 32)
        return STAT[0:m, i:i + 1]

    def dma(out, in_, q="sp"):
        S.add(q, lambda e: e.dma_start(out=out, in_=in_), reads=[in_], writes=[out], dma=True)

    def mm(out, lhsT, rhs, start, stop):
        S.add("pe", lambda e: e.matmul(out, lhsT, rhs, start=start, stop=stop), reads=[lhsT, rhs], writes=[out])

    def transp(out, in_, ident):
        S.add("pe", lambda e: e.transpose(out, in_, ident), reads=[in_, ident], writes=[out])

    def act(out, in_, func, bias=None, scale=None, accum=None):
        kw = {}
        rd = [in_]
        wr = [out]
        if bias is not None:
            kw["bias"] = bias
            if not isinstance(bias, float):
                rd.append(bias)
        if scale is not None:
            kw["scale"] = scale
            if not isinstance(scale, float):
                rd.append(scale)
        if accum is not None:
            kw["accum_out"] = accum
            wr.append(accum)
        S.add("act", lambda e: e.activation(out=out, in_=in_, func=func, **kw), reads=rd, writes=wr)

    def tt(eng, out, in0, in1, op):
        S.add(eng, lambda e: e.tensor_tensor(out=out, in0=in0, in1=in1, op=op), reads=[in0, in1], writes=[out])

    def ts(eng, out, in0, s1, op0, s2=None, op1=None):
        rd = [in0] + [s for s in (s1, s2) if s is not None and not isinstance(s, float)]
        if op1 is None:
            S.add(eng, lambda e: e.tensor_scalar(out=out, in0=in0, scalar1=s1, scalar2=None, op0=op0), reads=rd, writes=[out])
        else:
            S.add(eng, lambda e: e.tensor_scalar(out=out, in0=in0, scalar1=s1, scalar2=s2, op0=op0, op1=op1), reads=rd, writes=[out])

    def stt(eng, out, in0, scalar, in1, op0, op1, accum=None):
        rd = [in0, in1] + ([] if isinstance(scalar, float) else [scalar])
        if accum is None:
            S.add(eng, lambda e: e.scalar_tensor_tensor(out=out, in0=in0, scalar=scalar, in1=in1, op0=op0, op1=op1), reads=rd, writes=[out])
        else:
            S.add(eng, lambda e: e.scalar_tensor_tensor(out=out, in0=in0, scalar=scalar, in1=in1, op0=op0, op1=op1, accum_out=accum),
                  reads=rd, writes=[out, accum])

    def cp(eng, out, in_):
        if eng == "act":
            act(out, in_, AF.Identity)
        else:
            S.add(eng, lambda e: e.tensor_copy(out=out, in_=in_), reads=[in_], writes=[out])

    def recip(out, in_):
        S.add("dve", lambda e: e.reciprocal(out=out, in_=in_), reads=[in_], writes=[out])

    def memset(eng, ap, val):
        S.add(eng, lambda e: e.memset(ap, val), writes=[ap])

    def rsqrt_from(out, in_, scale, m=None):
        act(out, in_, AF.Ln, bias=EPSB[0:out.shape[0], 0:1] if m is None else EPSB[0:m, 0:1], scale=scale)
        act(out, out, AF.Exp, scale=-0.5)

    def wpiece(dst, srcs, ncols, scal=None, eng="dve", in_view=None, defer=False):
        slot = STG[rot("stg", 2)]
        for (src, p0, p1, c0, nc_) in srcs:
            dma(slot[p0:p1, c0:c0 + nc_], src, q="pool")
        src_ap = slot[:, 0:ncols] if in_view is None else in_view(slot)

        def cast():
            if scal is None:
                cp(eng, dst, src_ap)
            elif eng == "act":
                act(dst, src_ap, AF.Identity, scale=scal)
            else:
                ts(eng, dst, src_ap, scal, ALU.mult)
        if defer:
            return cast
        cast()

    EPSB = sb("EPSB", [128, 1])
    memset("pool", EPSB[:, :], EPS)
    memset("pool", ONESF[:, :], 1.0)
    wpiece(CST[:, :], [(cst[:, :], 0, 128, 0, 512)], 512, None, eng="dve")
    dma(GFM[:, :], gfm[:, :])
    dma(CONVP[:, :], convp[:, :])
    dma(QKG[:, :], qkg[:, :])
    dma(FFNP[:, :], ffnp[:, :])
    dma(MASK[:, :], maskd[:, :])
    memset("pool", VV[:, :, :, 64:65], 1.0)

    def load_weight_rows(dst3, src, ncols_total, gcol, col_pieces=None, rows_of_chunk=None, nchunks=8):
        for c in range(nchunks):
            for (c0, ncol) in (col_pieces or col_blocks(0, ncols_total, 1024)):
                if rows_of_chunk is None:
                    srcs = [(src[c * 128:(c + 1) * 128, c0:c0 + ncol], 0, 128, 0, ncol)]
                else:
                    srcs = [(src[r0:r0 + nr, c0:c0 + ncol], p0, p0 + nr, 0, ncol) for (r0, nr, p0) in rows_of_chunk(c)]
                scal = None if gcol is None else GFM[:, gcol * 8 + c:gcol * 8 + c + 1]
                wpiece(dst3[:, c, c0:c0 + ncol], srcs, ncol, scal)

    def norm_to_T(src_tile, m, dst, evac_eng):
        ss = stat1(m)
        hb = HB[rot("hb", 2)]
        act(hb[0:m, :], src_tile, AF.Square, accum=ss)
        rs = stat1(m)
        rsqrt_from(rs, ss, 1.0 / 1024.0, m)
        ts("dve", hb[0:m, :], src_tile, rs, ALU.mult)
        tb = 6 + rot("tp", 2)
        tpv = v3(bankbf(tb), 8)
        for c in range(8):
            transp(tpv[:, c, 0:m], hb[0:m, c * 128:(c + 1) * 128], IDENT[0:m, 0:m])
        cp(evac_eng, dst, tpv[:, :, 0:m])

    def nr1(src, n, gain, b_ss, b_rot):
        xg = TMPB[0][:, 0:n]
        sq = TMPB[1][:, 0:n]
        act(xg, src, AF.Identity, scale=gain)
        act(sq, src, AF.Square)
        mm(bank(b_ss, n), BONES, sq, True, True)
        mm(bank(b_rot, n), PERM, xg, True, True)

    def nr2(n, cos, sin, b_ss, b_rot):
        xg = TMPB[0][:, 0:n]
        rs = TMP[0][:, 0:n]
        act(rs, bank(b_ss, n), AF.Ln, bias=EPSB[:, 0:1], scale=1.0 / 64.0)
        act(rs, rs, AF.Exp, scale=-0.5)
        tt("dve", TMP[1][:, 0:n], xg, cos, ALU.mult)
        tt("dve", TMP[2][:, 0:n], bank(b_rot, n), sin, ALU.mult)

    def nr3(n, dst):
        t1 = TMP[1][:, 0:n]
        tt("pool", t1, t1, TMP[2][:, 0:n], ALU.add)
        tt("pool", dst, t1, TMP[0][:, 0:n], ALU.mult)

    def normrope(src, n, gain, cos, sin, dst, b_ss, b_rot):
        nr1(src, n, gain, b_ss, b_rot)
        nr2(n, cos, sin, b_ss, b_rot)
        nr3(n, dst)

    def phase_c0():
      for mt in range(2):
          xt = XT[rot("xt", 2)]
          dma(xt[:, :], memd[mt * 128:(mt + 1) * 128, :])
          norm_to_T(xt[:, :], 128, MEMT[:, :, mt * 128:(mt + 1) * 128], "dve")
      for oc in range(8):
          pb = bank(oc % 2, 256)
          for c in range(8):
              mm(pb, WKV[:, c, oc * 128:(oc + 1) * 128], MEMT[:, c, :], c == 0, c == 7)
          cp("act", KM[:, oc, :], pb)
      for mt in range(2):
          for hf in range(2):
              pb = bank(2 + (mt * 2 + hf) % 2)
              for c in range(8):
                  mm(pb, MEMT[:, c, mt * 128:(mt + 1) * 128], WKV[:, c, 1024 + hf * 512:1024 + (hf + 1) * 512], c == 0, c == 7)
              cp("dve", VM[:, mt, hf * 512:(hf + 1) * 512], pb)

    def load_win_chunk(c):
        sc = GFM[:, c:c + 1]
        wpiece(WIN[:, c, 0:1024], [(w_in[c * 128:(c + 1) * 128, 0:1024], 0, 128, 0, 1024)], 1024, sc)
        slot_view = lambda slot: slot[:, 0:512].rearrange("p (h j d) -> p j h d", h=2, j=4)
        wpiece(WIN[:, c, 1024:1536].rearrange("p (j h d) -> p j h d", j=4, h=2),
               [(w_in[c * 128:(c + 1) * 128, 1024:1792], 0, 128, 0, 768)], 512, sc, in_view=slot_view)
        last = STG[(cntr["stg"] + 1) % 2]
        ts("dve", WIN[:, c, 1536:1792], last[:, 512:768], sc, ALU.mult)

    XQ = [XT[0], XT[1], YT[0], YT[1]]

    def tile_load(t, src, m):
        dma(XQ[t % 4][0:m, :], src)

    JUNKA = sb("JUNKA", [128, 1024], BF16)
    JUNKD = sb("JUNKD", [128, 1024], BF16)
    tstat = {}

    def tile_f1(t, m):
        xt = XQ[t % 4]
        ss = stat1(m)
        tstat[t] = ss
        if t % 2 == 0:
            act(JUNKA[0:m, :], xt[0:m, :], AF.Square, accum=ss)
        else:
            stt("dve", JUNKD[0:m, :], xt[0:m, :], 1.0, xt[0:m, :], ALU.mult, ALU.mult, accum=ss)

    def tile_f2(t, m):
        xt = XQ[t % 4]
        hb = HB[t % 2]
        rs = stat1(m)
        rsqrt_from(rs, tstat[t], 1.0 / 1024.0, m)
        ts("dve", hb[0:m, :], xt[0:m, :], rs, ALU.mult)

    def tile_back(t, m, dst, evac_eng):
        hb = HB[t % 2]
        tpv = v3(bankbf(6 + t % 2), 8)
        for c in range(8):
            transp(tpv[:, c, 0:m], hb[0:m, c * 128:(c + 1) * 128], IDENT[0:m, 0:m])
        cp(evac_eng, dst, tpv[:, :, 0:m])

    def kv_job(hsrc, kb):
        i = kb
        rp_ = ROPE[i % 2]
        kp = bank(i % 2)
        vp = bank(2 + i % 2)

        def P():
            dma(rp_[:, 0:512], ropek[0, :, kb * 512:(kb + 1) * 512])
            dma(rp_[:, 512:1024], ropek[1, :, kb * 512:(kb + 1) * 512])
            for c in range(8):
                mm(kp, WIN[:, c, 1536:1664], hsrc[:, c, :], c == 0, c == 7)
            for t in range(4):
                for c in range(8):
                    mm(vp[:, t * 128:(t + 1) * 128], hsrc[:, c, t * 128:(t + 1) * 128], WIN[:, c, 1664:1792], c == 0, c == 7)

        def N1():
            nr1(kp, 512, QKG[:, 1:2], 4, 5)
            cp("act", VV[:, kb * 4:kb * 4 + 4, :, 0:64], vp.rearrange("p (t g d) -> p t g d", t=4, g=2))

        def N2():
            nr2(512, rp_[:, 0:512], rp_[:, 512:1024], 4, 5)

        def N3():
            nr3(512, KT[:, kb * 512:(kb + 1) * 512])
        return [P, N1, N2, N3]

    def run_tiles(tiles, jobs_ready, extra=None):
        active = []

        def step_jobs(k):
            for _ in range(k):
                if active:
                    active[0].pop(0)()
                    if not active[0]:
                        active.pop(0)
        for t0 in range(min(3, len(tiles))):
            tile_load(t0, tiles[t0][0], tiles[t0][1])
        nt = len(tiles)
        tile_f1(0, tiles[0][1])
        if nt > 1:
            tile_f1(1, tiles[1][1])
        tile_f2(0, tiles[0][1])
        for t in range(nt):
            if t + 3 < nt:
                tile_load(t + 3, tiles[t + 3][0], tiles[t + 3][1])
            if t + 2 < nt:
                tile_f1(t + 2, tiles[t + 2][1])
            if t + 1 < nt:
                tile_f2(t + 1, tiles[t + 1][1])
            tile_back(t, tiles[t][1], tiles[t][2], "act" if t % 2 else "dve")
            if extra is not None:
                extra(t)
            for job in jobs_ready.get(t, []):
                active.append(job)
            step_jobs(2 if len(active) > 1 else 1)
        while active:
            step_jobs(1)

    ext_tiles = col_blocks(0, NE, 128)
    tiles = [(x_ext[e0:e0 + m, :], m, HT[:, :, e0:e0 + m]) for (e0, m) in ext_tiles]
    jobs = {4 * kb + 4: [kv_job(HT[:, :, 16 + kb * 512:16 + (kb + 1) * 512], kb)] for kb in range(4)}
    kv_pieces = [(c, c0) for c in range(8) for c0 in (0, 1024)]
    cpend = []

    def ext_extra(t):
        if t < 4:
            load_win_chunk(2 * t)
            load_win_chunk(2 * t + 1)
            return
        while cpend:
            cpend.pop(0)()
        for _ in range(2):
            if kv_pieces:
                c, c0 = kv_pieces.pop(0)
                cpend.append(wpiece(WKV[:, c, c0:c0 + 1024], [(w_mem_kv[c * 128:(c + 1) * 128, c0:c0 + 1024], 0, 128, 0, 1024)],
                                    1024, GFM[:, 24 + c:24 + c + 1], defer=True))

    run_tiles(tiles, jobs, extra=ext_extra)
    while cpend:
        cpend.pop(0)()
    assert not kv_pieces
    phase_c0()
    tiles = []
    jobs = {}
    for rb in range(12):
        for t in range(4):
            r0 = rb * 512 + t * 128
            tiles.append((x_rest[r0:r0 + 128, :], 128, HTB[rb % 2][:, :, t * 128:(t + 1) * 128]))
        jobs[4 * rb + 3] = [kv_job(HTB[rb % 2], 4 + rb)]
    run_tiles(tiles, jobs)

    qitems = [(bk, e0, n, j) for bk, (e0, n) in enumerate(col_blocks(E_LO, E_HI)) for j in range(4)]

    def q_a(i):
        bk, e0, n, j = qitems[i]
        rp_ = ROPE[bk % 2]
        if j == 0:
            dma(rp_[:, 0:n], ropeq[0, :, e0:e0 + n])
            dma(rp_[:, 512:512 + n], ropeq[1, :, e0:e0 + n])
        qp = bank(6 + i % 2, n)
        for c in range(8):
            mm(qp, WIN[:, c, 1024 + j * 128:1024 + (j + 1) * 128], HT[:, c, e0:e0 + n], c == 0, c == 7)

    def q_n(i):
        bk, e0, n, j = qitems[i]
        rp_ = ROPE[bk % 2]
        normrope(bank(6 + i % 2, n), n, QKG[:, 0:1], rp_[:, 0:n], rp_[:, 512:512 + n], QT[:, j, e0:e0 + n], 4, 5)

    q_a(0)
    for i in range(len(qitems)):
        if i + 1 < len(qitems):
            q_a(i + 1)
        q_n(i)
    bi = 0
    for (e0, n) in col_blocks(0, NE):
        for ci in range(4):
            av = bank(2 * (bi % 2), n)
            ag = bank(2 * (bi % 2) + 1, n)
            bi += 1
            for c in range(8):
                mm(av, WIN[:, c, ci * 128:(ci + 1) * 128], HT[:, c, e0:e0 + n], c == 0, c == 7)
            for c in range(8):
                mm(ag, WIN[:, c, 512 + ci * 128:512 + (ci + 1) * 128], HT[:, c, e0:e0 + n], c == 0, c == 7)
            sg = TMP[3 + bi % 2][:, 0:n]
            act(sg, ag, AF.Sigmoid)
            tt("dve", AT[:, ci, e0:e0 + n], av, sg, ALU.mult)

    for ci in range(4):
        for k in range(31):
            ts("dve", DIAG[:, ci, k, :], IDENT, CONVP[:, ci * 34 + k:ci * 34 + k + 1], ALU.mult)
    for cbi, (e0, n) in enumerate(col_blocks(E_LO, E_HI)):
        sm = bank(4, n)
        sq_ = bank(5, n)
        cvb = [0, 1, 2, 3] if cbi % 2 == 0 else [6, 7, 2, 3]
        for ci in range(4):
            cv = bank(cvb[ci], n)
            for k in range(31):
                mm(cv, DIAG[:, ci, k, :], AT[:, ci, e0 + k - 15:e0 + k - 15 + n], k == 0, k == 30)
            bcol = CONVP[:, ci * 34 + 31:ci * 34 + 32]
            cb = TMPB[ci % 2][:, 0:n]
            cs = TMPB[2 + ci % 2][:, 0:n]
            act(cb, cv, AF.Identity, bias=bcol)
            act(cs, cv, AF.Square, bias=bcol)
            mm(sm, ONES, cb, ci == 0, ci == 3)
            mm(sq_, ONES, cs, ci == 0, ci == 3)
        mean = TMP[0][:, 0:n]
        ts("dve", mean, sm, 1.0 / 512.0, ALU.mult)
        msq = TMP[1][:, 0:n]
        tt("dve", msq, mean, mean, ALU.mult)
        var = TMP[2][:, 0:n]
        stt("dve", var, sq_, 1.0 / 512.0, msq, ALU.mult, ALU.subtract)
        act(var, var, AF.Ln, bias=EPSB[:, 0:1], scale=1.0)
        act(var, var, AF.Exp, scale=-0.5)
        for ci in range(4):
            cv = bank(cvb[ci], n)
            bcol = CONVP[:, ci * 34 + 31:ci * 34 + 32]
            t1 = TMP[3 + ci % 2][:, 0:n]
            stt("dve", t1, cv, bcol, mean, ALU.add, ALU.subtract)
            tt("pool", t1, t1, var, ALU.mult)
            act(HT[:, ci, e0:e0 + n], t1, AF.Silu, bias=CONVP[:, ci * 34 + 33:ci * 34 + 34],
                scale=CONVP[:, ci * 34 + 32:ci * 34 + 33])

    def wout_rows(c):
        if c < 4:
            return [(c * 128, 128, 0)]
        j = c - 4
        return [(512 + 64 * j, 64, 0), (512 + 64 * (4 + j), 64, 64)]
    load_weight_rows(WOUT, w_out, 1024, None, rows_of_chunk=wout_rows)
    dma(GB[:, :], gpost[0:1, :].partition_broadcast(128))
    load_weight_rows(WMQ, w_mem_q, 1024, 1)
    load_weight_rows(WMO, w_mem_o, 1024, None)

    PTX = [WREG[:, i * 2048:(i + 1) * 2048] for i in range(2)]
    PTY = [WREG[:, 4096 + i * 1024:4096 + (i + 1) * 1024] for i in range(2)]
    groups = [(j, e0, n) for j in range(4) for (e0, n) in col_blocks(E_LO, E_HI)]
    batches = []
    nxy = {"X": 0, "Y": 0}
    for gi in range(len(groups)):
        kt = 0
        turn = "X"
        while kt < 64:
            if turn == "X" and kt + 2 <= 64:
                kts = [kt, kt + 1]
                kind = "X"
            else:
                kts = [kt]
                kind = "Y"
            kt += len(kts)
            batches.append((gi, kts, kind, nxy[kind], kt == 64))
            nxy[kind] += 1
            turn = "Y" if turn == "X" else "X"

    def sbank(kind, i, h):
        return (2 * i + h) if kind == "X" else (4 + h)

    def emit_qk(bn):
        gi, kts, kind, ser, last = batches[bn]
        j, e0, n = groups[gi]
        for i, kt in enumerate(kts):
            for h in range(2):
                mm(bank(sbank(kind, i, h), n), KT[64 * h:64 * h + 64, kt * 128:(kt + 1) * 128],
                   QT[64 * h:64 * h + 64, j, e0:e0 + n], True, True)

    def emit_exp(bn):
        gi, kts, kind, ser, last = batches[bn]
        j, e0, n = groups[gi]
        nit = 2 * len(kts)
        b0 = 0 if kind == "X" else 4
        pt = (PTX if kind == "X" else PTY)[ser % 2]
        sv = PS[:, 512 * b0:512 * (b0 + nit)].rearrange("p (b t) -> p b t", b=nit)[:, :, 0:n]
        pv = pt[:, 0:512 * nit].rearrange("p (b t) -> p b t", b=nit)[:, :, 0:n]
        act(pv, sv, AF.Exp, scale=0.125)

    def emit_pv(bn):
        gi, kts, kind, ser, last = batches[bn]
        j, e0, n = groups[gi]
        pt = (PTX if kind == "X" else PTY)[ser % 2]
        for i, kt in enumerate(kts):
            for h in range(2):
                it = 2 * i + h
                mm(bank(6 + h, n)[0:65, :], VV[:, kt, h, 0:65], pt[:, 512 * it:512 * it + n], kt == 0, kt == 63)
        if last:
            for h in range(2):
                cp("dve", TMP[h][0:65, 0:n], bank(6 + h, n)[0:65, :])
            for h in range(2):
                recip(TMP[2 + h][64:65, 0:n], TMP[h][64:65, 0:n])
            for h in range(2):
                osb = TMP[h][0:65, 0:n]
                rd = TMP[2 + h][64:65, 0:n]
                rdb = TMP[4 + h][0:64, 0:n]
                row = 2 * gi + h
                dma(rds[row:row + 1, 0:n], rd)
                dma(rdb, rds[row:row + 1, 0:n].partition_broadcast(64))
                if h == 0:
                    tt("dve", HT[0:64, 4 + j, e0:e0 + n], osb[0:64, :], rdb, ALU.mult)
                else:
                    a1_ = ATT1[0:64, 0:n]
                    tt("dve", a1_, osb[0:64, :], rdb, ALU.mult)
                    dma(HT[64:128, 4 + j, e0:e0 + n], a1_)

    pend = {"X": 0, "Y": 0}
    nq = [0]

    def try_qk():
        while nq[0] < len(batches) and pend[batches[nq[0]][2]] == 0:
            emit_qk(nq[0])
            pend[batches[nq[0]][2]] += 1
            nq[0] += 1

    try_qk()
    for bn in range(len(batches)):
        emit_exp(bn)
        pend[batches[bn][2]] -= 1
        try_qk()
        emit_pv(bn)

    fgroups = [(f0, min(4, NFC - f0)) for f0 in range(0, NFC, 4)]
    fblocks = []
    o0 = 16
    while o0 < 2064:
        no = min(510, 2064 - o0)
        fblocks.append((o0, no))
        o0 += no

    def load_wdown(chunks, dst_fn):
        for f in chunks:
            wpiece(dst_fn(f), [(w_down[f * 128:(f + 1) * 128, :], 0, 128, 0, 1024)], 1024, None)

    def wup_piece(gi, c, defer=False):
        f0, nf = fgroups[gi]
        wu = WUG0 if gi == 0 else WU[gi % 2]
        ncol = nf * 128
        srcs = [(w_up[c * 128:(c + 1) * 128, f0 * 128:f0 * 128 + ncol], 0, 128, 0, ncol),
                (w_up[c * 128:(c + 1) * 128, DFF + f0 * 128:DFF + f0 * 128 + ncol], 0, 128, 512, ncol)]
        sc = GFM[:, 16 + c:16 + c + 1]
        if nf == 4:
            return wpiece(wu[:, c, :], srcs, 1024, sc, defer=defer)
        slot_view = lambda slot: slot[:, :].rearrange("p (a t) -> p a t", a=2)[:, :, 0:ncol]
        return wpiece(wu[:, c, :].rearrange("p (a t) -> p a t", a=2)[:, :, 0:ncol], srcs, 1024, sc, in_view=slot_view, defer=defer)

    def load_wup_group(gi):
        for c in range(8):
            wup_piece(gi, c)

    g0pend = []

    def b3_extra(it):
        while g0pend:
            g0pend.pop(0)()
        if it < 8:
            g0pend.append(wup_piece(0, it, defer=True))

    def outproj_phase(tiles, nk, lhs_fn, rhs_fn, xsrc_fn, xdst_fn, hdst_fn, extra=None):
        def ybuf(t):
            m = tiles[t][1]
            return PS[0:m, 1024 * (t % 2):1024 * (t % 2) + 1024]

        def st_a(t):
            y = ybuf(t)
            dma(XT[t % 2][0:tiles[t][1], :], xsrc_fn(t))
            for hf in range(2):
                for c in range(nk):
                    mm(y[:, hf * 512:(hf + 1) * 512], lhs_fn(t, c), rhs_fn(c, hf), c == 0, c == nk - 1)

        def st_b1(t):
            m = tiles[t][1]
            y = ybuf(t)
            xt = XT[t % 2]
            ss = stat1(m)
            yt = YT[t % 2]
            act(yt[0:m, :], y, AF.Square, accum=ss)
            rs = stat1(m)
            rsqrt_from(rs, ss, 1.0 / 1024.0, m)
            stt("dve", yt[0:m, :], y, rs, GB[0:m, :], ALU.mult, ALU.mult)
            tt("pool", yt[0:m, :], yt[0:m, :], xt[0:m, :], ALU.add)
            dma(xdst_fn(t), yt[0:m, :])

        def st_b2(t):
            if hdst_fn is None:
                return
            m = tiles[t][1]
            yt = YT[t % 2]
            hb = HB[t % 2]
            ss = stat1(m)
            act(hb[0:m, :], yt[0:m, :], AF.Square, accum=ss)
            rs = stat1(m)
            rsqrt_from(rs, ss, 1.0 / 1024.0, m)
            ts("dve", hb[0:m, :], yt[0:m, :], rs, ALU.mult)

        def st_c(t):
            if hdst_fn is None:
                return
            tile_back(t, tiles[t][1], hdst_fn(t), "act" if t % 2 else "dve")

        stages = [st_a, st_b1, st_b2, st_c]
        for s_ in range(len(tiles) + len(stages) - 1):
            for k, st in enumerate(stages):
                t = s_ - k
                if 0 <= t < len(tiles):
                    st(t)
            if extra is not None:
                extra(s_)

    tok_tiles = col_blocks(E_LO, E_HI, 128)

    def ht_tile(t):
        e0, m = tok_tiles[t]
        return HT[:, :, e0:e0 + m]

    outproj_phase(tok_tiles, 8,
                  lambda t, c: HT[:, c, tok_tiles[t][0]:tok_tiles[t][0] + tok_tiles[t][1]],
                  lambda c, hf: WOUT[:, c, hf * 512:(hf + 1) * 512],
                  lambda t: x_ext[tok_tiles[t][0]:tok_tiles[t][0] + tok_tiles[t][1], :],
                  lambda t: xs1[tok_tiles[t][0]:tok_tiles[t][0] + tok_tiles[t][1], :],
                  ht_tile, extra=b3_extra)
    while g0pend:
        g0pend.pop(0)()

    dma(GB[:, :], gpost[1:2, :].partition_broadcast(128))
    qi = 0
    for (e0, n) in col_blocks(E_LO, E_HI):
        for oc in range(8):
            qb = bank(qi % 2, n)
            qi += 1
            for c in range(8):
                mm(qb, WMQ[:, c, oc * 128:(oc + 1) * 128], HT[:, c, e0:e0 + n], c == 0, c == 7)
            cp("act", BQ[:, oc, e0:e0 + n], qb)
        for hd in range(4):
            for mt in range(2):
                sbk = bank(2 + mt, n)
                for dc in range(2):
                    mm(sbk, KM[:, 2 * hd + dc, mt * 128:(mt + 1) * 128], BQ[:, 2 * hd + dc, e0:e0 + n], dc == 0, dc == 1)
            pt = PTX[hd % 2]
            sv = PS[:, 1024:2048].rearrange("p (b t) -> p b t", b=2)[:, :, 0:n]
            pv = pt[:, 0:1024].rearrange("p (b t) -> p b t", b=2)[:, :, 0:n]
            act(pv, sv, AF.Exp, scale=1.0 / 16.0)
            den = bank(4, n)
            for mt in range(2):
                mm(den, ONES, pt[:, 512 * mt:512 * mt + n], mt == 0, mt == 1)
            rd = TMP[hd % 2][:, 0:n]
            act(rd, den, AF.Ln)
            act(rd, rd, AF.Exp, scale=-1.0)
            for dc in range(2):
                ob = bank(5 + dc, n)
                for mt in range(2):
                    mm(ob, VM[:, mt, (2 * hd + dc) * 128:(2 * hd + dc + 1) * 128], pt[:, 512 * mt:512 * mt + n], mt == 0, mt == 1)
                tt("dve", HT[:, 2 * hd + dc, e0:e0 + n], ob, rd, ALU.mult)
    outproj_phase(tok_tiles, 8,
                  lambda t, c: HT[:, c, tok_tiles[t][0]:tok_tiles[t][0] + tok_tiles[t][1]],
                  lambda c, hf: WMO[:, c, hf * 512:(hf + 1) * 512],
                  lambda t: xs1[tok_tiles[t][0]:tok_tiles[t][0] + tok_tiles[t][1], :],
                  lambda t: xs2[tok_tiles[t][0]:tok_tiles[t][0] + tok_tiles[t][1], :],
                  ht_tile)
    ts("dve", HT[:, :, 15:16], HT[:, :, 15:16], MASK[:, 0:1], ALU.mult)
    ts("dve", HT[:, :, 2064:2065], HT[:, :, 2064:2065], MASK[:, 1:2], ALU.mult)

    dma(GB[:, :], gpost[2:3, :].partition_broadcast(128))
    fitems = []
    for gi, (f0, nf) in enumerate(fgroups):
        for fl in range(nf):
            for bi_, (o0, no) in enumerate(fblocks):
                fitems.append((gi, fl, f0 + fl, o0, no, fl * len(fblocks) + bi_))
    wpend = []

    def f_a(i):
        gi, fl, f, o0, no, k = fitems[i]
        while wpend:
            wpend.pop(0)()
        if gi + 1 < len(fgroups) and k < 8:
            wpend.append(wup_piece(gi + 1, k, defer=True))
        if gi == len(fgroups) - 1 and k < 6:
            wpend.append(wpiece(WDA[:, k * 1024:(k + 1) * 1024], [(w_down[k * 128:(k + 1) * 128, :], 0, 128, 0, 1024)],
                                1024, None, defer=True))
        wu = WUG0 if gi == 0 else WU[gi % 2]
        ug = bank(2 * (i % 3), no + 2)
        uv = bank(2 * (i % 3) + 1, no + 2)
        for c in range(8):
            mm(ug, wu[:, c, fl * 128:(fl + 1) * 128], HT[:, c, o0 - 1:o0 + no + 1], c == 0, c == 7)
        for c in range(8):
            mm(uv, wu[:, c, 512 + fl * 128:512 + (fl + 1) * 128], HT[:, c, o0 - 1:o0 + no + 1], c == 0, c == 7)

    def f_b(i):
        gi, fl, f, o0, no, k = fitems[i]
        pg = FFNP[:, f * 4:f * 4 + 4]
        pv_ = FFNP[:, (NFC + f) * 4:(NFC + f) * 4 + 4]
        ug = bank(2 * (i % 3), no + 2)
        uv = bank(2 * (i % 3) + 1, no + 2)
        tg = TMP[i % 3][:, 0:no]
        tv = TMP[3 + i % 3][:, 0:no]
        act(tg, ug[:, 1:1 + no], AF.Identity, bias=pg[:, 3:4], scale=pg[:, 1:2])
        act(tv, uv[:, 1:1 + no], AF.Identity, bias=pv_[:, 3:4], scale=pv_[:, 1:2])
        stt("dve", tg, ug[:, 0:no], pg[:, 0:1], tg, ALU.mult, ALU.add)
        stt("dve", tg, ug[:, 2:2 + no], pg[:, 2:3], tg, ALU.mult, ALU.add)
        stt("dve", tv, uv[:, 0:no], pv_[:, 0:1], tv, ALU.mult, ALU.add)
        stt("dve", tv, uv[:, 2:2 + no], pv_[:, 2:3], tv, ALU.mult, ALU.add)

    def f_c(i):
        gi, fl, f, o0, no, k = fitems[i]
        tg = TMP[i % 3][:, 0:no]
        tv = TMP[3 + i % 3][:, 0:no]
        act(tg, tg, AF.Gelu_apprx_tanh)
        tt("pool", GT[:, f, o0 - 16:o0 - 16 + no], tg, tv, ALU.mult)

    fst = [f_a, f_b, f_c]
    for s_ in range(len(fitems) + 2):
        for k, st in enumerate(fst):
            t = s_ - k
            if 0 <= t < len(fitems):
                st(t)
    while wpend:
        wpend.pop(0)()
    WD1 = v3(BIG8[:, 0:16384], 16)
    load_wdown(range(6, NFC), lambda f: WD1[:, f - 6, :])

    def wd(f, hf):
        if f < 6:
            return WDA[:, f * 1024 + hf * 512:f * 1024 + (hf + 1) * 512]
        return WD1[:, f - 6, hf * 512:(hf + 1) * 512]

    ffn_tiles = [(16 + ti * 128, 128) for ti in range(16)]
    outproj_phase(ffn_tiles, NFC,
                  lambda t, c: GT[:, c, t * 128:(t + 1) * 128],
                  wd,
                  lambda t: xs2[16 + t * 128:16 + (t + 1) * 128, :],
                  lambda t: outd[t * 128:(t + 1) * 128, :],
                  None)

    S.emit(es)
    es.close()
    return nc


def _rope_tables(pos):
    pos = np.asarray(pos)
    inv = (10000.0 ** (-(np.arange(0, 32, 2, dtype=np.float32)) / np.float32(32))).astype(np.float32)
    r = (pos // 64).astype(np.float32)
    c = (pos % 64).astype(np.float32)
    ang_r = r[None, :] * inv[:, None]
    ang_c = c[None, :] * inv[:, None]
    ang = np.concatenate([ang_r, ang_r, ang_c, ang_c], axis=0).astype(np.float32)
    ang = np.concatenate([ang, ang], axis=0)
    return np.stack([np.cos(ang), np.sin(ang)]).astype(np.float32)


def _consts():
    ident = np.eye(128, dtype=np.float32)
    perm = np.zeros((128, 128), np.float32)
    for m in range(128):
        if (m % 32) < 16:
            perm[m + 16, m] = -1.0
        else:
            perm[m - 16, m] = 1.0
    bones = np.zeros((128, 128), np.float32)
    bones[:64, :64] = 1.0
    bones[64:, 64:] = 1.0
    ones = np.ones((128, 128), np.float32)
    return np.ascontiguousarray(np.concatenate([ident, perm, bones, ones], axis=1))


_NC_CACHE = {}


def kernel(x, mem, norm_mix_pre, w_in, conv_dw, conv_dw_b, conv_ln_g, conv_ln_b,
           q_norm_g, k_norm_g, w_out, norm_mix_post, norm_mem_pre, mem_norm_g,
           w_mem_q, w_mem_kv, w_mem_o, norm_mem_post, norm_ffn_pre, w_up, ffn_dw,
           ffn_dw_b, w_down, norm_ffn_post):
    f = lambda a: np.ascontiguousarray(np.asarray(a, dtype=np.float32))
    x = f(x); mem = f(mem)
    B, Sq, D = x.shape

    def fm(g):
        return f(g).reshape(8, 128).T
    gfm = np.ascontiguousarray(np.concatenate([fm(norm_mix_pre[0]), fm(norm_mem_pre[0]), fm(norm_ffn_pre[0]), fm(mem_norm_g[0])], axis=1))
    gpost = np.ascontiguousarray(np.stack([f(norm_mix_post[0]), f(norm_mem_post[0]), f(norm_ffn_post[0])]))
    cw = f(conv_dw[0])
    convp = np.zeros((128, 4, 34), np.float32)
    for ci in range(4):
        convp[:, ci, 0:31] = cw[:, ci * 128:(ci + 1) * 128].T
        convp[:, ci, 31] = f(conv_dw_b[0])[ci * 128:(ci + 1) * 128]
        convp[:, ci, 32] = f(conv_ln_g[0])[ci * 128:(ci + 1) * 128]
        convp[:, ci, 33] = f(conv_ln_b[0])[ci * 128:(ci + 1) * 128]
    convp = np.ascontiguousarray(convp.reshape(128, 136))
    qkg = np.ascontiguousarray(np.stack([np.tile(f(q_norm_g[0]), 2), np.tile(f(k_norm_g[0]), 2)], axis=1))
    fw = f(ffn_dw[0])
    fb = f(ffn_dw_b[0])
    ffnp = np.zeros((128, 44, 4), np.float32)
    for fc in range(44):
        ffnp[:, fc, 0:3] = fw[:, fc * 128:(fc + 1) * 128].T
        ffnp[:, fc, 3] = fb[fc * 128:(fc + 1) * 128]
    ffnp = np.ascontiguousarray(ffnp.reshape(128, 176))
    cst = _consts()
    shared = dict(w_in=f(w_in[0]), w_out=f(w_out[0]), w_mem_q=f(w_mem_q[0]), w_mem_kv=f(w_mem_kv[0]),
                  w_mem_o=f(w_mem_o[0]), w_up=f(w_up[0]), w_down=f(w_down[0]), gfm=gfm, gpost=gpost,
                  convp=convp, qkg=qkg, ffnp=ffnp, cst=cst)
    in_maps = []
    for core in range(8):
        b, j = core // 4, core % 4
        s = j * 2048
        xe = np.zeros((NE, 1024), np.float32)
        lo, hi = max(0, s - 16), min(Sq, s + 2064)
        xe[lo - (s - 16):hi - (s - 16)] = x[b, lo:hi]
        rest_idx = np.concatenate([np.arange(0, s), np.arange(s + 2048, Sq)])
        xr = np.ascontiguousarray(x[b, rest_idx])
        key_pos = np.concatenate([np.arange(s, s + 2048), rest_idx])
        ropek = _rope_tables(key_pos)
        ext_pos = np.clip(np.arange(s - 16, s - 16 + NE), 0, Sq - 1)
        ropeq = _rope_tables(ext_pos)
        mask = np.ones((128, 2), np.float32)
        if j == 0:
            mask[:, 0] = 0.0
        if j == 3:
            mask[:, 1] = 0.0
        m = dict(shared)
        m.update(x_ext=xe, x_rest=xr, mem=np.ascontiguousarray(mem[b]), ropek=ropek, ropeq=ropeq, mask=mask)
        in_maps.append(m)
    if "nc" not in _NC_CACHE:
        _NC_CACHE["nc"] = build_nc()
    nc = _NC_CACHE["nc"]
    res = run_bass_kernel_spmd(nc, in_maps, core_ids=list(range(8)))
    out = np.zeros((B, Sq, D), np.float32)
    for core in range(8):
        b, j = core // 4, core % 4
        out[b, j * 2048:(j + 1) * 2048] = np.asarray(res.results[core]["out"], dtype=np.float32)
    return out
```

```python
import numpy as np
from contextlib import ExitStack
import concourse.bass as bass
import concourse.mybir as mybir
from concourse.bass_utils import run_bass_kernel_spmd

F32 = mybir.dt.float32
BF16 = mybir.dt.bfloat16
AF = mybir.ActivationFunctionType
ALU = mybir.AluOpType

EPS = 1e-6
NE = 2080
E_LO, E_HI = 15, 2065
DFF = 2816
NFC = 22


def _esize(dt):
    return 2 if dt == BF16 else 4


class Sched:
    ENGS = ["pe", "act", "dve", "pool", "sp"]
    NDMA = 16

    def __init__(self, nc, tracked_dram=()):
        self.nc = nc
        self.ops = []
        self.w = {}
        self.r = {}
        self.tracked_dram = set(tracked_dram)

    def region(self, ap):
        t = ap.tensor
        name = t.name
        space = str(ap.space)
        if "DRAM" in space.upper() or "HBM" in space.upper() or type(t).__name__.startswith("DRam"):
            if name not in self.tracked_dram:
                return None
        es = _esize(ap.dtype)
        dims = ap.ap
        ps, pc = dims[0]
        off = int(ap.offset)
        if ps == 0:
            rs_ = int(t.shape[-1])
            p0, f0 = off // rs_, off % rs_
            p1 = p0 + 1
        else:
            p0 = off // ps
            f0 = off % ps
            p1 = p0 + pc
        ents = [(f0, 0)]
        rest = dims[1:]
        for (s, c) in rest[:-1]:
            s = abs(s)
            if s != 0 and len(ents) * c <= 64:
                ents = [(st + i * s, ex) for (st, ex) in ents for i in range(c)]
            else:
                ents = [(st, ex + (c - 1) * s) for (st, ex) in ents]
        if rest:
            s, c = rest[-1]
            ents = [(st, ex + (c - 1) * abs(s) + 1) for (st, ex) in ents]
        else:
            ents = [(st, ex + 1) for (st, ex) in ents]
        ivs = tuple(sorted((st * es, (st + ex) * es) for (st, ex) in ents))
        return (name, p0, p1, ivs)

    @staticmethod
    def _ov(a, b):
        if a[1] >= b[2] or b[1] >= a[2]:
            return False
        for (s0, e0) in a[3]:
            for (s1, e1) in b[3]:
                if s0 < e1 and s1 < e0:
                    return True
        return False

    @staticmethod
    def _covers(a, b):
        if len(a[3]) != 1:
            return a[1] <= b[1] and a[2] >= b[2] and a[3] == b[3]
        if a[1] > b[1] or a[2] < b[2]:
            return False
        s0, e0 = a[3][0]
        return all(s0 <= s1 and e1 <= e0 for (s1, e1) in b[3])

    def add(self, eng, fn, reads=(), writes=(), dma=False):
        op = {"eng": eng, "fn": fn, "deps": set(), "idx": len(self.ops), "inc": False, "dma": dma}
        for ap in reads:
            rg = self.region(ap)
            if rg is None:
                continue
            for key, d in self.w.get(rg[0], {}).items():
                if self._ov(rg, key):
                    op["deps"].update(d.values())
            self.r.setdefault(rg[0], {}).setdefault(rg, {})[eng] = op["idx"]
        for ap in writes:
            rg = self.region(ap)
            if rg is None:
                continue
            for table in (self.w, self.r):
                tb = table.get(rg[0], {})
                dead = []
                for key, d in tb.items():
                    if self._ov(rg, key):
                        op["deps"].update(d.values())
                        if self._covers(rg, key):
                            dead.append(key)
                for k in dead:
                    del tb[k]
            self.w.setdefault(rg[0], {})[rg] = {eng: op["idx"]}
        op["deps"].discard(op["idx"])
        self.ops.append(op)
        return op

    def emit(self, es):
        nc = self.nc
        ops = self.ops
        for op in ops:
            op["deps"] = {d for d in op["deps"] if not (op["eng"] == "pe" and ops[d]["eng"] == "pe" and not ops[d]["dma"])}
            latest = {}
            keep = set()
            for d in op["deps"]:
                p = ops[d]
                if p["dma"]:
                    keep.add(d)
                else:
                    latest[p["eng"]] = max(latest.get(p["eng"], -1), d)
            op["deps"] = keep | set(latest.values())
            for d in op["deps"]:
                ops[d]["inc"] = True
        esem = {e: es.enter_context(nc.semaphore("sem_" + e)) for e in self.ENGS}
        dsem = [es.enter_context(nc.semaphore("dsem%d" % i)) for i in range(self.NDMA)]
        cnt = {e: 0 for e in self.ENGS}
        ndma = 0
        last_out_dma = []
        for op in ops:
            if op["dma"]:
                op["dsem"] = ndma % self.NDMA
                op["dcnt"] = 16 * (ndma // self.NDMA + 1)
                ndma += 1
            elif op["inc"]:
                cnt[op["eng"]] += 1
                op["cnt"] = cnt[op["eng"]]
        self.ndma = ndma
        block = es.enter_context(nc.Block())

        def stream(engname, e):
            seen = {}

            def wait(sem, key, val):
                if seen.get(key, 0) >= val:
                    return
                seen[key] = val
                e.wait_ge(sem, val)

            for op in ops:
                if op["eng"] != engname:
                    continue
                need = {}
                for d in op["deps"]:
                    p = ops[d]
                    if p["dma"]:
                        k = ("d", p["dsem"])
                        need[k] = max(need.get(k, 0), p["dcnt"])
                    else:
                        k = ("e", p["eng"])
                        need[k] = max(need.get(k, 0), p["cnt"])
                if op["dma"] and op["dcnt"] > 16:
                    k = ("d", op["dsem"])
                    need[k] = max(need.get(k, 0), op["dcnt"] - 16)
                for k, v in need.items():
                    wait(dsem[k[1]] if k[0] == "d" else esem[k[1]], k, v)
                ins = op["fn"](e)
                if op["dma"]:
                    ins.then_inc(dsem[op["dsem"]], 16)
                elif op["inc"]:
                    ins.then_inc(esem[op["eng"]], 1)
            if engname == "sp":
                for i in range(min(self.NDMA, ndma)):
                    n_i = (ndma - 1 - i) // self.NDMA + 1
                    wait(dsem[i], ("d", i), 16 * n_i)

        @block.tensor
        def _(e):
            stream("pe", e)

        @block.scalar
        def _(e):
            stream("act", e)

        @block.vector
        def _(e):
            stream("dve", e)

        @block.gpsimd
        def _(e):
            stream("pool", e)

        @block.sync
        def _(e):
            stream("sp", e)


def col_blocks(lo, hi, w=512):
    out = []
    while lo < hi:
        n = min(w, hi - lo)
        out.append((lo, n))
        lo += n
    return out


def build_nc(debug=False):
    nc = bass.Bass("TRN2", target_bir_lowering=False)
    es = ExitStack()

    def di(name, shape, dt=F32):
        return nc.dram_tensor(name, shape, dt, kind="ExternalInput").ap()

    x_ext = di("x_ext", [NE, 1024])
    x_rest = di("x_rest", [6144, 1024])
    memd = di("mem", [256, 1024])
    w_in = di("w_in", [1024, 1792])
    w_out = di("w_out", [1024, 1024])
    w_mem_q = di("w_mem_q", [1024, 1024])
    w_mem_kv = di("w_mem_kv", [1024, 2048])
    w_mem_o = di("w_mem_o", [1024, 1024])
    w_up = di("w_up", [1024, 2 * DFF])
    w_down = di("w_down", [DFF, 1024])
    gfm = di("gfm", [128, 4 * 8])
    gpost = di("gpost", [3, 1024])
    convp = di("convp", [128, 4 * 34])
    qkg = di("qkg", [128, 2])
    ffnp = di("ffnp", [128, 44 * 4])
    cst = di("cst", [128, 512])
    ropek = di("ropek", [2, 128, 8192])
    ropeq = di("ropeq", [2, 128, NE])
    maskd = di("mask", [128, 2])
    outd = nc.dram_tensor("out", [2048, 1024], F32, kind="ExternalOutput").ap()
    xs1 = nc.dram_tensor("xs1", [NE, 1024], F32, kind="Internal").ap()
    xs2 = nc.dram_tensor("xs2", [NE, 1024], F32, kind="Internal").ap()
    rds = nc.dram_tensor("rds", [64, 512], F32, kind="Internal").ap()
    dbg = {}

    S = Sched(nc, tracked_dram=["xs1", "xs2", "out", "rds"])

    def sb(name, shape, dt=F32):
        return es.enter_context(nc.sbuf_tensor(name, shape, dt))

    BIG8 = sb("BIG8", [128, 8 * NE], BF16)
    BIGQ = sb("BIGQ", [128, 8 * NE], BF16)
    G = sb("G", [128, NFC * 2048], BF16)
    STG = [sb("STG%d" % i, [128, 1024]) for i in range(2)]
    XT = [sb("XT%d" % i, [128, 1024]) for i in range(2)]
    YT = [sb("YT%d" % i, [128, 1024]) for i in range(2)]
    HB = [sb("HB%d" % i, [128, 1024], BF16) for i in range(2)]
    TMP = [sb("TMP%d" % i, [128, 512]) for i in range(6)]
    TMPB = [sb("TMPB%d" % i, [128, 512], BF16) for i in range(4)]
    GB = sb("GB", [128, 1024])
    CST = sb("CST", [128, 512], BF16)
    ONESF = sb("ONESF", [128, 64])
    GFM = sb("GFM", [128, 32])
    CONVP = sb("CONVP", [128, 4 * 34])
    QKG = sb("QKG", [128, 2])
    FFNP = sb("FFNP", [128, 44 * 4])
    MASK = sb("MASK", [128, 2])
    STAT = sb("STAT", [128, 32])
    PS = es.enter_context(nc.psum_tensor("PS", [128, 4096], F32))

    IDENT = CST[:, 0:128]
    PERM = CST[:, 128:256]
    BONES = CST[:, 256:384]
    ONES = CST[:, 384:512]

    def bank(b, n=512):
        return PS[:, 512 * b:512 * b + n]

    def bankbf(b):
        return PS[:, 512 * b:512 * b + 512].bitcast(BF16)

    def v3(ap2d, c):
        return ap2d.rearrange("p (c t) -> p c t", c=c)

    HT = v3(BIG8[:, :], 8)
    BQ = v3(BIGQ[:, :], 8)
    AT = BQ[:, 0:4, :]
    QT = BQ[:, 4:8, :]
    KT = G[:, 0:8192]
    VV = G[:, 8192:8192 + 64 * 130].rearrange("p (t g d) -> p t g d", t=64, g=2)
    WREG = G[:, 16512:16512 + 15872]
    WIN = v3(WREG[:, 0:8 * 1792], 8)
    DIAG = WREG[:, 0:15872].rearrange("p (c k m) -> p c k m", c=4, k=31)
    KM = v3(G[:, 32384:34432], 8)
    VM = v3(G[:, 34432:36480], 2)
    MEMT = v3(G[:, 36480:38528], 8)
    WKV = v3(BIGQ[:, 0:16384], 8)
    HTB = [v3(BIGQ[:, i * 4096:(i + 1) * 4096], 8) for i in range(2)]
    WOUT = v3(BIGQ[:, 0:8192], 8)
    WMQ = v3(G[:, 16512 + 6144:16512 + 14336], 8)
    WMO = v3(G[:, 36480:44672], 8)
    WUG0 = v3(G[:, 8192:16384], 8)
    GT = v3(G[:, :], NFC)
    WU = [v3(BIGQ[:, i * 8192:(i + 1) * 8192], 8) for i in range(2)]
    PT = [WREG[:, i * 1536:(i + 1) * 1536] for i in range(4)]
    _rf = G[:, 38528:38528 + 4096].bitcast(F32)
    ROPE = [_rf[:, i * 1024:(i + 1) * 1024] for i in range(2)]
    WDA = BIGQ[:, 0:6144]
    ATT1 = TMPB[3]

    cntr = {"stg": 0, "xt": 0, "yt": 0, "hb": 0, "rope": 0, "tp": 0, "stat": 0}

    def rot(key, n):
        v = cntr[key]
        cntr[key] = (v + 1) % n
        return v

    def stat1(m):
        i = rot("stat", 32)
        return STAT[0:m, i:i + 1]

    def dma(out, in_, q="sp"):
        S.add(q, lambda e: e.dma_start(out=out, in_=in_), reads=[in_], writes=[out], dma=True)

    def mm(out, lhsT, rhs, start, stop):
        S.add("pe", lambda e: e.matmul(out, lhsT, rhs, start=start, stop=stop), reads=[lhsT, rhs], writes=[out])

    def transp(out, in_, ident):
        S.add("pe", lambda e: e.transpose(out, in_, ident), reads=[in_, ident], writes=[out])

    def act(out, in_, func, bias=None, scale=None, accum=None):
        kw = {}
        rd = [in_]
        wr = [out]
        if bias is not None:
            kw["bias"] = bias
            if not isinstance(bias, float):
                rd.append(bias)
        if scale is not None:
            kw["scale"] = scale
            if not isinstance(scale, float):
                rd.append(scale)
        if accum is not None:
            kw["accum_out"] = accum
            wr.append(accum)
        S.add("act", lambda e: e.activation(out=out, in_=in_, func=func, **kw), reads=rd, writes=wr)

    def tt(eng, out, in0, in1, op):
        S.add(eng, lambda e: e.tensor_tensor(out=out, in0=in0, in1=in1, op=op), reads=[in0, in1], writes=[out])

    def ts(eng, out, in0, s1, op0, s2=None, op1=None):
        rd = [in0] + [s for s in (s1, s2) if s is not None and not isinstance(s, float)]
        if op1 is None:
            S.add(eng, lambda e: e.tensor_scalar(out=out, in0=in0, scalar1=s1, scalar2=None, op0=op0), reads=rd, writes=[out])
        else:
            S.add(eng, lambda e: e.tensor_scalar(out=out, in0=in0, scalar1=s1, scalar2=s2, op0=op0, op1=op1), reads=rd, writes=[out])

    def stt(eng, out, in0, scalar, in1, op0, op1, accum=None):
        rd = [in0, in1] + ([] if isinstance(scalar, float) else [scalar])
        if accum is None:
            S.add(eng, lambda e: e.scalar_tensor_tensor(out=out, in0=in0, scalar=scalar, in1=in1, op0=op0, op1=op1), reads=rd, writes=[out])
        else:
            S.add(eng, lambda e: e.scalar_tensor_tensor(out=out, in0=in0, scalar=scalar, in1=in1, op0=op0, op1=op1, accum_out=accum),
                  reads=rd, writes=[out, accum])

    def cp(eng, out, in_):
        if eng == "act":
            act(out, in_, AF.Identity)
        else:
            S.add(eng, lambda e: e.tensor_copy(out=out, in_=in_), reads=[in_], writes=[out])

    def recip(out, in_):
        S.add("dve", lambda e: e.reciprocal(out=out, in_=in_), reads=[in_], writes=[out])

    def memset(eng, ap, val):
        S.add(eng, lambda e: e.memset(ap, val), writes=[ap])

    def rsqrt_from(out, in_, scale, m=None):
        act(out, in_, AF.Ln, bias=EPSB[0:out.shape[0], 0:1] if m is None else EPSB[0:m, 0:1], scale=scale)
        act(out, out, AF.Exp, scale=-0.5)

    def wpiece(dst, srcs, ncols, scal=None, eng="dve", in_view=None, defer=False):
        slot = STG[rot("stg", 2)]
        for (src, p0, p1, c0, nc_) in srcs:
            dma(slot[p0:p1, c0:c0 + nc_], src, q="pool")
        src_ap = slot[:, 0:ncols] if in_view is None else in_view(slot)

        def cast():
            if scal is None:
                cp(eng, dst, src_ap)
            elif eng == "act":
                act(dst, src_ap, AF.Identity, scale=scal)
            else:
                ts(eng, dst, src_ap, scal, ALU.mult)
        if defer:
            return cast
        cast()

    EPSB = sb("EPSB", [128, 1])
    memset("pool", EPSB[:, :], EPS)
    memset("pool", ONESF[:, :], 1.0)
    wpiece(CST[:, :], [(cst[:, :], 0, 128, 0, 512)], 512, None, eng="dve")
    dma(GFM[:, :], gfm[:, :])
    dma(CONVP[:, :], convp[:, :])
    dma(QKG[:, :], qkg[:, :])
    dma(FFNP[:, :], ffnp[:, :])
    dma(MASK[:, :], maskd[:, :])
    memset("pool", VV[:, :, :, 64:65], 1.0)

    def load_weight_rows(dst3, src, ncols_total, gcol, col_pieces=None, rows_of_chunk=None, nchunks=8):
        for c in range(nchunks):
            for (c0, ncol) in (col_pieces or col_blocks(0, ncols_total, 1024)):
                if rows_of_chunk is None:
                    srcs = [(src[c * 128:(c + 1) * 128, c0:c0 + ncol], 0, 128, 0, ncol)]
                else:
                    srcs = [(src[r0:r0 + nr, c0:c0 + ncol], p0, p0 + nr, 0, ncol) for (r0, nr, p0) in rows_of_chunk(c)]
                scal = None if gcol is None else GFM[:, gcol * 8 + c:gcol * 8 + c + 1]
                wpiece(dst3[:, c, c0:c0 + ncol], srcs, ncol, scal)

    def norm_to_T(src_tile, m, dst, evac_eng):
        ss = stat1(m)
        hb = HB[rot("hb", 2)]
        act(hb[0:m, :], src_tile, AF.Square, accum=ss)
        rs = stat1(m)
        rsqrt_from(rs, ss, 1.0 / 1024.0, m)
        ts("dve", hb[0:m, :], src_tile, rs, ALU.mult)
        tb = 6 + rot("tp", 2)
        tpv = v3(bankbf(tb), 8)
        for c in range(8):
            transp(tpv[:, c, 0:m], hb[0:m, c * 128:(c + 1) * 128], IDENT[0:m, 0:m])
        cp(evac_eng, dst, tpv[:, :, 0:m])

    def nr1(src, n, gain, b_ss, b_rot):
        xg = TMPB[0][:, 0:n]
        sq = TMPB[1][:, 0:n]
        act(xg, src, AF.Identity, scale=gain)
        act(sq, src, AF.Square)
        mm(bank(b_ss, n), BONES, sq, True, True)
        mm(bank(b_rot, n), PERM, xg, True, True)

    def nr2(n, cos, sin, b_ss, b_rot):
        xg = TMPB[0][:, 0:n]
        rs = TMP[0][:, 0:n]
        act(rs, bank(b_ss, n), AF.Ln, bias=EPSB[:, 0:1], scale=1.0 / 64.0)
        act(rs, rs, AF.Exp, scale=-0.5)
        tt("dve", TMP[1][:, 0:n], xg, cos, ALU.mult)
        tt("dve", TMP[2][:, 0:n], bank(b_rot, n), sin, ALU.mult)

    def nr3(n, dst):
        t1 = TMP[1][:, 0:n]
        tt("pool", t1, t1, TMP[2][:, 0:n], ALU.add)
        tt("pool", dst, t1, TMP[0][:, 0:n], ALU.mult)

    def normrope(src, n, gain, cos, sin, dst, b_ss, b_rot):
        nr1(src, n, gain, b_ss, b_rot)
        nr2(n, cos, sin, b_ss, b_rot)
        nr3(n, dst)

    def phase_c0():
      for mt in range(2):
          xt = XT[rot("xt", 2)]
          dma(xt[:, :], memd[mt * 128:(mt + 1) * 128, :])
          norm_to_T(xt[:, :], 128, MEMT[:, :, mt * 128:(mt + 1) * 128], "dve")
      for oc in range(8):
          pb = bank(oc % 2, 256)
          for c in range(8):
              mm(pb, WKV[:, c, oc * 128:(oc + 1) * 128], MEMT[:, c, :], c == 0, c == 7)
          cp("act", KM[:, oc, :], pb)
      for mt in range(2):
          for hf in range(2):
              pb = bank(2 + (mt * 2 + hf) % 2)
              for c in range(8):
                  mm(pb, MEMT[:, c, mt * 128:(mt + 1) * 128], WKV[:, c, 1024 + hf * 512:1024 + (hf + 1) * 512], c == 0, c == 7)
              cp("dve", VM[:, mt, hf * 512:(hf + 1) * 512], pb)

    def load_win_chunk(c):
        sc = GFM[:, c:c + 1]
        wpiece(WIN[:, c, 0:1024], [(w_in[c * 128:(c + 1) * 128, 0:1024], 0, 128, 0, 1024)], 1024, sc)
        slot_view = lambda slot: slot[:, 0:512].rearrange("p (h j d) -> p j h d", h=2, j=4)
        wpiece(WIN[:, c, 1024:1536].rearrange("p (j h d) -> p j h d", j=4, h=2),
               [(w_in[c * 128:(c + 1) * 128, 1024:1792], 0, 128, 0, 768)], 512, sc, in_view=slot_view)
        last = STG[(cntr["stg"] + 1) % 2]
        ts("dve", WIN[:, c, 1536:1792], last[:, 512:768], sc, ALU.mult)

    XQ = [XT[0], XT[1], YT[0], YT[1]]

    def tile_load(t, src, m):
        dma(XQ[t % 4][0:m, :], src)

    JUNKA = sb("JUNKA", [128, 1024], BF16)
    JUNKD = sb("JUNKD", [128, 1024], BF16)
    tstat = {}

    def tile_f1(t, m):
        xt = XQ[t % 4]
        ss = stat1(m)
        tstat[t] = ss
        if t % 2 == 0:
            act(JUNKA[0:m, :], xt[0:m, :], AF.Square, accum=ss)
        else:
            stt("dve", JUNKD[0:m, :], xt[0:m, :], 1.0, xt[0:m, :], ALU.mult, ALU.mult, accum=ss)

    def tile_f2(t, m):
        xt = XQ[t % 4]
        hb = HB[t % 2]
        rs = stat1(m)
        rsqrt_from(rs, tstat[t], 1.0 / 1024.0, m)
        ts("dve", hb[0:m, :], xt[0:m, :], rs, ALU.mult)

    def tile_back(t, m, dst, evac_eng):
        hb = HB[t % 2]
        tpv = v3(bankbf(6 + t % 2), 8)
        for c in range(8):
            transp(tpv[:, c, 0:m], hb[0:m, c * 128:(c + 1) * 128], IDENT[0:m, 0:m])
        cp(evac_eng, dst, tpv[:, :, 0:m])

    def kv_job(hsrc, kb):
        i = kb
        rp_ = ROPE[i % 2]
        kp = bank(i % 2)
        vp = bank(2 + i % 2)

        def P():
            dma(rp_[:, 0:512], ropek[0, :, kb * 512:(kb + 1) * 512])
            dma(rp_[:, 512:1024], ropek[1, :, kb * 512:(kb + 1) * 512])
            for c in range(8):
                mm(kp, WIN[:, c, 1536:1664], hsrc[:, c, :], c == 0, c == 7)
            for t in range(4):
                for c in range(8):
                    mm(vp[:, t * 128:(t + 1) * 128], hsrc[:, c, t * 128:(t + 1) * 128], WIN[:, c, 1664:1792], c == 0, c == 7)

        def N1():
            nr1(kp, 512, QKG[:, 1:2], 4, 5)
            cp("act", VV[:, kb * 4:kb * 4 + 4, :, 0:64], vp.rearrange("p (t g d) -> p t g d", t=4, g=2))

        def N2():
            nr2(512, rp_[:, 0:512], rp_[:, 512:1024], 4, 5)

        def N3():
            nr3(512, KT[:, kb * 512:(kb + 1) * 512])
        return [P, N1, N2, N3]

    def run_tiles(tiles, jobs_ready, extra=None):
        active = []

        def step_jobs(k):
            for _ in range(k):
                if active:
                    active[0].pop(0)()
                    if not active[0]:
                        active.pop(0)
        for t0 in range(min(3, len(tiles))):
            tile_load(t0, tiles[t0][0], tiles[t0][1])
        nt = len(tiles)
        tile_f1(0, tiles[0][1])
        if nt > 1:
            tile_f1(1, tiles[1][1])
        tile_f2(0, tiles[0][1])
        for t in range(nt):
            if t + 3 < nt:
                tile_load(t + 3, tiles[t + 3][0], tiles[t + 3][1])
            if t + 2 < nt:
                tile_f1(t + 2, tiles[t + 2][1])
            if t + 1 < nt:
                tile_f2(t + 1, tiles[t + 1][1])
            tile_back(t, tiles[t][1], tiles[t][2], "act" if t % 2 else "dve")
            if extra is not None:
                extra(t)
            for job in jobs_ready.get(t, []):
                active.append(job)
            step_jobs(2 if len(active) > 1 else 1)
        while active:
            step_jobs(1)

    ext_tiles = col_blocks(0, NE, 128)
    tiles = [(x_ext[e0:e0 + m, :], m, HT[:, :, e0:e0 + m]) for (e0, m) in ext_tiles]
    jobs = {4 * kb + 4: [kv_job(HT[:, :, 16 + kb * 512:16 + (kb + 1) * 512], kb)] for kb in range(4)}
    kv_pieces = [(c, c0) for c in range(8) for c0 in (0, 1024)]
    cpend = []

    def ext_extra(t):
        if t < 4:
            load_win_chunk(2 * t)
            load_win_chunk(2 * t + 1)
            return
        while cpend:
            cpend.pop(0)()
        for _ in range(2):
            if kv_pieces:
                c, c0 = kv_pieces.pop(0)
                cpend.append(wpiece(WKV[:, c, c0:c0 + 1024], [(w_mem_kv[c * 128:(c + 1) * 128, c0:c0 + 1024], 0, 128, 0, 1024)],
                                    1024, GFM[:, 24 + c:24 + c + 1], defer=True))

    run_tiles(tiles, jobs, extra=ext_extra)
    while cpend:
        cpend.pop(0)()
    assert not kv_pieces
    phase_c0()
    tiles = []
    jobs = {}
    for rb in range(12):
        for t in range(4):
            r0 = rb * 512 + t * 128
            tiles.append((x_rest[r0:r0 + 128, :], 128, HTB[rb % 2][:, :, t * 128:(t + 1) * 128]))
        jobs[4 * rb + 3] = [kv_job(HTB[rb % 2], 4 + rb)]
    run_tiles(tiles, jobs)

    qitems = [(bk, e0, n, j) for bk, (e0, n) in enumerate(col_blocks(E_LO, E_HI)) for j in range(4)]

    def q_a(i):
        bk, e0, n, j = qitems[i]
        rp_ = ROPE[bk % 2]
        if j == 0:
            dma(rp_[:, 0:n], ropeq[0, :, e0:e0 + n])
            dma(rp_[:, 512:512 + n], ropeq[1, :, e0:e0 + n])
        qp = bank(6 + i % 2, n)
        for c in range(8):
            mm(qp, WIN[:, c, 1024 + j * 128:1024 + (j + 1) * 128], HT[:, c, e0:e0 + n], c == 0, c == 7)

    def q_n(i):
        bk, e0, n, j = qitems[i]
        rp_ = ROPE[bk % 2]
        normrope(bank(6 + i % 2, n), n, QKG[:, 0:1], rp_[:, 0:n], rp_[:, 512:512 + n], QT[:, j, e0:e0 + n], 4, 5)

    q_a(0)
    for i in range(len(qitems)):
        if i + 1 < len(qitems):
            q_a(i + 1)
        q_n(i)
    bi = 0
    for (e0, n) in col_blocks(0, NE):
        for ci in range(4):
            av = bank(2 * (bi % 2), n)
            ag = bank(2 * (bi % 2) + 1, n)
            bi += 1
            for c in range(8):
                mm(av, WIN[:, c, ci * 128:(ci + 1) * 128], HT[:, c, e0:e0 + n], c == 0, c == 7)
            for c in range(8):
                mm(ag, WIN[:, c, 512 + ci * 128:512 + (ci + 1) * 128], HT[:, c, e0:e0 + n], c == 0, c == 7)
            sg = TMP[3 + bi % 2][:, 0:n]
            act(sg, ag, AF.Sigmoid)
            tt("dve", AT[:, ci, e0:e0 + n], av, sg, ALU.mult)

    for ci in range(4):
        for k in range(31):
            ts("dve", DIAG[:, ci, k, :], IDENT, CONVP[:, ci * 34 + k:ci * 34 + k + 1], ALU.mult)
    cblocks = col_blocks(E_LO, E_HI)

    def cv_bank(cbi, ci):
        return ([0, 1, 2, 3] if cbi % 2 == 0 else [6, 7, 2, 3])[ci]

    def conv_chunk(cbi, ci):
        e0, n = cblocks[cbi]
        cv = bank(cv_bank(cbi, ci), n)
        for k in range(31):
            mm(cv, DIAG[:, ci, k, :], AT[:, ci, e0 + k - 15:e0 + k - 15 + n], k == 0, k == 30)
        bcol = CONVP[:, ci * 34 + 31:ci * 34 + 32]
        act(TMPB[ci % 2][:, 0:n], cv, AF.Identity, bias=bcol)
        act(TMPB[2 + ci % 2][:, 0:n], cv, AF.Square, bias=bcol)

    def conv_stats(cbi, ci):
        e0, n = cblocks[cbi]
        mm(bank(4, n), ONES, TMPB[ci % 2][:, 0:n], ci == 0, ci == 3)
        mm(bank(5, n), ONES, TMPB[2 + ci % 2][:, 0:n], ci == 0, ci == 3)

    def conv_tail(cbi):
        e0, n = cblocks[cbi]
        sm = bank(4, n)
        sq_ = bank(5, n)
        mean = TMP[0][:, 0:n]
        ts("dve", mean, sm, 1.0 / 512.0, ALU.mult)
        msq = TMP[1][:, 0:n]
        tt("dve", msq, mean, mean, ALU.mult)
        var = TMP[2][:, 0:n]
        stt("dve", var, sq_, 1.0 / 512.0, msq, ALU.mult, ALU.subtract)
        act(var, var, AF.Ln, bias=EPSB[:, 0:1], scale=1.0)
        act(var, var, AF.Exp, scale=-0.5)
        for ci in range(4):
            cv = bank(cv_bank(cbi, ci), n)
            bcol = CONVP[:, ci * 34 + 31:ci * 34 + 32]
            t1 = TMP[3 + ci % 2][:, 0:n]
            stt("dve", t1, cv, bcol, mean, ALU.add, ALU.subtract)
            tt("pool", t1, t1, var, ALU.mult)
            act(HT[:, ci, e0:e0 + n], t1, AF.Silu, bias=CONVP[:, ci * 34 + 33:ci * 34 + 34],
                scale=CONVP[:, ci * 34 + 32:ci * 34 + 33])

    for cbi in range(len(cblocks)):
        conv_chunk(cbi, 0)
        if cbi > 0:
            conv_tail(cbi - 1)
        conv_stats(cbi, 0)
        for ci in range(1, 4):
            conv_chunk(cbi, ci)
            conv_stats(cbi, ci)
    conv_tail(len(cblocks) - 1)

    def wout_rows(c):
        if c < 4:
            return [(c * 128, 128, 0)]
        j = c - 4
        return [(512 + 64 * j, 64, 0), (512 + 64 * (4 + j), 64, 64)]
    load_weight_rows(WOUT, w_out, 1024, None, rows_of_chunk=wout_rows)
    dma(GB[:, :], gpost[0:1, :].partition_broadcast(128))
    load_weight_rows(WMQ, w_mem_q, 1024, 1)
    load_weight_rows(WMO, w_mem_o, 1024, None)

    PTX = [WREG[:, i * 2048:(i + 1) * 2048] for i in range(2)]
    PTY = [WREG[:, 4096 + i * 1024:4096 + (i + 1) * 1024] for i in range(2)]
    groups = [(j, e0, n) for j in range(4) for (e0, n) in col_blocks(E_LO, E_HI)]
    batches = []
    nxy = {"X": 0, "Y": 0}
    for gi in range(len(groups)):
        kt = 0
        turn = "X"
        while kt < 64:
            if turn == "X" and kt + 2 <= 64:
                kts = [kt, kt + 1]
                kind = "X"
            else:
                kts = [kt]
                kind = "Y"
            kt += len(kts)
            batches.append((gi, kts, kind, nxy[kind], kt == 64))
            nxy[kind] += 1
            turn = "Y" if turn == "X" else "X"

    def sbank(kind, i, h):
        return (2 * i + h) if kind == "X" else (4 + h)

    def emit_qk(bn):
        gi, kts, kind, ser, last = batches[bn]
        j, e0, n = groups[gi]
        for i, kt in enumerate(kts):
            for h in range(2):
                mm(bank(sbank(kind, i, h), n), KT[64 * h:64 * h + 64, kt * 128:(kt + 1) * 128],
                   QT[64 * h:64 * h + 64, j, e0:e0 + n], True, True)

    def emit_exp(bn):
        gi, kts, kind, ser, last = batches[bn]
        j, e0, n = groups[gi]
        nit = 2 * len(kts)
        b0 = 0 if kind == "X" else 4
        pt = (PTX if kind == "X" else PTY)[ser % 2]
        sv = PS[:, 512 * b0:512 * (b0 + nit)].rearrange("p (b t) -> p b t", b=nit)[:, :, 0:n]
        pv = pt[:, 0:512 * nit].rearrange("p (b t) -> p b t", b=nit)[:, :, 0:n]
        act(pv, sv, AF.Exp, scale=0.125)

    def emit_pv(bn):
        gi, kts, kind, ser, last = batches[bn]
        j, e0, n = groups[gi]
        pt = (PTX if kind == "X" else PTY)[ser % 2]
        for i, kt in enumerate(kts):
            for h in range(2):
                it = 2 * i + h
                mm(bank(6 + h, n)[0:65, :], VV[:, kt, h, 0:65], pt[:, 512 * it:512 * it + n], kt == 0, kt == 63)
        if last:
            for h in range(2):
                cp("dve", TMP[h][0:65, 0:n], bank(6 + h, n)[0:65, :])
            for h in range(2):
                recip(TMP[2 + h][64:65, 0:n], TMP[h][64:65, 0:n])
            for h in range(2):
                osb = TMP[h][0:65, 0:n]
                rd = TMP[2 + h][64:65, 0:n]
                rdb = TMP[4 + h][0:64, 0:n]
                row = 2 * gi + h
                dma(rds[row:row + 1, 0:n], rd)
                dma(rdb, rds[row:row + 1, 0:n].partition_broadcast(64))
                if h == 0:
                    tt("dve", HT[0:64, 4 + j, e0:e0 + n], osb[0:64, :], rdb, ALU.mult)
                else:
                    a1_ = ATT1[0:64, 0:n]
                    tt("dve", a1_, osb[0:64, :], rdb, ALU.mult)
                    dma(HT[64:128, 4 + j, e0:e0 + n], a1_)

    pend = {"X": 0, "Y": 0}
    nq = [0]

    def try_qk():
        while nq[0] < len(batches) and pend[batches[nq[0]][2]] == 0:
            emit_qk(nq[0])
            pend[batches[nq[0]][2]] += 1
            nq[0] += 1

    try_qk()
    for bn in range(len(batches)):
        emit_exp(bn)
        pend[batches[bn][2]] -= 1
        try_qk()
        emit_pv(bn)

    fgroups = [(f0, min(4, NFC - f0)) for f0 in range(0, NFC, 4)]
    fblocks = []
    o0 = 16
    while o0 < 2064:
        no = min(510, 2064 - o0)
        fblocks.append((o0, no))
        o0 += no

    def load_wdown(chunks, dst_fn):
        for f in chunks:
            wpiece(dst_fn(f), [(w_down[f * 128:(f + 1) * 128, :], 0, 128, 0, 1024)], 1024, None)

    def wup_piece(gi, c, defer=False):
        f0, nf = fgroups[gi]
        wu = WUG0 if gi == 0 else WU[gi % 2]
        ncol = nf * 128
        srcs = [(w_up[c * 128:(c + 1) * 128, f0 * 128:f0 * 128 + ncol], 0, 128, 0, ncol),
                (w_up[c * 128:(c + 1) * 128, DFF + f0 * 128:DFF + f0 * 128 + ncol], 0, 128, 512, ncol)]
        sc = GFM[:, 16 + c:16 + c + 1]
        if nf == 4:
            return wpiece(wu[:, c, :], srcs, 1024, sc, defer=defer)
        slot_view = lambda slot: slot[:, :].rearrange("p (a t) -> p a t", a=2)[:, :, 0:ncol]
        return wpiece(wu[:, c, :].rearrange("p (a t) -> p a t", a=2)[:, :, 0:ncol], srcs, 1024, sc, in_view=slot_view, defer=defer)

    def load_wup_group(gi):
        for c in range(8):
            wup_piece(gi, c)

    g0pend = []

    def b3_extra(it):
        while g0pend:
            g0pend.pop(0)()
        if it < 8:
            g0pend.append(wup_piece(0, it, defer=True))

    def outproj_phase(tiles, nk, lhs_fn, rhs_fn, xsrc_fn, xdst_fn, hdst_fn, extra=None):
        def ybuf(t):
            m = tiles[t][1]
            return PS[0:m, 1024 * (t % 2):1024 * (t % 2) + 1024]

        def st_a(t):
            y = ybuf(t)
            dma(XT[t % 2][0:tiles[t][1], :], xsrc_fn(t))
            for hf in range(2):
                for c in range(nk):
                    mm(y[:, hf * 512:(hf + 1) * 512], lhs_fn(t, c), rhs_fn(c, hf), c == 0, c == nk - 1)

        def st_b1(t):
            m = tiles[t][1]
            y = ybuf(t)
            xt = XT[t % 2]
            ss = stat1(m)
            yt = YT[t % 2]
            act(yt[0:m, :], y, AF.Square, accum=ss)
            rs = stat1(m)
            rsqrt_from(rs, ss, 1.0 / 1024.0, m)
            stt("dve", yt[0:m, :], y, rs, GB[0:m, :], ALU.mult, ALU.mult)
            tt("pool", yt[0:m, :], yt[0:m, :], xt[0:m, :], ALU.add)
            dma(xdst_fn(t), yt[0:m, :])

        def st_b2(t):
            if hdst_fn is None:
                return
            m = tiles[t][1]
            yt = YT[t % 2]
            hb = HB[t % 2]
            ss = stat1(m)
            act(hb[0:m, :], yt[0:m, :], AF.Square, accum=ss)
            rs = stat1(m)
            rsqrt_from(rs, ss, 1.0 / 1024.0, m)
            ts("dve", hb[0:m, :], yt[0:m, :], rs, ALU.mult)

        def st_c(t):
            if hdst_fn is None:
                return
            tile_back(t, tiles[t][1], hdst_fn(t), "act" if t % 2 else "dve")

        stages = [st_a, st_b1, st_b2, st_c]
        for s_ in range(len(tiles) + len(stages) - 1):
            for k, st in enumerate(stages):
                t = s_ - k
                if 0 <= t < len(tiles):
                    st(t)
            if extra is not None:
                extra(s_)

    tok_tiles = col_blocks(E_LO, E_HI, 128)

    def ht_tile(t):
        e0, m = tok_tiles[t]
        return HT[:, :, e0:e0 + m]

    outproj_phase(tok_tiles, 8,
                  lambda t, c: HT[:, c, tok_tiles[t][0]:tok_tiles[t][0] + tok_tiles[t][1]],
                  lambda c, hf: WOUT[:, c, hf * 512:(hf + 1) * 512],
                  lambda t: x_ext[tok_tiles[t][0]:tok_tiles[t][0] + tok_tiles[t][1], :],
                  lambda t: xs1[tok_tiles[t][0]:tok_tiles[t][0] + tok_tiles[t][1], :],
                  ht_tile, extra=b3_extra)
    while g0pend:
        g0pend.pop(0)()

    dma(GB[:, :], gpost[1:2, :].partition_broadcast(128))
    qi = 0
    for (e0, n) in col_blocks(E_LO, E_HI):
        for oc in range(8):
            qb = bank(qi % 2, n)
            qi += 1
            for c in range(8):
                mm(qb, WMQ[:, c, oc * 128:(oc + 1) * 128], HT[:, c, e0:e0 + n], c == 0, c == 7)
            cp("act", BQ[:, oc, e0:e0 + n], qb)
        for hd in range(4):
            for mt in range(2):
                sbk = bank(2 + mt, n)
                for dc in range(2):
                    mm(sbk, KM[:, 2 * hd + dc, mt * 128:(mt + 1) * 128], BQ[:, 2 * hd + dc, e0:e0 + n], dc == 0, dc == 1)
            pt = PTX[hd % 2]
            sv = PS[:, 1024:2048].rearrange("p (b t) -> p b t", b=2)[:, :, 0:n]
            pv = pt[:, 0:1024].rearrange("p (b t) -> p b t", b=2)[:, :, 0:n]
            act(pv, sv, AF.Exp, scale=1.0 / 16.0)
            den = bank(4, n)
            for mt in range(2):
                mm(den, ONES, pt[:, 512 * mt:512 * mt + n], mt == 0, mt == 1)
            rd = TMP[hd % 2][:, 0:n]
            act(rd, den, AF.Ln)
            act(rd, rd, AF.Exp, scale=-1.0)
            for dc in range(2):
                ob = bank(5 + dc, n)
                for mt in range(2):
                    mm(ob, VM[:, mt, (2 * hd + dc) * 128:(2 * hd + dc + 1) * 128], pt[:, 512 * mt:512 * mt + n], mt == 0, mt == 1)
                tt("dve", HT[:, 2 * hd + dc, e0:e0 + n], ob, rd, ALU.mult)
    outproj_phase(tok_tiles, 8,
                  lambda t, c: HT[:, c, tok_tiles[t][0]:tok_tiles[t][0] + tok_tiles[t][1]],
                  lambda c, hf: WMO[:, c, hf * 512:(hf + 1) * 512],
                  lambda t: xs1[tok_tiles[t][0]:tok_tiles[t][0] + tok_tiles[t][1], :],
                  lambda t: xs2[tok_tiles[t][0]:tok_tiles[t][0] + tok_tiles[t][1], :],
                  ht_tile)
    ts("dve", HT[:, :, 15:16], HT[:, :, 15:16], MASK[:, 0:1], ALU.mult)
    ts("dve", HT[:, :, 2064:2065], HT[:, :, 2064:2065], MASK[:, 1:2], ALU.mult)

    dma(GB[:, :], gpost[2:3, :].partition_broadcast(128))
    fitems = []
    for gi, (f0, nf) in enumerate(fgroups):
        for fl in range(nf):
            for bi_, (o0, no) in enumerate(fblocks):
                fitems.append((gi, fl, f0 + fl, o0, no, fl * len(fblocks) + bi_))
    wpend = []

    def f_a(i):
        gi, fl, f, o0, no, k = fitems[i]
        while wpend:
            wpend.pop(0)()
        if gi + 1 < len(fgroups) and k < 8:
            wpend.append(wup_piece(gi + 1, k, defer=True))
        if gi == len(fgroups) - 1 and k < 6:
            wpend.append(wpiece(WDA[:, k * 1024:(k + 1) * 1024], [(w_down[k * 128:(k + 1) * 128, :], 0, 128, 0, 1024)],
                                1024, None, defer=True))
        wu = WUG0 if gi == 0 else WU[gi % 2]
        ug = bank(2 * (i % 3), no + 2)
        uv = bank(2 * (i % 3) + 1, no + 2)
        for c in range(8):
            mm(ug, wu[:, c, fl * 128:(fl + 1) * 128], HT[:, c, o0 - 1:o0 + no + 1], c == 0, c == 7)
        for c in range(8):
            mm(uv, wu[:, c, 512 + fl * 128:512 + (fl + 1) * 128], HT[:, c, o0 - 1:o0 + no + 1], c == 0, c == 7)

    def f_b(i):
        gi, fl, f, o0, no, k = fitems[i]
        pg = FFNP[:, f * 4:f * 4 + 4]
        pv_ = FFNP[:, (NFC + f) * 4:(NFC + f) * 4 + 4]
        ug = bank(2 * (i % 3), no + 2)
        uv = bank(2 * (i % 3) + 1, no + 2)
        tg = TMP[i % 3][:, 0:no]
        tv = TMP[3 + i % 3][:, 0:no]
        act(tg, ug[:, 1:1 + no], AF.Identity, bias=pg[:, 3:4], scale=pg[:, 1:2])
        act(tv, uv[:, 1:1 + no], AF.Identity, bias=pv_[:, 3:4], scale=pv_[:, 1:2])
        stt("dve", tg, ug[:, 0:no], pg[:, 0:1], tg, ALU.mult, ALU.add)
        stt("dve", tg, ug[:, 2:2 + no], pg[:, 2:3], tg, ALU.mult, ALU.add)
        stt("dve", tv, uv[:, 0:no], pv_[:, 0:1], tv, ALU.mult, ALU.add)
        stt("dve", tv, uv[:, 2:2 + no], pv_[:, 2:3], tv, ALU.mult, ALU.add)

    def f_c(i):
        gi, fl, f, o0, no, k = fitems[i]
        tg = TMP[i % 3][:, 0:no]
        tv = TMP[3 + i % 3][:, 0:no]
        act(tg, tg, AF.Gelu_apprx_tanh)
        tt("pool", GT[:, f, o0 - 16:o0 - 16 + no], tg, tv, ALU.mult)

    fst = [f_a, f_b, f_c]
    for s_ in range(len(fitems) + 2):
        for k, st in enumerate(fst):
            t = s_ - k
            if 0 <= t < len(fitems):
                st(t)
    while wpend:
        wpend.pop(0)()
    WD1 = v3(BIG8[:, 0:16384], 16)
    load_wdown(range(6, NFC), lambda f: WD1[:, f - 6, :])

    def wd(f, hf):
        if f < 6:
            return WDA[:, f * 1024 + hf * 512:f * 1024 + (hf + 1) * 512]
        return WD1[:, f - 6, hf * 512:(hf + 1) * 512]

    ffn_tiles = [(16 + ti * 128, 128) for ti in range(16)]
    outproj_phase(ffn_tiles, NFC,
                  lambda t, c: GT[:, c, t * 128:(t + 1) * 128],
                  wd,
                  lambda t: xs2[16 + t * 128:16 + (t + 1) * 128, :],
                  lambda t: outd[t * 128:(t + 1) * 128, :],
                  None)

    S.emit(es)
    es.close()
    return nc


def _rope_tables(pos):
    pos = np.asarray(pos)
    inv = (10000.0 ** (-(np.arange(0, 32, 2, dtype=np.float32)) / np.float32(32))).astype(np.float32)
    r = (pos // 64).astype(np.float32)
    c = (pos % 64).astype(np.float32)
    ang_r = r[None, :] * inv[:, None]
    ang_c = c[None, :] * inv[:, None]
    ang = np.concatenate([ang_r, ang_r, ang_c, ang_c], axis=0).astype(np.float32)
    ang = np.concatenate([ang, ang], axis=0)
    return np.stack([np.cos(ang), np.sin(ang)]).astype(np.float32)


def _consts():
    ident = np.eye(128, dtype=np.float32)
    perm = np.zeros((128, 128), np.float32)
    for m in range(128):
        if (m % 32) < 16:
            perm[m + 16, m] = -1.0
        else:
            perm[m - 16, m] = 1.0
    bones = np.zeros((128, 128), np.float32)
    bones[:64, :64] = 1.0
    bones[64:, 64:] = 1.0
    ones = np.ones((128, 128), np.float32)
    return np.ascontiguousarray(np.concatenate([ident, perm, bones, ones], axis=1))


_NC_CACHE = {}


def kernel(x, mem, norm_mix_pre, w_in, conv_dw, conv_dw_b, conv_ln_g, conv_ln_b,
           q_norm_g, k_norm_g, w_out, norm_mix_post, norm_mem_pre, mem_norm_g,
           w_mem_q, w_mem_kv, w_mem_o, norm_mem_post, norm_ffn_pre, w_up, ffn_dw,
           ffn_dw_b, w_down, norm_ffn_post):
    f = lambda a: np.ascontiguousarray(np.asarray(a, dtype=np.float32))
    x = f(x); mem = f(mem)
    B, Sq, D = x.shape

    def fm(g):
        return f(g).reshape(8, 128).T
    gfm = np.ascontiguousarray(np.concatenate([fm(norm_mix_pre[0]), fm(norm_mem_pre[0]), fm(norm_ffn_pre[0]), fm(mem_norm_g[0])], axis=1))
    gpost = np.ascontiguousarray(np.stack([f(norm_mix_post[0]), f(norm_mem_post[0]), f(norm_ffn_post[0])]))
    cw = f(conv_dw[0])
    convp = np.zeros((128, 4, 34), np.float32)
    for ci in range(4):
        convp[:, ci, 0:31] = cw[:, ci * 128:(ci + 1) * 128].T
        convp[:, ci, 31] = f(conv_dw_b[0])[ci * 128:(ci + 1) * 128]
        convp[:, ci, 32] = f(conv_ln_g[0])[ci * 128:(ci + 1) * 128]
        convp[:, ci, 33] = f(conv_ln_b[0])[ci * 128:(ci + 1) * 128]
    convp = np.ascontiguousarray(convp.reshape(128, 136))
    qkg = np.ascontiguousarray(np.stack([np.tile(f(q_norm_g[0]), 2), np.tile(f(k_norm_g[0]), 2)], axis=1))
    fw = f(ffn_dw[0])
    fb = f(ffn_dw_b[0])
    ffnp = np.zeros((128, 44, 4), np.float32)
    for fc in range(44):
        ffnp[:, fc, 0:3] = fw[:, fc * 128:(fc + 1) * 128].T
        ffnp[:, fc, 3] = fb[fc * 128:(fc + 1) * 128]
    ffnp = np.ascontiguousarray(ffnp.reshape(128, 176))
    cst = _consts()
    shared = dict(w_in=f(w_in[0]), w_out=f(w_out[0]), w_mem_q=f(w_mem_q[0]), w_mem_kv=f(w_mem_kv[0]),
                  w_mem_o=f(w_mem_o[0]), w_up=f(w_up[0]), w_down=f(w_down[0]), gfm=gfm, gpost=gpost,
                  convp=convp, qkg=qkg, ffnp=ffnp, cst=cst)
    in_maps = []
    for core in range(8):
        b, j = core // 4, core % 4
        s = j * 2048
        xe = np.zeros((NE, 1024), np.float32)
        lo, hi = max(0, s - 16), min(Sq, s + 2064)
        xe[lo - (s - 16):hi - (s - 16)] = x[b, lo:hi]
        rest_idx = np.concatenate([np.arange(0, s), np.arange(s + 2048, Sq)])
        xr = np.ascontiguousarray(x[b, rest_idx])
        key_pos = np.concatenate([np.arange(s, s + 2048), rest_idx])
        ropek = _rope_tables(key_pos)
        ext_pos = np.clip(np.arange(s - 16, s - 16 + NE), 0, Sq - 1)
        ropeq = _rope_tables(ext_pos)
        mask = np.ones((128, 2), np.float32)
        if j == 0:
            mask[:, 0] = 0.0
        if j == 3:
            mask[:, 1] = 0.0
        m = dict(shared)
        m.update(x_ext=xe, x_rest=xr, mem=np.ascontiguousarray(mem[b]), ropek=ropek, ropeq=ropeq, mask=mask)
        in_maps.append(m)
    if "nc" not in _NC_CACHE:
        _NC_CACHE["nc"] = build_nc()
    nc = _NC_CACHE["nc"]
    res = run_bass_kernel_spmd(nc, in_maps, core_ids=list(range(8)))
    out = np.zeros((B, Sq, D), np.float32)
    for core in range(8):
        b, j = core // 4, core % 4
        out[b, j * 2048:(j + 1) * 2048] = np.asarray(res.results[core]["out"], dtype=np.float32)
    return out
```

```python
import numpy as np
from contextlib import ExitStack
import concourse.bass as bass
import concourse.mybir as mybir
from concourse.bass_utils import run_bass_kernel_spmd

F32 = mybir.dt.float32
BF16 = mybir.dt.bfloat16
AF = mybir.ActivationFunctionType
ALU = mybir.AluOpType

EPS = 1e-6
NE = 2080
E_LO, E_HI = 15, 2065
DFF = 2816
NFC = 22


def _esize(dt):
    return 2 if dt == BF16 else 4


class Sched:
    ENGS = ["pe", "act", "dve", "pool", "sp"]
    NDMA = 16

    def __init__(self, nc, tracked_dram=()):
        self.nc = nc
        self.ops = []
        self.w = {}
        self.r = {}
        self.tracked_dram = set(tracked_dram)

    def region(self, ap):
        t = ap.tensor
        name = t.name
        space = str(ap.space)
        if "DRAM" in space.upper() or "HBM" in space.upper() or type(t).__name__.startswith("DRam"):
            if name not in self.tracked_dram:
                return None
        es = _esize(ap.dtype)
        dims = ap.ap
        ps, pc = dims[0]
        off = int(ap.offset)
        if ps == 0:
            rs_ = int(t.shape[-1])
            p0, f0 = off // rs_, off % rs_
            p1 = p0 + 1
        else:
            p0 = off // ps
            f0 = off % ps
            p1 = p0 + pc
        ents = [(f0, 0)]
        rest = dims[1:]
        for (s, c) in rest[:-1]:
            s = abs(s)
            if s != 0 and len(ents) * c <= 64:
                ents = [(st + i * s, ex) for (st, ex) in ents for i in range(c)]
            else:
                ents = [(st, ex + (c - 1) * s) for (st, ex) in ents]
        if rest:
            s, c = rest[-1]
            ents = [(st, ex + (c - 1) * abs(s) + 1) for (st, ex) in ents]
        else:
            ents = [(st, ex + 1) for (st, ex) in ents]
        ivs = tuple(sorted((st * es, (st + ex) * es) for (st, ex) in ents))
        return (name, p0, p1, ivs)

    @staticmethod
    def _ov(a, b):
        if a[1] >= b[2] or b[1] >= a[2]:
            return False
        for (s0, e0) in a[3]:
            for (s1, e1) in b[3]:
                if s0 < e1 and s1 < e0:
                    return True
        return False

    @staticmethod
    def _covers(a, b):
        if len(a[3]) != 1:
            return a[1] <= b[1] and a[2] >= b[2] and a[3] == b[3]
        if a[1] > b[1] or a[2] < b[2]:
            return False
        s0, e0 = a[3][0]
        return all(s0 <= s1 and e1 <= e0 for (s1, e1) in b[3])

    def add(self, eng, fn, reads=(), writes=(), dma=False):
        op = {"eng": eng, "fn": fn, "deps": set(), "idx": len(self.ops), "inc": False, "dma": dma}
        for ap in reads:
            rg = self.region(ap)
            if rg is None:
                continue
            for key, d in self.w.get(rg[0], {}).items():
                if self._ov(rg, key):
                    op["deps"].update(d.values())
            self.r.setdefault(rg[0], {}).setdefault(rg, {})[eng] = op["idx"]
        for ap in writes:
            rg = self.region(ap)
            if rg is None:
                continue
            for table in (self.w, self.r):
                tb = table.get(rg[0], {})
                dead = []
                for key, d in tb.items():
                    if self._ov(rg, key):
                        op["deps"].update(d.values())
                        if self._covers(rg, key):
                            dead.append(key)
                for k in dead:
                    del tb[k]
            self.w.setdefault(rg[0], {})[rg] = {eng: op["idx"]}
        op["deps"].discard(op["idx"])
        self.ops.append(op)
        return op

    def emit(self, es):
        nc = self.nc
        ops = self.ops
        for op in ops:
            op["deps"] = {d for d in op["deps"] if not (op["eng"] == "pe" and ops[d]["eng"] == "pe" and not ops[d]["dma"])}
            latest = {}
            keep = set()
            for d in op["deps"]:
                p = ops[d]
                if p["dma"]:
                    keep.add(d)
                else:
                    latest[p["eng"]] = max(latest.get(p["eng"], -1), d)
            op["deps"] = keep | set(latest.values())
            for d in op["deps"]:
                ops[d]["inc"] = True
        esem = {e: es.enter_context(nc.semaphore("sem_" + e)) for e in self.ENGS}
        dsem = [es.enter_context(nc.semaphore("dsem%d" % i)) for i in range(self.NDMA)]
        cnt = {e: 0 for e in self.ENGS}
        ndma = 0
        last_out_dma = []
        for op in ops:
            if op["dma"]:
                op["dsem"] = ndma % self.NDMA
                op["dcnt"] = 16 * (ndma // self.NDMA + 1)
                ndma += 1
            elif op["inc"]:
                cnt[op["eng"]] += 1
                op["cnt"] = cnt[op["eng"]]
        self.ndma = ndma
        block = es.enter_context(nc.Block())

        def stream(engname, e):
            seen = {}

            def wait(sem, key, val):
                if seen.get(key, 0) >= val:
                    return
                seen[key] = val
                e.wait_ge(sem, val)

            for op in ops:
                if op["eng"] != engname:
                    continue
                need = {}
                for d in op["deps"]:
                    p = ops[d]
                    if p["dma"]:
                        k = ("d", p["dsem"])
                        need[k] = max(need.get(k, 0), p["dcnt"])
                    else:
                        k = ("e", p["eng"])
                        need[k] = max(need.get(k, 0), p["cnt"])
                if op["dma"] and op["dcnt"] > 16:
                    k = ("d", op["dsem"])
                    need[k] = max(need.get(k, 0), op["dcnt"] - 16)
                for k, v in need.items():
                    wait(dsem[k[1]] if k[0] == "d" else esem[k[1]], k, v)
                ins = op["fn"](e)
                if op["dma"]:
                    ins.then_inc(dsem[op["dsem"]], 16)
                elif op["inc"]:
                    ins.then_inc(esem[op["eng"]], 1)
            if engname == "sp":
                for i in range(min(self.NDMA, ndma)):
                    n_i = (ndma - 1 - i) // self.NDMA + 1
                    wait(dsem[i], ("d", i), 16 * n_i)

        @block.tensor
        def _(e):
            stream("pe", e)

        @block.scalar
        def _(e):
            stream("act", e)

        @block.vector
        def _(e):
            stream("dve", e)

        @block.gpsimd
        def _(e):
            stream("pool", e)

        @block.sync
        def _(e):
            stream("sp", e)


def col_blocks(lo, hi, w=512):
    out = []
    while lo < hi:
        n = min(w, hi - lo)
        out.append((lo, n))
        lo += n
    return out


def build_nc(debug=False):
    nc = bass.Bass("TRN2", target_bir_lowering=False)
    es = ExitStack()

    def di(name, shape, dt=F32):
        return nc.dram_tensor(name, shape, dt, kind="ExternalInput").ap()

    x_ext = di("x_ext", [NE, 1024])
    x_rest = di("x_rest", [6144, 1024])
    memd = di("mem", [256, 1024])
    w_in = di("w_in", [1024, 1792])
    w_out = di("w_out", [1024, 1024])
    w_mem_q = di("w_mem_q", [1024, 1024])
    w_mem_kv = di("w_mem_kv", [1024, 2048])
    w_mem_o = di("w_mem_o", [1024, 1024])
    w_up = di("w_up", [1024, 2 * DFF])
    w_down = di("w_down", [DFF, 1024])
    gfm = di("gfm", [128, 4 * 8])
    gpost = di("gpost", [3, 1024])
    convp = di("convp", [128, 4 * 34])
    qkg = di("qkg", [128, 2])
    ffnp = di("ffnp", [128, 44 * 4])
    cst = di("cst", [128, 512])
    ropek = di("ropek", [2, 128, 8192])
    ropeq = di("ropeq", [2, 128, NE])
    maskd = di("mask", [128, 2])
    outd = nc.dram_tensor("out", [2048, 1024], F32, kind="ExternalOutput").ap()
    xs1 = nc.dram_tensor("xs1", [NE, 1024], F32, kind="Internal").ap()
    xs2 = nc.dram_tensor("xs2", [NE, 1024], F32, kind="Internal").ap()
    rds = nc.dram_tensor("rds", [64, 512], F32, kind="Internal").ap()
    dbg = {}

    S = Sched(nc, tracked_dram=["xs1", "xs2", "out", "rds"])

    def sb(name, shape, dt=F32):
        return es.enter_context(nc.sbuf_tensor(name, shape, dt))

    BIG8 = sb("BIG8", [128, 8 * NE], BF16)
    BIGQ = sb("BIGQ", [128, 8 * NE], BF16)
    G = sb("G", [128, NFC * 2048], BF16)
    STG = [sb("STG%d" % i, [128, 1024]) for i in range(2)]
    XT = [sb("XT%d" % i, [128, 1024]) for i in range(2)]
    YT = [sb("YT%d" % i, [128, 1024]) for i in range(2)]
    HB = [sb("HB%d" % i, [128, 1024], BF16) for i in range(2)]
    TMP = [sb("TMP%d" % i, [128, 512]) for i in range(6)]
    TMPB = [sb("TMPB%d" % i, [128, 512], BF16) for i in range(4)]
    GB = sb("GB", [128, 1024])
    CST = sb("CST", [128, 512], BF16)
    ONESF = sb("ONESF", [128, 64])
    GFM = sb("GFM", [128, 32])
    CONVP = sb("CONVP", [128, 4 * 34])
    QKG = sb("QKG", [128, 2])
    FFNP = sb("FFNP", [128, 44 * 4])
    MASK = sb("MASK", [128, 2])
    STAT = sb("STAT", [128, 32])
    PS = es.enter_context(nc.psum_tensor("PS", [128, 4096], F32))

    IDENT = CST[:, 0:128]
    PERM = CST[:, 128:256]
    BONES = CST[:, 256:384]
    ONES = CST[:, 384:512]

    def bank(b, n=512):
        return PS[:, 512 * b:512 * b + n]

    def bankbf(b):
        return PS[:, 512 * b:512 * b + 512].bitcast(BF16)

    def v3(ap2d, c):
        return ap2d.rearrange("p (c t) -> p c t", c=c)

    HT = v3(BIG8[:, :], 8)
    BQ = v3(BIGQ[:, :], 8)
    AT = BQ[:, 0:4, :]
    QT = BQ[:, 4:8, :]
    KT = G[:, 0:8192]
    VV = G[:, 8192:8192 + 64 * 130].rearrange("p (t g d) -> p t g d", t=64, g=2)
    WREG = G[:, 16512:16512 + 15872]
    WIN = v3(WREG[:, 0:8 * 1792], 8)
    DIAG = WREG[:, 0:15872].rearrange("p (c k m) -> p c k m", c=4, k=31)
    KM = v3(G[:, 32384:34432], 8)
    VM = v3(G[:, 34432:36480], 2)
    MEMT = v3(G[:, 36480:38528], 8)
    WKV = v3(BIGQ[:, 0:16384], 8)
    HTB = [v3(BIGQ[:, i * 4096:(i + 1) * 4096], 8) for i in range(2)]
    WOUT = v3(BIGQ[:, 0:8192], 8)
    WMQ = v3(G[:, 16512 + 6144:16512 + 14336], 8)
    WMO = v3(G[:, 36480:44672], 8)
    WUG0 = v3(G[:, 8192:16384], 8)
    GT = v3(G[:, :], NFC)
    WU = [v3(BIGQ[:, i * 8192:(i + 1) * 8192], 8) for i in range(2)]
    PT = [WREG[:, i * 1536:(i + 1) * 1536] for i in range(4)]
    _rf = G[:, 38528:38528 + 4096].bitcast(F32)
    ROPE = [_rf[:, i * 1024:(i + 1) * 1024] for i in range(2)]
    WDA = BIGQ[:, 0:6144]
    ATT1 = TMPB[3]

    cntr = {"stg": 0, "xt": 0, "yt": 0, "hb": 0, "rope": 0, "tp": 0, "stat": 0}

    def rot(key, n):
        v = cntr[key]
        cntr[key] = (v + 1) % n
        return v

    def stat1(m):
        i = rot("stat", 32)
        return STAT[0:m, i:i + 1]

    def dma(out, in_, q="sp"):
        S.add(q, lambda e: e.dma_start(out=out, in_=in_), reads=[in_], writes=[out], dma=True)

    def mm(out, lhsT, rhs, start, stop):
        S.add("pe", lambda e: e.matmul(out, lhsT, rhs, start=start, stop=stop), reads=[lhsT, rhs], writes=[out])

    def transp(out, in_, ident):
        S.add("pe", lambda e: e.transpose(out, in_, ident), reads=[in_, ident], writes=[out])

    def act(out, in_, func, bias=None, scale=None, accum=None):
        kw = {}
        rd = [in_]
        wr = [out]
        if bias is not None:
            kw["bias"] = bias
            if not isinstance(bias, float):
                rd.append(bias)
        if scale is not None:
            kw["scale"] = scale
            if not isinstance(scale, float):
                rd.append(scale)
        if accum is not None:
            kw["accum_out"] = accum
            wr.append(accum)
        S.add("act", lambda e: e.activation(out=out, in_=in_, func=func, **kw), reads=rd, writes=wr)

    def tt(eng, out, in0, in1, op):
        S.add(eng, lambda e: e.tensor_tensor(out=out, in0=in0, in1=in1, op=op), reads=[in0, in1], writes=[out])

    def ts(eng, out, in0, s1, op0, s2=None, op1=None):
        rd = [in0] + [s for s in (s1, s2) if s is not None and not isinstance(s, float)]
        if op1 is None:
            S.add(eng, lambda e: e.tensor_scalar(out=out, in0=in0, scalar1=s1, scalar2=None, op0=op0), reads=rd, writes=[out])
        else:
            S.add(eng, lambda e: e.tensor_scalar(out=out, in0=in0, scalar1=s1, scalar2=s2, op0=op0, op1=op1), reads=rd, writes=[out])

    def stt(eng, out, in0, scalar, in1, op0, op1, accum=None):
        rd = [in0, in1] + ([] if isinstance(scalar, float) else [scalar])
        if accum is None:
            S.add(eng, lambda e: e.scalar_tensor_tensor(out=out, in0=in0, scalar=scalar, in1=in1, op0=op0, op1=op1), reads=rd, writes=[out])
        else:
            S.add(eng, lambda e: e.scalar_tensor_tensor(out=out, in0=in0, scalar=scalar, in1=in1, op0=op0, op1=op1, accum_out=accum),
                  reads=rd, writes=[out, accum])

    def cp(eng, out, in_):
        if eng == "act":
            act(out, in_, AF.Identity)
        else:
            S.add(eng, lambda e: e.tensor_copy(out=out, in_=in_), reads=[in_], writes=[out])

    def recip(out, in_):
        S.add("dve", lambda e: e.reciprocal(out=out, in_=in_), reads=[in_], writes=[out])

    def memset(eng, ap, val):
        S.add(eng, lambda e: e.memset(ap, val), writes=[ap])

    def rsqrt_from(out, in_, scale, m=None):
        act(out, in_, AF.Ln, bias=EPSB[0:out.shape[0], 0:1] if m is None else EPSB[0:m, 0:1], scale=scale)
        act(out, out, AF.Exp, scale=-0.5)

    def wpiece(dst, srcs, ncols, scal=None, eng="dve", in_view=None, defer=False):
        slot = STG[rot("stg", 2)]
        for (src, p0, p1, c0, nc_) in srcs:
            dma(slot[p0:p1, c0:c0 + nc_], src, q="pool")
        src_ap = slot[:, 0:ncols] if in_view is None else in_view(slot)

        def cast():
            if scal is None:
                cp(eng, dst, src_ap)
            elif eng == "act":
                act(dst, src_ap, AF.Identity, scale=scal)
            else:
                ts(eng, dst, src_ap, scal, ALU.mult)
        if defer:
            return cast
        cast()

    EPSB = sb("EPSB", [128, 1])
    memset("pool", EPSB[:, :], EPS)
    memset("pool", ONESF[:, :], 1.0)
    wpiece(CST[:, :], [(cst[:, :], 0, 128, 0, 512)], 512, None, eng="dve")
    dma(GFM[:, :], gfm[:, :])
    dma(CONVP[:, :], convp[:, :])
    dma(QKG[:, :], qkg[:, :])
    dma(FFNP[:, :], ffnp[:, :])
    dma(MASK[:, :], maskd[:, :])
    memset("pool", VV[:, :, :, 64:65], 1.0)

    def load_weight_rows(dst3, src, ncols_total, gcol, col_pieces=None, rows_of_chunk=None, nchunks=8):
        for c in range(nchunks):
            for (c0, ncol) in (col_pieces or col_blocks(0, ncols_total, 1024)):
                if rows_of_chunk is None:
                    srcs = [(src[c * 128:(c + 1) * 128, c0:c0 + ncol], 0, 128, 0, ncol)]
                else:
                    srcs = [(src[r0:r0 + nr, c0:c0 + ncol], p0, p0 + nr, 0, ncol) for (r0, nr, p0) in rows_of_chunk(c)]
                scal = None if gcol is None else GFM[:, gcol * 8 + c:gcol * 8 + c + 1]
                wpiece(dst3[:, c, c0:c0 + ncol], srcs, ncol, scal)

    def norm_to_T(src_tile, m, dst, evac_eng):
        ss = stat1(m)
        hb = HB[rot("hb", 2)]
        act(hb[0:m, :], src_tile, AF.Square, accum=ss)
        rs = stat1(m)
        rsqrt_from(rs, ss, 1.0 / 1024.0, m)
        ts("dve", hb[0:m, :], src_tile, rs, ALU.mult)
        tb = 6 + rot("tp", 2)
        tpv = v3(bankbf(tb), 8)
        for c in range(8):
            transp(tpv[:, c, 0:m], hb[0:m, c * 128:(c + 1) * 128], IDENT[0:m, 0:m])
        cp(evac_eng, dst, tpv[:, :, 0:m])

    def nr1(src, n, gain, b_ss, b_rot):
        xg = TMPB[0][:, 0:n]
        sq = TMPB[1][:, 0:n]
        act(xg, src, AF.Identity, scale=gain)
        act(sq, src, AF.Square)
        mm(bank(b_ss, n), BONES, sq, True, True)
        mm(bank(b_rot, n), PERM, xg, True, True)

    def nr2(n, cos, sin, b_ss, b_rot):
        xg = TMPB[0][:, 0:n]
        rs = TMP[0][:, 0:n]
        act(rs, bank(b_ss, n), AF.Ln, bias=EPSB[:, 0:1], scale=1.0 / 64.0)
        act(rs, rs, AF.Exp, scale=-0.5)
        tt("dve", TMP[1][:, 0:n], xg, cos, ALU.mult)
        tt("dve", TMP[2][:, 0:n], bank(b_rot, n), sin, ALU.mult)

    def nr3(n, dst):
        t1 = TMP[1][:, 0:n]
        tt("pool", t1, t1, TMP[2][:, 0:n], ALU.add)
        tt("pool", dst, t1, TMP[0][:, 0:n], ALU.mult)

    def normrope(src, n, gain, cos, sin, dst, b_ss, b_rot):
        nr1(src, n, gain, b_ss, b_rot)
        nr2(n, cos, sin, b_ss, b_rot)
        nr3(n, dst)

    def phase_c0():
      for mt in range(2):
          xt = XT[rot("xt", 2)]
          dma(xt[:, :], memd[mt * 128:(mt + 1) * 128, :])
          norm_to_T(xt[:, :], 128, MEMT[:, :, mt * 128:(mt + 1) * 128], "dve")
      for oc in range(8):
          pb = bank(oc % 2, 256)
          for c in range(8):
              mm(pb, WKV[:, c, oc * 128:(oc + 1) * 128], MEMT[:, c, :], c == 0, c == 7)
          cp("act", KM[:, oc, :], pb)
      for mt in range(2):
          for hf in range(2):
              pb = bank(2 + (mt * 2 + hf) % 2)
              for c in range(8):
                  mm(pb, MEMT[:, c, mt * 128:(mt + 1) * 128], WKV[:, c, 1024 + hf * 512:1024 + (hf + 1) * 512], c == 0, c == 7)
              cp("dve", VM[:, mt, hf * 512:(hf + 1) * 512], pb)

    def load_win_chunk(c):
        sc = GFM[:, c:c + 1]
        wpiece(WIN[:, c, 0:1024], [(w_in[c * 128:(c + 1) * 128, 0:1024], 0, 128, 0, 1024)], 1024, sc)
        slot_view = lambda slot: slot[:, 0:512].rearrange("p (h j d) -> p j h d", h=2, j=4)
        wpiece(WIN[:, c, 1024:1536].rearrange("p (j h d) -> p j h d", j=4, h=2),
               [(w_in[c * 128:(c + 1) * 128, 1024:1792], 0, 128, 0, 768)], 512, sc, in_view=slot_view)
        last = STG[(cntr["stg"] + 1) % 2]
        ts("dve", WIN[:, c, 1536:1792], last[:, 512:768], sc, ALU.mult)

    XQ = [XT[0], XT[1], YT[0], YT[1]]

    def tile_load(t, src, m):
        dma(XQ[t % 4][0:m, :], src)

    JUNKA = sb("JUNKA", [128, 1024], BF16)
    JUNKD = sb("JUNKD", [128, 1024], BF16)
    tstat = {}

    def tile_f1(t, m):
        xt = XQ[t % 4]
        ss = stat1(m)
        tstat[t] = ss
        if t % 2 == 0:
            act(JUNKA[0:m, :], xt[0:m, :], AF.Square, accum=ss)
        else:
            stt("dve", JUNKD[0:m, :], xt[0:m, :], 1.0, xt[0:m, :], ALU.mult, ALU.mult, accum=ss)

    def tile_f2(t, m):
        xt = XQ[t % 4]
        hb = HB[t % 2]
        rs = stat1(m)
        rsqrt_from(rs, tstat[t], 1.0 / 1024.0, m)
        ts("dve", hb[0:m, :], xt[0:m, :], rs, ALU.mult)

    def tile_back(t, m, dst, evac_eng):
        hb = HB[t % 2]
        tpv = v3(bankbf(6 + t % 2), 8)
        for c in range(8):
            transp(tpv[:, c, 0:m], hb[0:m, c * 128:(c + 1) * 128], IDENT[0:m, 0:m])
        cp(evac_eng, dst, tpv[:, :, 0:m])

    def kv_job(hsrc, kb):
        i = kb
        rp_ = ROPE[i % 2]
        kp = bank(i % 2)
        vp = bank(2 + i % 2)

        def P():
            dma(rp_[:, 0:512], ropek[0, :, kb * 512:(kb + 1) * 512])
            dma(rp_[:, 512:1024], ropek[1, :, kb * 512:(kb + 1) * 512])
            for c in range(8):
                mm(kp, WIN[:, c, 1536:1664], hsrc[:, c, :], c == 0, c == 7)
            for t in range(4):
                for c in range(8):
                    mm(vp[:, t * 128:(t + 1) * 128], hsrc[:, c, t * 128:(t + 1) * 128], WIN[:, c, 1664:1792], c == 0, c == 7)

        def N1():
            nr1(kp, 512, QKG[:, 1:2], 4, 5)
            cp("act", VV[:, kb * 4:kb * 4 + 4, :, 0:64], vp.rearrange("p (t g d) -> p t g d", t=4, g=2))

        def N2():
            nr2(512, rp_[:, 0:512], rp_[:, 512:1024], 4, 5)

        def N3():
            nr3(512, KT[:, kb * 512:(kb + 1) * 512])
        return [P, N1, N2, N3]

    def run_tiles(tiles, jobs_ready, extra=None):
        active = []

        def step_jobs(k):
            for _ in range(k):
                if active:
                    active[0].pop(0)()
                    if not active[0]:
                        active.pop(0)
        for t0 in range(min(3, len(tiles))):
            tile_load(t0, tiles[t0][0], tiles[t0][1])
        nt = len(tiles)
        tile_f1(0, tiles[0][1])
        if nt > 1:
            tile_f1(1, tiles[1][1])
        tile_f2(0, tiles[0][1])
        for t in range(nt):
            if t + 3 < nt:
                tile_load(t + 3, tiles[t + 3][0], tiles[t + 3][1])
            if t + 2 < nt:
                tile_f1(t + 2, tiles[t + 2][1])
            if t + 1 < nt:
                tile_f2(t + 1, tiles[t + 1][1])
            tile_back(t, tiles[t][1], tiles[t][2], "act" if t % 2 else "dve")
            if extra is not None:
                extra(t)
            for job in jobs_ready.get(t, []):
                active.append(job)
            step_jobs(2 if len(active) > 1 else 1)
        while active:
            step_jobs(1)

    ext_tiles = col_blocks(0, NE, 128)
    tiles = [(x_ext[e0:e0 + m, :], m, HT[:, :, e0:e0 + m]) for (e0, m) in ext_tiles]
    jobs = {4 * kb + 4: [kv_job(HT[:, :, 16 + kb * 512:16 + (kb + 1) * 512], kb)] for kb in range(4)}
    kv_pieces = [(c, c0) for c in range(8) for c0 in (0, 1024)]
    cpend = []

    def ext_extra(t):
        if t < 4:
            load_win_chunk(2 * t)
            load_win_chunk(2 * t + 1)
            return
        while cpend:
            cpend.pop(0)()
        for _ in range(2):
            if kv_pieces:
                c, c0 = kv_pieces.pop(0)
                cpend.append(wpiece(WKV[:, c, c0:c0 + 1024], [(w_mem_kv[c * 128:(c + 1) * 128, c0:c0 + 1024], 0, 128, 0, 1024)],
                                    1024, GFM[:, 24 + c:24 + c + 1], defer=True))

    run_tiles(tiles, jobs, extra=ext_extra)
    while cpend:
        cpend.pop(0)()
    assert not kv_pieces
    phase_c0()
    tiles = []
    jobs = {}
    for rb in range(12):
        for t in range(4):
            r0 = rb * 512 + t * 128
            tiles.append((x_rest[r0:r0 + 128, :], 128, HTB[rb % 2][:, :, t * 128:(t + 1) * 128]))
        jobs[4 * rb + 3] = [kv_job(HTB[rb % 2], 4 + rb)]
    run_tiles(tiles, jobs)

    qitems = [(bk, e0, n, j) for bk, (e0, n) in enumerate(col_blocks(E_LO, E_HI)) for j in range(4)]

    def q_a(i):
        bk, e0, n, j = qitems[i]
        rp_ = ROPE[bk % 2]
        if j == 0:
            dma(rp_[:, 0:n], ropeq[0, :, e0:e0 + n])
            dma(rp_[:, 512:512 + n], ropeq[1, :, e0:e0 + n])
        qp = bank(6 + i % 2, n)
        for c in range(8):
            mm(qp, WIN[:, c, 1024 + j * 128:1024 + (j + 1) * 128], HT[:, c, e0:e0 + n], c == 0, c == 7)

    def q_n(i):
        bk, e0, n, j = qitems[i]
        rp_ = ROPE[bk % 2]
        normrope(bank(6 + i % 2, n), n, QKG[:, 0:1], rp_[:, 0:n], rp_[:, 512:512 + n], QT[:, j, e0:e0 + n], 4, 5)

    q_a(0)
    for i in range(len(qitems)):
        if i + 1 < len(qitems):
            q_a(i + 1)
        q_n(i)
    bi = 0
    for (e0, n) in col_blocks(0, NE):
        for ci in range(4):
            av = bank(2 * (bi % 2), n)
            ag = bank(2 * (bi % 2) + 1, n)
            bi += 1
            for c in range(8):
                mm(av, WIN[:, c, ci * 128:(ci + 1) * 128], HT[:, c, e0:e0 + n], c == 0, c == 7)
            for c in range(8):
                mm(ag, WIN[:, c, 512 + ci * 128:512 + (ci + 1) * 128], HT[:, c, e0:e0 + n], c == 0, c == 7)
            sg = TMP[3 + bi % 2][:, 0:n]
            act(sg, ag, AF.Sigmoid)
            tt("dve", AT[:, ci, e0:e0 + n], av, sg, ALU.mult)

    for ci in range(4):
        for k in range(31):
            ts("dve", DIAG[:, ci, k, :], IDENT, CONVP[:, ci * 34 + k:ci * 34 + k + 1], ALU.mult)
    cblocks = col_blocks(E_LO, E_HI)

    def cv_bank(cbi, ci):
        return ([0, 1, 2, 3] if cbi % 2 == 0 else [6, 7, 2, 3])[ci]

    def conv_chunk(cbi, ci):
        e0, n = cblocks[cbi]
        cv = bank(cv_bank(cbi, ci), n)
        for k in range(31):
            mm(cv, DIAG[:, ci, k, :], AT[:, ci, e0 + k - 15:e0 + k - 15 + n], k == 0, k == 30)
        bcol = CONVP[:, ci * 34 + 31:ci * 34 + 32]
        act(TMPB[ci % 2][:, 0:n], cv, AF.Identity, bias=bcol)
        act(TMPB[2 + ci % 2][:, 0:n], cv, AF.Square, bias=bcol)

    def conv_stats(cbi, ci):
        e0, n = cblocks[cbi]
        mm(bank(4, n), ONES, TMPB[ci % 2][:, 0:n], ci == 0, ci == 3)
        mm(bank(5, n), ONES, TMPB[2 + ci % 2][:, 0:n], ci == 0, ci == 3)

    def conv_tail(cbi):
        e0, n = cblocks[cbi]
        sm = bank(4, n)
        sq_ = bank(5, n)
        mean = TMP[0][:, 0:n]
        ts("dve", mean, sm, 1.0 / 512.0, ALU.mult)
        msq = TMP[1][:, 0:n]
        tt("dve", msq, mean, mean, ALU.mult)
        var = TMP[2][:, 0:n]
        stt("dve", var, sq_, 1.0 / 512.0, msq, ALU.mult, ALU.subtract)
        act(var, var, AF.Ln, bias=EPSB[:, 0:1], scale=1.0)
        act(var, var, AF.Exp, scale=-0.5)
        for ci in range(4):
            cv = bank(cv_bank(cbi, ci), n)
            bcol = CONVP[:, ci * 34 + 31:ci * 34 + 32]
            t1 = TMP[3 + ci % 2][:, 0:n]
            stt("dve", t1, cv, bcol, mean, ALU.add, ALU.subtract)
            tt("pool", t1, t1, var, ALU.mult)
            act(HT[:, ci, e0:e0 + n], t1, AF.Silu, bias=CONVP[:, ci * 34 + 33:ci * 34 + 34],
                scale=CONVP[:, ci * 34 + 32:ci * 34 + 33])

    for cbi in range(len(cblocks)):
        conv_chunk(cbi, 0)
        if cbi > 0:
            conv_tail(cbi - 1)
        conv_stats(cbi, 0)
        for ci in range(1, 4):
            conv_chunk(cbi, ci)
            conv_stats(cbi, ci)
    conv_tail(len(cblocks) - 1)

    def wout_rows(c):
        if c < 4:
            return [(c * 128, 128, 0)]
        j = c - 4
        return [(512 + 64 * j, 64, 0), (512 + 64 * (4 + j), 64, 64)]
    load_weight_rows(WOUT, w_out, 1024, None, rows_of_chunk=wout_rows)
    dma(GB[:, :], gpost[0:1, :].partition_broadcast(128))
    load_weight_rows(WMQ, w_mem_q, 1024, 1)
    load_weight_rows(WMO, w_mem_o, 1024, None)

    PTX = [WREG[:, i * 2048:(i + 1) * 2048] for i in range(2)]
    PTY = [WREG[:, 4096 + i * 1024:4096 + (i + 1) * 1024] for i in range(2)]
    groups = [(j, e0, n) for j in range(4) for (e0, n) in col_blocks(E_LO, E_HI)]
    batches = []
    nxy = {"X": 0, "Y": 0}
    for gi in range(len(groups)):
        kt = 0
        turn = "X"
        while kt < 64:
            if turn == "X" and kt + 2 <= 64:
                kts = [kt, kt + 1]
                kind = "X"
            else:
                kts = [kt]
                kind = "Y"
            kt += len(kts)
            batches.append((gi, kts, kind, nxy[kind], kt == 64))
            nxy[kind] += 1
            turn = "Y" if turn == "X" else "X"

    def sbank(kind, i, h):
        return (2 * i + h) if kind == "X" else (4 + h)

    def emit_qk(bn):
        gi, kts, kind, ser, last = batches[bn]
        j, e0, n = groups[gi]
        for i, kt in enumerate(kts):
            for h in range(2):
                mm(bank(sbank(kind, i, h), n), KT[64 * h:64 * h + 64, kt * 128:(kt + 1) * 128],
                   QT[64 * h:64 * h + 64, j, e0:e0 + n], True, True)

    def emit_exp(bn):
        gi, kts, kind, ser, last = batches[bn]
        j, e0, n = groups[gi]
        nit = 2 * len(kts)
        b0 = 0 if kind == "X" else 4
        pt = (PTX if kind == "X" else PTY)[ser % 2]
        sv = PS[:, 512 * b0:512 * (b0 + nit)].rearrange("p (b t) -> p b t", b=nit)[:, :, 0:n]
        pv = pt[:, 0:512 * nit].rearrange("p (b t) -> p b t", b=nit)[:, :, 0:n]
        act(pv, sv, AF.Exp, scale=0.125)

    def emit_pv(bn):
        gi, kts, kind, ser, last = batches[bn]
        j, e0, n = groups[gi]
        pt = (PTX if kind == "X" else PTY)[ser % 2]
        for i, kt in enumerate(kts):
            for h in range(2):
                it = 2 * i + h
                mm(bank(6 + h, n)[0:65, :], VV[:, kt, h, 0:65], pt[:, 512 * it:512 * it + n], kt == 0, kt == 63)
        if last:
            for h in range(2):
                cp("dve", TMP[h][0:65, 0:n], bank(6 + h, n)[0:65, :])
            for h in range(2):
                recip(TMP[2 + h][64:65, 0:n], TMP[h][64:65, 0:n])
            for h in range(2):
                osb = TMP[h][0:65, 0:n]
                rd = TMP[2 + h][64:65, 0:n]
                rdb = TMP[4 + h][0:64, 0:n]
                row = 2 * gi + h
                dma(rds[row:row + 1, 0:n], rd)
                dma(rdb, rds[row:row + 1, 0:n].partition_broadcast(64))
                if h == 0:
                    tt("dve", HT[0:64, 4 + j, e0:e0 + n], osb[0:64, :], rdb, ALU.mult)
                else:
                    a1_ = ATT1[0:64, 0:n]
                    tt("dve", a1_, osb[0:64, :], rdb, ALU.mult)
                    dma(HT[64:128, 4 + j, e0:e0 + n], a1_)

    pend = {"X": 0, "Y": 0}
    nq = [0]

    def try_qk():
        while nq[0] < len(batches) and pend[batches[nq[0]][2]] == 0:
            emit_qk(nq[0])
            pend[batches[nq[0]][2]] += 1
            nq[0] += 1

    try_qk()
    for bn in range(len(batches)):
        emit_exp(bn)
        pend[batches[bn][2]] -= 1
        try_qk()
        emit_pv(bn)

    fgroups = [(f0, min(4, NFC - f0)) for f0 in range(0, NFC, 4)]
    fblocks = []
    o0 = 16
    while o0 < 2064:
        no = min(510, 2064 - o0)
        fblocks.append((o0, no))
        o0 += no

    def load_wdown(chunks, dst_fn):
        for f in chunks:
            wpiece(dst_fn(f), [(w_down[f * 128:(f + 1) * 128, :], 0, 128, 0, 1024)], 1024, None)

    def wup_piece(gi, c, defer=False):
        f0, nf = fgroups[gi]
        wu = WUG0 if gi == 0 else WU[gi % 2]
        ncol = nf * 128
        srcs = [(w_up[c * 128:(c + 1) * 128, f0 * 128:f0 * 128 + ncol], 0, 128, 0, ncol),
                (w_up[c * 128:(c + 1) * 128, DFF + f0 * 128:DFF + f0 * 128 + ncol], 0, 128, 512, ncol)]
        sc = GFM[:, 16 + c:16 + c + 1]
        if nf == 4:
            return wpiece(wu[:, c, :], srcs, 1024, sc, defer=defer)
        slot_view = lambda slot: slot[:, :].rearrange("p (a t) -> p a t", a=2)[:, :, 0:ncol]
        return wpiece(wu[:, c, :].rearrange("p (a t) -> p a t", a=2)[:, :, 0:ncol], srcs, 1024, sc, in_view=slot_view, defer=defer)

    def load_wup_group(gi):
        for c in range(8):
            wup_piece(gi, c)

    g0pend = []

    def b3_extra(it):
        while g0pend:
            g0pend.pop(0)()
        if it < 8:
            g0pend.append(wup_piece(0, it, defer=True))

    def outproj_phase(tiles, nk, lhs_fn, rhs_fn, xsrc_fn, xdst_fn, hdst_fn, extra=None):
        def ybuf(t):
            m = tiles[t][1]
            return PS[0:m, 1024 * (t % 2):1024 * (t % 2) + 1024]

        def st_a(t):
            y = ybuf(t)
            dma(XT[t % 2][0:tiles[t][1], :], xsrc_fn(t))
            for hf in range(2):
                for c in range(nk):
                    mm(y[:, hf * 512:(hf + 1) * 512], lhs_fn(t, c), rhs_fn(c, hf), c == 0, c == nk - 1)

        def st_b1(t):
            m = tiles[t][1]
            y = ybuf(t)
            xt = XT[t % 2]
            ss = stat1(m)
            yt = YT[t % 2]
            act(yt[0:m, :], y, AF.Square, accum=ss)
            rs = stat1(m)
            rsqrt_from(rs, ss, 1.0 / 1024.0, m)
            stt("dve", yt[0:m, :], y, rs, GB[0:m, :], ALU.mult, ALU.mult)
            tt("pool", yt[0:m, :], yt[0:m, :], xt[0:m, :], ALU.add)
            dma(xdst_fn(t), yt[0:m, :])

        def st_b2(t):
            if hdst_fn is None:
                return
            m = tiles[t][1]
            yt = YT[t % 2]
            hb = HB[t % 2]
            ss = stat1(m)
            act(hb[0:m, :], yt[0:m, :], AF.Square, accum=ss)
            rs = stat1(m)
            rsqrt_from(rs, ss, 1.0 / 1024.0, m)
            ts("dve", hb[0:m, :], yt[0:m, :], rs, ALU.mult)

        def st_c(t):
            if hdst_fn is None:
                return
            tile_back(t, tiles[t][1], hdst_fn(t), "act" if t % 2 else "dve")

        stages = [st_a, st_b1, st_b2, st_c]
        for s_ in range(len(tiles) + len(stages) - 1):
            for k, st in enumerate(stages):
                t = s_ - k
                if 0 <= t < len(tiles):
                    st(t)
            if extra is not None:
                extra(s_)

    tok_tiles = col_blocks(E_LO, E_HI, 128)

    def ht_tile(t):
        e0, m = tok_tiles[t]
        return HT[:, :, e0:e0 + m]

    outproj_phase(tok_tiles, 8,
                  lambda t, c: HT[:, c, tok_tiles[t][0]:tok_tiles[t][0] + tok_tiles[t][1]],
                  lambda c, hf: WOUT[:, c, hf * 512:(hf + 1) * 512],
                  lambda t: x_ext[tok_tiles[t][0]:tok_tiles[t][0] + tok_tiles[t][1], :],
                  lambda t: xs1[tok_tiles[t][0]:tok_tiles[t][0] + tok_tiles[t][1], :],
                  ht_tile, extra=b3_extra)
    while g0pend:
        g0pend.pop(0)()

    dma(GB[:, :], gpost[1:2, :].partition_broadcast(128))
    qi = 0
    for (e0, n) in col_blocks(E_LO, E_HI):
        for oc in range(8):
            qb = bank(qi % 2, n)
            qi += 1
            for c in range(8):
                mm(qb, WMQ[:, c, oc * 128:(oc + 1) * 128], HT[:, c, e0:e0 + n], c == 0, c == 7)
            cp("act", BQ[:, oc, e0:e0 + n], qb)
        for hd in range(4):
            for mt in range(2):
                sbk = bank(2 + mt, n)
                for dc in range(2):
                    mm(sbk, KM[:, 2 * hd + dc, mt * 128:(mt + 1) * 128], BQ[:, 2 * hd + dc, e0:e0 + n], dc == 0, dc == 1)
            pt = PTX[hd % 2]
            sv = PS[:, 1024:2048].rearrange("p (b t) -> p b t", b=2)[:, :, 0:n]
            pv = pt[:, 0:1024].rearrange("p (b t) -> p b t", b=2)[:, :, 0:n]
            act(pv, sv, AF.Exp, scale=1.0 / 16.0)
            den = bank(4, n)
            for mt in range(2):
                mm(den, ONES, pt[:, 512 * mt:512 * mt + n], mt == 0, mt == 1)
            rd = TMP[hd % 2][:, 0:n]
            act(rd, den, AF.Ln)
            act(rd, rd, AF.Exp, scale=-1.0)
            for dc in range(2):
                ob = bank(5 + dc, n)
                for mt in range(2):
                    mm(ob, VM[:, mt, (2 * hd + dc) * 128:(2 * hd + dc + 1) * 128], pt[:, 512 * mt:512 * mt + n], mt == 0, mt == 1)
                tt("dve", HT[:, 2 * hd + dc, e0:e0 + n], ob, rd, ALU.mult)
    outproj_phase(tok_tiles, 8,
                  lambda t, c: HT[:, c, tok_tiles[t][0]:tok_tiles[t][0] + tok_tiles[t][1]],
                  lambda c, hf: WMO[:, c, hf * 512:(hf + 1) * 512],
                  lambda t: xs1[tok_tiles[t][0]:tok_tiles[t][0] + tok_tiles[t][1], :],
                  lambda t: xs2[tok_tiles[t][0]:tok_tiles[t][0] + tok_tiles[t][1], :],
                  ht_tile)
    ts("dve", HT[:, :, 15:16], HT[:, :, 15:16], MASK[:, 0:1], ALU.mult)
    ts("dve", HT[:, :, 2064:2065], HT[:, :, 2064:2065], MASK[:, 1:2], ALU.mult)

    dma(GB[:, :], gpost[2:3, :].partition_broadcast(128))
    WD1 = v3(BIG8[:, 0:16384], 16)
    WDX = [JUNKA, JUNKD, HB[0], HB[1]]

    def wd_full(f):
        if f < 6:
            return WDA[:, f * 1024:(f + 1) * 1024]
        if f < 10:
            return WDX[f - 6][:, 0:1024]
        return WD1[:, f - 10, :]

    fitems = []
    for gi, (f0, nf) in enumerate(fgroups):
        for fl in range(nf):
            for bi_, (o0, no) in enumerate(fblocks):
                fitems.append((gi, fl, f0 + fl, o0, no, fl * len(fblocks) + bi_))
    wpend = []

    def f_a(i):
        gi, fl, f, o0, no, k = fitems[i]
        while wpend:
            wpend.pop(0)()
        if gi + 1 < len(fgroups) and k < 8:
            wpend.append(wup_piece(gi + 1, k, defer=True))
        if gi == len(fgroups) - 1 and k < 10:
            wpend.append(wpiece(wd_full(k), [(w_down[k * 128:(k + 1) * 128, :], 0, 128, 0, 1024)],
                                1024, None, defer=True))
        wu = WUG0 if gi == 0 else WU[gi % 2]
        ug = bank(2 * (i % 3), no + 2)
        uv = bank(2 * (i % 3) + 1, no + 2)
        for c in range(8):
            mm(ug, wu[:, c, fl * 128:(fl + 1) * 128], HT[:, c, o0 - 1:o0 + no + 1], c == 0, c == 7)
        for c in range(8):
            mm(uv, wu[:, c, 512 + fl * 128:512 + (fl + 1) * 128], HT[:, c, o0 - 1:o0 + no + 1], c == 0, c == 7)

    def f_b(i):
        gi, fl, f, o0, no, k = fitems[i]
        pg = FFNP[:, f * 4:f * 4 + 4]
        pv_ = FFNP[:, (NFC + f) * 4:(NFC + f) * 4 + 4]
        ug = bank(2 * (i % 3), no + 2)
        uv = bank(2 * (i % 3) + 1, no + 2)
        tg = TMP[i % 3][:, 0:no]
        tv = TMP[3 + i % 3][:, 0:no]
        act(tg, ug[:, 1:1 + no], AF.Identity, bias=pg[:, 3:4], scale=pg[:, 1:2])
        act(tv, uv[:, 1:1 + no], AF.Identity, bias=pv_[:, 3:4], scale=pv_[:, 1:2])
        stt("dve", tg, ug[:, 0:no], pg[:, 0:1], tg, ALU.mult, ALU.add)
        stt("dve", tg, ug[:, 2:2 + no], pg[:, 2:3], tg, ALU.mult, ALU.add)
        stt("dve", tv, uv[:, 0:no], pv_[:, 0:1], tv, ALU.mult, ALU.add)
        stt("dve", tv, uv[:, 2:2 + no], pv_[:, 2:3], tv, ALU.mult, ALU.add)

    def f_c(i):
        gi, fl, f, o0, no, k = fitems[i]
        tg = TMP[i % 3][:, 0:no]
        tv = TMP[3 + i % 3][:, 0:no]
        act(tg, tg, AF.Gelu_apprx_tanh)
        tt("pool", GT[:, f, o0 - 16:o0 - 16 + no], tg, tv, ALU.mult)

    fst = [f_a, f_b, f_c]
    for s_ in range(len(fitems) + 2):
        for k, st in enumerate(fst):
            t = s_ - k
            if 0 <= t < len(fitems):
                st(t)
    while wpend:
        wpend.pop(0)()
    load_wdown(range(10, NFC), wd_full)

    def wd(f, hf):
        return wd_full(f)[:, hf * 512:(hf + 1) * 512]

    ffn_tiles = [(16 + ti * 128, 128) for ti in range(16)]
    outproj_phase(ffn_tiles, NFC,
                  lambda t, c: GT[:, c, t * 128:(t + 1) * 128],
                  wd,
                  lambda t: xs2[16 + t * 128:16 + (t + 1) * 128, :],
                  lambda t: outd[t * 128:(t + 1) * 128, :],
                  None)

    S.emit(es)
    es.close()
    return nc


def _rope_tables(pos):
    pos = np.asarray(pos)
    inv = (10000.0 ** (-(np.arange(0, 32, 2, dtype=np.float32)) / np.float32(32))).astype(np.float32)
    r = (pos // 64).astype(np.float32)
    c = (pos % 64).astype(np.float32)
    ang_r = r[None, :] * inv[:, None]
    ang_c = c[None, :] * inv[:, None]
    ang = np.concatenate([ang_r, ang_r, ang_c, ang_c], axis=0).astype(np.float32)
    ang = np.concatenate([ang, ang], axis=0)
    return np.stack([np.cos(ang), np.sin(ang)]).astype(np.float32)


def _consts():
    ident = np.eye(128, dtype=np.float32)
    perm = np.zeros((128, 128), np.float32)
    for m in range(128):
        if (m % 32) < 16:
            perm[m + 16, m] = -1.0
        else:
            perm[m - 16, m] = 1.0
    bones = np.zeros((128, 128), np.float32)
    bones[:64, :64] = 1.0
    bones[64:, 64:] = 1.0
    ones = np.ones((128, 128), np.float32)
    return np.ascontiguousarray(np.concatenate([ident, perm, bones, ones], axis=1))


_NC_CACHE = {}


def kernel(x, mem, norm_mix_pre, w_in, conv_dw, conv_dw_b, conv_ln_g, conv_ln_b,
           q_norm_g, k_norm_g, w_out, norm_mix_post, norm_mem_pre, mem_norm_g,
           w_mem_q, w_mem_kv, w_mem_o, norm_mem_post, norm_ffn_pre, w_up, ffn_dw,
           ffn_dw_b, w_down, norm_ffn_post):
    f = lambda a: np.ascontiguousarray(np.asarray(a, dtype=np.float32))
    x = f(x); mem = f(mem)
    B, Sq, D = x.shape

    def fm(g):
        return f(g).reshape(8, 128).T
    gfm = np.ascontiguousarray(np.concatenate([fm(norm_mix_pre[0]), fm(norm_mem_pre[0]), fm(norm_ffn_pre[0]), fm(mem_norm_g[0])], axis=1))
    gpost = np.ascontiguousarray(np.stack([f(norm_mix_post[0]), f(norm_mem_post[0]), f(norm_ffn_post[0])]))
    cw = f(conv_dw[0])
    convp = np.zeros((128, 4, 34), np.float32)
    for ci in range(4):
        convp[:, ci, 0:31] = cw[:, ci * 128:(ci + 1) * 128].T
        convp[:, ci, 31] = f(conv_dw_b[0])[ci * 128:(ci + 1) * 128]
        convp[:, ci, 32] = f(conv_ln_g[0])[ci * 128:(ci + 1) * 128]
        convp[:, ci, 33] = f(conv_ln_b[0])[ci * 128:(ci + 1) * 128]
    convp = np.ascontiguousarray(convp.reshape(128, 136))
    qkg = np.ascontiguousarray(np.stack([np.tile(f(q_norm_g[0]), 2), np.tile(f(k_norm_g[0]), 2)], axis=1))
    fw = f(ffn_dw[0])
    fb = f(ffn_dw_b[0])
    ffnp = np.zeros((128, 44, 4), np.float32)
    for fc in range(44):
        ffnp[:, fc, 0:3] = fw[:, fc * 128:(fc + 1) * 128].T
        ffnp[:, fc, 3] = fb[fc * 128:(fc + 1) * 128]
    ffnp = np.ascontiguousarray(ffnp.reshape(128, 176))
    cst = _consts()
    shared = dict(w_in=f(w_in[0]), w_out=f(w_out[0]), w_mem_q=f(w_mem_q[0]), w_mem_kv=f(w_mem_kv[0]),
                  w_mem_o=f(w_mem_o[0]), w_up=f(w_up[0]), w_down=f(w_down[0]), gfm=gfm, gpost=gpost,
                  convp=convp, qkg=qkg, ffnp=ffnp, cst=cst)
    in_maps = []
    for core in range(8):
        b, j = core // 4, core % 4
        s = j * 2048
        xe = np.zeros((NE, 1024), np.float32)
        lo, hi = max(0, s - 16), min(Sq, s + 2064)
        xe[lo - (s - 16):hi - (s - 16)] = x[b, lo:hi]
        rest_idx = np.concatenate([np.arange(0, s), np.arange(s + 2048, Sq)])
        xr = np.ascontiguousarray(x[b, rest_idx])
        key_pos = np.concatenate([np.arange(s, s + 2048), rest_idx])
        ropek = _rope_tables(key_pos)
        ext_pos = np.clip(np.arange(s - 16, s - 16 + NE), 0, Sq - 1)
        ropeq = _rope_tables(ext_pos)
        mask = np.ones((128, 2), np.float32)
        if j == 0:
            mask[:, 0] = 0.0
        if j == 3:
            mask[:, 1] = 0.0
        m = dict(shared)
        m.update(x_ext=xe, x_rest=xr, mem=np.ascontiguousarray(mem[b]), ropek=ropek, ropeq=ropeq, mask=mask)
        in_maps.append(m)
    if "nc" not in _NC_CACHE:
        _NC_CACHE["nc"] = build_nc()
    nc = _NC_CACHE["nc"]
    res = run_bass_kernel_spmd(nc, in_maps, core_ids=list(range(8)))
    out = np.zeros((B, Sq, D), np.float32)
    for core in range(8):
        b, j = core // 4, core % 4
        out[b, j * 2048:(j + 1) * 2048] = np.asarray(res.results[core]["out"], dtype=np.float32)
    return out
```

```python
import numpy as np
from contextlib import ExitStack
import concourse.bass as bass
import concourse.mybir as mybir
from concourse.bass_utils import run_bass_kernel_spmd

F32 = mybir.dt.float32
BF16 = mybir.dt.bfloat16
AF = mybir.ActivationFunctionType
ALU = mybir.AluOpType

EPS = 1e-6
NE = 2080
E_LO, E_HI = 15, 2065
DFF = 2816
NFC = 22


def _esize(dt):
    return 2 if dt == BF16 else 4


class Sched:
    ENGS = ["pe", "act", "dve", "pool", "sp"]
    NDMA = 16

    def __init__(self, nc, tracked_dram=()):
        self.nc = nc
        self.ops = []
        self.w = {}
        self.r = {}
        self.tracked_dram = set(tracked_dram)

    def region(self, ap):
        t = ap.tensor
        name = t.name
        space = str(ap.space)
        if "DRAM" in space.upper() or "HBM" in space.upper() or type(t).__name__.startswith("DRam"):
            if name not in self.tracked_dram:
                return None
        es = _esize(ap.dtype)
        dims = ap.ap
        ps, pc = dims[0]
        off = int(ap.offset)
        if ps == 0:
            rs_ = int(t.shape[-1])
            p0, f0 = off // rs_, off % rs_
            p1 = p0 + 1
        else:
            p0 = off // ps
            f0 = off % ps
            p1 = p0 + pc
        ents = [(f0, 0)]
        rest = dims[1:]
        for (s, c) in rest[:-1]:
            s = abs(s)
            if s != 0 and len(ents) * c <= 64:
                ents = [(st + i * s, ex) for (st, ex) in ents for i in range(c)]
            else:
                ents = [(st, ex + (c - 1) * s) for (st, ex) in ents]
        if rest:
            s, c = rest[-1]
            ents = [(st, ex + (c - 1) * abs(s) + 1) for (st, ex) in ents]
        else:
            ents = [(st, ex + 1) for (st, ex) in ents]
        ivs = tuple(sorted((st * es, (st + ex) * es) for (st, ex) in ents))
        return (name, p0, p1, ivs)

    @staticmethod
    def _ov(a, b):
        if a[1] >= b[2] or b[1] >= a[2]:
            return False
        for (s0, e0) in a[3]:
            for (s1, e1) in b[3]:
                if s0 < e1 and s1 < e0:
                    return True
        return False

    @staticmethod
    def _covers(a, b):
        if len(a[3]) != 1:
            return a[1] <= b[1] and a[2] >= b[2] and a[3] == b[3]
        if a[1] > b[1] or a[2] < b[2]:
            return False
        s0, e0 = a[3][0]
        return all(s0 <= s1 and e1 <= e0 for (s1, e1) in b[3])

    def add(self, eng, fn, reads=(), writes=(), dma=False):
        op = {"eng": eng, "fn": fn, "deps": set(), "idx": len(self.ops), "inc": False, "dma": dma}
        for ap in reads:
            rg = self.region(ap)
            if rg is None:
                continue
            for key, d in self.w.get(rg[0], {}).items():
                if self._ov(rg, key):
                    op["deps"].update(d.values())
            self.r.setdefault(rg[0], {}).setdefault(rg, {})[eng] = op["idx"]
        for ap in writes:
            rg = self.region(ap)
            if rg is None:
                continue
            for table in (self.w, self.r):
                tb = table.get(rg[0], {})
                dead = []
                for key, d in tb.items():
                    if self._ov(rg, key):
                        op["deps"].update(d.values())
                        if self._covers(rg, key):
                            dead.append(key)
                for k in dead:
                    del tb[k]
            self.w.setdefault(rg[0], {})[rg] = {eng: op["idx"]}
        op["deps"].discard(op["idx"])
        self.ops.append(op)
        return op

    def emit(self, es):
        nc = self.nc
        ops = self.ops
        for op in ops:
            op["deps"] = {d for d in op["deps"] if not (op["eng"] == "pe" and ops[d]["eng"] == "pe" and not ops[d]["dma"])}
            latest = {}
            keep = set()
            for d in op["deps"]:
                p = ops[d]
                if p["dma"]:
                    keep.add(d)
                else:
                    latest[p["eng"]] = max(latest.get(p["eng"], -1), d)
            op["deps"] = keep | set(latest.values())
            for d in op["deps"]:
                ops[d]["inc"] = True
        esem = {e: es.enter_context(nc.semaphore("sem_" + e)) for e in self.ENGS}
        dsem = [es.enter_context(nc.semaphore("dsem%d" % i)) for i in range(self.NDMA)]
        cnt = {e: 0 for e in self.ENGS}
        ndma = 0
        last_out_dma = []
        for op in ops:
            if op["dma"]:
                op["dsem"] = ndma % self.NDMA
                op["dcnt"] = 16 * (ndma // self.NDMA + 1)
                ndma += 1
            elif op["inc"]:
                cnt[op["eng"]] += 1
                op["cnt"] = cnt[op["eng"]]
        self.ndma = ndma
        block = es.enter_context(nc.Block())

        def stream(engname, e):
            seen = {}

            def wait(sem, key, val):
                if seen.get(key, 0) >= val:
                    return
                seen[key] = val
                e.wait_ge(sem, val)

            for op in ops:
                if op["eng"] != engname:
                    continue
                need = {}
                for d in op["deps"]:
                    p = ops[d]
                    if p["dma"]:
                        k = ("d", p["dsem"])
                        need[k] = max(need.get(k, 0), p["dcnt"])
                    else:
                        k = ("e", p["eng"])
                        need[k] = max(need.get(k, 0), p["cnt"])
                if op["dma"] and op["dcnt"] > 16:
                    k = ("d", op["dsem"])
                    need[k] = max(need.get(k, 0), op["dcnt"] - 16)
                for k, v in need.items():
                    wait(dsem[k[1]] if k[0] == "d" else esem[k[1]], k, v)
                ins = op["fn"](e)
                if op["dma"]:
                    ins.then_inc(dsem[op["dsem"]], 16)
                elif op["inc"]:
                    ins.then_inc(esem[op["eng"]], 1)
            if engname == "sp":
                for i in range(min(self.NDMA, ndma)):
                    n_i = (ndma - 1 - i) // self.NDMA + 1
                    wait(dsem[i], ("d", i), 16 * n_i)

        @block.tensor
        def _(e):
            stream("pe", e)

        @block.scalar
        def _(e):
            stream("act", e)

        @block.vector
        def _(e):
            stream("dve", e)

        @block.gpsimd
        def _(e):
            stream("pool", e)

        @block.sync
        def _(e):
            stream("sp", e)


def col_blocks(lo, hi, w=512):
    out = []
    while lo < hi:
        n = min(w, hi - lo)
        out.append((lo, n))
        lo += n
    return out


def build_nc(debug=False):
    nc = bass.Bass("TRN2", target_bir_lowering=False)
    es = ExitStack()

    def di(name, shape, dt=F32):
        return nc.dram_tensor(name, shape, dt, kind="ExternalInput").ap()

    x_ext = di("x_ext", [NE, 1024])
    x_rest = di("x_rest", [6144, 1024])
    memd = di("mem", [256, 1024])
    w_in = di("w_in", [1024, 1792])
    w_out = di("w_out", [1024, 1024])
    w_mem_q = di("w_mem_q", [1024, 1024])
    w_mem_kv = di("w_mem_kv", [1024, 2048])
    w_mem_o = di("w_mem_o", [1024, 1024])
    w_up = di("w_up", [1024, 2 * DFF])
    w_down = di("w_down", [DFF, 1024])
    gfm = di("gfm", [128, 4 * 8])
    gpost = di("gpost", [3, 1024])
    convp = di("convp", [128, 4 * 34])
    qkg = di("qkg", [128, 2])
    ffnp = di("ffnp", [128, 44 * 4])
    cst = di("cst", [128, 512])
    ropek = di("ropek", [2, 128, 8192])
    ropeq = di("ropeq", [2, 128, NE])
    maskd = di("mask", [128, 2])
    outd = nc.dram_tensor("out", [2048, 1024], F32, kind="ExternalOutput").ap()
    xs1 = nc.dram_tensor("xs1", [NE, 1024], F32, kind="Internal").ap()
    xs2 = nc.dram_tensor("xs2", [NE, 1024], F32, kind="Internal").ap()
    rds = nc.dram_tensor("rds", [64, 512], F32, kind="Internal").ap()
    dbg = {}

    S = Sched(nc, tracked_dram=["xs1", "xs2", "out", "rds"])

    def sb(name, shape, dt=F32):
        return es.enter_context(nc.sbuf_tensor(name, shape, dt))

    BIG8 = sb("BIG8", [128, 8 * NE], BF16)
    BIGQ = sb("BIGQ", [128, 8 * NE], BF16)
    G = sb("G", [128, NFC * 2048], BF16)
    STG = [sb("STG%d" % i, [128, 1024]) for i in range(2)]
    XT = [sb("XT%d" % i, [128, 1024]) for i in range(2)]
    YT = [sb("YT%d" % i, [128, 1024]) for i in range(2)]
    HB = [sb("HB%d" % i, [128, 1024], BF16) for i in range(2)]
    TMP = [sb("TMP%d" % i, [128, 512]) for i in range(6)]
    TMPB = [sb("TMPB%d" % i, [128, 512], BF16) for i in range(4)]
    GB = sb("GB", [128, 1024])
    CST = sb("CST", [128, 512], BF16)
    ONESF = sb("ONESF", [128, 64])
    GFM = sb("GFM", [128, 32])
    CONVP = sb("CONVP", [128, 4 * 34])
    QKG = sb("QKG", [128, 2])
    FFNP = sb("FFNP", [128, 44 * 4])
    MASK = sb("MASK", [128, 2])
    STAT = sb("STAT", [128, 32])
    PS = es.enter_context(nc.psum_tensor("PS", [128, 4096], F32))

    IDENT = CST[:, 0:128]
    PERM = CST[:, 128:256]
    BONES = CST[:, 256:384]
    ONES = CST[:, 384:512]

    def bank(b, n=512):
        return PS[:, 512 * b:512 * b + n]

    def bankbf(b):
        return PS[:, 512 * b:512 * b + 512].bitcast(BF16)

    def v3(ap2d, c):
        return ap2d.rearrange("p (c t) -> p c t", c=c)

    HT = v3(BIG8[:, :], 8)
    BQ = v3(BIGQ[:, :], 8)
    AT = BQ[:, 0:4, :]
    QT = BQ[:, 4:8, :]
    KT = G[:, 0:8192]
    VV = G[:, 8192:8192 + 64 * 130].rearrange("p (t g d) -> p t g d", t=64, g=2)
    WREG = G[:, 16512:16512 + 15872]
    WIN = v3(WREG[:, 0:8 * 1792], 8)
    DIAG = WREG[:, 0:15872].rearrange("p (c k m) -> p c k m", c=4, k=31)
    KM = v3(G[:, 32384:34432], 8)
    VM = v3(G[:, 34432:36480], 2)
    MEMT = v3(G[:, 36480:38528], 8)
    WKV = v3(BIGQ[:, 0:16384], 8)
    HTB = [v3(BIGQ[:, i * 4096:(i + 1) * 4096], 8) for i in range(2)]
    WOUT = v3(BIGQ[:, 0:8192], 8)
    WMQ = v3(G[:, 16512 + 6144:16512 + 14336], 8)
    WMO = v3(G[:, 36480:44672], 8)
    WUG0 = v3(G[:, 8192:16384], 8)
    GT = v3(G[:, :], NFC)
    WU = [v3(BIGQ[:, i * 8192:(i + 1) * 8192], 8) for i in range(2)]
    PT = [WREG[:, i * 1536:(i + 1) * 1536] for i in range(4)]
    _rf = G[:, 38528:38528 + 4096].bitcast(F32)
    ROPE = [_rf[:, i * 1024:(i + 1) * 1024] for i in range(2)]
    WDA = BIGQ[:, 0:6144]
    ATT1 = TMPB[3]

    cntr = {"stg": 0, "xt": 0, "yt": 0, "hb": 0, "rope": 0, "tp": 0, "stat": 0}

    def rot(key, n):
        v = cntr[key]
        cntr[key] = (v + 1) % n
        return v

    def stat1(m):
        i = rot("stat", 32)
        return STAT[0:m, i:i + 1]

    def dma(out, in_, q="sp"):
        S.add(q, lambda e: e.dma_start(out=out, in_=in_), reads=[in_], writes=[out], dma=True)

    def mm(out, lhsT, rhs, start, stop):
        S.add("pe", lambda e: e.matmul(out, lhsT, rhs, start=start, stop=stop), reads=[lhsT, rhs], writes=[out])

    def transp(out, in_, ident):
        S.add("pe", lambda e: e.transpose(out, in_, ident), reads=[in_, ident], writes=[out])

    def act(out, in_, func, bias=None, scale=None, accum=None):
        kw = {}
        rd = [in_]
        wr = [out]
        if bias is not None:
            kw["bias"] = bias
            if not isinstance(bias, float):
                rd.append(bias)
        if scale is not None:
            kw["scale"] = scale
            if not isinstance(scale, float):
                rd.append(scale)
        if accum is not None:
            kw["accum_out"] = accum
            wr.append(accum)
        S.add("act", lambda e: e.activation(out=out, in_=in_, func=func, **kw), reads=rd, writes=wr)

    def tt(eng, out, in0, in1, op):
        S.add(eng, lambda e: e.tensor_tensor(out=out, in0=in0, in1=in1, op=op), reads=[in0, in1], writes=[out])

    def ts(eng, out, in0, s1, op0, s2=None, op1=None):
        rd = [in0] + [s for s in (s1, s2) if s is not None and not isinstance(s, float)]
        if op1 is None:
            S.add(eng, lambda e: e.tensor_scalar(out=out, in0=in0, scalar1=s1, scalar2=None, op0=op0), reads=rd, writes=[out])
        else:
            S.add(eng, lambda e: e.tensor_scalar(out=out, in0=in0, scalar1=s1, scalar2=s2, op0=op0, op1=op1), reads=rd, writes=[out])

    def stt(eng, out, in0, scalar, in1, op0, op1, accum=None):
        rd = [in0, in1] + ([] if isinstance(scalar, float) else [scalar])
        if accum is None:
            S.add(eng, lambda e: e.scalar_tensor_tensor(out=out, in0=in0, scalar=scalar, in1=in1, op0=op0, op1=op1), reads=rd, writes=[out])
        else:
            S.add(eng, lambda e: e.scalar_tensor_tensor(out=out, in0=in0, scalar=scalar, in1=in1, op0=op0, op1=op1, accum_out=accum),
                  reads=rd, writes=[out, accum])

    def cp(eng, out, in_):
        if eng == "act":
            act(out, in_, AF.Identity)
        else:
            S.add(eng, lambda e: e.tensor_copy(out=out, in_=in_), reads=[in_], writes=[out])

    def recip(out, in_):
        S.add("dve", lambda e: e.reciprocal(out=out, in_=in_), reads=[in_], writes=[out])

    def memset(eng, ap, val):
        S.add(eng, lambda e: e.memset(ap, val), writes=[ap])

    def rsqrt_from(out, in_, scale, m=None):
        act(out, in_, AF.Ln, bias=EPSB[0:out.shape[0], 0:1] if m is None else EPSB[0:m, 0:1], scale=scale)
        act(out, out, AF.Exp, scale=-0.5)

    def wpiece(dst, srcs, ncols, scal=None, eng="dve", in_view=None, defer=False):
        slot = STG[rot("stg", 2)]
        for (src, p0, p1, c0, nc_) in srcs:
            dma(slot[p0:p1, c0:c0 + nc_], src, q="pool")
        src_ap = slot[:, 0:ncols] if in_view is None else in_view(slot)

        def cast():
            if scal is None:
                cp(eng, dst, src_ap)
            elif eng == "act":
                act(dst, src_ap, AF.Identity, scale=scal)
            else:
                ts(eng, dst, src_ap, scal, ALU.mult)
        if defer:
            return cast
        cast()

    EPSB = sb("EPSB", [128, 1])
    memset("pool", EPSB[:, :], EPS)
    memset("pool", ONESF[:, :], 1.0)
    wpiece(CST[:, :], [(cst[:, :], 0, 128, 0, 512)], 512, None, eng="dve")
    dma(GFM[:, :], gfm[:, :])
    dma(CONVP[:, :], convp[:, :])
    dma(QKG[:, :], qkg[:, :])
    dma(FFNP[:, :], ffnp[:, :])
    dma(MASK[:, :], maskd[:, :])
    memset("pool", VV[:, :, :, 64:65], 1.0)

    def load_weight_rows(dst3, src, ncols_total, gcol, col_pieces=None, rows_of_chunk=None, nchunks=8):
        for c in range(nchunks):
            for (c0, ncol) in (col_pieces or col_blocks(0, ncols_total, 1024)):
                if rows_of_chunk is None:
                    srcs = [(src[c * 128:(c + 1) * 128, c0:c0 + ncol], 0, 128, 0, ncol)]
                else:
                    srcs = [(src[r0:r0 + nr, c0:c0 + ncol], p0, p0 + nr, 0, ncol) for (r0, nr, p0) in rows_of_chunk(c)]
                scal = None if gcol is None else GFM[:, gcol * 8 + c:gcol * 8 + c + 1]
                wpiece(dst3[:, c, c0:c0 + ncol], srcs, ncol, scal)

    def norm_to_T(src_tile, m, dst, evac_eng):
        ss = stat1(m)
        hb = HB[rot("hb", 2)]
        act(hb[0:m, :], src_tile, AF.Square, accum=ss)
        rs = stat1(m)
        rsqrt_from(rs, ss, 1.0 / 1024.0, m)
        ts("dve", hb[0:m, :], src_tile, rs, ALU.mult)
        tb = 6 + rot("tp", 2)
        tpv = v3(bankbf(tb), 8)
        for c in range(8):
            transp(tpv[:, c, 0:m], hb[0:m, c * 128:(c + 1) * 128], IDENT[0:m, 0:m])
        cp(evac_eng, dst, tpv[:, :, 0:m])

    def nr1(src, n, gain, b_ss, b_rot):
        xg = TMPB[0][:, 0:n]
        sq = TMPB[1][:, 0:n]
        act(xg, src, AF.Identity, scale=gain)
        act(sq, src, AF.Square)
        mm(bank(b_ss, n), BONES, sq, True, True)
        mm(bank(b_rot, n), PERM, xg, True, True)

    def nr2(n, cos, sin, b_ss, b_rot):
        xg = TMPB[0][:, 0:n]
        rs = TMP[0][:, 0:n]
        act(rs, bank(b_ss, n), AF.Ln, bias=EPSB[:, 0:1], scale=1.0 / 64.0)
        act(rs, rs, AF.Exp, scale=-0.5)
        tt("dve", TMP[1][:, 0:n], xg, cos, ALU.mult)
        tt("dve", TMP[2][:, 0:n], bank(b_rot, n), sin, ALU.mult)

    def nr3(n, dst):
        t1 = TMP[1][:, 0:n]
        tt("pool", t1, t1, TMP[2][:, 0:n], ALU.add)
        tt("pool", dst, t1, TMP[0][:, 0:n], ALU.mult)

    def normrope(src, n, gain, cos, sin, dst, b_ss, b_rot):
        nr1(src, n, gain, b_ss, b_rot)
        nr2(n, cos, sin, b_ss, b_rot)
        nr3(n, dst)

    def phase_c0():
      for mt in range(2):
          xt = XT[rot("xt", 2)]
          dma(xt[:, :], memd[mt * 128:(mt + 1) * 128, :])
          norm_to_T(xt[:, :], 128, MEMT[:, :, mt * 128:(mt + 1) * 128], "dve")
      for oc in range(8):
          pb = bank(oc % 2, 256)
          for c in range(8):
              mm(pb, WKV[:, c, oc * 128:(oc + 1) * 128], MEMT[:, c, :], c == 0, c == 7)
          cp("act", KM[:, oc, :], pb)
      for mt in range(2):
          for hf in range(2):
              pb = bank(2 + (mt * 2 + hf) % 2)
              for c in range(8):
                  mm(pb, MEMT[:, c, mt * 128:(mt + 1) * 128], WKV[:, c, 1024 + hf * 512:1024 + (hf + 1) * 512], c == 0, c == 7)
              cp("dve", VM[:, mt, hf * 512:(hf + 1) * 512], pb)

    def load_win_chunk(c):
        sc = GFM[:, c:c + 1]
        wpiece(WIN[:, c, 0:1024], [(w_in[c * 128:(c + 1) * 128, 0:1024], 0, 128, 0, 1024)], 1024, sc)
        slot_view = lambda slot: slot[:, 0:512].rearrange("p (h j d) -> p j h d", h=2, j=4)
        wpiece(WIN[:, c, 1024:1536].rearrange("p (j h d) -> p j h d", j=4, h=2),
               [(w_in[c * 128:(c + 1) * 128, 1024:1792], 0, 128, 0, 768)], 512, sc, in_view=slot_view)
        last = STG[(cntr["stg"] + 1) % 2]
        ts("dve", WIN[:, c, 1536:1792], last[:, 512:768], sc, ALU.mult)

    def win_piece(c, part):
        sc = GFM[:, c:c + 1]
        if part == 0:
            return wpiece(WIN[:, c, 0:1024], [(w_in[c * 128:(c + 1) * 128, 0:1024], 0, 128, 0, 1024)], 1024, sc, defer=True)
        slot = STG[rot("stg", 2)]
        dma(slot[:, 0:768], w_in[c * 128:(c + 1) * 128, 1024:1792], q="pool")

        def cast():
            ts("dve", WIN[:, c, 1024:1536].rearrange("p (j h d) -> p j h d", j=4, h=2),
               slot[:, 0:512].rearrange("p (h j d) -> p j h d", h=2, j=4), sc, ALU.mult)
            ts("dve", WIN[:, c, 1536:1792], slot[:, 512:768], sc, ALU.mult)
        return cast

    XQ = [XT[0], XT[1], YT[0], YT[1]]

    def tile_load(t, src, m):
        dma(XQ[t % 4][0:m, :], src)

    JUNKA = sb("JUNKA", [128, 1024], BF16)
    JUNKD = sb("JUNKD", [128, 1024], BF16)
    tstat = {}

    def tile_f1(t, m):
        xt = XQ[t % 4]
        ss = stat1(m)
        tstat[t] = ss
        if t % 2 == 0:
            act(JUNKA[0:m, :], xt[0:m, :], AF.Square, accum=ss)
        else:
            stt("dve", JUNKD[0:m, :], xt[0:m, :], 1.0, xt[0:m, :], ALU.mult, ALU.mult, accum=ss)

    def tile_f2(t, m):
        xt = XQ[t % 4]
        hb = HB[t % 2]
        rs = stat1(m)
        rsqrt_from(rs, tstat[t], 1.0 / 1024.0, m)
        ts("dve", hb[0:m, :], xt[0:m, :], rs, ALU.mult)

    def tile_back(t, m, dst, evac_eng):
        hb = HB[t % 2]
        tpv = v3(bankbf(6 + t % 2), 8)
        for c in range(8):
            transp(tpv[:, c, 0:m], hb[0:m, c * 128:(c + 1) * 128], IDENT[0:m, 0:m])
        cp(evac_eng, dst, tpv[:, :, 0:m])

    def kv_job(hsrc, kb):
        i = kb
        rp_ = ROPE[i % 2]
        kp = bank(i % 2)
        vp = bank(2 + i % 2)

        def P():
            dma(rp_[:, 0:512], ropek[0, :, kb * 512:(kb + 1) * 512])
            dma(rp_[:, 512:1024], ropek[1, :, kb * 512:(kb + 1) * 512])
            for c in range(8):
                mm(kp, WIN[:, c, 1536:1664], hsrc[:, c, :], c == 0, c == 7)
            for t in range(4):
                for c in range(8):
                    mm(vp[:, t * 128:(t + 1) * 128], hsrc[:, c, t * 128:(t + 1) * 128], WIN[:, c, 1664:1792], c == 0, c == 7)

        def N1():
            nr1(kp, 512, QKG[:, 1:2], 4, 5)
            cp("act", VV[:, kb * 4:kb * 4 + 4, :, 0:64], vp.rearrange("p (t g d) -> p t g d", t=4, g=2))

        def N2():
            nr2(512, rp_[:, 0:512], rp_[:, 512:1024], 4, 5)

        def N3():
            nr3(512, KT[:, kb * 512:(kb + 1) * 512])
        return [P, N1, N2, N3]

    def run_tiles(tiles, jobs_ready, extra=None):
        active = []

        def step_jobs(k):
            for _ in range(k):
                if active:
                    active[0].pop(0)()
                    if not active[0]:
                        active.pop(0)
        for t0 in range(min(3, len(tiles))):
            tile_load(t0, tiles[t0][0], tiles[t0][1])
        nt = len(tiles)
        tile_f1(0, tiles[0][1])
        if nt > 1:
            tile_f1(1, tiles[1][1])
        tile_f2(0, tiles[0][1])
        for t in range(nt):
            if t + 3 < nt:
                tile_load(t + 3, tiles[t + 3][0], tiles[t + 3][1])
            if t + 2 < nt:
                tile_f1(t + 2, tiles[t + 2][1])
            if t + 1 < nt:
                tile_f2(t + 1, tiles[t + 1][1])
            tile_back(t, tiles[t][1], tiles[t][2], "act" if t % 2 else "dve")
            if extra is not None:
                extra(t)
            for job in jobs_ready.get(t, []):
                active.append(job)
            step_jobs(2 if len(active) > 1 else 1)
        while active:
            step_jobs(1)

    ext_tiles = col_blocks(0, NE, 128)
    tiles = [(x_ext[e0:e0 + m, :], m, HT[:, :, e0:e0 + m]) for (e0, m) in ext_tiles]
    _kj = [kv_job(HT[:, :, 16 + kb * 512:16 + (kb + 1) * 512], kb) for kb in range(4)]
    jobs = {8: [_kj[0], _kj[1]], 12: [_kj[2]], 16: [_kj[3]]}
    kv_pieces = [(c, c0) for c in range(8) for c0 in (0, 1024)]
    cpend = []

    def ext_extra(t):
        while cpend:
            cpend.pop(0)()
        if t < 8:
            cpend.append(win_piece(t, 0))
            cpend.append(win_piece(t, 1))
            return
        for _ in range(2):
            if kv_pieces:
                c, c0 = kv_pieces.pop(0)
                cpend.append(wpiece(WKV[:, c, c0:c0 + 1024], [(w_mem_kv[c * 128:(c + 1) * 128, c0:c0 + 1024], 0, 128, 0, 1024)],
                                    1024, GFM[:, 24 + c:24 + c + 1], defer=True))

    run_tiles(tiles, jobs, extra=ext_extra)
    while cpend:
        cpend.pop(0)()
    assert not kv_pieces
    phase_c0()
    tiles = []
    jobs = {}
    for rb in range(12):
        for t in range(4):
            r0 = rb * 512 + t * 128
            tiles.append((x_rest[r0:r0 + 128, :], 128, HTB[rb % 2][:, :, t * 128:(t + 1) * 128]))
        jobs[4 * rb + 3] = [kv_job(HTB[rb % 2], 4 + rb)]
    run_tiles(tiles, jobs)

    qitems = [(bk, e0, n, j) for bk, (e0, n) in enumerate(col_blocks(E_LO, E_HI)) for j in range(4)]

    def q_a(i):
        bk, e0, n, j = qitems[i]
        rp_ = ROPE[bk % 2]
        if j == 0:
            dma(rp_[:, 0:n], ropeq[0, :, e0:e0 + n])
            dma(rp_[:, 512:512 + n], ropeq[1, :, e0:e0 + n])
        qp = bank(6 + i % 2, n)
        for c in range(8):
            mm(qp, WIN[:, c, 1024 + j * 128:1024 + (j + 1) * 128], HT[:, c, e0:e0 + n], c == 0, c == 7)

    def q_n(i):
        bk, e0, n, j = qitems[i]
        rp_ = ROPE[bk % 2]
        normrope(bank(6 + i % 2, n), n, QKG[:, 0:1], rp_[:, 0:n], rp_[:, 512:512 + n], QT[:, j, e0:e0 + n], 4, 5)

    q_a(0)
    for i in range(len(qitems)):
        if i + 1 < len(qitems):
            q_a(i + 1)
        q_n(i)
    bi = 0
    for (e0, n) in col_blocks(0, NE):
        for ci in range(4):
            av = bank(2 * (bi % 2), n)
            ag = bank(2 * (bi % 2) + 1, n)
            bi += 1
            for c in range(8):
                mm(av, WIN[:, c, ci * 128:(ci + 1) * 128], HT[:, c, e0:e0 + n], c == 0, c == 7)
            for c in range(8):
                mm(ag, WIN[:, c, 512 + ci * 128:512 + (ci + 1) * 128], HT[:, c, e0:e0 + n], c == 0, c == 7)
            sg = TMP[3 + bi % 2][:, 0:n]
            act(sg, ag, AF.Sigmoid)
            tt("dve", AT[:, ci, e0:e0 + n], av, sg, ALU.mult)

    for ci in range(4):
        for k in range(31):
            ts("dve", DIAG[:, ci, k, :], IDENT, CONVP[:, ci * 34 + k:ci * 34 + k + 1], ALU.mult)
    cblocks = col_blocks(E_LO, E_HI)

    def cv_bank(cbi, ci):
        return ([0, 1, 2, 3] if cbi % 2 == 0 else [6, 7, 2, 3])[ci]

    def conv_chunk(cbi, ci):
        e0, n = cblocks[cbi]
        cv = bank(cv_bank(cbi, ci), n)
        for k in range(31):
            mm(cv, DIAG[:, ci, k, :], AT[:, ci, e0 + k - 15:e0 + k - 15 + n], k == 0, k == 30)
        bcol = CONVP[:, ci * 34 + 31:ci * 34 + 32]
        act(TMPB[ci % 2][:, 0:n], cv, AF.Identity, bias=bcol)
        act(TMPB[2 + ci % 2][:, 0:n], cv, AF.Square, bias=bcol)

    def conv_stats(cbi, ci):
        e0, n = cblocks[cbi]
        mm(bank(4, n), ONES, TMPB[ci % 2][:, 0:n], ci == 0, ci == 3)
        mm(bank(5, n), ONES, TMPB[2 + ci % 2][:, 0:n], ci == 0, ci == 3)

    def conv_tail(cbi):
        e0, n = cblocks[cbi]
        sm = bank(4, n)
        sq_ = bank(5, n)
        mean = TMP[0][:, 0:n]
        ts("dve", mean, sm, 1.0 / 512.0, ALU.mult)
        msq = TMP[1][:, 0:n]
        tt("dve", msq, mean, mean, ALU.mult)
        var = TMP[2][:, 0:n]
        stt("dve", var, sq_, 1.0 / 512.0, msq, ALU.mult, ALU.subtract)
        act(var, var, AF.Ln, bias=EPSB[:, 0:1], scale=1.0)
        act(var, var, AF.Exp, scale=-0.5)
        for ci in range(4):
            cv = bank(cv_bank(cbi, ci), n)
            bcol = CONVP[:, ci * 34 + 31:ci * 34 + 32]
            t1 = TMP[3 + ci % 2][:, 0:n]
            stt("dve", t1, cv, bcol, mean, ALU.add, ALU.subtract)
            tt("pool", t1, t1, var, ALU.mult)
            act(HT[:, ci, e0:e0 + n], t1, AF.Silu, bias=CONVP[:, ci * 34 + 33:ci * 34 + 34],
                scale=CONVP[:, ci * 34 + 32:ci * 34 + 33])

    for cbi in range(len(cblocks)):
        conv_chunk(cbi, 0)
        if cbi > 0:
            conv_tail(cbi - 1)
        conv_stats(cbi, 0)
        for ci in range(1, 4):
            conv_chunk(cbi, ci)
            conv_stats(cbi, ci)
    conv_tail(len(cblocks) - 1)

    def wout_rows(c):
        if c < 4:
            return [(c * 128, 128, 0)]
        j = c - 4
        return [(512 + 64 * j, 64, 0), (512 + 64 * (4 + j), 64, 64)]
    dma(GB[:, :], gpost[0:1, :].partition_broadcast(128))
    att_pieces = []
    for c_ in range(8):
        att_pieces.append(lambda c=c_: wpiece(WOUT[:, c, :], [(w_out[r0:r0 + nr, :], p0, p0 + nr, 0, 1024) for (r0, nr, p0) in wout_rows(c)],
                                              1024, None, defer=True))
    for c_ in range(8):
        att_pieces.append(lambda c=c_: wpiece(WMQ[:, c, :], [(w_mem_q[c * 128:(c + 1) * 128, :], 0, 128, 0, 1024)], 1024,
                                              GFM[:, 8 + c:8 + c + 1], defer=True))
    for c_ in range(8):
        att_pieces.append(lambda c=c_: wpiece(WMO[:, c, :], [(w_mem_o[c * 128:(c + 1) * 128, :], 0, 128, 0, 1024)], 1024, None, defer=True))
    apend = []

    PTX = [WREG[:, i * 2048:(i + 1) * 2048] for i in range(2)]
    PTY = [WREG[:, 4096 + i * 1024:4096 + (i + 1) * 1024] for i in range(2)]
    groups = [(j, e0, n) for j in range(4) for (e0, n) in col_blocks(E_LO, E_HI)]
    batches = []
    nxy = {"X": 0, "Y": 0}
    for gi in range(len(groups)):
        kt = 0
        turn = "X"
        while kt < 64:
            if turn == "X" and kt + 2 <= 64:
                kts = [kt, kt + 1]
                kind = "X"
            else:
                kts = [kt]
                kind = "Y"
            kt += len(kts)
            batches.append((gi, kts, kind, nxy[kind], kt == 64))
            nxy[kind] += 1
            turn = "Y" if turn == "X" else "X"

    def sbank(kind, i, h):
        return (2 * i + h) if kind == "X" else (4 + h)

    def emit_qk(bn):
        gi, kts, kind, ser, last = batches[bn]
        j, e0, n = groups[gi]
        for i, kt in enumerate(kts):
            for h in range(2):
                mm(bank(sbank(kind, i, h), n), KT[64 * h:64 * h + 64, kt * 128:(kt + 1) * 128],
                   QT[64 * h:64 * h + 64, j, e0:e0 + n], True, True)

    def emit_exp(bn):
        gi, kts, kind, ser, last = batches[bn]
        j, e0, n = groups[gi]
        nit = 2 * len(kts)
        b0 = 0 if kind == "X" else 4
        pt = (PTX if kind == "X" else PTY)[ser % 2]
        sv = PS[:, 512 * b0:512 * (b0 + nit)].rearrange("p (b t) -> p b t", b=nit)[:, :, 0:n]
        pv = pt[:, 0:512 * nit].rearrange("p (b t) -> p b t", b=nit)[:, :, 0:n]
        act(pv, sv, AF.Exp, scale=0.125)

    def emit_pv(bn):
        gi, kts, kind, ser, last = batches[bn]
        j, e0, n = groups[gi]
        pt = (PTX if kind == "X" else PTY)[ser % 2]
        for i, kt in enumerate(kts):
            for h in range(2):
                it = 2 * i + h
                mm(bank(6 + h, n)[0:65, :], VV[:, kt, h, 0:65], pt[:, 512 * it:512 * it + n], kt == 0, kt == 63)
        if last:
            for h in range(2):
                cp("dve", TMP[h][0:65, 0:n], bank(6 + h, n)[0:65, :])
            for h in range(2):
                recip(TMP[2 + h][64:65, 0:n], TMP[h][64:65, 0:n])
            for h in range(2):
                osb = TMP[h][0:65, 0:n]
                rd = TMP[2 + h][64:65, 0:n]
                rdb = TMP[4 + h][0:64, 0:n]
                row = 2 * gi + h
                dma(rds[row:row + 1, 0:n], rd)
                dma(rdb, rds[row:row + 1, 0:n].partition_broadcast(64))
                if h == 0:
                    tt("dve", HT[0:64, 4 + j, e0:e0 + n], osb[0:64, :], rdb, ALU.mult)
                else:
                    a1_ = ATT1[0:64, 0:n]
                    tt("dve", a1_, osb[0:64, :], rdb, ALU.mult)
                    dma(HT[64:128, 4 + j, e0:e0 + n], a1_)

    pend = {"X": 0, "Y": 0}
    nq = [0]

    def try_qk():
        while nq[0] < len(batches) and pend[batches[nq[0]][2]] == 0:
            emit_qk(nq[0])
            pend[batches[nq[0]][2]] += 1
            nq[0] += 1

    try_qk()
    for bn in range(len(batches)):
        while apend:
            apend.pop(0)()
        if att_pieces:
            apend.append(att_pieces.pop(0)())
        emit_exp(bn)
        pend[batches[bn][2]] -= 1
        try_qk()
        emit_pv(bn)
    while apend:
        apend.pop(0)()
    assert not att_pieces

    fgroups = [(f0, min(4, NFC - f0)) for f0 in range(0, NFC, 4)]
    fblocks = []
    o0 = 16
    while o0 < 2064:
        no = min(510, 2064 - o0)
        fblocks.append((o0, no))
        o0 += no

    def load_wdown(chunks, dst_fn):
        for f in chunks:
            wpiece(dst_fn(f), [(w_down[f * 128:(f + 1) * 128, :], 0, 128, 0, 1024)], 1024, None)

    def wup_piece(gi, c, defer=False):
        f0, nf = fgroups[gi]
        wu = WUG0 if gi == 0 else WU[gi % 2]
        ncol = nf * 128
        srcs = [(w_up[c * 128:(c + 1) * 128, f0 * 128:f0 * 128 + ncol], 0, 128, 0, ncol),
                (w_up[c * 128:(c + 1) * 128, DFF + f0 * 128:DFF + f0 * 128 + ncol], 0, 128, 512, ncol)]
        sc = GFM[:, 16 + c:16 + c + 1]
        if nf == 4:
            return wpiece(wu[:, c, :], srcs, 1024, sc, defer=defer)
        slot_view = lambda slot: slot[:, :].rearrange("p (a t) -> p a t", a=2)[:, :, 0:ncol]
        return wpiece(wu[:, c, :].rearrange("p (a t) -> p a t", a=2)[:, :, 0:ncol], srcs, 1024, sc, in_view=slot_view, defer=defer)

    def load_wup_group(gi):
        for c in range(8):
            wup_piece(gi, c)

    g0pend = []

    def b3_extra(it):
        while g0pend:
            g0pend.pop(0)()
        if it < 8:
            g0pend.append(wup_piece(0, it, defer=True))

    def outproj_phase(tiles, nk, lhs_fn, rhs_fn, xsrc_fn, xdst_fn, hdst_fn, extra=None):
        def ybuf(t):
            m = tiles[t][1]
            return PS[0:m, 1024 * (t % 2):1024 * (t % 2) + 1024]

        def st_a(t):
            y = ybuf(t)
            dma(XT[t % 2][0:tiles[t][1], :], xsrc_fn(t))
            for hf in range(2):
                for c in range(nk):
                    mm(y[:, hf * 512:(hf + 1) * 512], lhs_fn(t, c), rhs_fn(c, hf), c == 0, c == nk - 1)

        def st_b1(t):
            m = tiles[t][1]
            y = ybuf(t)
            xt = XT[t % 2]
            ss = stat1(m)
            yt = YT[t % 2]
            act(yt[0:m, :], y, AF.Square, accum=ss)
            rs = stat1(m)
            rsqrt_from(rs, ss, 1.0 / 1024.0, m)
            stt("dve", yt[0:m, :], y, rs, GB[0:m, :], ALU.mult, ALU.mult)
            tt("pool", yt[0:m, :], yt[0:m, :], xt[0:m, :], ALU.add)
            dma(xdst_fn(t), yt[0:m, :])

        def st_b2(t):
            if hdst_fn is None:
                return
            m = tiles[t][1]
            yt = YT[t % 2]
            hb = HB[t % 2]
            ss = stat1(m)
            act(hb[0:m, :], yt[0:m, :], AF.Square, accum=ss)
            rs = stat1(m)
            rsqrt_from(rs, ss, 1.0 / 1024.0, m)
            ts("dve", hb[0:m, :], yt[0:m, :], rs, ALU.mult)

        def st_c(t):
            if hdst_fn is None:
                return
            tile_back(t, tiles[t][1], hdst_fn(t), "act" if t % 2 else "dve")

        stages = [st_a, st_b1, st_b2, st_c]
        for s_ in range(len(tiles) + len(stages) - 1):
            for k, st in enumerate(stages):
                t = s_ - k
                if 0 <= t < len(tiles):
                    st(t)
            if extra is not None:
                extra(s_)

    tok_tiles = col_blocks(E_LO, E_HI, 128)

    def ht_tile(t):
        e0, m = tok_tiles[t]
        return HT[:, :, e0:e0 + m]

    outproj_phase(tok_tiles, 8,
                  lambda t, c: HT[:, c, tok_tiles[t][0]:tok_tiles[t][0] + tok_tiles[t][1]],
                  lambda c, hf: WOUT[:, c, hf * 512:(hf + 1) * 512],
                  lambda t: x_ext[tok_tiles[t][0]:tok_tiles[t][0] + tok_tiles[t][1], :],
                  lambda t: xs1[tok_tiles[t][0]:tok_tiles[t][0] + tok_tiles[t][1], :],
                  ht_tile, extra=b3_extra)
    while g0pend:
        g0pend.pop(0)()

    dma(GB[:, :], gpost[1:2, :].partition_broadcast(128))
    qi = 0
    for (e0, n) in col_blocks(E_LO, E_HI):
        for oc in range(8):
            qb = bank(qi % 2, n)
            qi += 1
            for c in range(8):
                mm(qb, WMQ[:, c, oc * 128:(oc + 1) * 128], HT[:, c, e0:e0 + n], c == 0, c == 7)
            cp("act", BQ[:, oc, e0:e0 + n], qb)
        for hd in range(4):
            for mt in range(2):
                sbk = bank(2 + mt, n)
                for dc in range(2):
                    mm(sbk, KM[:, 2 * hd + dc, mt * 128:(mt + 1) * 128], BQ[:, 2 * hd + dc, e0:e0 + n], dc == 0, dc == 1)
            pt = PTX[hd % 2]
            sv = PS[:, 1024:2048].rearrange("p (b t) -> p b t", b=2)[:, :, 0:n]
            pv = pt[:, 0:1024].rearrange("p (b t) -> p b t", b=2)[:, :, 0:n]
            act(pv, sv, AF.Exp, scale=1.0 / 16.0)
            den = bank(4, n)
            for mt in range(2):
                mm(den, ONES, pt[:, 512 * mt:512 * mt + n], mt == 0, mt == 1)
            rd = TMP[hd % 2][:, 0:n]
            act(rd, den, AF.Ln)
            act(rd, rd, AF.Exp, scale=-1.0)
            for dc in range(2):
                ob = bank(5 + dc, n)
                for mt in range(2):
                    mm(ob, VM[:, mt, (2 * hd + dc) * 128:(2 * hd + dc + 1) * 128], pt[:, 512 * mt:512 * mt + n], mt == 0, mt == 1)
                tt("dve", HT[:, 2 * hd + dc, e0:e0 + n], ob, rd, ALU.mult)
    outproj_phase(tok_tiles, 8,
                  lambda t, c: HT[:, c, tok_tiles[t][0]:tok_tiles[t][0] + tok_tiles[t][1]],
                  lambda c, hf: WMO[:, c, hf * 512:(hf + 1) * 512],
                  lambda t: xs1[tok_tiles[t][0]:tok_tiles[t][0] + tok_tiles[t][1], :],
                  lambda t: xs2[tok_tiles[t][0]:tok_tiles[t][0] + tok_tiles[t][1], :],
                  ht_tile)
    ts("dve", HT[:, :, 15:16], HT[:, :, 15:16], MASK[:, 0:1], ALU.mult)
    ts("dve", HT[:, :, 2064:2065], HT[:, :, 2064:2065], MASK[:, 1:2], ALU.mult)

    dma(GB[:, :], gpost[2:3, :].partition_broadcast(128))
    WD1 = v3(BIG8[:, 0:16384], 16)
    WDX = [JUNKA, JUNKD, HB[0], HB[1]]

    def wd_full(f):
        if f < 6:
            return WDA[:, f * 1024:(f + 1) * 1024]
        if f < 10:
            return WDX[f - 6][:, 0:1024]
        return WD1[:, f - 10, :]

    fitems = []
    for gi, (f0, nf) in enumerate(fgroups):
        for fl in range(nf):
            for bi_, (o0, no) in enumerate(fblocks):
                fitems.append((gi, fl, f0 + fl, o0, no, fl * len(fblocks) + bi_))
    wpend = []

    def f_a(i):
        gi, fl, f, o0, no, k = fitems[i]
        while wpend:
            wpend.pop(0)()
        if gi + 1 < len(fgroups) and k < 8:
            wpend.append(wup_piece(gi + 1, k, defer=True))
        if gi == len(fgroups) - 1 and k < 10:
            wpend.append(wpiece(wd_full(k), [(w_down[k * 128:(k + 1) * 128, :], 0, 128, 0, 1024)],
                                1024, None, defer=True))
        wu = WUG0 if gi == 0 else WU[gi % 2]
        ug = bank(2 * (i % 3), no + 2)
        uv = bank(2 * (i % 3) + 1, no + 2)
        for c in range(8):
            mm(ug, wu[:, c, fl * 128:(fl + 1) * 128], HT[:, c, o0 - 1:o0 + no + 1], c == 0, c == 7)
        for c in range(8):
            mm(uv, wu[:, c, 512 + fl * 128:512 + (fl + 1) * 128], HT[:, c, o0 - 1:o0 + no + 1], c == 0, c == 7)

    def f_b(i):
        gi, fl, f, o0, no, k = fitems[i]
        pg = FFNP[:, f * 4:f * 4 + 4]
        pv_ = FFNP[:, (NFC + f) * 4:(NFC + f) * 4 + 4]
        ug = bank(2 * (i % 3), no + 2)
        uv = bank(2 * (i % 3) + 1, no + 2)
        tg = TMP[i % 3][:, 0:no]
        tv = TMP[3 + i % 3][:, 0:no]
        act(tg, ug[:, 1:1 + no], AF.Identity, bias=pg[:, 3:4], scale=pg[:, 1:2])
        act(tv, uv[:, 1:1 + no], AF.Identity, bias=pv_[:, 3:4], scale=pv_[:, 1:2])
        stt("dve", tg, ug[:, 0:no], pg[:, 0:1], tg, ALU.mult, ALU.add)
        stt("dve", tg, ug[:, 2:2 + no], pg[:, 2:3], tg, ALU.mult, ALU.add)
        stt("dve", tv, uv[:, 0:no], pv_[:, 0:1], tv, ALU.mult, ALU.add)
        stt("dve", tv, uv[:, 2:2 + no], pv_[:, 2:3], tv, ALU.mult, ALU.add)

    def f_c(i):
        gi, fl, f, o0, no, k = fitems[i]
        tg = TMP[i % 3][:, 0:no]
        tv = TMP[3 + i % 3][:, 0:no]
        act(tg, tg, AF.Gelu_apprx_tanh)
        tt("pool", GT[:, f, o0 - 16:o0 - 16 + no], tg, tv, ALU.mult)

    fst = [f_a, f_b, f_c]
    for s_ in range(len(fitems) + 2):
        for k, st in enumerate(fst):
            t = s_ - k
            if 0 <= t < len(fitems):
                st(t)
    while wpend:
        wpend.pop(0)()
    load_wdown(range(10, NFC), wd_full)

    def wd(f, hf):
        return wd_full(f)[:, hf * 512:(hf + 1) * 512]

    ffn_tiles = [(16 + ti * 128, 128) for ti in range(16)]
    outproj_phase(ffn_tiles, NFC,
                  lambda t, c: GT[:, c, t * 128:(t + 1) * 128],
                  wd,
                  lambda t: xs2[16 + t * 128:16 + (t + 1) * 128, :],
                  lambda t: outd[t * 128:(t + 1) * 128, :],
                  None)

    S.emit(es)
    es.close()
    return nc


def _rope_tables(pos):
    pos = np.asarray(pos)
    inv = (10000.0 ** (-(np.arange(0, 32, 2, dtype=np.float32)) / np.float32(32))).astype(np.float32)
    r = (pos // 64).astype(np.float32)
    c = (pos % 64).astype(np.float32)
    ang_r = r[None, :] * inv[:, None]
    ang_c = c[None, :] * inv[:, None]
    ang = np.concatenate([ang_r, ang_r, ang_c, ang_c], axis=0).astype(np.float32)
    ang = np.concatenate([ang, ang], axis=0)
    return np.stack([np.cos(ang), np.sin(ang)]).astype(np.float32)


def _consts():
    ident = np.eye(128, dtype=np.float32)
    perm = np.zeros((128, 128), np.float32)
    for m in range(128):
        if (m % 32) < 16:
            perm[m + 16, m] = -1.0
        else:
            perm[m - 16, m] = 1.0
    bones = np.zeros((128, 128), np.float32)
    bones[:64, :64] = 1.0
    bones[64:, 64:] = 1.0
    ones = np.ones((128, 128), np.float32)
    return np.ascontiguousarray(np.concatenate([ident, perm, bones, ones], axis=1))


_NC_CACHE = {}


def kernel(x, mem, norm_mix_pre, w_in, conv_dw, conv_dw_b, conv_ln_g, conv_ln_b,
           q_norm_g, k_norm_g, w_out, norm_mix_post, norm_mem_pre, mem_norm_g,
           w_mem_q, w_mem_kv, w_mem_o, norm_mem_post, norm_ffn_pre, w_up, ffn_dw,
           ffn_dw_b, w_down, norm_ffn_post):
    f = lambda a: np.ascontiguousarray(np.asarray(a, dtype=np.float32))
    x = f(x); mem = f(mem)
    B, Sq, D = x.shape

    def fm(g):
        return f(g).reshape(8, 128).T
    gfm = np.ascontiguousarray(np.concatenate([fm(norm_mix_pre[0]), fm(norm_mem_pre[0]), fm(norm_ffn_pre[0]), fm(mem_norm_g[0])], axis=1))
    gpost = np.ascontiguousarray(np.stack([f(norm_mix_post[0]), f(norm_mem_post[0]), f(norm_ffn_post[0])]))
    cw = f(conv_dw[0])
    convp = np.zeros((128, 4, 34), np.float32)
    for ci in range(4):
        convp[:, ci, 0:31] = cw[:, ci * 128:(ci + 1) * 128].T
        convp[:, ci, 31] = f(conv_dw_b[0])[ci * 128:(ci + 1) * 128]
        convp[:, ci, 32] = f(conv_ln_g[0])[ci * 128:(ci + 1) * 128]
        convp[:, ci, 33] = f(conv_ln_b[0])[ci * 128:(ci + 1) * 128]
    convp = np.ascontiguousarray(convp.reshape(128, 136))
    qkg = np.ascontiguousarray(np.stack([np.tile(f(q_norm_g[0]), 2), np.tile(f(k_norm_g[0]), 2)], axis=1))
    fw = f(ffn_dw[0])
    fb = f(ffn_dw_b[0])
    ffnp = np.zeros((128, 44, 4), np.float32)
    for fc in range(44):
        ffnp[:, fc, 0:3] = fw[:, fc * 128:(fc + 1) * 128].T
        ffnp[:, fc, 3] = fb[fc * 128:(fc + 1) * 128]
    ffnp = np.ascontiguousarray(ffnp.reshape(128, 176))
    cst = _consts()
    shared = dict(w_in=f(w_in[0]), w_out=f(w_out[0]), w_mem_q=f(w_mem_q[0]), w_mem_kv=f(w_mem_kv[0]),
                  w_mem_o=f(w_mem_o[0]), w_up=f(w_up[0]), w_down=f(w_down[0]), gfm=gfm, gpost=gpost,
                  convp=convp, qkg=qkg, ffnp=ffnp, cst=cst)
    in_maps = []
    for core in range(8):
        b, j = core // 4, core % 4
        s = j * 2048
        xe = np.zeros((NE, 1024), np.float32)
        lo, hi = max(0, s - 16), min(Sq, s + 2064)
        xe[lo - (s - 16):hi - (s - 16)] = x[b, lo:hi]
        rest_idx = np.concatenate([np.arange(0, s), np.arange(s + 2048, Sq)])
        xr = np.ascontiguousarray(x[b, rest_idx])
        key_pos = np.concatenate([np.arange(s, s + 2048), rest_idx])
        ropek = _rope_tables(key_pos)
        ext_pos = np.clip(np.arange(s - 16, s - 16 + NE), 0, Sq - 1)
        ropeq = _rope_tables(ext_pos)
        mask = np.ones((128, 2), np.float32)
        if j == 0:
            mask[:, 0] = 0.0
        if j == 3:
            mask[:, 1] = 0.0
        m = dict(shared)
        m.update(x_ext=xe, x_rest=xr, mem=np.ascontiguousarray(mem[b]), ropek=ropek, ropeq=ropeq, mask=mask)
        in_maps.append(m)
    if "nc" not in _NC_CACHE:
        _NC_CACHE["nc"] = build_nc()
    nc = _NC_CACHE["nc"]
    res = run_bass_kernel_spmd(nc, in_maps, core_ids=list(range(8)))
    out = np.zeros((B, Sq, D), np.float32)
    for core in range(8):
        b, j = core // 4, core % 4
        out[b, j * 2048:(j + 1) * 2048] = np.asarray(res.results[core]["out"], dtype=np.float32)
    return out
```
